# Optimizing a Trainium2 kernel written in Bass

```python
import jax, jax.numpy as jnp
from jax import lax
import numpy as np

D_MODEL = 1024
BATCH = 16
SEQ = 4096
DEPTH = 1
DEC_BATCH = 128
DEC_SEQ = 8
PAST_LEN = 8192
PAGE_SIZE = 128

GLA_HEADS = 4
GLA_DV = D_MODEL // 2 // GLA_HEADS
GLA_DK = GLA_DV // 2
GLA_RANK = 16
GLA_NORMALIZER = 16.0
GLA_CHUNK = 16
NSA_HEADS = 8
NSA_DH = (D_MODEL - GLA_HEADS * GLA_DV) // NSA_HEADS
NSA_KVH = 2
NSA_G = NSA_HEADS // NSA_KVH
CMP_STRIDE = 16
CMP_LEN = 2 * CMP_STRIDE
CMP_HIDDEN = 4 * NSA_DH
SLC_BLOCK = 64
SLC_TOPN = 16
SEL_FORCE = 1.0e4
WINDOW = 512
Q_BLOCK = 128
SLC_QBLOCK = 32
D_FF = 4 * D_MODEL
EPS = 1e-6
GLA_QK = GLA_HEADS * GLA_DK
GLA_V = GLA_HEADS * GLA_DV
NSA_Q = NSA_HEADS * NSA_DH
NSA_KV = 6 * NSA_KVH * NSA_DH
NSA_GATE = 3 * NSA_HEADS
IN_SIZES = (GLA_QK, GLA_QK, GLA_V, GLA_RANK, GLA_V, NSA_Q, NSA_KV, NSA_GATE)
D_IN = 2 * GLA_QK + 2 * GLA_V + GLA_RANK + NSA_Q + NSA_KV + NSA_GATE

kernel_name = 'hymba_gla_nsa_decoder_step'


def rmsnorm(x, g):
    xf = x.astype(jnp.float32)
    y = xf * lax.rsqrt(jnp.mean(xf * xf, axis=-1, keepdims=True) + EPS)
    return (y * g.astype(jnp.float32)).astype(x.dtype)


def masked_softmax(s, mask):
    s = jnp.where(mask, s, -jnp.inf)
    m = jnp.max(s, axis=-1, keepdims=True)
    m = jnp.where(jnp.isfinite(m), m, 0.0)
    p = jnp.exp(s - m)
    return p / jnp.maximum(jnp.sum(p, axis=-1, keepdims=True), 1e-30)


def alibi_slopes():
    h = np.arange(1, NSA_HEADS + 1, dtype=np.float32)
    return jnp.asarray(np.exp2(-8.0 * h / NSA_HEADS).astype(np.float32)).reshape(NSA_KVH, NSA_G)


def gla_chunked(q, k, v, logf, s0):
    f32 = jnp.float32
    B, T, H, DK = q.shape
    DV = v.shape[-1]
    C = GLA_CHUNK
    n = -(-T // C)
    pad = ((0, 0), (0, n * C - T), (0, 0), (0, 0))

    def chunks(a):
        a = jnp.pad(a.astype(f32), pad)
        return a.reshape(B, n, C, H, a.shape[-1]).transpose(1, 0, 3, 2, 4)

    qc, kc, vc, gc = chunks(q), chunks(k), chunks(v), chunks(logf)
    b = jnp.cumsum(gc, axis=3)
    b_last = b[:, :, :, -1:, :]
    q_in = qc * jnp.exp(b)
    k_in = kc * jnp.exp(-b)
    k_out = kc * jnp.exp(b_last - b)
    decay = jnp.exp(b_last[:, :, :, 0, :])
    causal = jnp.tril(jnp.ones((C, C), dtype=bool))
    a_intra = jnp.where(causal, jnp.einsum('nbhik,nbhjk->nbhij', q_in, k_in), 0.0)
    o_intra = jnp.einsum('nbhij,nbhjv->nbhiv', a_intra, vc)

    def step(s, xs):
        q_n, k_n, v_n, d_n = xs
        o_n = jnp.einsum('bhik,bhkv->bhiv', q_n, s)
        s = s * d_n[..., None] + jnp.einsum('bhjk,bhjv->bhkv', k_n, v_n)
        return s, o_n

    s_fin, o_inter = lax.scan(step, s0.astype(f32), (q_in, k_out, vc, decay))
    o = (o_intra + o_inter).transpose(1, 0, 3, 2, 4).reshape(B, n * C, H, DV)[:, :T]
    return o, s_fin


def compress(kv, pos, w1, b1, w2, b2):
    B, L = kv.shape[:2]
    nch = L // CMP_STRIDE
    ch = kv[:, :nch * CMP_STRIDE].reshape(B, nch, CMP_STRIDE, NSA_KVH, NSA_DH)
    pos = pos.reshape(2, CMP_STRIDE, 1, NSA_DH)
    w1 = w1.reshape(2, CMP_STRIDE, NSA_DH, CMP_HIDDEN)
    z_first = jnp.einsum('bcjhd,jdf->bchf', ch + pos[0], w1[0])
    z_second = jnp.einsum('bcjhd,jdf->bchf', ch + pos[1], w1[1])
    hid = jax.nn.silu(z_first[:, :-1] + z_second[:, 1:] + b1)
    return hid @ w2 + b2


def cmp_to_slc_matrix(nc, ns):
    start = np.arange(nc) * CMP_STRIDE
    bs = np.arange(ns) * SLC_BLOCK
    ov = np.minimum(start[:, None] + CMP_LEN, bs[None, :] + SLC_BLOCK) - np.maximum(start[:, None], bs[None, :])
    return (np.clip(ov, 0, None) / CMP_LEN).astype(np.float32)


def nsa_cmp_slc(q, qpos, kc, vc, k_slc, v_slc, slopes):
    f32 = jnp.float32
    B, T = q.shape[:2]
    L = k_slc.shape[1]
    nc = kc.shape[1]
    ns = -(-L // SLC_BLOCK)
    n_sel = min(SLC_TOPN, ns)
    m_map = jnp.asarray(cmp_to_slc_matrix(nc, ns))
    cmp_end = jnp.arange(nc, dtype=jnp.int32) * CMP_STRIDE + (CMP_LEN - 1)
    pad = ((0, 0), (0, ns * SLC_BLOCK - L), (0, 0), (0, 0))
    kb = jnp.pad(k_slc, pad).reshape(B, ns, SLC_BLOCK, NSA_KVH, NSA_DH)
    vb = jnp.pad(v_slc, pad).reshape(B, ns, SLC_BLOCK, NSA_KVH, NSA_DH)
    vc32 = vc.astype(f32)
    bi = jnp.arange(B)[:, None, None, None]
    hi = jnp.arange(NSA_KVH)[None, None, :, None]
    blk = jnp.arange(ns, dtype=jnp.int32)
    offs = jnp.arange(SLC_BLOCK, dtype=jnp.int32)
    sl = slopes[None, None, :, :, None]

    def block(args):
        qb, pb = args
        qbl = qb.shape[1]
        d_c = pb[:, None] - cmp_end[None, :]
        s_c = jnp.einsum('bqhgd,bchd->bqhgc', qb, kc).astype(f32) - sl * d_c[None, :, None, None, :]
        p_c = masked_softmax(s_c, (d_c >= 0)[None, :, None, None, :])
        o_c = jnp.einsum('bqhgc,bchd->bqhgd', p_c, vc32)
        imp = jnp.einsum('bqhgc,cs->bqhs', p_c, m_map)
        cur = pb // SLC_BLOCK
        valid = blk[None, :] <= cur[:, None]
        forced = (blk[None, :] == 0) | (blk[None, :] == cur[:, None]) | (blk[None, :] == cur[:, None] - 1)
        score = jnp.where((valid & forced)[None, :, None, :], SEL_FORCE,
                          jnp.where(valid[None, :, None, :], imp, -SEL_FORCE))
        _, idx = lax.top_k(score, n_sel)
        ks = kb[bi, idx, :, hi, :].reshape(B, qbl, NSA_KVH, n_sel * SLC_BLOCK, NSA_DH)
        vs = vb[bi, idx, :, hi, :].reshape(B, qbl, NSA_KVH, n_sel * SLC_BLOCK, NSA_DH)
        kpos = (idx[..., None] * SLC_BLOCK + offs).reshape(B, qbl, NSA_KVH, n_sel * SLC_BLOCK)
        d_s = pb[None, :, None, None] - kpos
        s_s = jnp.einsum('bqhgd,bqhkd->bqhgk', qb, ks).astype(f32) - sl * d_s[:, :, :, None, :]
        p_s = masked_softmax(s_s, (d_s >= 0)[:, :, :, None, :])
        o_s = jnp.einsum('bqhgk,bqhkd->bqhgd', p_s, vs.astype(f32))
        return o_c, o_s

    qbl = SLC_QBLOCK if T % SLC_QBLOCK == 0 else T
    nb = T // qbl
    qs = q.reshape(B, nb, qbl, NSA_KVH, NSA_G, NSA_DH).swapaxes(0, 1)
    o_c, o_s = lax.map(block, (qs, qpos.reshape(nb, qbl)))
    o_c = o_c.swapaxes(0, 1).reshape(B, T, NSA_KVH, NSA_G, NSA_DH)
    o_s = o_s.swapaxes(0, 1).reshape(B, T, NSA_KVH, NSA_G, NSA_DH)
    return o_c, o_s


def nsa_window(q, qpos, k_w, v_w, slopes):
    f32 = jnp.float32
    B, T = q.shape[:2]
    p_w = k_w.shape[1] - T
    qbl = Q_BLOCK if T % Q_BLOCK == 0 else T
    nb = T // qbl
    span = WINDOW + qbl
    pad = ((0, 0), (WINDOW, 0), (0, 0), (0, 0))
    kp = jnp.pad(k_w, pad)
    vp = jnp.pad(v_w, pad)
    kpos0 = qpos[0] - p_w - WINDOW
    sl = slopes[None, None, :, :, None]

    def block(args):
        qb, pq, start = args
        kb = lax.dynamic_slice_in_dim(kp, start, span, axis=1)
        vb = lax.dynamic_slice_in_dim(vp, start, span, axis=1)
        ridx = start + jnp.arange(span, dtype=jnp.int32)
        dist = pq[:, None] - (kpos0 + ridx)[None, :]
        mask = (ridx >= WINDOW)[None, :] & (dist >= 0) & (dist <= WINDOW)
        s = jnp.einsum('bqhgd,bkhd->bqhgk', qb, kb).astype(f32) - sl * dist[None, :, None, None, :]
        p = masked_softmax(s, mask[None, :, None, None, :])
        return jnp.einsum('bqhgk,bkhd->bqhgd', p, vb.astype(f32))

    starts = p_w + jnp.arange(nb, dtype=jnp.int32) * qbl
    qs = q.reshape(B, nb, qbl, NSA_KVH, NSA_G, NSA_DH).swapaxes(0, 1)
    o = lax.map(block, (qs, qpos.reshape(nb, qbl), starts))
    return o.swapaxes(0, 1).reshape(B, T, NSA_KVH, NSA_G, NSA_DH)


def decoder_layer(x, c, past_kv, win_past, gla_s0, wbuf,
                  norm_mix_pre, norm_mix_post, norm_ffn_pre, norm_ffn_post, w_ada, b_ada, w_in,
                  gla_w_gate, gla_b_gate, gla_norm, cmp_pos, cmp_w1, cmp_b1, cmp_w2, cmp_b2,
                  w_out, w_up, w_down):
    f32 = jnp.float32
    B, T, _ = x.shape
    P = past_kv.shape[1]
    qpos = P + jnp.arange(T, dtype=jnp.int32)
    slopes = alibi_slopes()

    ada = jax.nn.silu(c) @ w_ada + b_ada
    sh_m, sc_m, gt_m, sh_f, sc_f, gt_f = [a[:, None, :] for a in jnp.split(ada, 6, axis=-1)]

    h = rmsnorm(x, norm_mix_pre) * (1.0 + sc_m) + sh_m
    z = h @ w_in
    split_points = [int(s) for s in np.cumsum(IN_SIZES)[:-1]]
    zq, zk, zv, za, zr, zqn, zkv, zg = jnp.split(z, split_points, axis=-1)

    q_g = zq.reshape(B, T, GLA_HEADS, GLA_DK) * (GLA_DK ** -0.5)
    k_g = zk.reshape(B, T, GLA_HEADS, GLA_DK)
    v_g = zv.reshape(B, T, GLA_HEADS, GLA_DV)
    logf = jax.nn.log_sigmoid((za @ gla_w_gate + gla_b_gate).astype(f32)) / GLA_NORMALIZER
    o_g, s_new = gla_chunked(q_g, k_g, v_g, logf.reshape(B, T, GLA_HEADS, GLA_DK), gla_s0)
    o_g = rmsnorm(o_g.astype(x.dtype), gla_norm) * jax.nn.silu(zr).reshape(B, T, GLA_HEADS, GLA_DV)
    o_g = o_g.reshape(B, T, GLA_V)

    q_n = zqn.reshape(B, T, NSA_KVH, NSA_G, NSA_DH) * (NSA_DH ** -0.5)
    kv_new = zkv.reshape(B, T, 6, NSA_KVH, NSA_DH)
    rows_new = kv_new[:, :, :4]
    kv_all = jnp.concatenate([past_kv.astype(x.dtype), rows_new], axis=1)
    kc = compress(kv_all[:, :, 0], cmp_pos[0], cmp_w1[0], cmp_b1[0], cmp_w2[0], cmp_b2[0])
    vc = compress(kv_all[:, :, 1], cmp_pos[1], cmp_w1[1], cmp_b1[1], cmp_w2[1], cmp_b2[1])
    o_c, o_s = nsa_cmp_slc(q_n, qpos, kc, vc, kv_all[:, :, 2], kv_all[:, :, 3], slopes)
    win_all = jnp.concatenate([win_past.astype(x.dtype), kv_new[:, :, 4:]], axis=1)
    o_w = nsa_window(q_n, qpos, win_all[:, :, 0], win_all[:, :, 1], slopes)
    gts = jax.nn.sigmoid(zg.astype(f32)).reshape(B, T, NSA_KVH, NSA_G, 3)
    o_n = gts[..., 0:1] * o_c + gts[..., 1:2] * o_s + gts[..., 2:3] * o_w
    o_n = o_n.astype(x.dtype).reshape(B, T, NSA_Q)

    mix = jnp.concatenate([o_g, o_n], axis=-1) @ w_out
    x = x + gt_m * rmsnorm(mix, norm_mix_post)

    h2 = rmsnorm(x, norm_ffn_pre) * (1.0 + sc_f) + sh_f
    f = jnp.square(jax.nn.relu(h2 @ w_up)) @ w_down
    x = x + gt_f * rmsnorm(f, norm_ffn_post)

    lw = win_all.shape[1]
    win_new = jnp.pad(win_all, ((0, 0), (max(wbuf - lw, 0), 0), (0, 0), (0, 0), (0, 0)))[:, -wbuf:]
    return x, rows_new, win_new, s_new.astype(x.dtype)


def setup_inputs(seed: int = 0) -> dict:
    key = jax.random.key(seed)
    ks = jax.random.split(key, 32)
    f32 = jnp.float32
    n_pages = PAST_LEN // PAGE_SIZE
    n_used = DEC_BATCH * n_pages
    n_pool = n_used + n_used // 4
    win_buf = min(WINDOW, PAST_LEN)

    def nrm(k, shape, scale):
        return scale * jax.random.normal(k, shape, f32)

    def gain(k, n):
        return 1.0 + nrm(k, (DEPTH, n), 0.1)

    page_table = jax.random.permutation(ks[3], n_pool)[:n_used].reshape(DEC_BATCH, n_pages).astype(jnp.int32)
    return {
        'x_prompt': nrm(ks[0], (BATCH, SEQ, D_MODEL), 1.0),
        'x_sample': nrm(ks[1], (DEC_BATCH, DEC_SEQ, D_MODEL), 1.0),
        'cache_kv': nrm(ks[2], (DEPTH, n_pool, PAGE_SIZE, 4, NSA_KVH, NSA_DH), 1.0),
        'state_win': nrm(ks[4], (DEPTH, DEC_BATCH, win_buf, 2, NSA_KVH, NSA_DH), 1.0),
        'state_gla': nrm(ks[5], (DEPTH, DEC_BATCH, GLA_HEADS, GLA_DK, GLA_DV), 0.5),
        'page_table': page_table,
        'c_prompt': nrm(ks[6], (BATCH, D_MODEL), 1.0),
        'c_sample': nrm(ks[7], (DEC_BATCH, D_MODEL), 1.0),
        'norm_mix_pre': gain(ks[8], D_MODEL),
        'norm_mix_post': gain(ks[9], D_MODEL),
        'norm_ffn_pre': gain(ks[10], D_MODEL),
        'norm_ffn_post': gain(ks[11], D_MODEL),
        'w_ada': nrm(ks[12], (DEPTH, D_MODEL, 6 * D_MODEL), D_MODEL ** -0.5),
        'b_ada': nrm(ks[13], (DEPTH, 6 * D_MODEL), 0.01),
        'w_in': nrm(ks[14], (DEPTH, D_MODEL, D_IN), D_MODEL ** -0.5),
        'gla_w_gate': nrm(ks[15], (DEPTH, GLA_RANK, GLA_QK), GLA_RANK ** -0.5),
        'gla_b_gate': nrm(ks[16], (DEPTH, GLA_QK), 0.01),
        'gla_norm': gain(ks[17], GLA_DV),
        'cmp_pos': nrm(ks[18], (DEPTH, 2, CMP_LEN, NSA_DH), 0.1),
        'cmp_w1': nrm(ks[19], (DEPTH, 2, CMP_LEN, NSA_DH, CMP_HIDDEN), (CMP_LEN * NSA_DH) ** -0.5),
        'cmp_b1': nrm(ks[20], (DEPTH, 2, CMP_HIDDEN), 0.01),
        'cmp_w2': nrm(ks[21], (DEPTH, 2, CMP_HIDDEN, NSA_DH), CMP_HIDDEN ** -0.5),
        'cmp_b2': nrm(ks[22], (DEPTH, 2, NSA_DH), 0.01),
        'w_out': nrm(ks[23], (DEPTH, D_MODEL, D_MODEL), D_MODEL ** -0.5),
        'w_up': nrm(ks[24], (DEPTH, D_MODEL, D_FF), D_MODEL ** -0.5),
        'w_down': nrm(ks[25], (DEPTH, D_FF, D_MODEL), D_FF ** -0.5),
    }


def reference(x_prompt, x_sample, cache_kv, state_win, state_gla, page_table, c_prompt, c_sample,
              norm_mix_pre, norm_mix_post, norm_ffn_pre, norm_ffn_post, w_ada, b_ada, w_in,
              gla_w_gate, gla_b_gate, gla_norm, cmp_pos, cmp_w1, cmp_b1, cmp_w2, cmp_b2,
              w_out, w_up, w_down):
    dec_b = page_table.shape[0]
    bsz = x_prompt.shape[0]
    wbuf = state_win.shape[2]
    y_p, y_s = x_prompt, x_sample
    rows_p, rows_s, win_p, win_s, gla_p, gla_s = [], [], [], [], [], []
    for l in range(DEPTH):
        lw = (norm_mix_pre[l], norm_mix_post[l], norm_ffn_pre[l], norm_ffn_post[l], w_ada[l], b_ada[l],
              w_in[l], gla_w_gate[l], gla_b_gate[l], gla_norm[l], cmp_pos[l], cmp_w1[l], cmp_b1[l],
              cmp_w2[l], cmp_b2[l], w_out[l], w_up[l], w_down[l])
        past_p = jnp.zeros((bsz, 0, 4, NSA_KVH, NSA_DH), x_prompt.dtype)
        win_p0 = jnp.zeros((bsz, 0, 2, NSA_KVH, NSA_DH), x_prompt.dtype)
        gla_p0 = jnp.zeros((bsz, GLA_HEADS, GLA_DK, GLA_DV), jnp.float32)
        y_p, r_p, w_p, s_p = decoder_layer(y_p, c_prompt, past_p, win_p0, gla_p0, wbuf, *lw)
        past_s = cache_kv[l][page_table].reshape(dec_b, -1, 4, NSA_KVH, NSA_DH)
        y_s, r_s, w_s, s_s = decoder_layer(y_s, c_sample, past_s, state_win[l], state_gla[l], wbuf, *lw)
        rows_p.append(r_p)
        rows_s.append(r_s)
        win_p.append(w_p)
        win_s.append(w_s)
        gla_p.append(s_p)
        gla_s.append(s_s)
    return (y_p, y_s, jnp.stack(rows_p), jnp.stack(rows_s), jnp.stack(win_p), jnp.stack(win_s), jnp.stack(gla_p), jnp.stack(gla_s))
```

```python
import numpy as np
from contextlib import ExitStack
import concourse.bass as bass
import concourse.mybir as mybir
from concourse.bass_utils import run_bass_kernel_spmd

F32 = mybir.dt.float32
BF16 = mybir.dt.bfloat16
I32 = mybir.dt.int32
ALU = mybir.AluOpType
AF = mybir.ActivationFunctionType

NCORES = 8
D = 1024
SEQ = 4096
PB = 2
SB_ = 16
TS = 8
NT_P = SEQ // 128
D_IN = 2856
D_FF = 4096
C_ZQ, C_ZK, C_ZV, C_ZA, C_ZR, C_QN, C_KV, C_ZG = 0, 256, 512, 1024, 1040, 1552, 2064, 2832
EPS = 1e-6
PAST = 8192
NPG = 64
WBUF = 512
NEG = 4096.0
NTOK = PB * SEQ + SB_ * TS
NTILE = NTOK // 128
NKCOL = PAST + 128
class _Stop(Exception):
    pass


def chk(k):
    if CFG["stage"] < k:
        raise _Stop()


CFG = {"stage": 99, "pool_pages": 10240, "n_ptiles": NT_P, "sample": True, "ffn": True, "kinds": PB}


class Buf:
    __slots__ = ("name", "t", "last_w", "readers", "dsem", "dcnt")

    def __init__(self, name, t):
        self.name = name
        self.t = t
        self.last_w = None
        self.readers = {}
        self.dsem = None
        self.dcnt = 0

    def __getitem__(self, idx):
        return self.t[idx]


class Op:
    __slots__ = ("eng", "fn", "deps", "signal", "isdma", "buf", "cnt")

    def __init__(self, eng, fn, isdma=False, buf=None):
        self.eng = eng
        self.fn = fn
        self.deps = set()
        self.signal = False
        self.isdma = isdma
        self.buf = buf
        self.cnt = None


class Prog:
    ENGS = ("pe", "dve", "act", "pool", "sp")

    def __init__(self, nc, stack):
        self.nc = nc
        self.stack = stack
        self.ops = []
        self.bufs = []
        self.sems = {e: stack.enter_context(nc.semaphore("s_" + e)) for e in self.ENGS}
        self.cnt = {e: 0 for e in self.ENGS}
        self.waited = {e: {} for e in self.ENGS}
        self.dbufs = []
        self.engmap = {"pe": nc.tensor, "dve": nc.vector, "act": nc.scalar, "pool": nc.gpsimd, "sp": nc.sync}

    def buf(self, name, t):
        b = Buf(name, t)
        self.bufs.append(b)
        return b

    def op(self, eng, fn, reads=(), writes=(), isdma=False, dmabuf=None):
        o = Op(eng, fn, isdma, dmabuf)
        i = len(self.ops)
        for b in reads:
            if b.last_w is not None:
                o.deps.add(b.last_w)
        for b in writes:
            if b.last_w is not None:
                o.deps.add(b.last_w)
            for r in b.readers.values():
                o.deps.add(r)
        rk = ("dma", i) if isdma else eng
        for b in reads:
            b.readers[rk] = i
        for b in writes:
            b.last_w = i
            b.readers = {}
        o.deps.discard(i)
        self.ops.append(o)
        return o

    def pe(self, fn, reads=(), writes=()):
        return self.op("pe", fn, reads, writes)

    def dve(self, fn, reads=(), writes=()):
        return self.op("dve", fn, reads, writes)

    def act(self, fn, reads=(), writes=()):
        return self.op("act", fn, reads, writes)

    def pool(self, fn, reads=(), writes=()):
        return self.op("pool", fn, reads, writes)

    def dma(self, q, fn, reads=(), writes=(), sembuf=None):
        return self.op(q, fn, reads, writes, isdma=True, dmabuf=sembuf)

    def _wait(self, ename, s, v):
        w = self.waited[ename]
        k = id(s)
        if w.get(k, 0) >= v:
            return
        self.engmap[ename].wait_ge(s, v)
        w[k] = v

    def flush(self):
        ops = self.ops
        for o in ops:
            for d in o.deps:
                p = ops[d]
                if p.isdma or not (p.eng == o.eng and p.eng == "pe"):
                    p.signal = True
        last = {}
        for o in ops:
            if not o.isdma:
                last[o.eng] = o
        for o in last.values():
            o.signal = True
        for o in ops:
            if o.isdma:
                b = o.buf
                if b.dsem is None:
                    b.dsem = self.stack.enter_context(self.nc.semaphore("d_" + b.name))
                    self.dbufs.append(b)
                b.dcnt += 16
                o.cnt = (b.dsem, b.dcnt)
            elif o.signal:
                self.cnt[o.eng] += 1
                o.cnt = (self.sems[o.eng], self.cnt[o.eng])
        for o in ops:
            need = {}
            for d in o.deps:
                p = ops[d]
                if p.cnt is None:
                    continue
                if (not p.isdma) and (not o.isdma) and p.eng == "pe" and o.eng == "pe":
                    continue
                s, v = p.cnt
                k = id(s)
                if k not in need or need[k][1] < v:
                    need[k] = (s, v)
            for k, (s, v) in need.items():
                self._wait(o.eng, s, v)
            ins = o.fn(self.engmap[o.eng])
            if o.cnt is not None:
                ins.then_inc(o.cnt[0], 16 if o.isdma else 1)
        for e in self.ENGS:
            for e2 in self.ENGS:
                if e2 != e and self.cnt[e2] > 0:
                    self._wait(e, self.sems[e2], self.cnt[e2])
            for b in self.dbufs:
                self._wait(e, b.dsem, b.dcnt)
        self.ops = []
        for b in self.bufs:
            b.last_w = None
            b.readers = {}


def build_nc():
    nc = bass.Bass("TRN2", target_bir_lowering=False)

    def din(name, shape, dt=F32):
        return nc.dram_tensor(name, list(shape), dt, kind="ExternalInput").ap()

    def dout(name, shape, dt=F32):
        return nc.dram_tensor(name, list(shape), dt, kind="ExternalOutput").ap()

    xs = din("xs", [NTOK, D])
    cT = din("cT", [128, 8, 18])
    w_ada = din("w_ada", [D, 6 * D])
    b_ada = din("b_ada", [1, 6 * D])
    gains = din("gains", [4, D])
    w_in = din("w_in", [D, D_IN])
    w_out = din("w_out", [D, D])
    w_up = din("w_up", [D, D_FF])
    w_down = din("w_down", [D_FF, D])
    wg = din("wg", [16, 256])
    bg = din("bg", [1, 256])
    gnorm4 = din("gnorm4", [1, 512])
    sgla = din("sgla", [SB_, 4, 64, 128])
    swin = din("swin", [SB_, WBUF, 256])
    cache = din("cache", [CFG["pool_pages"] * 128 * 2, 256])
    ptab = din("ptab", [SB_, NPG], I32)
    w1p = din("w1p", [128, 2, 2, 8, 256])
    posrep = din("posrep", [128, 2, 2, 8, 16])
    b1r = din("b1r", [2, 256])
    w2l = din("w2l", [128, 2, 2, 64])
    b2k = din("b2k", [128, 1])
    b2v = din("b2v", [1, 64])
    ident_f = din("ident_f", [128, 128])
    tri = din("tri", [2, 2, 128, 128])
    seqind = din("seqind", [2, 128, 16])
    seqmask = din("seqmask", [128, 16])
    causm = din("causm", [2, 128, 512], BF16)
    ck = din("ck", [96, NKCOL], BF16)
    ckc = din("ckc", [4, 512], BF16)
    qab = din("qab", [2, 2, 5, 512], BF16)
    maskc = din("maskc", [17, 128, 128], BF16)
    maskcs = din("maskcs", [128, 32], BF16)
    mmap_p = din("mmap_p", [128, 2, 64], BF16)
    mmap_s = din("mmap_s", [128, 4, 128], BF16)
    vmam_p = din("vmam_p", [NT_P, 128, 2, 2, 64])
    vmam_s = din("vmam_s", [128, 2, 2, 128])
    caus4 = din("caus4", [2, 128, 512], BF16)
    newmask = din("newmask", [128, 16, 32], BF16)
    wmasks = din("wmasks", [128, 32], BF16)
    selp = din("selp", [128, 2, 64], BF16)

    y_all = dout("y_all", [NTOK, D])
    kvr = dout("kvr", [NTOK, 512])
    win_p = dout("win_p", [PB, WBUF, 256])
    win_s = dout("win_s", [SB_, WBUF, 256])
    gla_p = dout("gla_p", [PB, 4, 64, 128])
    gla_s = dout("gla_s", [SB_, 4, 64, 128])
    modrows = dout("modrows", [18, 6 * D])

    with ExitStack() as G:
        P = Prog(nc, G)

        def mk(stack, suf=""):
            def sb(name, shape, dt=F32):
                return P.buf(name + suf, stack.enter_context(nc.sbuf_tensor(name + suf, list(shape), dt)))
            return sb

        gsb = mk(G)

        def ld(q, dst, dst_ap, src_ap, reads=()):
            P.dma(q, lambda e: e.dma_start(out=dst_ap, in_=src_ap), reads=reads, writes=[dst], sembuf=dst)

        pb = [P.buf("pb%d" % i, G.enter_context(nc.psum_tensor("pb%d" % i, [128, 512], F32))) for i in range(7)]
        pT = P.buf("pT", G.enter_context(nc.psum_tensor("pT", [128, 1024], BF16)))
        dy = P.buf("y_all", y_all); dkvr = P.buf("kvr", kvr); dwinp = P.buf("win_p", win_p)
        dwins = P.buf("win_s", win_s); dglap = P.buf("gla_p", gla_p); dglas = P.buf("gla_s", gla_s)
        dmod = P.buf("modrows", modrows)

        identf = gsb("identf", [128, 128]); identb = gsb("identb", [128, 128], BF16)
        scT = gsb("scT", [128, 8, 18], BF16)
        epsb = gsb("epsb", [128, 1])
        ones1 = gsb("ones1", [1, 128])
        ld("sp", identf, identf[:], ident_f[:, :])
        P.dve(lambda e: e.memset(ones1[:], 1.0), writes=[ones1])
        P.dve(lambda e: e.memset(epsb[:], EPS), writes=[epsb])
        P.dve(lambda e: e.tensor_copy(out=identb[:], in_=identf[:]), reads=[identf], writes=[identb])
        with ExitStack() as S0:
            sb0 = mk(S0)
            cTt = sb0("cTt", [128, 8, 18]); sgm = sb0("sgm", [128, 8, 18])
            ld("sp", cTt, cTt[:], cT[:, :, :])
            P.act(lambda e: e.activation(out=sgm[:], in_=cTt[:], func=AF.Sigmoid), reads=[cTt], writes=[sgm])
            P.dve(lambda e: e.tensor_tensor(out=scT[:], in0=sgm[:], in1=cTt[:], op=ALU.mult),
                  reads=[sgm, cTt], writes=[scT])
            P.flush()

        def ada_rows(sb_, col0, gidx_a, gidx_g):
            adar = sb_("adar%d" % col0, [18, 3 * D])
            wst = sb_("wada_st%d" % col0, [128, 8, 256]); wbf = sb_("wada_bf%d" % col0, [128, 8, 256], BF16)
            bch = sb_("bch%d" % col0, [18, 256]); grow = sb_("grow%d" % col0, [18, D])
            for ncn in range(12):
                c0 = col0 + ncn * 256
                ld("sp", wst, wst[:], w_ada[:, c0:c0 + 256].rearrange("(k p) n -> p k n", p=128))
                ld("pool", bch, bch[:], b_ada[0:1, c0:c0 + 256].partition_broadcast(18))
                P.dve(lambda e: e.tensor_copy(out=wbf[:], in_=wst[:]), reads=[wst], writes=[wbf])
                pbk = pb[ncn % 2]
                for kc in range(8):
                    P.pe(lambda e, kc=kc, pbk=pbk: e.matmul(pbk[0:18, 0:256], lhsT=scT[:, kc, :], rhs=wbf[:, kc, :],
                                                             start=(kc == 0), stop=(kc == 7)),
                         reads=[scT, wbf], writes=[pbk])
                P.dve(lambda e, pbk=pbk, ncn=ncn: e.tensor_tensor(out=adar[:, ncn * 256:(ncn + 1) * 256], in0=pbk[0:18, 0:256],
                                                                  in1=bch[:], op=ALU.add),
                      reads=[pbk, bch], writes=[adar])
            ld("pool", grow, grow[:], gains[gidx_a:gidx_a + 1, :].partition_broadcast(18))
            P.dve(lambda e: e.scalar_tensor_tensor(out=adar[:, D:2 * D], in0=adar[:, D:2 * D], scalar=1.0, in1=grow[:],
                                                   op0=ALU.add, op1=ALU.mult), reads=[adar, grow], writes=[adar])
            ld("pool", grow, grow[:], gains[gidx_g:gidx_g + 1, :].partition_broadcast(18))
            P.dve(lambda e: e.tensor_tensor(out=adar[:, 2 * D:3 * D], in0=adar[:, 2 * D:3 * D], in1=grow[:], op=ALU.mult),
                  reads=[adar, grow], writes=[adar])
            P.dma("sp", lambda e: e.dma_start(out=modrows[:, col0:col0 + 3 * D], in_=adar[:]), reads=[adar], writes=[dmod],
                  sembuf=dmod)

        def load_mod(MODt, col0, kind):
            if kind < 2:
                P.dma("sp", lambda e: e.dma_start(out=MODt[:], in_=modrows[kind:kind + 1, col0:col0 + 3 * D].partition_broadcast(128)),
                      reads=[dmod], writes=[MODt], sembuf=MODt)
            else:
                for s_ in range(SB_):
                    P.dma("sp" if s_ % 2 == 0 else "pool", lambda e, s_=s_: e.dma_start(
                        out=MODt[s_ * TS:(s_ + 1) * TS, :],
                        in_=modrows[2 + s_:3 + s_, col0:col0 + 3 * D].partition_broadcast(TS)),
                        reads=[dmod], writes=[MODt], sembuf=MODt)

        def rms_rstd(src_ap, src_bufs, junk_ap, junk_buf, stb, c0, n):
            P.act(lambda e: e.activation(out=junk_ap, in_=src_ap, func=AF.Square, accum_out=stb[:, c0:c0 + 1]),
                  reads=src_bufs, writes=[stb, junk_buf])
            P.act(lambda e: e.activation(out=stb[:, c0 + 1:c0 + 2], in_=stb[:, c0:c0 + 1], func=AF.Sqrt, scale=1.0 / n, bias=epsb[:]),
                  reads=[stb, epsb], writes=[stb])
            P.dve(lambda e: e.reciprocal(out=stb[:, c0 + 1:c0 + 2], in_=stb[:, c0 + 1:c0 + 2]), reads=[stb], writes=[stb])

        def transpose8(src_bf, dstT, evac="act"):
            for kc in range(8):
                P.pe(lambda e, kc=kc: e.transpose(out=pT[:, kc * 128:(kc + 1) * 128], in_=src_bf[:, kc * 128:(kc + 1) * 128],
                                                  identity=identb[:]), reads=[src_bf, identb], writes=[pT])
            if evac == "act":
                P.act(lambda e: e.activation(out=dstT[:], in_=pT[:], func=AF.Copy), reads=[pT], writes=[dstT])
            else:
                P.dve(lambda e: e.tensor_copy(out=dstT[:], in_=pT[:]), reads=[pT], writes=[dstT])

        with ExitStack() as S1:
            sb1 = mk(S1)
            win_bf = sb1("win_bf", [128, 8, D_IN], BF16)
            wout_bf = sb1("wout_bf", [128, 8, D], BF16)
            w1b = sb1("w1b", [128, 2, 2, 8, 256], BF16)
            w2b = sb1("w2b", [128, 2, 2, 64], BF16)
            w2dup = sb1("w2dup", [128, 2, 128], BF16)
            pbias = sb1("pbias", [128, 2, 256])
            b2kt = sb1("b2kt", [128, 1]); b2vt = sb1("b2vt", [128, 64])
            wgt = sb1("wgt", [16, 256]); bgt = sb1("bgt", [1, 256])
            trit = sb1("trit", [128, 2, 2, 128]); seqi = sb1("seqi", [128, 2, 16]); seqm = sb1("seqm", [128, 16])
            causmt = sb1("causmt", [128, 2, 512], BF16)
            gnb = sb1("gnb", [128, 512])
            ckct = sb1("ckct", [4, 512], BF16)
            selpt = sb1("selpt", [128, 2, 64], BF16)
            ld("sp", ckct, ckct[:], ckc[:, :])
            ld("sp", selpt, selpt[:], selp[:, :, :])
            ld("sp", causmt, causmt[:], causm.rearrange("k p c -> p k c"))
            with ExitStack() as S1a:
                sba = mk(S1a)
                stg = [sba("stg%d" % i, [128, D_IN]) for i in range(2)]
                nst = [0]

                def ldcast(dst, dst_ap, src_ap, shape):
                    s = stg[nst[0] % 2]
                    q = "sp" if nst[0] % 2 == 0 else "pool"
                    np_, nf = shape
                    sap = s[0:np_, 0:nf]
                    ld(q, s, sap, src_ap)
                    if nst[0] % 2 == 0:
                        P.dve(lambda e: e.tensor_copy(out=dst_ap, in_=sap), reads=[s], writes=[dst])
                    else:
                        P.act(lambda e: e.activation(out=dst_ap, in_=sap, func=AF.Copy), reads=[s], writes=[dst])
                    nst[0] += 1

                for kc in range(8):
                    ldcast(win_bf, win_bf[:, kc, :], w_in[kc * 128:(kc + 1) * 128, :], (128, D_IN))
                for kc in range(8):
                    ldcast(wout_bf, wout_bf[:, kc, :], w_out[kc * 128:(kc + 1) * 128, :], (128, D))
                for ty in range(2):
                    for hf in range(2):
                        ldcast(w1b, w1b[:, ty, hf, :, :].rearrange("p a b -> p (a b)"),
                               w1p[:, ty, hf, :, :].rearrange("p a b -> p (a b)"), (128, 2048))
                ldcast(w2b, w2b[:].rearrange("p a b c -> p (a b c)"), w2l.rearrange("p a b c -> p (a b c)"), (128, 256))
                posb = sba("posb", [128, 2, 2, 8, 16], BF16)
                ldcast(posb, posb[:].rearrange("p a b c d -> p (a b c d)"), posrep.rearrange("p a b c d -> p (a b c d)"),
                       (128, 512))
                b1t = sba("b1t", [1, 2, 256]); ones16 = sba("ones16", [1, 16]); pb16 = sba("pb16", [16, 2, 256])
                ld("sp", b1t, b1t[:], b1r.rearrange("(o t) f -> o t f", o=1))
                P.dve(lambda e: e.memset(ones16[:], 1.0), writes=[ones16])
                ld("sp", b2kt, b2kt[:], b2k[:, :])
                ld("sp", b2vt, b2vt[:], b2v[0:1, :].partition_broadcast(128))
                ld("sp", wgt, wgt[:], wg[:, :]); ld("sp", bgt, bgt[:], bg[:, :])
                ld("sp", trit, trit[:], tri.rearrange("k m p t -> p k m t"))
                ld("sp", seqi, seqi[:], seqind.rearrange("k p s -> p k s"))
                ld("sp", seqm, seqm[:], seqmask[:, :])
                ld("sp", gnb, gnb[:], gnorm4[0:1, :].partition_broadcast(128))
                for fc in range(2):
                    for dup in range(2):
                        P.dve(lambda e, fc=fc, dup=dup: e.tensor_copy(out=w2dup[:, fc, dup * 64:(dup + 1) * 64],
                                                                      in_=w2b[:, 0, fc, :]), reads=[w2b], writes=[w2dup])
                for ty in range(2):
                    pbk = pb[2 + ty]
                    n = 0
                    for hf in range(2):
                        for j2 in range(8):
                            P.pe(lambda e, ty=ty, hf=hf, j2=j2, pbk=pbk, n=n: e.matmul(
                                pbk[0:16, 0:256], lhsT=posb[:, ty, hf, j2, :], rhs=w1b[:, ty, hf, j2, :],
                                start=(n == 0), stop=False), reads=[posb, w1b], writes=[pbk])
                            n += 1
                    P.pe(lambda e, ty=ty, pbk=pbk: e.matmul(pbk[0:16, 0:256], lhsT=ones16[:], rhs=b1t[:, ty, :],
                                                            start=False, stop=True), reads=[ones16, b1t], writes=[pbk])
                    P.act(lambda e, ty=ty, pbk=pbk: e.activation(out=pb16[0:16, ty, :], in_=pbk[0:16, 0:256], func=AF.Copy),
                          reads=[pbk], writes=[pb16])
                    P.pe(lambda e, ty=ty, pbk=pbk: e.matmul(pbk[:, 256:512], lhsT=ones1[0:1, :], rhs=pb16[0:1, ty, :],
                                                            start=True, stop=True), reads=[ones1, pb16], writes=[pbk])
                    P.act(lambda e, ty=ty, pbk=pbk: e.activation(out=pbias[:, ty, :], in_=pbk[:, 256:512], func=AF.Copy),
                          reads=[pbk], writes=[pbias])
                ada_rows(sba, 0, 0, 1)
                P.flush()

            MOD = sb1("MOD", [128, 3 * D])
            xt = [sb1("xt%d" % i, [128, D]) for i in range(2)]
            junkb = sb1("junkb", [128, 512], BF16)
            st4 = sb1("st4", [128, 16])
            tmpf = sb1("tmpf", [128, D])
            hbf = sb1("hbf", [128, D], BF16)
            hT = sb1("hT", [128, D], BF16)
            z = sb1("z", [128, D_IN])
            zaT = sb1("zaT", [16, 128])
            Lt = sb1("Lt", [128, 256]); Eout = sb1("Eout", [128, 256])
            EqT = sb1("EqT", [128, 256]); EkT = sb1("EkT", [128, 256])
            qinT = sb1("qinT", [128, 256], BF16)
            qz = [sb1("qz%d" % k, [128, 256], BF16) for k in range(2)]
            kz = [sb1("kz%d" % k, [128, 256], BF16) for k in range(2)]
            ATs = sb1("ATs", [128, 512], BF16)
            kout = sb1("kout", [128, 256], BF16); vbf = sb1("vbf", [128, 512], BF16)
            koutm = sb1("koutm", [128, 256], BF16)
            dec = sb1("dec", [128, 2, 16])
            Sf = sb1("Sf", [128, 2, 128]); Sbf = sb1("Sbf", [128, 2, 128], BF16)
            SX = {}
            szr = sb1("szr", [128, 512])
            obf = sb1("obf", [128, D], BF16)
            qnbf = sb1("qnbf", [128, 512], BF16); kvbf = sb1("kvbf", [128, 768], BF16)
            qTz = [sb1("qTz%d" % k, [128, 512], BF16) for k in range(2)]
            augq = [sb1("augq%d" % k, [96, 512], BF16) for k in range(2)]
            qb0 = sb1("qb0", [1, 2, 512], BF16)
            PTs = [sb1("PT%d" % i, [128, 512], BF16) for i in range(2)]
            oacc = sb1("oacc", [128, 512])
            sg = sb1("sg", [128, 24]); coef = sb1("coef", [128, 8]); rcs = sb1("rcs", [128, 8])
            impn = sb1("impn", [128, 2, 128])
            score = sb1("score", [128, 2, 128]); sc2 = EqT
            m8 = sb1("m8", [128, 16]); selpos = sb1("selpos", [128, 2, 160], BF16)
            vma = sb1("vma", [128, 2, 2, 64])
            hid = Eout; hidb = sb1("hidb", [128, 256], BF16)
            hidT = sb1("hidT", [128, 2, 128], BF16)
            vcn = sb1("vcn", [128, 64], BF16)

            for k_ in range(2):
                P.pool(lambda e, k_=k_: e.memset(qTz[k_][:], 0.0), writes=[qTz[k_]])
                P.pool(lambda e, k_=k_: e.memset(qz[k_][:], 0.0), writes=[qz[k_]])
                P.pool(lambda e, k_=k_: e.memset(kz[k_][:], 0.0), writes=[kz[k_]])
            P.pool(lambda e: e.memset(selpos[:], 0.0), writes=[selpos])
            sbank = [pb[2], pb[3], pb[4]]
            obank = [pb[5], pb[6]]
            ucount = [0]

            def unit(mms, nq, vT_ap, vbufs, ob, ocols, first, last, keep=None):
                sbk = sbank[ucount[0] % 3]
                pt = keep if keep is not None else PTs[ucount[0] % 2]
                ucount[0] += 1
                n = len(mms)
                for mi, (l_ap, r_ap, rd, cols) in enumerate(mms):
                    c0, cn = cols
                    P.pe(lambda e, l_ap=l_ap, r_ap=r_ap, mi=mi, c0=c0, cn=cn: e.matmul(
                        sbk[:, c0:c0 + cn], lhsT=l_ap, rhs=r_ap, start=(mi == 0), stop=(mi == n - 1)),
                        reads=rd, writes=[sbk])
                P.act(lambda e: e.activation(out=pt[:, 0:nq], in_=sbk[:, 0:nq], func=AF.Exp), reads=[sbk], writes=[pt])
                P.pe(lambda e: e.matmul(ob[0:65, ocols[0]:ocols[0] + nq], lhsT=vT_ap, rhs=pt[:, 0:nq], start=first, stop=last),
                     reads=[pt] + vbufs, writes=[ob])
                return pt

            def branch_epilogue(br, samp, first_branch):
                for kvh in range(2):
                    if not samp:
                        P.act(lambda e, kvh=kvh: e.activation(out=tmpf[0:65, kvh * 512:(kvh + 1) * 512], in_=obank[kvh][0:65, :],
                                                              func=AF.Copy), reads=[obank[kvh]], writes=[tmpf])
                for kvh in range(2):
                    ob = obank[kvh]
                    for g in range(4):
                        src = tmpf[0:65, kvh * 512 + g * 128:kvh * 512 + (g + 1) * 128]
                        P.pe(lambda e, src=src, g=g, ob=ob: e.transpose(out=ob[:, g * 65:(g + 1) * 65], in_=src,
                                                                         identity=identf[0:65, 0:65]),
                             reads=[tmpf, identf], writes=[ob])
                    ov = ob[:, 0:260].rearrange("p (g c) -> p g c", g=4)
                    P.dve(lambda e, ov=ov, kvh=kvh: e.tensor_scalar(out=rcs[:, kvh * 4:(kvh + 1) * 4], in0=ov[:, :, 64],
                                                                    scalar1=1e-30, scalar2=None, op0=ALU.max),
                          reads=[ob], writes=[rcs])
                P.dve(lambda e: e.reciprocal(out=rcs[:], in_=rcs[:]), reads=[rcs], writes=[rcs])
                sgv = sg[:].rearrange("p (h b) -> p h b", b=3)
                P.dve(lambda e: e.tensor_tensor(out=coef[:], in0=rcs[:], in1=sgv[:, :, br], op=ALU.mult),
                      reads=[rcs, sg], writes=[coef])
                for kvh in range(2):
                    ob = obank[kvh]
                    for g in range(4):
                        h = kvh * 4 + g
                        if first_branch:
                            P.dve(lambda e, h=h, g=g, ob=ob: e.tensor_scalar(out=oacc[:, h * 64:(h + 1) * 64],
                                                                             in0=ob[:, g * 65:g * 65 + 64],
                                                                             scalar1=coef[:, h:h + 1], scalar2=None, op0=ALU.mult),
                                  reads=[ob, coef], writes=[oacc])
                        else:
                            P.dve(lambda e, h=h, g=g, ob=ob: e.scalar_tensor_tensor(
                                out=oacc[:, h * 64:(h + 1) * 64], in0=ob[:, g * 65:g * 65 + 64], scalar=coef[:, h:h + 1],
                                in1=oacc[:, h * 64:(h + 1) * 64], op0=ALU.mult, op1=ALU.add),
                                reads=[ob, coef, oacc], writes=[oacc])

            def select_blocks(ns, vm_ap, am_ap, vmbufs, kth, samp):
                for kvh in range(2):
                    ob = obank[kvh]
                    for g in range(4):
                        src = z[0:ns, kvh * 512 + g * 128:kvh * 512 + (g + 1) * 128]
                        P.pe(lambda e, src=src, g=g, ob=ob: e.transpose(out=ob[:, g * 128:g * 128 + ns], in_=src,
                                                                         identity=identf[0:ns, 0:ns]),
                             reads=[z, identf], writes=[ob])
                    for g in range(4):
                        h = kvh * 4 + g
                        if g == 0:
                            P.dve(lambda e, h=h, g=g, ob=ob, kvh=kvh: e.tensor_scalar(
                                out=impn[:, kvh, 0:ns], in0=ob[:, g * 128:g * 128 + ns], scalar1=rcs[:, h:h + 1], scalar2=None,
                                op0=ALU.mult), reads=[ob, rcs], writes=[impn])
                        else:
                            P.dve(lambda e, h=h, g=g, ob=ob, kvh=kvh: e.scalar_tensor_tensor(
                                out=impn[:, kvh, 0:ns], in0=ob[:, g * 128:g * 128 + ns], scalar=rcs[:, h:h + 1],
                                in1=impn[:, kvh, 0:ns], op0=ALU.mult, op1=ALU.add), reads=[ob, rcs, impn], writes=[impn])
                for kvh in range(2):
                    P.dve(lambda e, kvh=kvh: e.tensor_tensor(out=score[:, kvh, 0:ns], in0=impn[:, kvh, 0:ns], in1=vm_ap(kvh),
                                                             op=ALU.mult), reads=[impn] + vmbufs, writes=[score])
                    P.dve(lambda e, kvh=kvh: e.tensor_tensor(out=score[:, kvh, 0:ns], in0=score[:, kvh, 0:ns], in1=am_ap(kvh),
                                                             op=ALU.add), reads=[score] + vmbufs, writes=[score])
                    P.dve(lambda e, kvh=kvh: e.max(out=m8[:, 0:8], in_=score[:, kvh, 0:ns]), reads=[score], writes=[m8])
                    P.dve(lambda e, kvh=kvh: e.match_replace(out=sc2[:, 0:ns], in_to_replace=m8[:, 0:8],
                                                             in_values=score[:, kvh, 0:ns], imm_value=-3.0e4),
                          reads=[score, m8], writes=[sc2])
                    P.dve(lambda e: e.max(out=m8[:, 8:16], in_=sc2[:, 0:ns]), reads=[sc2], writes=[m8])
                    P.dve(lambda e, kvh=kvh: e.tensor_scalar(out=selpos[:, kvh, 32:32 + ns], in0=score[:, kvh, 0:ns],
                                                             scalar1=m8[:, 8 + kth:9 + kth], scalar2=NEG, op0=ALU.is_ge,
                                                             op1=ALU.mult), reads=[score, m8], writes=[selpos])

            def build_augq(aq, kvh, blk0, samp):
                P.pe(lambda e: e.transpose(out=pT[0:96, 0:128], in_=selpos[:, kvh, blk0:blk0 + 96], identity=identb[:]),
                     reads=[selpos, identb], writes=[pT])
                for (p0, p1) in ((32, 64), (64, 96)):
                    if not samp:
                        for g in range(4):
                            P.dve(lambda e, g=g, p0=p0, p1=p1: e.tensor_copy(out=aq[p0:p1, g * 128:(g + 1) * 128], in_=pT[p0:p1, 0:128]),
                                  reads=[pT], writes=[aq])
                    else:
                        av = aq[p0:p1, :].rearrange("p (s g q) -> p s g q", s=16, g=4)
                        pv = pT[p0:p1, 0:128].rearrange("p (s q) -> p s q", s=16)
                        for g in range(4):
                            P.dve(lambda e, g=g, av=av, pv=pv: e.tensor_copy(out=av[:, :, g, :], in_=pv), reads=[pT], writes=[aq])

            def compress(Y, nb, kcT, kc_col0, vcs, c0, skip_first):
                R1 = 32 if nb <= 32 else 64
                NR = R1 + nb
                for ty in range(2):
                    pbk = pb[ty]
                    for kvh in range(2):
                        n = 0
                        for hf in range(2):
                            for j2 in range(8):
                                base = 8 * hf + j2
                                lap = Y[:, ty, kvh, base:base + 8 * (nb - 1) + 1:8]
                                P.pe(lambda e, lap=lap, ty=ty, hf=hf, j2=j2, pbk=pbk, n=n, kvh=kvh: e.matmul(
                                    pbk[kvh * R1:kvh * R1 + nb, 0:256], lhsT=lap, rhs=w1b[:, ty, hf, j2, :],
                                    start=(n == 0), stop=(n == 15)), reads=[Y, w1b], writes=[pbk])
                                n += 1
                    P.dve(lambda e, ty=ty, pbk=pbk: e.tensor_tensor(out=hid[0:NR, :], in0=pbk[0:NR, 0:256],
                                                                    in1=pbias[0:NR, ty, :], op=ALU.add),
                          reads=[pbk, pbias], writes=[hid])
                    P.act(lambda e: e.activation(out=hidb[0:NR, :], in_=hid[0:NR, :], func=AF.Silu),
                          reads=[hid], writes=[hidb])
                    for fc in range(2):
                        P.pe(lambda e, fc=fc: e.transpose(out=pT[:, fc * 128:fc * 128 + NR],
                                                          in_=hidb[0:NR, fc * 128:(fc + 1) * 128],
                                                          identity=identb[0:NR, 0:NR]),
                             reads=[hidb, identb], writes=[pT])
                    P.act(lambda e: e.activation(out=hidT[:, :, 0:NR],
                                                 in_=pT[:, 0:256].rearrange("p (a b) -> p a b", a=2)[:, :, 0:NR],
                                                 func=AF.Copy), reads=[pT], writes=[hidT])
                    if ty == 0:
                        pk = pb[1]
                        for fc in range(2):
                            P.pe(lambda e, fc=fc, pk=pk: e.matmul(pk[:, 256:256 + NR], lhsT=w2dup[:, fc, :],
                                                                   rhs=hidT[:, fc, 0:NR], start=(fc == 0), stop=(fc == 1)),
                                 reads=[w2dup, hidT], writes=[pk])
                        for kvh in range(2):
                            r = slice(kvh * 64, kvh * 64 + 64)
                            P.act(lambda e, kvh=kvh, r=r, pk=pk: e.activation(
                                out=kcT[r, kc_col0:kc_col0 + nb], in_=pk[r, 256 + kvh * R1:256 + kvh * R1 + nb],
                                func=AF.Identity, bias=b2kt[r, 0:1]), reads=[pk, b2kt], writes=[kcT])
                    else:
                        pk = pb[0]
                        for fc in range(2):
                            P.pe(lambda e, fc=fc, pk=pk: e.matmul(pk[0:NR, 256:320], lhsT=hidT[:, fc, 0:NR],
                                                                   rhs=w2b[:, 1, fc, :], start=(fc == 0), stop=(fc == 1)),
                                 reads=[w2b, hidT], writes=[pk])
                        P.dve(lambda e, pk=pk: e.tensor_tensor(out=vcn[0:NR, :], in0=pk[0:NR, 256:320],
                                                               in1=b2vt[0:NR, :], op=ALU.add), reads=[pk, b2vt], writes=[vcn])
                        for kvh in range(2):
                            r0 = 1 if skip_first else 0
                            c = c0 + r0
                            while c < c0 + nb:
                                ce = min(c0 + nb, (c // 128 + 1) * 128)
                                P.dma("sp", lambda e, kvh=kvh, c=c, ce=ce: e.dma_start(
                                    out=vcs[c % 128:c % 128 + (ce - c), c // 128, kvh, 0:64],
                                    in_=vcn[kvh * R1 + (c - c0):kvh * R1 + (ce - c0), :]), reads=[vcn], writes=[vcs], sembuf=vcs)
                                c = ce

            def front(ti, x, kind):
                ld("sp", x, x[:], xs[ti * 128:(ti + 1) * 128, :])
                rms_rstd(x[:], [x], hbf[:], hbf, st4, 0, D)
                P.dve(lambda e: e.scalar_tensor_tensor(out=tmpf[:], in0=x[:], scalar=st4[:, 1:2], in1=MOD[:, D:2 * D],
                                                       op0=ALU.mult, op1=ALU.mult), reads=[x, st4, MOD], writes=[tmpf])
                P.pool(lambda e: e.tensor_tensor(out=hbf[:], in0=tmpf[:], in1=MOD[:, 0:D], op=ALU.add),
                       reads=[tmpf, MOD], writes=[hbf])
                transpose8(hbf, hT)
                zch = [(0, 512), (512, 512), (1024, 512), (1536, 512), (2048, 512), (2560, 296)]
                for ci, (c0, cn) in enumerate(zch):
                    pbk = pb[ci % 2]
                    for kc in range(8):
                        P.pe(lambda e, kc=kc, pbk=pbk, c0=c0, cn=cn: e.matmul(
                            pbk[:, 0:cn], lhsT=hT[:, kc * 128:(kc + 1) * 128], rhs=win_bf[:, kc, c0:c0 + cn],
                            start=(kc == 0), stop=(kc == 7)), reads=[hT, win_bf], writes=[pbk])
                    if ci % 2 == 0:
                        P.dve(lambda e, pbk=pbk, c0=c0, cn=cn: e.tensor_copy(out=z[:, c0:c0 + cn], in_=pbk[:, 0:cn]),
                              reads=[pbk], writes=[z])
                    else:
                        P.act(lambda e, pbk=pbk, c0=c0, cn=cn: e.activation(out=z[:, c0:c0 + cn], in_=pbk[:, 0:cn], func=AF.Copy),
                              reads=[pbk], writes=[z])
                P.dma("sp", lambda e: e.dma_start(out=kvr[ti * 128:(ti + 1) * 128, :], in_=z[:, C_KV:C_KV + 512]),
                      reads=[z], writes=[dkvr], sembuf=dkvr)
                P.act(lambda e: e.activation(out=qnbf[:].rearrange("p (g k d) -> p k g d", g=4, k=2),
                                             in_=z[:, C_QN:C_QN + 512].rearrange("p (k g d) -> p k g d", k=2, g=4),
                                             func=AF.Copy, scale=0.125), reads=[z], writes=[qnbf])
                P.pool(lambda e: e.tensor_copy(out=kvbf[:], in_=z[:, C_KV:C_KV + 768]), reads=[z], writes=[kvbf])
                P.act(lambda e: e.activation(out=sg[:], in_=z[:, C_ZG:C_ZG + 24], func=AF.Sigmoid), reads=[z], writes=[sg])
                for g in range(4):
                    P.pe(lambda e, g=g: e.transpose(out=pT[:, g * 128:(g + 1) * 128], in_=qnbf[:, g * 128:(g + 1) * 128], identity=identb[:]),
                         reads=[qnbf, identb], writes=[pT])
                for kvh in range(2):
                    r = slice(kvh * 64, kvh * 64 + 64)
                    P.dve(lambda e, kvh=kvh, r=r: e.tensor_copy(out=qTz[kvh][r, :], in_=pT[r, 0:512]), reads=[pT], writes=[qTz[kvh]])

            def gla(sample, it, kind):
                tk = 1 if sample else 0
                pa, pb_, pc, pd = pb[2], pb[3], pb[4], pb[5]
                P.pe(lambda e: e.transpose(out=pa[0:16, 0:128], in_=z[:, C_ZA:C_ZA + 16], identity=identf[:]),
                     reads=[z, identf], writes=[pa])
                P.act(lambda e: e.activation(out=zaT[:], in_=pa[0:16, 0:128], func=AF.Copy), reads=[pa], writes=[zaT])
                P.pe(lambda e: e.matmul(pb_[:, 0:256], lhsT=zaT[:], rhs=wgt[:], start=True, stop=False),
                     reads=[zaT, wgt], writes=[pb_])
                P.pe(lambda e: e.matmul(pb_[:, 0:256], lhsT=ones1[:], rhs=bgt[:], start=False, stop=True),
                     reads=[ones1, bgt], writes=[pb_])
                P.act(lambda e: e.activation(out=Lt[:], in_=pb_[:, 0:256], func=AF.Exp, scale=-1.0), reads=[pb_], writes=[Lt])
                P.act(lambda e: e.activation(out=Lt[:], in_=Lt[:], func=AF.Ln, bias=1.0), reads=[Lt], writes=[Lt])
                P.pe(lambda e: e.matmul(pc[:, 0:256], lhsT=trit[:, tk, 0, :], rhs=Lt[:], start=True, stop=True),
                     reads=[trit, Lt], writes=[pc])
                for fc in range(2):
                    P.pe(lambda e, fc=fc: e.matmul(pd[:, fc * 16:fc * 16 + 16], lhsT=Lt[:, fc * 128:(fc + 1) * 128],
                                                   rhs=seqi[:, tk, :], start=True, stop=True), reads=[Lt, seqi], writes=[pd])
                    P.pe(lambda e, fc=fc: e.matmul(pd[:, 128 + fc * 128:256 + fc * 128], lhsT=Lt[:, fc * 128:(fc + 1) * 128],
                                                   rhs=trit[:, tk, 1, :], start=True, stop=True), reads=[Lt, trit], writes=[pd])
                P.act(lambda e: e.activation(out=Eout[:], in_=pc[:, 0:256], func=AF.Exp), reads=[pc], writes=[Eout])
                P.act(lambda e: e.activation(out=dec[:].rearrange("p a b -> p (a b)"), in_=pd[:, 0:32], func=AF.Exp),
                      reads=[pd], writes=[dec])
                P.act(lambda e: e.activation(out=EqT[:], in_=pd[:, 128:384], func=AF.Exp), reads=[pd], writes=[EqT])
                P.act(lambda e: e.activation(out=EkT[:], in_=pd[:, 128:384], func=AF.Exp, scale=-1.0), reads=[pd], writes=[EkT])
                P.dve(lambda e: e.tensor_tensor(out=kout[:], in0=z[:, C_ZK:C_ZK + 256], in1=Eout[:], op=ALU.mult),
                      reads=[z, Eout], writes=[kout])
                P.pool(lambda e: e.tensor_copy(out=vbf[:], in_=z[:, C_ZV:C_ZV + 512]), reads=[z], writes=[vbf])
                for a in range(4):
                    c0 = (C_ZQ if a < 2 else C_ZK) + (a % 2) * 128
                    P.pe(lambda e, a=a, c0=c0: e.transpose(out=pa[:, a * 128:(a + 1) * 128], in_=z[:, c0:c0 + 128],
                                                           identity=identf[:]), reads=[z, identf], writes=[pa])
                P.dve(lambda e: e.scalar_tensor_tensor(out=qinT[:], in0=pa[:, 0:256], scalar=0.125, in1=EqT[:],
                                                       op0=ALU.mult, op1=ALU.mult), reads=[pa, EqT], writes=[qinT])
                for hh in range(2):
                    r = slice(hh * 64, hh * 64 + 64)
                    P.dve(lambda e, hh=hh, r=r: e.tensor_tensor(out=kz[hh][r, :], in0=pa[r, 256:512], in1=EkT[r, :], op=ALU.mult),
                          reads=[pa, EkT], writes=[kz[hh]])
                    P.pool(lambda e, hh=hh, r=r: e.tensor_copy(out=qz[hh][r, :], in_=qinT[r, :]), reads=[qinT], writes=[qz[hh]])
                for h in range(4):
                    r = slice((h % 2) * 64, (h % 2) * 64 + 64)
                    fc = h // 2
                    P.pe(lambda e, h=h, r=r, fc=fc: e.matmul(pb_[:, h * 128:(h + 1) * 128], lhsT=kz[h % 2][:, fc * 128:(fc + 1) * 128],
                                                             rhs=qinT[:, fc * 128:(fc + 1) * 128], start=True, stop=True),
                         reads=[kz[h % 2], qinT], writes=[pb_])
                P.dve(lambda e: e.tensor_tensor(out=ATs[:], in0=pb_[:, :], in1=causmt[:, tk, :], op=ALU.mult),
                      reads=[pb_, causmt], writes=[ATs])
                po = pc
                if not sample:
                    if it == 0:
                        P.dve(lambda e: e.memset(Sf[:], 0.0), writes=[Sf])
                        P.dve(lambda e: e.memset(Sbf[:], 0.0), writes=[Sbf])
                    for h in range(4):
                        r = slice((h % 2) * 64, (h % 2) * 64 + 64)
                        fc = h // 2
                        P.pe(lambda e, h=h, r=r, fc=fc: e.matmul(po[:, h * 128:(h + 1) * 128], lhsT=qz[h % 2][:, fc * 128:(fc + 1) * 128],
                                                                 rhs=Sbf[:, fc, :], start=True, stop=False),
                             reads=[qz[h % 2], Sbf], writes=[po])
                        P.pe(lambda e, h=h: e.matmul(po[:, h * 128:(h + 1) * 128], lhsT=ATs[:, h * 128:(h + 1) * 128],
                                                     rhs=vbf[:, h * 128:(h + 1) * 128], start=False, stop=True),
                             reads=[ATs, vbf], writes=[po])
                else:
                    for s in range(SB_):
                        S = SX["Ssm"][s % 2]
                        ld("sp", S, S[:], sgla[s].rearrange("(fc hh) k v -> (hh k) fc v", hh=2))
                        P.dve(lambda e, s=s, S=S: e.tensor_copy(out=SX["S0bf"][:, s, :, :], in_=S[:]), reads=[S], writes=[SX["S0bf"]])
                    for h in range(4):
                        r = slice((h % 2) * 64, (h % 2) * 64 + 64)
                        fc = h // 2
                        P.pe(lambda e, h=h: e.matmul(po[:, h * 128:(h + 1) * 128], lhsT=vbf[:, h * 128:(h + 1) * 128],
                                                     rhs=ATs[:, h * 128:(h + 1) * 128], start=True, stop=False),
                             reads=[ATs, vbf], writes=[po])
                        for s in range(SB_):
                            P.pe(lambda e, h=h, r=r, fc=fc, s=s: e.matmul(
                                po[:, h * 128 + s * TS:h * 128 + (s + 1) * TS], lhsT=SX["S0bf"][:, s, fc, :],
                                rhs=qz[h % 2][:, fc * 128 + s * TS:fc * 128 + (s + 1) * TS], start=False, stop=(s == SB_ - 1)),
                                reads=[SX["S0bf"], qz[h % 2]], writes=[po])
                    P.act(lambda e: e.activation(out=SX["ogT"][:, 0:512], in_=po[:, :], func=AF.Copy), reads=[po], writes=[SX["ogT"]])
                    for h in range(4):
                        P.pe(lambda e, h=h: e.transpose(out=po[:, h * 128:(h + 1) * 128], in_=SX["ogT"][:, h * 128:(h + 1) * 128],
                                                        identity=identf[:]), reads=[SX["ogT"], identf], writes=[po])
                for h in range(4):
                    P.act(lambda e, h=h: e.activation(out=junkb[:, 0:128], in_=po[:, h * 128:(h + 1) * 128], func=AF.Square,
                                                      accum_out=st4[:, 4 + h:5 + h]), reads=[po], writes=[st4, junkb])
                P.act(lambda e: e.activation(out=st4[:, 8:12], in_=st4[:, 4:8], func=AF.Sqrt, scale=1.0 / 128, bias=epsb[:]),
                      reads=[st4, epsb], writes=[st4])
                P.dve(lambda e: e.reciprocal(out=st4[:, 8:12], in_=st4[:, 8:12]), reads=[st4], writes=[st4])
                P.act(lambda e: e.activation(out=szr[:], in_=z[:, C_ZR:C_ZR + 512], func=AF.Sigmoid), reads=[z], writes=[szr])
                P.dve(lambda e: e.tensor_tensor(out=szr[:], in0=szr[:], in1=z[:, C_ZR:C_ZR + 512], op=ALU.mult),
                      reads=[szr, z], writes=[szr])
                P.pool(lambda e: e.tensor_tensor(out=szr[:], in0=szr[:], in1=gnb[:], op=ALU.mult), reads=[szr, gnb], writes=[szr])
                for h in range(4):
                    P.dve(lambda e, h=h: e.scalar_tensor_tensor(out=obf[:, h * 128:(h + 1) * 128], in0=po[:, h * 128:(h + 1) * 128],
                                                                scalar=st4[:, 8 + h:9 + h], in1=szr[:, h * 128:(h + 1) * 128],
                                                                op0=ALU.mult, op1=ALU.mult), reads=[po, st4, szr], writes=[obf])

                def state_update(S, kl, si, pu):
                    for fc in range(2):
                        P.pe(lambda e, fc=fc: e.matmul(pu[:, fc * 256:(fc + 1) * 256], lhsT=kl[:, fc * 128:(fc + 1) * 128],
                                                       rhs=vbf[:, fc * 256:(fc + 1) * 256], start=True, stop=True),
                             reads=[kl, vbf], writes=[pu])
                    for fc in range(2):
                        for hh in range(2):
                            r = slice(hh * 64, hh * 64 + 64)
                            P.dve(lambda e, fc=fc, hh=hh, r=r: e.scalar_tensor_tensor(
                                out=S[r, fc, :], in0=S[r, fc, :], scalar=dec[r, fc, si:si + 1],
                                in1=pu[r, fc * 256 + hh * 128:fc * 256 + hh * 128 + 128], op0=ALU.mult, op1=ALU.add),
                                reads=[S, dec, pu], writes=[S])

                if not sample:
                    state_update(Sf, kout, 0, pb[6])
                    P.pool(lambda e: e.tensor_copy(out=Sbf[:], in_=Sf[:]), reads=[Sf], writes=[Sbf])
                    if it == NT_P - 1:
                        P.dma("sp", lambda e: e.dma_start(out=gla_p[kind].rearrange("(fc hh) k v -> (hh k) fc v", hh=2),
                                                          in_=Sf[:]), reads=[Sf], writes=[dglap], sembuf=dglap)
                else:
                    for s in range(SB_):
                        S = SX["Ssm"][s % 2]
                        ld("sp", S, S[:], sgla[s].rearrange("(fc hh) k v -> (hh k) fc v", hh=2))
                        P.dve(lambda e, s=s: e.tensor_scalar(out=koutm[:], in0=kout[:], scalar1=seqm[:, s:s + 1], scalar2=None,
                                                             op0=ALU.mult), reads=[kout, seqm], writes=[koutm])
                        state_update(S, koutm, s, pb[5 + (s % 2)])
                        P.dma("sp", lambda e, s=s, S=S: e.dma_start(
                            out=gla_s[s].rearrange("(fc hh) k v -> (hh k) fc v", hh=2), in_=S[:]),
                            reads=[S], writes=[dglas], sembuf=dglas)

            def back(ti, x):
                P.dve(lambda e: e.tensor_copy(out=obf[:, 512:1024], in_=oacc[:]), reads=[oacc], writes=[obf])
                transpose8(obf, hT, evac="dve")
                for nh in range(2):
                    pbk = pb[nh]
                    for kc in range(8):
                        P.pe(lambda e, kc=kc, pbk=pbk, nh=nh: e.matmul(pbk[:, :], lhsT=hT[:, kc * 128:(kc + 1) * 128],
                                                                       rhs=wout_bf[:, kc, nh * 512:(nh + 1) * 512],
                                                                       start=(kc == 0), stop=(kc == 7)),
                             reads=[hT, wout_bf], writes=[pbk])
                    P.act(lambda e, pbk=pbk, nh=nh: e.activation(out=junkb[:, 0:512], in_=pbk[:, :], func=AF.Square,
                                                                 accum_out=st4[:, 12 + nh:13 + nh]), reads=[pbk], writes=[st4, junkb])
                P.dve(lambda e: e.tensor_tensor(out=st4[:, 14:15], in0=st4[:, 12:13], in1=st4[:, 13:14], op=ALU.add),
                      reads=[st4], writes=[st4])
                P.act(lambda e: e.activation(out=st4[:, 15:16], in_=st4[:, 14:15], func=AF.Sqrt, scale=1.0 / D, bias=epsb[:]),
                      reads=[st4, epsb], writes=[st4])
                P.dve(lambda e: e.reciprocal(out=st4[:, 15:16], in_=st4[:, 15:16]), reads=[st4], writes=[st4])
                for nh in range(2):
                    pbk = pb[nh]
                    P.dve(lambda e, pbk=pbk, nh=nh: e.scalar_tensor_tensor(
                        out=tmpf[:, nh * 512:(nh + 1) * 512], in0=pbk[:, :], scalar=st4[:, 15:16],
                        in1=MOD[:, 2 * D + nh * 512:2 * D + (nh + 1) * 512], op0=ALU.mult, op1=ALU.mult),
                        reads=[pbk, st4, MOD], writes=[tmpf])
                P.pool(lambda e: e.tensor_tensor(out=tmpf[:], in0=tmpf[:], in1=x[:], op=ALU.add), reads=[tmpf, x], writes=[tmpf])
                P.dma("sp", lambda e: e.dma_start(out=y_all[ti * 128:(ti + 1) * 128, :], in_=tmpf[:]),
                      reads=[tmpf], writes=[dy], sembuf=dy)

            for kind in range(CFG["kinds"]):
                with ExitStack() as SP:
                    sbp = mk(SP, "_p%d" % kind)
                    kslcT = sbp("kslcT", [128, SEQ], BF16); kwinT = sbp("kwinT", [128, 8 * 128], BF16)
                    Yp = sbp("Yp", [128, 2, 2, 8 + 64], BF16)
                    vslc = sbp("vslc", [128, NT_P, 2, 65], BF16); vwin = sbp("vwin", [128, 8, 2, 65], BF16)
                    ckt = sbp("ckt", [96, SEQ], BF16)
                    maskct = sbp("maskct", [128, 17, 128], BF16); mmpt = sbp("mmpt", [128, 2, 64], BF16)
                    caus4t = sbp("caus4t", [128, 2, 512], BF16)
                    ld("sp", ckt, ckt[:], ck[:, 0:SEQ])
                    ld("sp", maskct, maskct[:], maskc.rearrange("r p c -> p r c"))
                    ld("sp", mmpt, mmpt[:], mmap_p[:, :, :])
                    ld("sp", caus4t, caus4t[:], caus4.rearrange("k p c -> p k c"))
                    kcT = sbp("kcT", [128, 264], BF16); vcs = sbp("vcs", [128, 2, 2, 65], BF16)
                    PTc = [sbp("PTc%d" % i, [128, 512], BF16) for i in range(4)]
                    load_mod(MOD, 0, kind)
                    for kvh in range(2):
                        P.pool(lambda e, kvh=kvh: e.memset(augq[kvh][:], 0.0), writes=[augq[kvh]])
                        ld("sp", augq[kvh], augq[kvh][0:5, :], qab[0, kvh])
                        ld("sp", qb0, qb0[0:1, kvh, :], qab[0, kvh, 0:1, :])
                    P.pool(lambda e: e.memset(Yp[:], 0.0), writes=[Yp])
                    P.pool(lambda e: e.memset(kcT[:], 0.0), writes=[kcT])
                    P.pool(lambda e: e.memset(vcs[:], 0.0), writes=[vcs])
                    P.pool(lambda e: e.memset(vcs[:, :, :, 64:65], 1.0), writes=[vcs])
                    P.pool(lambda e: e.memset(vslc[:, :, :, 64:65], 1.0), writes=[vslc])
                    P.pool(lambda e: e.memset(vwin[:, :, :, 64:65], 1.0), writes=[vwin])
                    for it in range(CFG["n_ptiles"]):
                      try:
                          ti = kind * NT_P + it
                          x = xt[ti % 2]
                          front(ti, x, kind)
                          if it >= NT_P - 4:
                              r0 = (it - (NT_P - 4)) * 128
                              P.dma("sp", lambda e, r0=r0: e.dma_start(out=win_p[kind, r0:r0 + 128, :],
                                                                       in_=z[:, C_KV + 512:C_KV + 768]),
                                    reads=[z], writes=[dwinp], sembuf=dwinp)
                          chk(1)
                          gla(False, it, kind)
                          chk(2)
                          t0 = it * 128
                          P.pe(lambda e: e.transpose(out=pT[:, 512:640], in_=kvbf[:, 256:384], identity=identb[:]),
                               reads=[kvbf, identb], writes=[pT])
                          P.pe(lambda e: e.transpose(out=pT[:, 640:768], in_=kvbf[:, 512:640], identity=identb[:]),
                               reads=[kvbf, identb], writes=[pT])
                          P.dve(lambda e, t0=t0: e.tensor_copy(out=kslcT[:, t0:t0 + 128], in_=pT[:, 512:640]), reads=[pT], writes=[kslcT])
                          P.dve(lambda e, it=it: e.tensor_copy(out=kwinT[:, (it % 8) * 128:(it % 8 + 1) * 128], in_=pT[:, 640:768]), reads=[pT], writes=[kwinT])
                          P.pool(lambda e, it=it: e.tensor_copy(out=vslc[:, it, :, 0:64],
                                                                in_=kvbf[:, 384:512].rearrange("p (k d) -> p k d", k=2)),
                                 reads=[kvbf], writes=[vslc])
                          P.pool(lambda e, it=it: e.tensor_copy(out=vwin[:, it % 8, :, 0:64],
                                                                in_=kvbf[:, 640:768].rearrange("p (k d) -> p k d", k=2)),
                                 reads=[kvbf], writes=[vwin])
                          pY = pb[0]
                          for ty in range(2):
                              for kvh in range(2):
                                  for r in range(2):
                                      c0 = ty * 128 + kvh * 64
                                      P.pe(lambda e, ty=ty, kvh=kvh, r=r, c0=c0: e.matmul(
                                          pY[r * 64:(r + 1) * 64, (ty * 2 + kvh) * 64:(ty * 2 + kvh + 1) * 64],
                                          lhsT=kvbf[:, c0:c0 + 64], rhs=selpt[:, r, :], start=True, stop=True),
                                          reads=[kvbf, selpt], writes=[pY])
                          if it > 0:
                              P.dve(lambda e: e.tensor_copy(out=Yp[:, :, :, 0:8], in_=Yp[:, :, :, 64:72]), reads=[Yp], writes=[Yp])
                          P.act(lambda e, it=it: e.activation(out=Yp[:, :, :, 8:72],
                                                              in_=pY[:, 0:256].rearrange("p (a b c) -> p a b c", a=2, b=2),
                                                              func=AF.Copy), reads=[pY], writes=[Yp])
                          chk(3)
                          cb0 = 8 * it - 1
                          compress(Yp, 8, kcT, 1 + cb0, vcs, cb0, it == 0)
                          for kvh in range(2):
                              P.dve(lambda e, kvh=kvh, it=it: e.tensor_scalar(out=augq[kvh][0:1, :], in0=qb0[0:1, kvh, :],
                                                                              scalar1=float(it), scalar2=None, op0=ALU.mult),
                                    reads=[qb0], writes=[augq[kvh]])
                          ld("pool", vma, vma[:], vmam_p[it])
                          chk(4)
                          jl = (8 * it + 6) // 128
                          for kvh in range(2):
                              r = slice(kvh * 64, kvh * 64 + 64)
                              for jc in range(jl + 1):
                                  mms = [(kcT[:, 1 + jc * 128:1 + (jc + 1) * 128], qTz[kvh][:, :], [kcT, qTz[kvh]], (0, 512)),
                                         (ckct[0:4, jc * 128:(jc + 1) * 128], augq[kvh][0:4, :], [ckct, augq[kvh]], (0, 512))]
                                  rr = it - 16 * jc
                                  if rr <= 16:
                                      for g in range(4):
                                          mms.append((identb[:], maskct[:, rr, :], [identb, maskct], (g * 128, 128)))
                                  unit(mms, 512, vcs[:, jc, kvh, :], [vcs], obank[kvh], (0, 512), jc == 0, jc == jl,
                                       keep=PTc[kvh * 2 + jc])
                          for kvh in range(2):
                              pim = pb[kvh]
                              for jc in range(jl + 1):
                                  P.pe(lambda e, kvh=kvh, jc=jc, pim=pim: e.matmul(pim[0:64, :], lhsT=mmpt[:, jc, :],
                                                                                   rhs=PTc[kvh * 2 + jc][:, :], start=(jc == 0),
                                                                                   stop=(jc == jl)),
                                       reads=[mmpt, PTc[kvh * 2 + jc]], writes=[pim])
                              P.act(lambda e, kvh=kvh, pim=pim: e.activation(out=z[0:64, kvh * 512:(kvh + 1) * 512], in_=pim[0:64, :], func=AF.Copy),
                                    reads=[pim], writes=[z])
                          chk(5)
                          branch_epilogue(0, False, True)
                          chk(6)
                          select_blocks(64, lambda kvh: vma[:, 0, kvh, :], lambda kvh: vma[:, 1, kvh, :], [vma], 7, False)
                          for kvh in range(2):
                              build_augq(augq[kvh], kvh, 0, False)
                          chk(7)
                          for kvh in range(2):
                              r = slice(kvh * 64, kvh * 64 + 64)
                              for j in range(it + 1):
                                  mms = [(kslcT[:, j * 128:(j + 1) * 128], qTz[kvh][:, :], [kslcT, qTz[kvh]], (0, 512)),
                                         (ckt[0:96, j * 128:(j + 1) * 128], augq[kvh][0:96, :], [ckt, augq[kvh]], (0, 512))]
                                  if j == it:
                                      mms.append((identb[:], caus4t[:, 0, :], [identb, caus4t], (0, 512)))
                                  unit(mms, 512, vslc[:, j, kvh, :], [vslc], obank[kvh], (0, 512), j == 0, j == it)
                          branch_epilogue(1, False, False)
                          chk(8)
                          j0 = max(0, it - 4)
                          for kvh in range(2):
                              r = slice(kvh * 64, kvh * 64 + 64)
                              for j in range(j0, it + 1):
                                  mms = [(kwinT[:, (j % 8) * 128:(j % 8 + 1) * 128], qTz[kvh][:, :], [kwinT, qTz[kvh]], (0, 512)),
                                         (ckt[0:4, j * 128:(j + 1) * 128], augq[kvh][0:4, :], [ckt, augq[kvh]], (0, 512))]
                                  if j == it:
                                      mms.append((identb[:], caus4t[:, 0, :], [identb, caus4t], (0, 512)))
                                  if j == it - 4:
                                      mms.append((identb[:], caus4t[:, 1, :], [identb, caus4t], (0, 512)))
                                  unit(mms, 512, vwin[:, j % 8, kvh, :], [vwin], obank[kvh], (0, 512), j == j0, j == it)
                          branch_epilogue(2, False, False)
                          chk(9)
                          back(ti, x)
                      except _Stop:
                          pass
                    P.flush()

            with ExitStack() as SS:
              if CFG["sample"]:
                sbs = mk(SS)
                Ys = sbs("Ys", [128, 2, 2, 8 + 512], BF16)
                kcTs = sbs("kcTs", [128, 520], BF16); vcss = sbs("vcss", [128, 4, 2, 65], BF16)
                ckt = sbs("ckt", [96, NKCOL], BF16)
                maskcst = sbs("maskcst", [128, 32], BF16); mmst = sbs("mmst", [128, 4, 128], BF16)
                newmt = sbs("newmt", [128, 16, 32], BF16); wmst = sbs("wmst", [128, 32], BF16)
                augq2 = [sbs("augq2_%d" % k, [96, 512], BF16) for k in range(2)]
                SX["Ssm"] = [Sf, sbs("Ssm1", [128, 2, 128])]
                SX["S0bf"] = sbs("S0bf", [128, 16, 2, 128], BF16)
                SX["ogT"] = tmpf
                ld("sp", ckt, ckt[:], ck[:, :])
                ld("sp", maskcst, maskcst[:], maskcs[:, :])
                ld("sp", mmst, mmst[:], mmap_s[:, :, :])
                ld("sp", newmt, newmt[:], newmask[:, :, :])
                ld("sp", wmst, wmst[:], wmasks[:, :])
                pgf = [sbs("pgf%d" % i, [128, 256]) for i in range(2)]
                pgb = [sbs("pgb%d" % i, [128, 256], BF16) for i in range(2)]
                kTp = [sbs("kTp%d" % i, [128, 128], BF16) for i in range(2)]
                vpg = [sbs("vpg%d" % i, [128, 2, 65], BF16) for i in range(2)]
                ptt = sbs("ptt", [128, NPG], I32); idxt = sbs("idxt", [128, NPG], I32); iot = sbs("iot", [128, 1], I32)
                idxt2 = sbs("idxt2", [128, NPG], I32)
                kTn = sbs("kTn", [128, 2, 128], BF16)
                vnw = sbs("vnw", [128, 2, 2, 65], BF16)
                PTk = [sbs("PTk%d" % i, [128, 32], BF16) for i in range(8)]
                ti = PB * NT_P
                x = xt[ti % 2]
                load_mod(MOD, 0, 2)
                for kvh in range(2):
                    P.pool(lambda e, kvh=kvh: e.memset(augq[kvh][:], 0.0), writes=[augq[kvh]])
                    P.pool(lambda e, kvh=kvh: e.memset(augq2[kvh][:], 0.0), writes=[augq2[kvh]])
                    ld("sp", augq[kvh], augq[kvh][0:5, :], qab[1, kvh])
                    ld("sp", augq2[kvh], augq2[kvh][0:5, :], qab[1, kvh])
                front(ti, x, 2)
                for s in range(SB_):
                    P.dma("sp", lambda e, s=s: e.dma_start(out=win_s[s, WBUF - TS:WBUF, :],
                                                           in_=z[s * TS:(s + 1) * TS, C_KV + 512:C_KV + 768]),
                          reads=[z], writes=[dwins], sembuf=dwins)
                P.dma("pool", lambda e: e.dma_start(out=win_s[:, 0:WBUF - TS, :].rearrange("s t f -> s (t f)"),
                                                    in_=swin[:, TS:WBUF, :].rearrange("s t f -> s (t f)")),
                      writes=[dwins], sembuf=dwins)
                gla(True, 0, 2)
                P.pe(lambda e: e.transpose(out=pT[:, 512:640], in_=kvbf[:, 256:384], identity=identb[:]),
                     reads=[kvbf, identb], writes=[pT])
                P.pe(lambda e: e.transpose(out=pT[:, 640:768], in_=kvbf[:, 512:640], identity=identb[:]),
                     reads=[kvbf, identb], writes=[pT])
                P.dve(lambda e: e.tensor_copy(out=kTn[:].rearrange("p a b -> p (a b)"), in_=pT[:, 512:768]), reads=[pT], writes=[kTn])
                P.pool(lambda e: e.memset(vnw[:, :, :, 64:65], 1.0), writes=[vnw])
                P.pool(lambda e: e.tensor_copy(out=vnw[:, 0, :, 0:64], in_=kvbf[:, 384:512].rearrange("p (k d) -> p k d", k=2)),
                       reads=[kvbf], writes=[vnw])
                P.pool(lambda e: e.tensor_copy(out=vnw[:, 1, :, 0:64], in_=kvbf[:, 640:768].rearrange("p (k d) -> p k d", k=2)),
                       reads=[kvbf], writes=[vnw])
                P.pool(lambda e: e.iota(iot[:], pattern=[[0, 1]], base=0, channel_multiplier=2), writes=[iot])
                for kvh in range(2):
                    P.pool(lambda e, kvh=kvh: e.memset(vpg[kvh][:, :, 64:65], 1.0), writes=[vpg[kvh]])
                qTsz = [qnbf, kvbf]
                for kvh in range(2):
                    r = slice(kvh * 64, kvh * 64 + 64)
                    ro = slice((1 - kvh) * 64, (1 - kvh) * 64 + 64)
                    P.dve(lambda e, kvh=kvh, ro=ro: e.memset(qTsz[kvh][ro, 0:512], 0.0), writes=[qTsz[kvh]])
                    P.dve(lambda e, kvh=kvh, r=r: e.tensor_copy(
                        out=qTsz[kvh][r, 0:512].rearrange("p (s g q) -> p g s q", s=16, g=4),
                        in_=qTz[kvh][r, :].rearrange("p (g s q) -> p g s q", g=4, s=16)), reads=[qTz[kvh]], writes=[qTsz[kvh]])
                P.pool(lambda e: e.memset(vcss[:], 0.0), writes=[vcss])
                P.pool(lambda e: e.memset(vcss[:, :, :, 64:65], 1.0), writes=[vcss])
                P.pool(lambda e: e.memset(kcTs[:], 0.0), writes=[kcTs])
                P.pool(lambda e: e.memset(Ys[:], 0.0), writes=[Ys])
                npg = [0]

                def evac_seq(kvh, s):
                    P.act(lambda e: e.activation(
                        out=tmpf[0:65, kvh * 512:(kvh + 1) * 512].rearrange("p (g s q) -> p s g q", g=4, s=16)[:, s, :, :],
                        in_=obank[kvh][0:65, s * 32:(s + 1) * 32].rearrange("p (g q) -> p g q", g=4),
                        func=AF.Copy), reads=[obank[kvh]], writes=[tmpf])

                def gather(s, j, col0):
                    pf = pgf[npg[0] % 2]
                    pbf = pgb[npg[0] % 2]
                    npg[0] += 1
                    P.dma("pool", lambda e: e.indirect_dma_start(
                        out=pf[:], out_offset=None, in_=cache[:, :],
                        in_offset=bass.IndirectOffsetOnAxis(ap=(idxt if col0 == 0 else idxt2)[:, j:j + 1], axis=0)),
                        reads=[idxt, idxt2], writes=[pf], sembuf=pf)
                    if npg[0] % 2 == 0:
                        P.dve(lambda e: e.tensor_copy(out=pbf[:], in_=pf[:]), reads=[pf], writes=[pbf])
                    else:
                        P.pool(lambda e: e.tensor_copy(out=pbf[:], in_=pf[:]), reads=[pf], writes=[pbf])
                    return pbf

                for s in range(SB_):
                    qc = (s * 32, 32)
                    ld("sp", ptt, ptt[:], ptab[s:s + 1, :].partition_broadcast(128))
                    P.dve(lambda e: e.tensor_scalar(out=idxt[:], in0=ptt[:], scalar1=256, scalar2=iot[:, 0:1], op0=ALU.mult,
                                                    op1=ALU.add), reads=[ptt, iot], writes=[idxt])
                    P.dve(lambda e: e.tensor_scalar(out=idxt2[:], in0=idxt[:], scalar1=1, scalar2=None, op0=ALU.add),
                          reads=[idxt], writes=[idxt2])
                    for j in range(NPG):
                        pbf = gather(s, j, 0)
                        pY = pb[0]
                        for ty in range(2):
                            for kvh in range(2):
                                for r in range(2):
                                    c0 = ty * 128 + kvh * 64
                                    P.pe(lambda e, ty=ty, kvh=kvh, r=r, c0=c0, pbf=pbf: e.matmul(
                                        pY[r * 64:(r + 1) * 64, (ty * 2 + kvh) * 64:(ty * 2 + kvh + 1) * 64],
                                        lhsT=pbf[:, c0:c0 + 64], rhs=selpt[:, r, :], start=True, stop=True),
                                        reads=[pbf, selpt], writes=[pY])
                        P.act(lambda e, j=j: e.activation(out=Ys[:, :, :, 8 + (j % 8) * 64:8 + (j % 8 + 1) * 64],
                                                          in_=pY[:, 0:256].rearrange("p (a b c) -> p a b c", a=2, b=2),
                                                          func=AF.Copy), reads=[pY], writes=[Ys])
                        if j % 8 == 7:
                            m = j // 8
                            compress(Ys, 64, kcTs, 64 * m, vcss, 64 * m - 1, m == 0)
                            P.dve(lambda e: e.tensor_copy(out=Ys[:, :, :, 0:8], in_=Ys[:, :, :, 512:520]), reads=[Ys], writes=[Ys])
                    for kvh in range(2):
                        r = slice(kvh * 64, kvh * 64 + 64)
                        for jc in range(4):
                            mms = [(kcTs[:, 1 + jc * 128:1 + (jc + 1) * 128], qTsz[kvh][:, s * 32:(s + 1) * 32], [kcTs, qTsz[kvh]], (0, 32)),
                                   (ckct[0:4, jc * 128:(jc + 1) * 128], augq[kvh][0:4, qc[0]:qc[0] + 32], [ckct, augq[kvh]], (0, 32))]
                            if jc == 3:
                                mms.append((identb[:], maskcst[:, :], [identb, maskcst], (0, 32)))
                            unit(mms, 32, vcss[:, jc, kvh, :], [vcss], obank[kvh], qc, jc == 0, jc == 3, keep=PTk[kvh * 4 + jc])
                        evac_seq(kvh, s)
                        pim = pb[kvh]
                        for jc in range(4):
                            P.pe(lambda e, kvh=kvh, jc=jc, pim=pim, qc=qc: e.matmul(pim[:, qc[0]:qc[0] + 32], lhsT=mmst[:, jc, :],
                                                                             rhs=PTk[kvh * 4 + jc][:, :], start=(jc == 0), stop=(jc == 3)),
                                 reads=[mmst, PTk[kvh * 4 + jc]], writes=[pim])
                        P.act(lambda e, kvh=kvh, pim=pim, s=s: e.activation(
                            out=z[:, kvh * 512:(kvh + 1) * 512].rearrange("p (g s q) -> p s g q", g=4, s=16)[:, s, :, :],
                            in_=pim[:, s * 32:(s + 1) * 32].rearrange("p (g q) -> p g q", g=4),
                            func=AF.Copy), reads=[pim], writes=[z])
                branch_epilogue(0, True, True)
                ld("sp", z, z[:, 2048:2560], vmam_s.rearrange("p a b c -> p (a b c)"))
                select_blocks(128, lambda kvh: z[:, 2048 + kvh * 128:2048 + (kvh + 1) * 128],
                              lambda kvh: z[:, 2304 + kvh * 128:2304 + (kvh + 1) * 128], [z], 6, True)
                for kvh in range(2):
                    build_augq(augq[kvh], kvh, 0, True)
                    build_augq(augq2[kvh], kvh, 64, True)
                for s in range(SB_):
                    qc = (s * 32, 32)
                    ld("sp", ptt, ptt[:], ptab[s:s + 1, :].partition_broadcast(128))
                    P.dve(lambda e: e.tensor_scalar(out=idxt[:], in0=ptt[:], scalar1=256, scalar2=iot[:, 0:1], op0=ALU.mult,
                                                    op1=ALU.add), reads=[ptt, iot], writes=[idxt])
                    P.dve(lambda e: e.tensor_scalar(out=idxt2[:], in0=idxt[:], scalar1=1, scalar2=None, op0=ALU.add),
                          reads=[idxt], writes=[idxt2])
                    for j in range(NPG + 1):
                        if j < NPG:
                            pbf = gather(s, j, 256)
                            kt = kTp[j % 2]; vp = vpg[j % 2]
                            P.pe(lambda e, pbf=pbf: e.transpose(out=pT[:, 768:896], in_=pbf[:, 0:128], identity=identb[:]),
                                 reads=[pbf, identb], writes=[pT])
                            P.act(lambda e, kt=kt: e.activation(out=kt[:], in_=pT[:, 768:896], func=AF.Copy), reads=[pT], writes=[kt])
                            P.pool(lambda e, vp=vp, pbf=pbf: e.tensor_copy(out=vp[:, :, 0:64],
                                                                           in_=pbf[:, 128:256].rearrange("p (k d) -> p k d", k=2)),
                                   reads=[pbf], writes=[vp])
                        for kvh in range(2):
                            r = slice(kvh * 64, kvh * 64 + 64)
                            if j < NPG:
                                aq = augq[kvh] if j < 32 else augq2[kvh]
                                mms = [(kt[:, :], qTsz[kvh][:, s * 32:(s + 1) * 32], [kt, qTsz[kvh]], (0, 32)),
                                       (ckt[0:96, j * 128:(j + 1) * 128], aq[0:96, qc[0]:qc[0] + 32], [ckt, aq], (0, 32))]
                                unit(mms, 32, vp[:, kvh, :], [vp], obank[kvh], qc, j == 0, False)
                            else:
                                mms = [(kTn[:, 0, :], qTsz[kvh][:, s * 32:(s + 1) * 32], [kTn, qTsz[kvh]], (0, 32)),
                                       (ckt[0:4, PAST:PAST + 128], augq[kvh][0:4, qc[0]:qc[0] + 32], [ckt, augq[kvh]], (0, 32)),
                                       (identb[:], newmt[:, s, :], [identb, newmt], (0, 32))]
                                unit(mms, 32, vnw[:, 0, kvh, :], [vnw], obank[kvh], qc, False, True)
                                evac_seq(kvh, s)
                branch_epilogue(1, True, False)
                for s in range(SB_):
                    qc = (s * 32, 32)
                    for w in range(5):
                        if w < 4:
                            pf = pgf[npg[0] % 2]; pbf = pgb[npg[0] % 2]
                            npg[0] += 1
                            ld("sp", pf, pf[:], swin[s, w * 128:(w + 1) * 128, :])
                            P.dve(lambda e, pf=pf, pbf=pbf: e.tensor_copy(out=pbf[:], in_=pf[:]), reads=[pf], writes=[pbf])
                            kt = kTp[w % 2]; vp = vpg[w % 2]
                            P.pe(lambda e, pbf=pbf: e.transpose(out=pT[:, 768:896], in_=pbf[:, 0:128], identity=identb[:]),
                                 reads=[pbf, identb], writes=[pT])
                            P.act(lambda e, kt=kt: e.activation(out=kt[:], in_=pT[:, 768:896], func=AF.Copy), reads=[pT], writes=[kt])
                            P.pool(lambda e, vp=vp, pbf=pbf: e.tensor_copy(out=vp[:, :, 0:64],
                                                                           in_=pbf[:, 128:256].rearrange("p (k d) -> p k d", k=2)),
                                   reads=[pbf], writes=[vp])
                        for kvh in range(2):
                            r = slice(kvh * 64, kvh * 64 + 64)
                            if w < 4:
                                c0 = PAST - WBUF + w * 128
                                mms = [(kt[:, :], qTsz[kvh][:, s * 32:(s + 1) * 32], [kt, qTsz[kvh]], (0, 32)),
                                       (ckt[0:4, c0:c0 + 128], augq[kvh][0:4, qc[0]:qc[0] + 32], [ckt, augq[kvh]], (0, 32))]
                                if w == 0:
                                    mms.append((identb[:], wmst[:, :], [identb, wmst], (0, 32)))
                                unit(mms, 32, vp[:, kvh, :], [vp], obank[kvh], qc, w == 0, False)
                            else:
                                mms = [(kTn[:, 1, :], qTsz[kvh][:, s * 32:(s + 1) * 32], [kTn, qTsz[kvh]], (0, 32)),
                                       (ckt[0:4, PAST:PAST + 128], augq[kvh][0:4, qc[0]:qc[0] + 32], [ckt, augq[kvh]], (0, 32)),
                                       (identb[:], newmt[:, s, :], [identb, newmt], (0, 32))]
                                unit(mms, 32, vnw[:, 1, kvh, :], [vnw], obank[kvh], qc, False, True)
                                evac_seq(kvh, s)
                branch_epilogue(2, True, False)
                back(ti, x)
                P.flush()

        with ExitStack() as S2:
          if CFG["ffn"]:
            sb2 = mk(S2)
            wup_bf = sb2("wup_bf", [128, 8, D_FF], BF16)
            wdn_bf = sb2("wdn_bf", [128, 32, D], BF16)
            MOD2 = sb2("MOD2", [128, 3 * D])
            x1t = [sb2("x1t%d" % i, [128, D]) for i in range(2)]
            junk2 = sb2("junk2", [128, D], BF16)
            st2 = sb2("st2", [128, 8])
            tmp2 = sb2("tmp2", [128, D])
            h2bf = sb2("h2bf", [128, D], BF16); h2T = sb2("h2T", [128, D], BF16)
            hr = [sb2("hr%d" % i, [128, 512], BF16) for i in range(2)]
            hsq = sb2("hsq", [128, 32, 128], BF16)
            yt = sb2("yt", [128, D])
            with ExitStack() as S2a0:
                ada_rows(mk(S2a0), 3 * D, 2, 3)
                P.flush()
            with ExitStack() as S2a:
                sba = mk(S2a)
                stg2 = [sba("stg2_%d" % i, [128, 2048]) for i in range(2)]
                for kc in range(16):
                    s_ = stg2[kc % 2]
                    k8, hh = kc // 2, kc % 2
                    ld("sp" if kc % 2 == 0 else "pool", s_, s_[:], w_up[k8 * 128:(k8 + 1) * 128, hh * 2048:(hh + 1) * 2048])
                    if kc % 2 == 0:
                        P.dve(lambda e, s_=s_, k8=k8, hh=hh: e.tensor_copy(out=wup_bf[:, k8, hh * 2048:(hh + 1) * 2048], in_=s_[:]),
                              reads=[s_], writes=[wup_bf])
                    else:
                        P.act(lambda e, s_=s_, k8=k8, hh=hh: e.activation(out=wup_bf[:, k8, hh * 2048:(hh + 1) * 2048], in_=s_[:],
                                                                          func=AF.Copy), reads=[s_], writes=[wup_bf])
                for q4 in range(16):
                    s_ = stg2[q4 % 2]
                    ld("sp" if q4 % 2 == 0 else "pool", s_, s_[:].rearrange("p (a n) -> p a n", a=2),
                       w_down[q4 * 256:(q4 + 1) * 256, :].rearrange("(a p) n -> p a n", p=128))
                    if q4 % 2 == 0:
                        P.dve(lambda e, s_=s_, q4=q4: e.tensor_copy(out=wdn_bf[:, q4 * 2:(q4 + 1) * 2, :].rearrange("p a n -> p (a n)"),
                                                                    in_=s_[:]), reads=[s_], writes=[wdn_bf])
                    else:
                        P.act(lambda e, s_=s_, q4=q4: e.activation(out=wdn_bf[:, q4 * 2:(q4 + 1) * 2, :].rearrange("p a n -> p (a n)"),
                                                                   in_=s_[:], func=AF.Copy), reads=[s_], writes=[wdn_bf])
                P.flush()
            for ti in range(NTILE):
                kind = min(ti // NT_P, 2)
                if ti % NT_P == 0:
                    load_mod(MOD2, 3 * D, kind)
                xx = x1t[ti % 2]
                ld("sp", xx, xx[:], y_all[ti * 128:(ti + 1) * 128, :])
                rms_rstd(xx[:], [xx], junk2[:], junk2, st2, 0, D)
                P.dve(lambda e, xx=xx: e.scalar_tensor_tensor(out=tmp2[:], in0=xx[:], scalar=st2[:, 1:2], in1=MOD2[:, D:2 * D],
                                                              op0=ALU.mult, op1=ALU.mult), reads=[xx, st2, MOD2], writes=[tmp2])
                P.pool(lambda e: e.tensor_tensor(out=h2bf[:], in0=tmp2[:], in1=MOD2[:, 0:D], op=ALU.add),
                       reads=[tmp2, MOD2], writes=[h2bf])
                transpose8(h2bf, h2T)
                for f4 in range(8):
                    pbk = pb[2 + (f4 % 3)]
                    for ff in range(4):
                        fch = f4 * 4 + ff
                        for kc in range(8):
                            P.pe(lambda e, kc=kc, pbk=pbk, ff=ff, fch=fch: e.matmul(
                                pbk[:, ff * 128:(ff + 1) * 128], lhsT=wup_bf[:, kc, fch * 128:(fch + 1) * 128],
                                rhs=h2T[:, kc * 128:(kc + 1) * 128], start=(kc == 0), stop=(kc == 7)),
                                reads=[wup_bf, h2T], writes=[pbk])
                    hrr = hr[f4 % 2]
                    P.act(lambda e, pbk=pbk, hrr=hrr: e.activation(out=hrr[:], in_=pbk[:, :], func=AF.Relu), reads=[pbk], writes=[hrr])
                    if f4 % 2 == 0:
                        P.dve(lambda e, hrr=hrr, f4=f4: e.tensor_tensor(out=hsq[:, f4 * 4:(f4 + 1) * 4, :].rearrange("p a t -> p (a t)"),
                                                                        in0=hrr[:], in1=hrr[:], op=ALU.mult), reads=[hrr], writes=[hsq])
                    else:
                        P.pool(lambda e, hrr=hrr, f4=f4: e.tensor_tensor(out=hsq[:, f4 * 4:(f4 + 1) * 4, :].rearrange("p a t -> p (a t)"),
                                                                         in0=hrr[:], in1=hrr[:], op=ALU.mult), reads=[hrr], writes=[hsq])
                for nh in range(2):
                    pbk = pb[nh]
                    for fch in range(32):
                        P.pe(lambda e, fch=fch, pbk=pbk, nh=nh: e.matmul(pbk[:, :], lhsT=hsq[:, fch, :],
                                                                         rhs=wdn_bf[:, fch, nh * 512:(nh + 1) * 512],
                                                                         start=(fch == 0), stop=(fch == 31)),
                             reads=[hsq, wdn_bf], writes=[pbk])
                    P.act(lambda e, pbk=pbk, nh=nh: e.activation(out=junk2[:, 0:512], in_=pbk[:, :], func=AF.Square,
                                                                 accum_out=st2[:, 2 + nh:3 + nh]), reads=[pbk], writes=[st2, junk2])
                P.dve(lambda e: e.tensor_tensor(out=st2[:, 4:5], in0=st2[:, 2:3], in1=st2[:, 3:4], op=ALU.add), reads=[st2], writes=[st2])
                P.act(lambda e: e.activation(out=st2[:, 5:6], in_=st2[:, 4:5], func=AF.Sqrt, scale=1.0 / D, bias=epsb[:]),
                      reads=[st2, epsb], writes=[st2])
                P.dve(lambda e: e.reciprocal(out=st2[:, 5:6], in_=st2[:, 5:6]), reads=[st2], writes=[st2])
                for nh in range(2):
                    pbk = pb[nh]
                    P.dve(lambda e, pbk=pbk, nh=nh: e.scalar_tensor_tensor(
                        out=tmp2[:, nh * 512:(nh + 1) * 512], in0=pbk[:, :], scalar=st2[:, 5:6],
                        in1=MOD2[:, 2 * D + nh * 512:2 * D + (nh + 1) * 512], op0=ALU.mult, op1=ALU.mult),
                        reads=[pbk, st2, MOD2], writes=[tmp2])
                P.pool(lambda e, xx=xx: e.tensor_tensor(out=yt[:], in0=tmp2[:], in1=xx[:], op=ALU.add), reads=[tmp2, xx], writes=[yt])
                P.dma("sp", lambda e, ti=ti: e.dma_start(out=y_all[ti * 128:(ti + 1) * 128, :], in_=yt[:]),
                      reads=[yt], writes=[dy], sembuf=dy)
            P.flush()
    return nc


_NC = None


def _consts():
    f32 = np.float32
    c = {}
    c["ident_f"] = np.eye(128, dtype=f32)
    sel = np.zeros((3, 18, 128), f32)
    sel[0, 0, :] = 1.0
    sel[1, 1, :] = 1.0
    for m in range(128):
        sel[2, 2 + m // TS, m] = 1.0
    tp = np.arange(128)
    same = [np.ones((128, 128), bool), (tp[:, None] // TS == tp[None, :] // TS)]
    tri = np.zeros((2, 2, 128, 128), f32)
    causm = np.zeros((2, 128, 512), f32)
    for k in range(2):
        tri[k, 0] = ((tp[:, None] > tp[None, :]) & same[k]) * (-1.0 / 16)
        tri[k, 1] = ((tp[:, None] <= tp[None, :]) & same[k]) * (-1.0 / 16)
        causm[k] = np.tile(((tp[:, None] <= tp[None, :]) & same[k]).astype(f32), (1, 4))
    c["tri"] = tri
    c["causm"] = causm
    seqind = np.zeros((2, 128, 16), f32)
    seqind[0, :, 0] = -1.0 / 16
    seqmask = np.zeros((128, 16), f32)
    for s in range(16):
        seqind[1, s * TS:(s + 1) * TS, s] = -1.0 / 16
        seqmask[s * TS:(s + 1) * TS, s] = 1.0
    c["seqind"] = seqind
    c["seqmask"] = seqmask
    ck = np.zeros((96, NKCOL), f32)
    key = np.arange(PAST)
    ck[32 + (key // 64) % 64, key] = 1.0
    ck[0, :] = 1.0
    ck[1, :] = 1.0
    ck[2, :PAST] = key % 128
    ck[3, :PAST] = key - key % 128
    ck[4, :PAST] = 1.0
    m = np.arange(128)
    ck[2, PAST:] = m % TS
    ck[3, PAST:] = PAST
    c["ck"] = ck
    ckc = np.zeros((4, 512), f32)
    cend = 16 * np.arange(512) + 31
    ckc[0] = 1.0
    ckc[1] = 1.0
    ckc[2] = cend % 128
    ckc[3] = cend - cend % 128
    c["ckc"] = ckc
    slope = np.array([2.0 ** -(h + 1) for h in range(8)], f32).reshape(2, 4)
    qab = np.zeros((2, 2, 5, 512), f32)
    for kvh in range(2):
        for g in range(4):
            sl = slope[kvh, g]
            cols = slice(g * 128, (g + 1) * 128)
            qab[0, kvh, 0, cols] = -128.0 * sl
            qab[0, kvh, 1, cols] = -sl * np.arange(128)
            qab[0, kvh, 2, cols] = sl
            qab[0, kvh, 3, cols] = sl
            qab[0, kvh, 4, cols] = -NEG
            for s in range(16):
                cs = slice(s * 32 + g * 8, s * 32 + g * 8 + 8)
                qab[1, kvh, 0, cs] = -128.0 * 64 * sl
                qab[1, kvh, 1, cs] = -sl * np.arange(8)
                qab[1, kvh, 2, cs] = sl
                qab[1, kvh, 3, cs] = sl
                qab[1, kvh, 4, cs] = -NEG
    c["qab"] = qab
    maskc = np.zeros((17, 128, 128), f32)
    for r in range(17):
        maskc[r] = np.where(16 * tp[:, None] + 31 > 128 * r + tp[None, :], -NEG, 0.0)
    c["maskc"] = maskc
    maskcs = np.zeros((128, 32), f32)
    maskcs[127, :] = -NEG
    c["maskcs"] = maskcs

    def mmap(ncb, nsb):
        start = np.arange(ncb) * 16
        bs = np.arange(nsb) * 64
        ov = np.minimum(start[:, None] + 32, bs[None, :] + 64) - np.maximum(start[:, None], bs[None, :])
        return (np.clip(ov, 0, None) / 32).astype(f32)
    mp = np.zeros((256, 64), f32)
    mp[:255] = mmap(255, 64)
    c["mmap_p"] = np.ascontiguousarray(mp.reshape(2, 128, 64).transpose(1, 0, 2))
    ms = np.zeros((512, 128), f32)
    ms[:511] = mmap(511, 129)[:, :128]
    c["mmap_s"] = np.ascontiguousarray(ms.reshape(4, 128, 128).transpose(1, 0, 2))
    vmam_p = np.zeros((NT_P, 128, 2, 2, 64), f32)
    blk = np.arange(64)
    for it in range(NT_P):
        cur = (128 * it + tp) // 64
        valid = blk[None, :] <= cur[:, None]
        forced = valid & ((blk[None, :] == 0) | (blk[None, :] == cur[:, None]) | (blk[None, :] == cur[:, None] - 1))
        vm = (valid & ~forced).astype(f32)
        am = np.where(forced, 1.0e4, np.where(valid, 0.0, -1.0e4)).astype(f32)
        vmam_p[it, :, 0, :, :] = vm[:, None, :]
        vmam_p[it, :, 1, :, :] = am[:, None, :]
    c["vmam_p"] = vmam_p
    vmam_s = np.zeros((128, 2, 2, 128), f32)
    vmam_s[:, 0, :, :] = 1.0
    vmam_s[:, 0, :, 0] = 0.0
    vmam_s[:, 0, :, 127] = 0.0
    vmam_s[:, 1, :, 0] = 1.0e4
    vmam_s[:, 1, :, 127] = 1.0e4
    c["vmam_s"] = vmam_s
    caus4 = np.zeros((2, 128, 512), f32)
    caus4[0] = np.tile(np.where(tp[:, None] > tp[None, :], -NEG, 0.0), (1, 4))
    caus4[1] = np.tile(np.where(tp[:, None] < tp[None, :], -NEG, 0.0), (1, 4))
    c["caus4"] = caus4
    newmask = np.full((128, 16, 32), -NEG, f32)
    for s in range(16):
        for qk in range(TS):
            for g in range(4):
                for q in range(TS):
                    if qk <= q:
                        newmask[s * TS + qk, s, g * 8 + q] = 0.0
    c["newmask"] = newmask
    wm = np.zeros((128, 32), f32)
    for g in range(4):
        for q in range(TS):
            wm[:q, g * 8 + q] = -NEG
    c["wmasks"] = wm
    selp = np.zeros((128, 2, 64), f32)
    for p in range(64):
        for r in range(2):
            selp[2 * p + r, r, p] = 1.0
    c["selp"] = selp
    import ml_dtypes
    for k in ("causm", "ck", "ckc", "qab", "maskc", "maskcs", "mmap_p", "mmap_s", "caus4", "newmask", "wmasks", "selp"):
        c[k] = c[k].astype(ml_dtypes.bfloat16)
    return c


def kernel(x_prompt, x_sample, cache_kv, state_win, state_gla, page_table, c_prompt, c_sample,
           norm_mix_pre, norm_mix_post, norm_ffn_pre, norm_ffn_post, w_ada, b_ada, w_in,
           gla_w_gate, gla_b_gate, gla_norm, cmp_pos, cmp_w1, cmp_b1, cmp_w2, cmp_b2,
           w_out, w_up, w_down):
    global _NC
    f32 = np.float32
    if _NC is None:
        _NC = build_nc()
    nc = _NC
    A = lambda a: np.asarray(a, f32)
    x_prompt = A(x_prompt); x_sample = A(x_sample)
    consts = _consts()
    w1 = A(cmp_w1)[0].reshape(2, 2, 8, 2, 64, 256)
    w1p = np.ascontiguousarray(w1.transpose(3, 4, 0, 1, 2, 5).reshape(128, 2, 2, 8, 256))
    pos = A(cmp_pos)[0].reshape(2, 2, 8, 2, 64)
    posp = pos.transpose(3, 4, 0, 1, 2).reshape(128, 2, 2, 8)
    posrep = np.ascontiguousarray(np.repeat(posp[..., None], 16, axis=-1))
    w2 = A(cmp_w2)[0].reshape(2, 2, 128, 64)
    w2l = np.ascontiguousarray(w2.transpose(2, 0, 1, 3))
    b2 = A(cmp_b2)[0]
    shared = {
        "w_ada": A(w_ada)[0], "b_ada": A(b_ada)[0].reshape(1, -1),
        "gains": np.ascontiguousarray(np.stack([A(norm_mix_pre)[0], A(norm_mix_post)[0], A(norm_ffn_pre)[0], A(norm_ffn_post)[0]])),
        "w_in": A(w_in)[0], "w_out": A(w_out)[0], "w_up": A(w_up)[0], "w_down": A(w_down)[0],
        "wg": A(gla_w_gate)[0], "bg": A(gla_b_gate)[0].reshape(1, -1),
        "gnorm4": np.ascontiguousarray(np.tile(A(gla_norm)[0], 4).reshape(1, 512)),
        "cache": A(cache_kv)[0].reshape(-1, 256),
        "w1p": w1p, "posrep": posrep, "b1r": A(cmp_b1)[0], "w2l": w2l,
        "b2k": np.ascontiguousarray(np.concatenate([b2[0], b2[0]]).reshape(128, 1)), "b2v": b2[1].reshape(1, 64),
    }
    shared.update(consts)
    in_maps = []
    pt = np.asarray(page_table).astype(np.int32)
    for c in range(NCORES):
        xp = x_prompt[PB * c:PB * (c + 1)].reshape(PB * SEQ, D)
        xsm = x_sample[SB_ * c:SB_ * (c + 1)].reshape(SB_ * TS, D)
        cc = np.concatenate([A(c_prompt)[PB * c:PB * (c + 1)], A(c_sample)[SB_ * c:SB_ * (c + 1)]], axis=0)
        cTa = np.ascontiguousarray(cc.T.reshape(8, 128, 18).transpose(1, 0, 2))
        m = dict(shared)
        m.update({
            "xs": np.ascontiguousarray(np.concatenate([xp, xsm], axis=0)),
            "cT": cTa,
            "sgla": np.ascontiguousarray(A(state_gla)[0, SB_ * c:SB_ * (c + 1)]),
            "swin": np.ascontiguousarray(A(state_win)[0, SB_ * c:SB_ * (c + 1)].reshape(SB_, WBUF, 256)),
            "ptab": np.ascontiguousarray(pt[SB_ * c:SB_ * (c + 1)]),
        })
        in_maps.append(m)
    res = run_bass_kernel_spmd(nc, in_maps, core_ids=list(range(NCORES)))
    R = res.results
    NP = PB * SEQ
    y_p = np.concatenate([r["y_all"][:NP].reshape(PB, SEQ, D) for r in R], axis=0)
    y_s = np.concatenate([r["y_all"][NP:].reshape(SB_, TS, D) for r in R], axis=0)
    kv_p = np.concatenate([r["kvr"][:NP].reshape(PB, SEQ, 4, 2, 64) for r in R], axis=0)[None]
    kv_s = np.concatenate([r["kvr"][NP:].reshape(SB_, TS, 4, 2, 64) for r in R], axis=0)[None]
    wp = np.concatenate([r["win_p"].reshape(PB, WBUF, 2, 2, 64) for r in R], axis=0)[None]
    ws = np.concatenate([r["win_s"].reshape(SB_, WBUF, 2, 2, 64) for r in R], axis=0)[None]
    gp = np.concatenate([r["gla_p"] for r in R], axis=0)[None]
    gs = np.concatenate([r["gla_s"] for r in R], axis=0)[None]
    return (y_p.astype(f32), y_s.astype(f32), kv_p.astype(f32), kv_s.astype(f32),
            wp.astype(f32), ws.astype(f32), gp.astype(f32), gs.astype(f32))
```

```python
import numpy as np
from contextlib import ExitStack
import concourse.bass as bass
import concourse.mybir as mybir
from concourse.bass_utils import run_bass_kernel_spmd

F32 = mybir.dt.float32
BF16 = mybir.dt.bfloat16
I32 = mybir.dt.int32
ALU = mybir.AluOpType
AF = mybir.ActivationFunctionType

NCORES = 8
D = 1024
SEQ = 4096
PB = 2
SB_ = 16
TS = 8
NT_P = SEQ // 128
D_IN = 2856
D_FF = 4096
C_ZQ, C_ZK, C_ZV, C_ZA, C_ZR, C_QN, C_KV, C_ZG = 0, 256, 512, 1024, 1040, 1552, 2064, 2832
EPS = 1e-6
PAST = 8192
NPG = 64
WBUF = 512
NEG = 4096.0
NTOK = PB * SEQ + SB_ * TS
NTILE = NTOK // 128
NKCOL = PAST + 128
class _Stop(Exception):
    pass


def chk(k):
    if CFG["stage"] < k:
        raise _Stop()


CFG = {"stage": 99, "pool_pages": 10240, "n_ptiles": NT_P, "sample": True, "ffn": True, "kinds": PB}


class Buf:
    __slots__ = ("name", "t", "last_w", "readers", "dsem", "dcnt")

    def __init__(self, name, t):
        self.name = name
        self.t = t
        self.last_w = None
        self.readers = {}
        self.dsem = None
        self.dcnt = 0

    def __getitem__(self, idx):
        return self.t[idx]


class Op:
    __slots__ = ("eng", "fn", "deps", "signal", "isdma", "buf", "cnt")

    def __init__(self, eng, fn, isdma=False, buf=None):
        self.eng = eng
        self.fn = fn
        self.deps = set()
        self.signal = False
        self.isdma = isdma
        self.buf = buf
        self.cnt = None


class Prog:
    ENGS = ("pe", "dve", "act", "pool", "sp")

    def __init__(self, nc, stack):
        self.nc = nc
        self.stack = stack
        self.ops = []
        self.bufs = []
        self.sems = {e: stack.enter_context(nc.semaphore("s_" + e)) for e in self.ENGS}
        self.cnt = {e: 0 for e in self.ENGS}
        self.waited = {e: {} for e in self.ENGS}
        self.dbufs = []
        self.engmap = {"pe": nc.tensor, "dve": nc.vector, "act": nc.scalar, "pool": nc.gpsimd, "sp": nc.sync}

    def buf(self, name, t):
        b = Buf(name, t)
        self.bufs.append(b)
        return b

    def op(self, eng, fn, reads=(), writes=(), isdma=False, dmabuf=None):
        o = Op(eng, fn, isdma, dmabuf)
        i = len(self.ops)
        for b in reads:
            if b.last_w is not None:
                o.deps.add(b.last_w)
        for b in writes:
            if b.last_w is not None:
                o.deps.add(b.last_w)
            for r in b.readers.values():
                o.deps.add(r)
        rk = ("dma", i) if isdma else eng
        for b in reads:
            b.readers[rk] = i
        for b in writes:
            b.last_w = i
            b.readers = {}
        o.deps.discard(i)
        self.ops.append(o)
        return o

    def pe(self, fn, reads=(), writes=()):
        return self.op("pe", fn, reads, writes)

    def dve(self, fn, reads=(), writes=()):
        return self.op("dve", fn, reads, writes)

    def act(self, fn, reads=(), writes=()):
        return self.op("act", fn, reads, writes)

    def pool(self, fn, reads=(), writes=()):
        return self.op("pool", fn, reads, writes)

    def dma(self, q, fn, reads=(), writes=(), sembuf=None):
        return self.op(q, fn, reads, writes, isdma=True, dmabuf=sembuf)

    def _wait(self, ename, s, v):
        w = self.waited[ename]
        k = id(s)
        if w.get(k, 0) >= v:
            return
        self.engmap[ename].wait_ge(s, v)
        w[k] = v

    def flush(self):
        ops = self.ops
        for o in ops:
            for d in o.deps:
                p = ops[d]
                if p.isdma or not (p.eng == o.eng and p.eng == "pe"):
                    p.signal = True
        last = {}
        for o in ops:
            if not o.isdma:
                last[o.eng] = o
        for o in last.values():
            o.signal = True
        for o in ops:
            if o.isdma:
                b = o.buf
                if b.dsem is None:
                    b.dsem = self.stack.enter_context(self.nc.semaphore("d_" + b.name))
                    self.dbufs.append(b)
                b.dcnt += 16
                o.cnt = (b.dsem, b.dcnt)
            elif o.signal:
                self.cnt[o.eng] += 1
                o.cnt = (self.sems[o.eng], self.cnt[o.eng])
        for o in ops:
            need = {}
            for d in o.deps:
                p = ops[d]
                if p.cnt is None:
                    continue
                if (not p.isdma) and (not o.isdma) and p.eng == "pe" and o.eng == "pe":
                    continue
                s, v = p.cnt
                k = id(s)
                if k not in need or need[k][1] < v:
                    need[k] = (s, v)
            for k, (s, v) in need.items():
                self._wait(o.eng, s, v)
            ins = o.fn(self.engmap[o.eng])
            if o.cnt is not None:
                ins.then_inc(o.cnt[0], 16 if o.isdma else 1)
        for e in self.ENGS:
            for e2 in self.ENGS:
                if e2 != e and self.cnt[e2] > 0:
                    self._wait(e, self.sems[e2], self.cnt[e2])
            for b in self.dbufs:
                self._wait(e, b.dsem, b.dcnt)
        self.ops = []
        for b in self.bufs:
            b.last_w = None
            b.readers = {}


def build_nc():
    nc = bass.Bass("TRN2", target_bir_lowering=False)

    def din(name, shape, dt=F32):
        return nc.dram_tensor(name, list(shape), dt, kind="ExternalInput").ap()

    def dout(name, shape, dt=F32):
        return nc.dram_tensor(name, list(shape), dt, kind="ExternalOutput").ap()

    xs = din("xs", [NTOK, D])
    cT = din("cT", [128, 8, 18])
    w_ada = din("w_ada", [D, 6 * D])
    b_ada = din("b_ada", [1, 6 * D])
    gains = din("gains", [4, D])
    w_in = din("w_in", [D, D_IN])
    w_out = din("w_out", [D, D])
    w_up = din("w_up", [D, D_FF])
    w_down = din("w_down", [D_FF, D])
    wg = din("wg", [16, 256])
    bg = din("bg", [1, 256])
    gnorm4 = din("gnorm4", [1, 512])
    sgla = din("sgla", [SB_, 4, 64, 128])
    swin = din("swin", [SB_, WBUF, 256])
    cache = din("cache", [CFG["pool_pages"] * 128 * 2, 256])
    ptab = din("ptab", [SB_, NPG], I32)
    w1p = din("w1p", [128, 2, 2, 8, 256])
    posrep = din("posrep", [128, 2, 2, 8, 16])
    b1r = din("b1r", [2, 256])
    w2l = din("w2l", [128, 2, 2, 64])
    b2k = din("b2k", [128, 1])
    b2v = din("b2v", [1, 64])
    ident_f = din("ident_f", [128, 128])
    tri = din("tri", [2, 2, 128, 128])
    seqind = din("seqind", [2, 128, 16])
    seqmask = din("seqmask", [128, 16])
    causm = din("causm", [2, 128, 512], BF16)
    ck = din("ck", [96, NKCOL], BF16)
    ckc = din("ckc", [4, 512], BF16)
    qab = din("qab", [2, 2, 5, 512], BF16)
    maskc = din("maskc", [17, 128, 128], BF16)
    maskcs = din("maskcs", [128, 32], BF16)
    mmap_p = din("mmap_p", [128, 2, 64], BF16)
    mmap_s = din("mmap_s", [128, 4, 128], BF16)
    vmam_p = din("vmam_p", [NT_P, 128, 2, 2, 64])
    vmam_s = din("vmam_s", [128, 2, 2, 128])
    caus4 = din("caus4", [2, 128, 512], BF16)
    newmask = din("newmask", [128, 16, 32], BF16)
    wmasks = din("wmasks", [128, 32], BF16)
    selp = din("selp", [128, 2, 64], BF16)

    y_all = dout("y_all", [NTOK, D])
    kvr = dout("kvr", [NTOK, 512])
    win_p = dout("win_p", [PB, WBUF, 256])
    win_s = dout("win_s", [SB_, WBUF, 256])
    gla_p = dout("gla_p", [PB, 4, 64, 128])
    gla_s = dout("gla_s", [SB_, 4, 64, 128])
    modrows = dout("modrows", [18, 6 * D])

    with ExitStack() as G:
        P = Prog(nc, G)

        def mk(stack, suf=""):
            def sb(name, shape, dt=F32):
                return P.buf(name + suf, stack.enter_context(nc.sbuf_tensor(name + suf, list(shape), dt)))
            return sb

        gsb = mk(G)

        def ld(q, dst, dst_ap, src_ap, reads=()):
            P.dma(q, lambda e: e.dma_start(out=dst_ap, in_=src_ap), reads=reads, writes=[dst], sembuf=dst)

        pb = [P.buf("pb%d" % i, G.enter_context(nc.psum_tensor("pb%d" % i, [128, 512], F32))) for i in range(7)]
        pT = P.buf("pT", G.enter_context(nc.psum_tensor("pT", [128, 1024], BF16)))
        dy = P.buf("y_all", y_all); dkvr = P.buf("kvr", kvr); dwinp = P.buf("win_p", win_p)
        dwins = P.buf("win_s", win_s); dglap = P.buf("gla_p", gla_p); dglas = P.buf("gla_s", gla_s)
        dmod = P.buf("modrows", modrows)

        identf = gsb("identf", [128, 128]); identb = gsb("identb", [128, 128], BF16)
        scT = gsb("scT", [128, 8, 18], BF16)
        epsb = gsb("epsb", [128, 1])
        ones1 = gsb("ones1", [1, 128])
        ld("sp", identf, identf[:], ident_f[:, :])
        P.dve(lambda e: e.memset(ones1[:], 1.0), writes=[ones1])
        P.dve(lambda e: e.memset(epsb[:], EPS), writes=[epsb])
        P.dve(lambda e: e.tensor_copy(out=identb[:], in_=identf[:]), reads=[identf], writes=[identb])
        with ExitStack() as S0:
            sb0 = mk(S0)
            cTt = sb0("cTt", [128, 8, 18]); sgm = sb0("sgm", [128, 8, 18])
            ld("sp", cTt, cTt[:], cT[:, :, :])
            P.act(lambda e: e.activation(out=sgm[:], in_=cTt[:], func=AF.Sigmoid), reads=[cTt], writes=[sgm])
            P.dve(lambda e: e.tensor_tensor(out=scT[:], in0=sgm[:], in1=cTt[:], op=ALU.mult),
                  reads=[sgm, cTt], writes=[scT])
            P.flush()

        def ada_rows(sb_, col0, gidx_a, gidx_g):
            adar = sb_("adar%d" % col0, [18, 3 * D])
            wst = sb_("wada_st%d" % col0, [128, 8, 256]); wbf = sb_("wada_bf%d" % col0, [128, 8, 256], BF16)
            bch = sb_("bch%d" % col0, [18, 256]); grow = sb_("grow%d" % col0, [18, D])
            for ncn in range(12):
                c0 = col0 + ncn * 256
                ld("sp", wst, wst[:], w_ada[:, c0:c0 + 256].rearrange("(k p) n -> p k n", p=128))
                ld("pool", bch, bch[:], b_ada[0:1, c0:c0 + 256].partition_broadcast(18))
                P.dve(lambda e: e.tensor_copy(out=wbf[:], in_=wst[:]), reads=[wst], writes=[wbf])
                pbk = pb[ncn % 2]
                for kc in range(8):
                    P.pe(lambda e, kc=kc, pbk=pbk: e.matmul(pbk[0:18, 0:256], lhsT=scT[:, kc, :], rhs=wbf[:, kc, :],
                                                             start=(kc == 0), stop=(kc == 7)),
                         reads=[scT, wbf], writes=[pbk])
                P.dve(lambda e, pbk=pbk, ncn=ncn: e.tensor_tensor(out=adar[:, ncn * 256:(ncn + 1) * 256], in0=pbk[0:18, 0:256],
                                                                  in1=bch[:], op=ALU.add),
                      reads=[pbk, bch], writes=[adar])
            ld("pool", grow, grow[:], gains[gidx_a:gidx_a + 1, :].partition_broadcast(18))
            P.dve(lambda e: e.scalar_tensor_tensor(out=adar[:, D:2 * D], in0=adar[:, D:2 * D], scalar=1.0, in1=grow[:],
                                                   op0=ALU.add, op1=ALU.mult), reads=[adar, grow], writes=[adar])
            ld("pool", grow, grow[:], gains[gidx_g:gidx_g + 1, :].partition_broadcast(18))
            P.dve(lambda e: e.tensor_tensor(out=adar[:, 2 * D:3 * D], in0=adar[:, 2 * D:3 * D], in1=grow[:], op=ALU.mult),
                  reads=[adar, grow], writes=[adar])
            P.dma("sp", lambda e: e.dma_start(out=modrows[:, col0:col0 + 3 * D], in_=adar[:]), reads=[adar], writes=[dmod],
                  sembuf=dmod)

        def load_mod(MODt, col0, kind):
            if kind < 2:
                P.dma("sp", lambda e: e.dma_start(out=MODt[:], in_=modrows[kind:kind + 1, col0:col0 + 3 * D].partition_broadcast(128)),
                      reads=[dmod], writes=[MODt], sembuf=MODt)
            else:
                for s_ in range(SB_):
                    P.dma("sp" if s_ % 2 == 0 else "pool", lambda e, s_=s_: e.dma_start(
                        out=MODt[s_ * TS:(s_ + 1) * TS, :],
                        in_=modrows[2 + s_:3 + s_, col0:col0 + 3 * D].partition_broadcast(TS)),
                        reads=[dmod], writes=[MODt], sembuf=MODt)

        def rms_rstd(src_ap, src_bufs, junk_ap, junk_buf, stb, c0, n):
            P.act(lambda e: e.activation(out=junk_ap, in_=src_ap, func=AF.Square, accum_out=stb[:, c0:c0 + 1]),
                  reads=src_bufs, writes=[stb, junk_buf])
            P.act(lambda e: e.activation(out=stb[:, c0 + 1:c0 + 2], in_=stb[:, c0:c0 + 1], func=AF.Sqrt, scale=1.0 / n, bias=epsb[:]),
                  reads=[stb, epsb], writes=[stb])
            P.dve(lambda e: e.reciprocal(out=stb[:, c0 + 1:c0 + 2], in_=stb[:, c0 + 1:c0 + 2]), reads=[stb], writes=[stb])

        def transpose8(src_bf, dstT, evac="act"):
            for kc in range(8):
                P.pe(lambda e, kc=kc: e.transpose(out=pT[:, kc * 128:(kc + 1) * 128], in_=src_bf[:, kc * 128:(kc + 1) * 128],
                                                  identity=identb[:]), reads=[src_bf, identb], writes=[pT])
            if evac == "act":
                P.act(lambda e: e.activation(out=dstT[:], in_=pT[:], func=AF.Copy), reads=[pT], writes=[dstT])
            else:
                P.dve(lambda e: e.tensor_copy(out=dstT[:], in_=pT[:]), reads=[pT], writes=[dstT])

        with ExitStack() as S1:
            sb1 = mk(S1)
            win_bf = sb1("win_bf", [128, 8, D_IN], BF16)
            wout_bf = sb1("wout_bf", [128, 8, D], BF16)
            w1b = sb1("w1b", [128, 2, 2, 8, 256], BF16)
            w2b = sb1("w2b", [128, 2, 2, 64], BF16)
            w2dup = sb1("w2dup", [128, 2, 128], BF16)
            pbias = sb1("pbias", [128, 2, 256])
            b2kt = sb1("b2kt", [128, 1]); b2vt = sb1("b2vt", [128, 64])
            wgt = sb1("wgt", [16, 256]); bgt = sb1("bgt", [1, 256])
            trit = sb1("trit", [128, 2, 2, 128]); seqi = sb1("seqi", [128, 2, 16]); seqm = sb1("seqm", [128, 16])
            causmt = sb1("causmt", [128, 2, 512], BF16)
            gnb = sb1("gnb", [128, 512])
            ckct = sb1("ckct", [4, 512], BF16)
            selpt = sb1("selpt", [128, 2, 64], BF16)
            ld("sp", ckct, ckct[:], ckc[:, :])
            ld("sp", selpt, selpt[:], selp[:, :, :])
            ld("sp", causmt, causmt[:], causm.rearrange("k p c -> p k c"))
            with ExitStack() as S1a:
                sba = mk(S1a)
                stg = [sba("stg%d" % i, [128, D_IN]) for i in range(2)]
                nst = [0]

                def ldcast(dst, dst_ap, src_ap, shape):
                    s = stg[nst[0] % 2]
                    q = "sp" if nst[0] % 2 == 0 else "pool"
                    np_, nf = shape
                    sap = s[0:np_, 0:nf]
                    ld(q, s, sap, src_ap)
                    if nst[0] % 2 == 0:
                        P.dve(lambda e: e.tensor_copy(out=dst_ap, in_=sap), reads=[s], writes=[dst])
                    else:
                        P.act(lambda e: e.activation(out=dst_ap, in_=sap, func=AF.Copy), reads=[s], writes=[dst])
                    nst[0] += 1

                for kc in range(8):
                    ldcast(win_bf, win_bf[:, kc, :], w_in[kc * 128:(kc + 1) * 128, :], (128, D_IN))
                for kc in range(8):
                    ldcast(wout_bf, wout_bf[:, kc, :], w_out[kc * 128:(kc + 1) * 128, :], (128, D))
                for ty in range(2):
                    for hf in range(2):
                        ldcast(w1b, w1b[:, ty, hf, :, :].rearrange("p a b -> p (a b)"),
                               w1p[:, ty, hf, :, :].rearrange("p a b -> p (a b)"), (128, 2048))
                ldcast(w2b, w2b[:].rearrange("p a b c -> p (a b c)"), w2l.rearrange("p a b c -> p (a b c)"), (128, 256))
                posb = sba("posb", [128, 2, 2, 8, 16], BF16)
                ldcast(posb, posb[:].rearrange("p a b c d -> p (a b c d)"), posrep.rearrange("p a b c d -> p (a b c d)"),
                       (128, 512))
                b1t = sba("b1t", [1, 2, 256]); ones16 = sba("ones16", [1, 16]); pb16 = sba("pb16", [16, 2, 256])
                ld("sp", b1t, b1t[:], b1r.rearrange("(o t) f -> o t f", o=1))
                P.dve(lambda e: e.memset(ones16[:], 1.0), writes=[ones16])
                ld("sp", b2kt, b2kt[:], b2k[:, :])
                ld("sp", b2vt, b2vt[:], b2v[0:1, :].partition_broadcast(128))
                ld("sp", wgt, wgt[:], wg[:, :]); ld("sp", bgt, bgt[:], bg[:, :])
                ld("sp", trit, trit[:], tri.rearrange("k m p t -> p k m t"))
                ld("sp", seqi, seqi[:], seqind.rearrange("k p s -> p k s"))
                ld("sp", seqm, seqm[:], seqmask[:, :])
                ld("sp", gnb, gnb[:], gnorm4[0:1, :].partition_broadcast(128))
                for fc in range(2):
                    for dup in range(2):
                        P.dve(lambda e, fc=fc, dup=dup: e.tensor_copy(out=w2dup[:, fc, dup * 64:(dup + 1) * 64],
                                                                      in_=w2b[:, 0, fc, :]), reads=[w2b], writes=[w2dup])
                for ty in range(2):
                    pbk = pb[2 + ty]
                    n = 0
                    for hf in range(2):
                        for j2 in range(8):
                            P.pe(lambda e, ty=ty, hf=hf, j2=j2, pbk=pbk, n=n: e.matmul(
                                pbk[0:16, 0:256], lhsT=posb[:, ty, hf, j2, :], rhs=w1b[:, ty, hf, j2, :],
                                start=(n == 0), stop=False), reads=[posb, w1b], writes=[pbk])
                            n += 1
                    P.pe(lambda e, ty=ty, pbk=pbk: e.matmul(pbk[0:16, 0:256], lhsT=ones16[:], rhs=b1t[:, ty, :],
                                                            start=False, stop=True), reads=[ones16, b1t], writes=[pbk])
                    P.act(lambda e, ty=ty, pbk=pbk: e.activation(out=pb16[0:16, ty, :], in_=pbk[0:16, 0:256], func=AF.Copy),
                          reads=[pbk], writes=[pb16])
                    P.pe(lambda e, ty=ty, pbk=pbk: e.matmul(pbk[:, 256:512], lhsT=ones1[0:1, :], rhs=pb16[0:1, ty, :],
                                                            start=True, stop=True), reads=[ones1, pb16], writes=[pbk])
                    P.act(lambda e, ty=ty, pbk=pbk: e.activation(out=pbias[:, ty, :], in_=pbk[:, 256:512], func=AF.Copy),
                          reads=[pbk], writes=[pbias])
                ada_rows(sba, 0, 0, 1)
                P.flush()

            MOD = sb1("MOD", [128, 3 * D])
            xt = [sb1("xt%d" % i, [128, D]) for i in range(2)]
            junkb = sb1("junkb", [128, 512], BF16)
            st4 = sb1("st4", [128, 16])
            tmpf = sb1("tmpf", [128, D])
            hbf = sb1("hbf", [128, D], BF16)
            hT = sb1("hT", [128, D], BF16)
            z = sb1("z", [128, D_IN])
            zaT = sb1("zaT", [16, 128])
            Lt = sb1("Lt", [128, 256]); Eout = sb1("Eout", [128, 256])
            EqT = sb1("EqT", [128, 256]); EkT = sb1("EkT", [128, 256])
            qinT = sb1("qinT", [128, 256], BF16)
            qz = [sb1("qz%d" % k, [128, 256], BF16) for k in range(2)]
            kz = [sb1("kz%d" % k, [128, 256], BF16) for k in range(2)]
            ATs = sb1("ATs", [128, 512], BF16)
            kout = sb1("kout", [128, 256], BF16); vbf = sb1("vbf", [128, 512], BF16)
            koutm = sb1("koutm", [128, 256], BF16)
            dec = sb1("dec", [128, 2, 16])
            Sf = sb1("Sf", [128, 2, 128]); Sbf = sb1("Sbf", [128, 2, 128], BF16)
            SX = {}
            szr = sb1("szr", [128, 512])
            obf = sb1("obf", [128, D], BF16)
            qnbf = sb1("qnbf", [128, 512], BF16); kvbf = sb1("kvbf", [128, 768], BF16)
            qTz = [sb1("qTz%d" % k, [128, 512], BF16) for k in range(2)]
            augq = [sb1("augq%d" % k, [96, 512], BF16) for k in range(2)]
            qb0 = sb1("qb0", [1, 2, 512], BF16)
            PTs = [sb1("PT%d" % i, [128, 512], BF16) for i in range(2)]
            oacc = sb1("oacc", [128, 512])
            sg = sb1("sg", [128, 24]); coef = sb1("coef", [128, 8]); rcs = sb1("rcs", [128, 8])
            impn = sb1("impn", [128, 2, 128])
            score = sb1("score", [128, 2, 128]); sc2 = EqT
            m8 = sb1("m8", [128, 16]); selpos = sb1("selpos", [128, 2, 160], BF16)
            vma = sb1("vma", [128, 2, 2, 64])
            hid = Eout; hidb = sb1("hidb", [128, 256], BF16)
            hidT = sb1("hidT", [128, 2, 128], BF16)
            vcn = sb1("vcn", [128, 64], BF16)

            for k_ in range(2):
                P.pool(lambda e, k_=k_: e.memset(qTz[k_][:], 0.0), writes=[qTz[k_]])
                P.pool(lambda e, k_=k_: e.memset(qz[k_][:], 0.0), writes=[qz[k_]])
                P.pool(lambda e, k_=k_: e.memset(kz[k_][:], 0.0), writes=[kz[k_]])
            P.pool(lambda e: e.memset(selpos[:], 0.0), writes=[selpos])
            sbank = [pb[2], pb[3], pb[4]]
            obank = [pb[5], pb[6]]
            ucount = [0]

            pending = [None]

            def unit_flush():
                if pending[0] is not None:
                    f = pending[0]
                    pending[0] = None
                    f()

            def unit(mms, nq, vT_ap, vbufs, ob, ocols, first, last, keep=None):
                sbk = sbank[ucount[0] % 3]
                pt = keep if keep is not None else PTs[ucount[0] % 2]
                ucount[0] += 1
                n = len(mms)
                for mi, (l_ap, r_ap, rd, cols) in enumerate(mms):
                    c0, cn = cols
                    P.pe(lambda e, l_ap=l_ap, r_ap=r_ap, mi=mi, c0=c0, cn=cn: e.matmul(
                        sbk[:, c0:c0 + cn], lhsT=l_ap, rhs=r_ap, start=(mi == 0), stop=(mi == n - 1)),
                        reads=rd, writes=[sbk])
                unit_flush()
                P.act(lambda e: e.activation(out=pt[:, 0:nq], in_=sbk[:, 0:nq], func=AF.Exp), reads=[sbk], writes=[pt])

                def pv():
                    P.pe(lambda e: e.matmul(ob[0:65, ocols[0]:ocols[0] + nq], lhsT=vT_ap, rhs=pt[:, 0:nq], start=first, stop=last),
                         reads=[pt] + vbufs, writes=[ob])
                pending[0] = pv
                return pt

            def branch_epilogue(br, samp, first_branch):
                unit_flush()
                for kvh in range(2):
                    if not samp:
                        P.act(lambda e, kvh=kvh: e.activation(out=tmpf[0:65, kvh * 512:(kvh + 1) * 512], in_=obank[kvh][0:65, :],
                                                              func=AF.Copy), reads=[obank[kvh]], writes=[tmpf])
                for kvh in range(2):
                    ob = obank[kvh]
                    for g in range(4):
                        src = tmpf[0:65, kvh * 512 + g * 128:kvh * 512 + (g + 1) * 128]
                        P.pe(lambda e, src=src, g=g, ob=ob: e.transpose(out=ob[:, g * 65:(g + 1) * 65], in_=src,
                                                                         identity=identf[0:65, 0:65]),
                             reads=[tmpf, identf], writes=[ob])
                    ov = ob[:, 0:260].rearrange("p (g c) -> p g c", g=4)
                    P.dve(lambda e, ov=ov, kvh=kvh: e.tensor_scalar(out=rcs[:, kvh * 4:(kvh + 1) * 4], in0=ov[:, :, 64],
                                                                    scalar1=1e-30, scalar2=None, op0=ALU.max),
                          reads=[ob], writes=[rcs])
                P.dve(lambda e: e.reciprocal(out=rcs[:], in_=rcs[:]), reads=[rcs], writes=[rcs])
                sgv = sg[:].rearrange("p (h b) -> p h b", b=3)
                P.dve(lambda e: e.tensor_tensor(out=coef[:], in0=rcs[:], in1=sgv[:, :, br], op=ALU.mult),
                      reads=[rcs, sg], writes=[coef])
                for kvh in range(2):
                    ob = obank[kvh]
                    for g in range(4):
                        h = kvh * 4 + g
                        if first_branch:
                            P.dve(lambda e, h=h, g=g, ob=ob: e.tensor_scalar(out=oacc[:, h * 64:(h + 1) * 64],
                                                                             in0=ob[:, g * 65:g * 65 + 64],
                                                                             scalar1=coef[:, h:h + 1], scalar2=None, op0=ALU.mult),
                                  reads=[ob, coef], writes=[oacc])
                        else:
                            P.dve(lambda e, h=h, g=g, ob=ob: e.scalar_tensor_tensor(
                                out=oacc[:, h * 64:(h + 1) * 64], in0=ob[:, g * 65:g * 65 + 64], scalar=coef[:, h:h + 1],
                                in1=oacc[:, h * 64:(h + 1) * 64], op0=ALU.mult, op1=ALU.add),
                                reads=[ob, coef, oacc], writes=[oacc])

            def select_blocks(ns, vm_ap, am_ap, vmbufs, kth, samp):
                for kvh in range(2):
                    ob = obank[kvh]
                    for g in range(4):
                        src = z[0:ns, kvh * 512 + g * 128:kvh * 512 + (g + 1) * 128]
                        P.pe(lambda e, src=src, g=g, ob=ob: e.transpose(out=ob[:, g * 128:g * 128 + ns], in_=src,
                                                                         identity=identf[0:ns, 0:ns]),
                             reads=[z, identf], writes=[ob])
                    for g in range(4):
                        h = kvh * 4 + g
                        if g == 0:
                            P.dve(lambda e, h=h, g=g, ob=ob, kvh=kvh: e.tensor_scalar(
                                out=impn[:, kvh, 0:ns], in0=ob[:, g * 128:g * 128 + ns], scalar1=rcs[:, h:h + 1], scalar2=None,
                                op0=ALU.mult), reads=[ob, rcs], writes=[impn])
                        else:
                            P.dve(lambda e, h=h, g=g, ob=ob, kvh=kvh: e.scalar_tensor_tensor(
                                out=impn[:, kvh, 0:ns], in0=ob[:, g * 128:g * 128 + ns], scalar=rcs[:, h:h + 1],
                                in1=impn[:, kvh, 0:ns], op0=ALU.mult, op1=ALU.add), reads=[ob, rcs, impn], writes=[impn])
                for kvh in range(2):
                    P.dve(lambda e, kvh=kvh: e.tensor_tensor(out=score[:, kvh, 0:ns], in0=impn[:, kvh, 0:ns], in1=vm_ap(kvh),
                                                             op=ALU.mult), reads=[impn] + vmbufs, writes=[score])
                    P.dve(lambda e, kvh=kvh: e.tensor_tensor(out=score[:, kvh, 0:ns], in0=score[:, kvh, 0:ns], in1=am_ap(kvh),
                                                             op=ALU.add), reads=[score] + vmbufs, writes=[score])
                    P.dve(lambda e, kvh=kvh: e.max(out=m8[:, 0:8], in_=score[:, kvh, 0:ns]), reads=[score], writes=[m8])
                    P.dve(lambda e, kvh=kvh: e.match_replace(out=sc2[:, 0:ns], in_to_replace=m8[:, 0:8],
                                                             in_values=score[:, kvh, 0:ns], imm_value=-3.0e4),
                          reads=[score, m8], writes=[sc2])
                    P.dve(lambda e: e.max(out=m8[:, 8:16], in_=sc2[:, 0:ns]), reads=[sc2], writes=[m8])
                    P.dve(lambda e, kvh=kvh: e.tensor_scalar(out=selpos[:, kvh, 32:32 + ns], in0=score[:, kvh, 0:ns],
                                                             scalar1=m8[:, 8 + kth:9 + kth], scalar2=NEG, op0=ALU.is_ge,
                                                             op1=ALU.mult), reads=[score, m8], writes=[selpos])

            def build_augq(aq, kvh, blk0, samp):
                P.pe(lambda e: e.transpose(out=pT[0:96, 0:128], in_=selpos[:, kvh, blk0:blk0 + 96], identity=identb[:]),
                     reads=[selpos, identb], writes=[pT])
                for (p0, p1) in ((32, 64), (64, 96)):
                    if not samp:
                        for g in range(4):
                            P.dve(lambda e, g=g, p0=p0, p1=p1: e.tensor_copy(out=aq[p0:p1, g * 128:(g + 1) * 128], in_=pT[p0:p1, 0:128]),
                                  reads=[pT], writes=[aq])
                    else:
                        av = aq[p0:p1, :].rearrange("p (s g q) -> p s g q", s=16, g=4)
                        pv = pT[p0:p1, 0:128].rearrange("p (s q) -> p s q", s=16)
                        for g in range(4):
                            P.dve(lambda e, g=g, av=av, pv=pv: e.tensor_copy(out=av[:, :, g, :], in_=pv), reads=[pT], writes=[aq])

            def compress(Y, nb, kcT, kc_col0, vcs, c0, skip_first):
                R1 = 32 if nb <= 32 else 64
                NR = R1 + nb
                for ty in range(2):
                    pbk = pb[ty]
                    for kvh in range(2):
                        n = 0
                        for hf in range(2):
                            for j2 in range(8):
                                base = 8 * hf + j2
                                lap = Y[:, ty, kvh, base:base + 8 * (nb - 1) + 1:8]
                                P.pe(lambda e, lap=lap, ty=ty, hf=hf, j2=j2, pbk=pbk, n=n, kvh=kvh: e.matmul(
                                    pbk[kvh * R1:kvh * R1 + nb, 0:256], lhsT=lap, rhs=w1b[:, ty, hf, j2, :],
                                    start=(n == 0), stop=(n == 15)), reads=[Y, w1b], writes=[pbk])
                                n += 1
                    P.dve(lambda e, ty=ty, pbk=pbk: e.tensor_tensor(out=hid[0:NR, :], in0=pbk[0:NR, 0:256],
                                                                    in1=pbias[0:NR, ty, :], op=ALU.add),
                          reads=[pbk, pbias], writes=[hid])
                    P.act(lambda e: e.activation(out=hidb[0:NR, :], in_=hid[0:NR, :], func=AF.Silu),
                          reads=[hid], writes=[hidb])
                    for fc in range(2):
                        P.pe(lambda e, fc=fc: e.transpose(out=pT[:, fc * 128:fc * 128 + NR],
                                                          in_=hidb[0:NR, fc * 128:(fc + 1) * 128],
                                                          identity=identb[0:NR, 0:NR]),
                             reads=[hidb, identb], writes=[pT])
                    P.act(lambda e: e.activation(out=hidT[:, :, 0:NR],
                                                 in_=pT[:, 0:256].rearrange("p (a b) -> p a b", a=2)[:, :, 0:NR],
                                                 func=AF.Copy), reads=[pT], writes=[hidT])
                    if ty == 0:
                        pk = pb[1]
                        for fc in range(2):
                            P.pe(lambda e, fc=fc, pk=pk: e.matmul(pk[:, 256:256 + NR], lhsT=w2dup[:, fc, :],
                                                                   rhs=hidT[:, fc, 0:NR], start=(fc == 0), stop=(fc == 1)),
                                 reads=[w2dup, hidT], writes=[pk])
                        for kvh in range(2):
                            r = slice(kvh * 64, kvh * 64 + 64)
                            P.act(lambda e, kvh=kvh, r=r, pk=pk: e.activation(
                                out=kcT[r, kc_col0:kc_col0 + nb], in_=pk[r, 256 + kvh * R1:256 + kvh * R1 + nb],
                                func=AF.Identity, bias=b2kt[r, 0:1]), reads=[pk, b2kt], writes=[kcT])
                    else:
                        pk = pb[0]
                        for fc in range(2):
                            P.pe(lambda e, fc=fc, pk=pk: e.matmul(pk[0:NR, 256:320], lhsT=hidT[:, fc, 0:NR],
                                                                   rhs=w2b[:, 1, fc, :], start=(fc == 0), stop=(fc == 1)),
                                 reads=[w2b, hidT], writes=[pk])
                        P.dve(lambda e, pk=pk: e.tensor_tensor(out=vcn[0:NR, :], in0=pk[0:NR, 256:320],
                                                               in1=b2vt[0:NR, :], op=ALU.add), reads=[pk, b2vt], writes=[vcn])
                        for kvh in range(2):
                            r0 = 1 if skip_first else 0
                            c = c0 + r0
                            while c < c0 + nb:
                                ce = min(c0 + nb, (c // 128 + 1) * 128)
                                P.dma("sp", lambda e, kvh=kvh, c=c, ce=ce: e.dma_start(
                                    out=vcs[c % 128:c % 128 + (ce - c), c // 128, kvh, 0:64],
                                    in_=vcn[kvh * R1 + (c - c0):kvh * R1 + (ce - c0), :]), reads=[vcn], writes=[vcs], sembuf=vcs)
                                c = ce

            def front(ti, x, kind):
                ld("sp", x, x[:], xs[ti * 128:(ti + 1) * 128, :])
                rms_rstd(x[:], [x], hbf[:], hbf, st4, 0, D)
                P.dve(lambda e: e.scalar_tensor_tensor(out=tmpf[:], in0=x[:], scalar=st4[:, 1:2], in1=MOD[:, D:2 * D],
                                                       op0=ALU.mult, op1=ALU.mult), reads=[x, st4, MOD], writes=[tmpf])
                P.pool(lambda e: e.tensor_tensor(out=hbf[:], in0=tmpf[:], in1=MOD[:, 0:D], op=ALU.add),
                       reads=[tmpf, MOD], writes=[hbf])
                transpose8(hbf, hT)
                zch = [(0, 512), (512, 512), (1024, 512), (1536, 512), (2048, 512), (2560, 296)]
                for ci, (c0, cn) in enumerate(zch):
                    pbk = pb[ci % 2]
                    for kc in range(8):
                        P.pe(lambda e, kc=kc, pbk=pbk, c0=c0, cn=cn: e.matmul(
                            pbk[:, 0:cn], lhsT=hT[:, kc * 128:(kc + 1) * 128], rhs=win_bf[:, kc, c0:c0 + cn],
                            start=(kc == 0), stop=(kc == 7)), reads=[hT, win_bf], writes=[pbk])
                    if ci % 2 == 0:
                        P.dve(lambda e, pbk=pbk, c0=c0, cn=cn: e.tensor_copy(out=z[:, c0:c0 + cn], in_=pbk[:, 0:cn]),
                              reads=[pbk], writes=[z])
                    else:
                        P.act(lambda e, pbk=pbk, c0=c0, cn=cn: e.activation(out=z[:, c0:c0 + cn], in_=pbk[:, 0:cn], func=AF.Copy),
                              reads=[pbk], writes=[z])
                P.dma("sp", lambda e: e.dma_start(out=kvr[ti * 128:(ti + 1) * 128, :], in_=z[:, C_KV:C_KV + 512]),
                      reads=[z], writes=[dkvr], sembuf=dkvr)
                P.act(lambda e: e.activation(out=qnbf[:].rearrange("p (g k d) -> p k g d", g=4, k=2),
                                             in_=z[:, C_QN:C_QN + 512].rearrange("p (k g d) -> p k g d", k=2, g=4),
                                             func=AF.Copy, scale=0.125), reads=[z], writes=[qnbf])
                P.pool(lambda e: e.tensor_copy(out=kvbf[:], in_=z[:, C_KV:C_KV + 768]), reads=[z], writes=[kvbf])
                P.act(lambda e: e.activation(out=sg[:], in_=z[:, C_ZG:C_ZG + 24], func=AF.Sigmoid), reads=[z], writes=[sg])
                for g in range(4):
                    P.pe(lambda e, g=g: e.transpose(out=pT[:, g * 128:(g + 1) * 128], in_=qnbf[:, g * 128:(g + 1) * 128], identity=identb[:]),
                         reads=[qnbf, identb], writes=[pT])
                for kvh in range(2):
                    r = slice(kvh * 64, kvh * 64 + 64)
                    P.dve(lambda e, kvh=kvh, r=r: e.tensor_copy(out=qTz[kvh][r, :], in_=pT[r, 0:512]), reads=[pT], writes=[qTz[kvh]])

            def gla(sample, it, kind):
                tk = 1 if sample else 0
                pa, pb_, pc, pd = pb[2], pb[3], pb[4], pb[5]
                P.pe(lambda e: e.transpose(out=pa[0:16, 0:128], in_=z[:, C_ZA:C_ZA + 16], identity=identf[:]),
                     reads=[z, identf], writes=[pa])
                P.act(lambda e: e.activation(out=zaT[:], in_=pa[0:16, 0:128], func=AF.Copy), reads=[pa], writes=[zaT])
                P.pe(lambda e: e.matmul(pb_[:, 0:256], lhsT=zaT[:], rhs=wgt[:], start=True, stop=False),
                     reads=[zaT, wgt], writes=[pb_])
                P.pe(lambda e: e.matmul(pb_[:, 0:256], lhsT=ones1[:], rhs=bgt[:], start=False, stop=True),
                     reads=[ones1, bgt], writes=[pb_])
                P.act(lambda e: e.activation(out=Lt[:], in_=pb_[:, 0:256], func=AF.Exp, scale=-1.0), reads=[pb_], writes=[Lt])
                P.act(lambda e: e.activation(out=Lt[:], in_=Lt[:], func=AF.Ln, bias=1.0), reads=[Lt], writes=[Lt])
                P.pe(lambda e: e.matmul(pc[:, 0:256], lhsT=trit[:, tk, 0, :], rhs=Lt[:], start=True, stop=True),
                     reads=[trit, Lt], writes=[pc])
                for fc in range(2):
                    P.pe(lambda e, fc=fc: e.matmul(pd[:, fc * 16:fc * 16 + 16], lhsT=Lt[:, fc * 128:(fc + 1) * 128],
                                                   rhs=seqi[:, tk, :], start=True, stop=True), reads=[Lt, seqi], writes=[pd])
                    P.pe(lambda e, fc=fc: e.matmul(pd[:, 128 + fc * 128:256 + fc * 128], lhsT=Lt[:, fc * 128:(fc + 1) * 128],
                                                   rhs=trit[:, tk, 1, :], start=True, stop=True), reads=[Lt, trit], writes=[pd])
                P.act(lambda e: e.activation(out=Eout[:], in_=pc[:, 0:256], func=AF.Exp), reads=[pc], writes=[Eout])
                P.act(lambda e: e.activation(out=dec[:].rearrange("p a b -> p (a b)"), in_=pd[:, 0:32], func=AF.Exp),
                      reads=[pd], writes=[dec])
                P.act(lambda e: e.activation(out=EqT[:], in_=pd[:, 128:384], func=AF.Exp), reads=[pd], writes=[EqT])
                P.act(lambda e: e.activation(out=EkT[:], in_=pd[:, 128:384], func=AF.Exp, scale=-1.0), reads=[pd], writes=[EkT])
                P.dve(lambda e: e.tensor_tensor(out=kout[:], in0=z[:, C_ZK:C_ZK + 256], in1=Eout[:], op=ALU.mult),
                      reads=[z, Eout], writes=[kout])
                P.pool(lambda e: e.tensor_copy(out=vbf[:], in_=z[:, C_ZV:C_ZV + 512]), reads=[z], writes=[vbf])
                for a in range(4):
                    c0 = (C_ZQ if a < 2 else C_ZK) + (a % 2) * 128
                    P.pe(lambda e, a=a, c0=c0: e.transpose(out=pa[:, a * 128:(a + 1) * 128], in_=z[:, c0:c0 + 128],
                                                           identity=identf[:]), reads=[z, identf], writes=[pa])
                P.dve(lambda e: e.scalar_tensor_tensor(out=qinT[:], in0=pa[:, 0:256], scalar=0.125, in1=EqT[:],
                                                       op0=ALU.mult, op1=ALU.mult), reads=[pa, EqT], writes=[qinT])
                for hh in range(2):
                    r = slice(hh * 64, hh * 64 + 64)
                    P.dve(lambda e, hh=hh, r=r: e.tensor_tensor(out=kz[hh][r, :], in0=pa[r, 256:512], in1=EkT[r, :], op=ALU.mult),
                          reads=[pa, EkT], writes=[kz[hh]])
                    P.pool(lambda e, hh=hh, r=r: e.tensor_copy(out=qz[hh][r, :], in_=qinT[r, :]), reads=[qinT], writes=[qz[hh]])
                for h in range(4):
                    r = slice((h % 2) * 64, (h % 2) * 64 + 64)
                    fc = h // 2
                    P.pe(lambda e, h=h, r=r, fc=fc: e.matmul(pb_[:, h * 128:(h + 1) * 128], lhsT=kz[h % 2][:, fc * 128:(fc + 1) * 128],
                                                             rhs=qinT[:, fc * 128:(fc + 1) * 128], start=True, stop=True),
                         reads=[kz[h % 2], qinT], writes=[pb_])
                P.dve(lambda e: e.tensor_tensor(out=ATs[:], in0=pb_[:, :], in1=causmt[:, tk, :], op=ALU.mult),
                      reads=[pb_, causmt], writes=[ATs])
                po = pc
                if not sample:
                    if it == 0:
                        P.dve(lambda e: e.memset(Sf[:], 0.0), writes=[Sf])
                        P.dve(lambda e: e.memset(Sbf[:], 0.0), writes=[Sbf])
                    for h in range(4):
                        r = slice((h % 2) * 64, (h % 2) * 64 + 64)
                        fc = h // 2
                        P.pe(lambda e, h=h, r=r, fc=fc: e.matmul(po[:, h * 128:(h + 1) * 128], lhsT=qz[h % 2][:, fc * 128:(fc + 1) * 128],
                                                                 rhs=Sbf[:, fc, :], start=True, stop=False),
                             reads=[qz[h % 2], Sbf], writes=[po])
                        P.pe(lambda e, h=h: e.matmul(po[:, h * 128:(h + 1) * 128], lhsT=ATs[:, h * 128:(h + 1) * 128],
                                                     rhs=vbf[:, h * 128:(h + 1) * 128], start=False, stop=True),
                             reads=[ATs, vbf], writes=[po])
                else:
                    for s in range(SB_):
                        S = SX["Ssm"][s % 2]
                        ld("sp", S, S[:], sgla[s].rearrange("(fc hh) k v -> (hh k) fc v", hh=2))
                        P.dve(lambda e, s=s, S=S: e.tensor_copy(out=SX["S0bf"][:, s, :, :], in_=S[:]), reads=[S], writes=[SX["S0bf"]])
                    for h in range(4):
                        r = slice((h % 2) * 64, (h % 2) * 64 + 64)
                        fc = h // 2
                        P.pe(lambda e, h=h: e.matmul(po[:, h * 128:(h + 1) * 128], lhsT=vbf[:, h * 128:(h + 1) * 128],
                                                     rhs=ATs[:, h * 128:(h + 1) * 128], start=True, stop=False),
                             reads=[ATs, vbf], writes=[po])
                        for s in range(SB_):
                            P.pe(lambda e, h=h, r=r, fc=fc, s=s: e.matmul(
                                po[:, h * 128 + s * TS:h * 128 + (s + 1) * TS], lhsT=SX["S0bf"][:, s, fc, :],
                                rhs=qz[h % 2][:, fc * 128 + s * TS:fc * 128 + (s + 1) * TS], start=False, stop=(s == SB_ - 1)),
                                reads=[SX["S0bf"], qz[h % 2]], writes=[po])
                    P.act(lambda e: e.activation(out=SX["ogT"][:, 0:512], in_=po[:, :], func=AF.Copy), reads=[po], writes=[SX["ogT"]])
                    for h in range(4):
                        P.pe(lambda e, h=h: e.transpose(out=po[:, h * 128:(h + 1) * 128], in_=SX["ogT"][:, h * 128:(h + 1) * 128],
                                                        identity=identf[:]), reads=[SX["ogT"], identf], writes=[po])
                for h in range(4):
                    P.act(lambda e, h=h: e.activation(out=junkb[:, 0:128], in_=po[:, h * 128:(h + 1) * 128], func=AF.Square,
                                                      accum_out=st4[:, 4 + h:5 + h]), reads=[po], writes=[st4, junkb])
                P.act(lambda e: e.activation(out=st4[:, 8:12], in_=st4[:, 4:8], func=AF.Sqrt, scale=1.0 / 128, bias=epsb[:]),
                      reads=[st4, epsb], writes=[st4])
                P.dve(lambda e: e.reciprocal(out=st4[:, 8:12], in_=st4[:, 8:12]), reads=[st4], writes=[st4])
                P.act(lambda e: e.activation(out=szr[:], in_=z[:, C_ZR:C_ZR + 512], func=AF.Sigmoid), reads=[z], writes=[szr])
                P.dve(lambda e: e.tensor_tensor(out=szr[:], in0=szr[:], in1=z[:, C_ZR:C_ZR + 512], op=ALU.mult),
                      reads=[szr, z], writes=[szr])
                P.pool(lambda e: e.tensor_tensor(out=szr[:], in0=szr[:], in1=gnb[:], op=ALU.mult), reads=[szr, gnb], writes=[szr])
                for h in range(4):
                    P.dve(lambda e, h=h: e.scalar_tensor_tensor(out=obf[:, h * 128:(h + 1) * 128], in0=po[:, h * 128:(h + 1) * 128],
                                                                scalar=st4[:, 8 + h:9 + h], in1=szr[:, h * 128:(h + 1) * 128],
                                                                op0=ALU.mult, op1=ALU.mult), reads=[po, st4, szr], writes=[obf])

                def state_update(S, kl, si, pu):
                    for fc in range(2):
                        P.pe(lambda e, fc=fc: e.matmul(pu[:, fc * 256:(fc + 1) * 256], lhsT=kl[:, fc * 128:(fc + 1) * 128],
                                                       rhs=vbf[:, fc * 256:(fc + 1) * 256], start=True, stop=True),
                             reads=[kl, vbf], writes=[pu])
                    for fc in range(2):
                        for hh in range(2):
                            r = slice(hh * 64, hh * 64 + 64)
                            P.dve(lambda e, fc=fc, hh=hh, r=r: e.scalar_tensor_tensor(
                                out=S[r, fc, :], in0=S[r, fc, :], scalar=dec[r, fc, si:si + 1],
                                in1=pu[r, fc * 256 + hh * 128:fc * 256 + hh * 128 + 128], op0=ALU.mult, op1=ALU.add),
                                reads=[S, dec, pu], writes=[S])

                if not sample:
                    state_update(Sf, kout, 0, pb[6])
                    P.pool(lambda e: e.tensor_copy(out=Sbf[:], in_=Sf[:]), reads=[Sf], writes=[Sbf])
                    if it == NT_P - 1:
                        P.dma("sp", lambda e: e.dma_start(out=gla_p[kind].rearrange("(fc hh) k v -> (hh k) fc v", hh=2),
                                                          in_=Sf[:]), reads=[Sf], writes=[dglap], sembuf=dglap)
                else:
                    for s in range(SB_):
                        S = SX["Ssm"][s % 2]
                        ld("sp", S, S[:], sgla[s].rearrange("(fc hh) k v -> (hh k) fc v", hh=2))
                        P.dve(lambda e, s=s: e.tensor_scalar(out=koutm[:], in0=kout[:], scalar1=seqm[:, s:s + 1], scalar2=None,
                                                             op0=ALU.mult), reads=[kout, seqm], writes=[koutm])
                        state_update(S, koutm, s, pb[5 + (s % 2)])
                        P.dma("sp", lambda e, s=s, S=S: e.dma_start(
                            out=gla_s[s].rearrange("(fc hh) k v -> (hh k) fc v", hh=2), in_=S[:]),
                            reads=[S], writes=[dglas], sembuf=dglas)

            def back(ti, x):
                P.dve(lambda e: e.tensor_copy(out=obf[:, 512:1024], in_=oacc[:]), reads=[oacc], writes=[obf])
                transpose8(obf, hT, evac="dve")
                for nh in range(2):
                    pbk = pb[nh]
                    for kc in range(8):
                        P.pe(lambda e, kc=kc, pbk=pbk, nh=nh: e.matmul(pbk[:, :], lhsT=hT[:, kc * 128:(kc + 1) * 128],
                                                                       rhs=wout_bf[:, kc, nh * 512:(nh + 1) * 512],
                                                                       start=(kc == 0), stop=(kc == 7)),
                             reads=[hT, wout_bf], writes=[pbk])
                    P.act(lambda e, pbk=pbk, nh=nh: e.activation(out=junkb[:, 0:512], in_=pbk[:, :], func=AF.Square,
                                                                 accum_out=st4[:, 12 + nh:13 + nh]), reads=[pbk], writes=[st4, junkb])
                P.dve(lambda e: e.tensor_tensor(out=st4[:, 14:15], in0=st4[:, 12:13], in1=st4[:, 13:14], op=ALU.add),
                      reads=[st4], writes=[st4])
                P.act(lambda e: e.activation(out=st4[:, 15:16], in_=st4[:, 14:15], func=AF.Sqrt, scale=1.0 / D, bias=epsb[:]),
                      reads=[st4, epsb], writes=[st4])
                P.dve(lambda e: e.reciprocal(out=st4[:, 15:16], in_=st4[:, 15:16]), reads=[st4], writes=[st4])
                for nh in range(2):
                    pbk = pb[nh]
                    P.dve(lambda e, pbk=pbk, nh=nh: e.scalar_tensor_tensor(
                        out=tmpf[:, nh * 512:(nh + 1) * 512], in0=pbk[:, :], scalar=st4[:, 15:16],
                        in1=MOD[:, 2 * D + nh * 512:2 * D + (nh + 1) * 512], op0=ALU.mult, op1=ALU.mult),
                        reads=[pbk, st4, MOD], writes=[tmpf])
                P.pool(lambda e: e.tensor_tensor(out=tmpf[:], in0=tmpf[:], in1=x[:], op=ALU.add), reads=[tmpf, x], writes=[tmpf])
                P.dma("sp", lambda e: e.dma_start(out=y_all[ti * 128:(ti + 1) * 128, :], in_=tmpf[:]),
                      reads=[tmpf], writes=[dy], sembuf=dy)

            for kind in range(CFG["kinds"]):
                with ExitStack() as SP:
                    sbp = mk(SP, "_p%d" % kind)
                    kslcT = sbp("kslcT", [128, SEQ], BF16); kwinT = sbp("kwinT", [128, 8 * 128], BF16)
                    Yp = sbp("Yp", [128, 2, 2, 8 + 64], BF16)
                    vslc = sbp("vslc", [128, NT_P, 2, 65], BF16); vwin = sbp("vwin", [128, 8, 2, 65], BF16)
                    ckt = sbp("ckt", [96, SEQ], BF16)
                    maskct = sbp("maskct", [128, 17, 128], BF16); mmpt = sbp("mmpt", [128, 2, 64], BF16)
                    caus4t = sbp("caus4t", [128, 2, 512], BF16)
                    ld("sp", ckt, ckt[:], ck[:, 0:SEQ])
                    ld("sp", maskct, maskct[:], maskc.rearrange("r p c -> p r c"))
                    ld("sp", mmpt, mmpt[:], mmap_p[:, :, :])
                    ld("sp", caus4t, caus4t[:], caus4.rearrange("k p c -> p k c"))
                    kcT = sbp("kcT", [128, 264], BF16); vcs = sbp("vcs", [128, 2, 2, 65], BF16)
                    PTc = [sbp("PTc%d" % i, [128, 512], BF16) for i in range(4)]
                    load_mod(MOD, 0, kind)
                    for kvh in range(2):
                        P.pool(lambda e, kvh=kvh: e.memset(augq[kvh][:], 0.0), writes=[augq[kvh]])
                        ld("sp", augq[kvh], augq[kvh][0:5, :], qab[0, kvh])
                        ld("sp", qb0, qb0[0:1, kvh, :], qab[0, kvh, 0:1, :])
                    P.pool(lambda e: e.memset(Yp[:], 0.0), writes=[Yp])
                    P.pool(lambda e: e.memset(kcT[:], 0.0), writes=[kcT])
                    P.pool(lambda e: e.memset(vcs[:], 0.0), writes=[vcs])
                    P.pool(lambda e: e.memset(vcs[:, :, :, 64:65], 1.0), writes=[vcs])
                    P.pool(lambda e: e.memset(vslc[:, :, :, 64:65], 1.0), writes=[vslc])
                    P.pool(lambda e: e.memset(vwin[:, :, :, 64:65], 1.0), writes=[vwin])
                    for it in range(CFG["n_ptiles"]):
                      try:
                          ti = kind * NT_P + it
                          x = xt[ti % 2]
                          front(ti, x, kind)
                          if it >= NT_P - 4:
                              r0 = (it - (NT_P - 4)) * 128
                              P.dma("sp", lambda e, r0=r0: e.dma_start(out=win_p[kind, r0:r0 + 128, :],
                                                                       in_=z[:, C_KV + 512:C_KV + 768]),
                                    reads=[z], writes=[dwinp], sembuf=dwinp)
                          chk(1)
                          gla(False, it, kind)
                          chk(2)
                          t0 = it * 128
                          P.pe(lambda e: e.transpose(out=pT[:, 512:640], in_=kvbf[:, 256:384], identity=identb[:]),
                               reads=[kvbf, identb], writes=[pT])
                          P.pe(lambda e: e.transpose(out=pT[:, 640:768], in_=kvbf[:, 512:640], identity=identb[:]),
                               reads=[kvbf, identb], writes=[pT])
                          P.dve(lambda e, t0=t0: e.tensor_copy(out=kslcT[:, t0:t0 + 128], in_=pT[:, 512:640]), reads=[pT], writes=[kslcT])
                          P.dve(lambda e, it=it: e.tensor_copy(out=kwinT[:, (it % 8) * 128:(it % 8 + 1) * 128], in_=pT[:, 640:768]), reads=[pT], writes=[kwinT])
                          P.pool(lambda e, it=it: e.tensor_copy(out=vslc[:, it, :, 0:64],
                                                                in_=kvbf[:, 384:512].rearrange("p (k d) -> p k d", k=2)),
                                 reads=[kvbf], writes=[vslc])
                          P.pool(lambda e, it=it: e.tensor_copy(out=vwin[:, it % 8, :, 0:64],
                                                                in_=kvbf[:, 640:768].rearrange("p (k d) -> p k d", k=2)),
                                 reads=[kvbf], writes=[vwin])
                          pY = pb[0]
                          for ty in range(2):
                              for kvh in range(2):
                                  for r in range(2):
                                      c0 = ty * 128 + kvh * 64
                                      P.pe(lambda e, ty=ty, kvh=kvh, r=r, c0=c0: e.matmul(
                                          pY[r * 64:(r + 1) * 64, (ty * 2 + kvh) * 64:(ty * 2 + kvh + 1) * 64],
                                          lhsT=kvbf[:, c0:c0 + 64], rhs=selpt[:, r, :], start=True, stop=True),
                                          reads=[kvbf, selpt], writes=[pY])
                          if it > 0:
                              P.dve(lambda e: e.tensor_copy(out=Yp[:, :, :, 0:8], in_=Yp[:, :, :, 64:72]), reads=[Yp], writes=[Yp])
                          P.act(lambda e, it=it: e.activation(out=Yp[:, :, :, 8:72],
                                                              in_=pY[:, 0:256].rearrange("p (a b c) -> p a b c", a=2, b=2),
                                                              func=AF.Copy), reads=[pY], writes=[Yp])
                          chk(3)
                          cb0 = 8 * it - 1
                          compress(Yp, 8, kcT, 1 + cb0, vcs, cb0, it == 0)
                          for kvh in range(2):
                              P.dve(lambda e, kvh=kvh, it=it: e.tensor_scalar(out=augq[kvh][0:1, :], in0=qb0[0:1, kvh, :],
                                                                              scalar1=float(it), scalar2=None, op0=ALU.mult),
                                    reads=[qb0], writes=[augq[kvh]])
                          ld("pool", vma, vma[:], vmam_p[it])
                          chk(4)
                          jl = (8 * it + 6) // 128
                          for kvh in range(2):
                              r = slice(kvh * 64, kvh * 64 + 64)
                              for jc in range(jl + 1):
                                  mms = [(kcT[:, 1 + jc * 128:1 + (jc + 1) * 128], qTz[kvh][:, :], [kcT, qTz[kvh]], (0, 512)),
                                         (ckct[0:4, jc * 128:(jc + 1) * 128], augq[kvh][0:4, :], [ckct, augq[kvh]], (0, 512))]
                                  rr = it - 16 * jc
                                  if rr <= 16:
                                      for g in range(4):
                                          mms.append((identb[:], maskct[:, rr, :], [identb, maskct], (g * 128, 128)))
                                  unit(mms, 512, vcs[:, jc, kvh, :], [vcs], obank[kvh], (0, 512), jc == 0, jc == jl,
                                       keep=PTc[kvh * 2 + jc])
                          for kvh in range(2):
                              pim = pb[kvh]
                              for jc in range(jl + 1):
                                  P.pe(lambda e, kvh=kvh, jc=jc, pim=pim: e.matmul(pim[0:64, :], lhsT=mmpt[:, jc, :],
                                                                                   rhs=PTc[kvh * 2 + jc][:, :], start=(jc == 0),
                                                                                   stop=(jc == jl)),
                                       reads=[mmpt, PTc[kvh * 2 + jc]], writes=[pim])
                              P.act(lambda e, kvh=kvh, pim=pim: e.activation(out=z[0:64, kvh * 512:(kvh + 1) * 512], in_=pim[0:64, :], func=AF.Copy),
                                    reads=[pim], writes=[z])
                          chk(5)
                          branch_epilogue(0, False, True)
                          chk(6)
                          select_blocks(64, lambda kvh: vma[:, 0, kvh, :], lambda kvh: vma[:, 1, kvh, :], [vma], 7, False)
                          for kvh in range(2):
                              build_augq(augq[kvh], kvh, 0, False)
                          chk(7)
                          for kvh in range(2):
                              r = slice(kvh * 64, kvh * 64 + 64)
                              for j in range(it + 1):
                                  mms = [(kslcT[:, j * 128:(j + 1) * 128], qTz[kvh][:, :], [kslcT, qTz[kvh]], (0, 512)),
                                         (ckt[0:96, j * 128:(j + 1) * 128], augq[kvh][0:96, :], [ckt, augq[kvh]], (0, 512))]
                                  if j == it:
                                      mms.append((identb[:], caus4t[:, 0, :], [identb, caus4t], (0, 512)))
                                  unit(mms, 512, vslc[:, j, kvh, :], [vslc], obank[kvh], (0, 512), j == 0, j == it)
                          branch_epilogue(1, False, False)
                          chk(8)
                          j0 = max(0, it - 4)
                          for kvh in range(2):
                              r = slice(kvh * 64, kvh * 64 + 64)
                              for j in range(j0, it + 1):
                                  mms = [(kwinT[:, (j % 8) * 128:(j % 8 + 1) * 128], qTz[kvh][:, :], [kwinT, qTz[kvh]], (0, 512)),
                                         (ckt[0:4, j * 128:(j + 1) * 128], augq[kvh][0:4, :], [ckt, augq[kvh]], (0, 512))]
                                  if j == it:
                                      mms.append((identb[:], caus4t[:, 0, :], [identb, caus4t], (0, 512)))
                                  if j == it - 4:
                                      mms.append((identb[:], caus4t[:, 1, :], [identb, caus4t], (0, 512)))
                                  unit(mms, 512, vwin[:, j % 8, kvh, :], [vwin], obank[kvh], (0, 512), j == j0, j == it)
                          branch_epilogue(2, False, False)
                          chk(9)
                          back(ti, x)
                      except _Stop:
                          pass
                    P.flush()

            with ExitStack() as SS:
              if CFG["sample"]:
                sbs = mk(SS)
                Ys = sbs("Ys", [128, 2, 2, 8 + 512], BF16)
                kcTs = sbs("kcTs", [128, 520], BF16); vcss = sbs("vcss", [128, 4, 2, 65], BF16)
                ckt = sbs("ckt", [96, NKCOL], BF16)
                maskcst = sbs("maskcst", [128, 32], BF16); mmst = sbs("mmst", [128, 4, 128], BF16)
                newmt = sbs("newmt", [128, 16, 32], BF16); wmst = sbs("wmst", [128, 32], BF16)
                augq2 = [sbs("augq2_%d" % k, [96, 512], BF16) for k in range(2)]
                SX["Ssm"] = [Sf, sbs("Ssm1", [128, 2, 128])]
                SX["S0bf"] = sbs("S0bf", [128, 16, 2, 128], BF16)
                SX["ogT"] = tmpf
                ld("sp", ckt, ckt[:], ck[:, :])
                ld("sp", maskcst, maskcst[:], maskcs[:, :])
                ld("sp", mmst, mmst[:], mmap_s[:, :, :])
                ld("sp", newmt, newmt[:], newmask[:, :, :])
                ld("sp", wmst, wmst[:], wmasks[:, :])
                pgf = [sbs("pgf%d" % i, [128, 256]) for i in range(2)]
                pgb = [sbs("pgb%d" % i, [128, 256], BF16) for i in range(2)]
                kTp = [sbs("kTp%d" % i, [128, 128], BF16) for i in range(2)]
                vpg = [sbs("vpg%d" % i, [128, 2, 65], BF16) for i in range(2)]
                ptt = sbs("ptt", [128, NPG], I32); idxt = sbs("idxt", [128, NPG], I32); iot = sbs("iot", [128, 1], I32)
                idxt2 = sbs("idxt2", [128, NPG], I32)
                kTn = sbs("kTn", [128, 2, 128], BF16)
                vnw = sbs("vnw", [128, 2, 2, 65], BF16)
                PTk = [sbs("PTk%d" % i, [128, 32], BF16) for i in range(8)]
                ti = PB * NT_P
                x = xt[ti % 2]
                load_mod(MOD, 0, 2)
                for kvh in range(2):
                    P.pool(lambda e, kvh=kvh: e.memset(augq[kvh][:], 0.0), writes=[augq[kvh]])
                    P.pool(lambda e, kvh=kvh: e.memset(augq2[kvh][:], 0.0), writes=[augq2[kvh]])
                    ld("sp", augq[kvh], augq[kvh][0:5, :], qab[1, kvh])
                    ld("sp", augq2[kvh], augq2[kvh][0:5, :], qab[1, kvh])
                front(ti, x, 2)
                for s in range(SB_):
                    P.dma("sp", lambda e, s=s: e.dma_start(out=win_s[s, WBUF - TS:WBUF, :],
                                                           in_=z[s * TS:(s + 1) * TS, C_KV + 512:C_KV + 768]),
                          reads=[z], writes=[dwins], sembuf=dwins)
                P.dma("pool", lambda e: e.dma_start(out=win_s[:, 0:WBUF - TS, :].rearrange("s t f -> s (t f)"),
                                                    in_=swin[:, TS:WBUF, :].rearrange("s t f -> s (t f)")),
                      writes=[dwins], sembuf=dwins)
                gla(True, 0, 2)
                P.pe(lambda e: e.transpose(out=pT[:, 512:640], in_=kvbf[:, 256:384], identity=identb[:]),
                     reads=[kvbf, identb], writes=[pT])
                P.pe(lambda e: e.transpose(out=pT[:, 640:768], in_=kvbf[:, 512:640], identity=identb[:]),
                     reads=[kvbf, identb], writes=[pT])
                P.dve(lambda e: e.tensor_copy(out=kTn[:].rearrange("p a b -> p (a b)"), in_=pT[:, 512:768]), reads=[pT], writes=[kTn])
                P.pool(lambda e: e.memset(vnw[:, :, :, 64:65], 1.0), writes=[vnw])
                P.pool(lambda e: e.tensor_copy(out=vnw[:, 0, :, 0:64], in_=kvbf[:, 384:512].rearrange("p (k d) -> p k d", k=2)),
                       reads=[kvbf], writes=[vnw])
                P.pool(lambda e: e.tensor_copy(out=vnw[:, 1, :, 0:64], in_=kvbf[:, 640:768].rearrange("p (k d) -> p k d", k=2)),
                       reads=[kvbf], writes=[vnw])
                P.pool(lambda e: e.iota(iot[:], pattern=[[0, 1]], base=0, channel_multiplier=2), writes=[iot])
                for kvh in range(2):
                    P.pool(lambda e, kvh=kvh: e.memset(vpg[kvh][:, :, 64:65], 1.0), writes=[vpg[kvh]])
                qTsz = [qnbf, kvbf]
                for kvh in range(2):
                    r = slice(kvh * 64, kvh * 64 + 64)
                    ro = slice((1 - kvh) * 64, (1 - kvh) * 64 + 64)
                    P.dve(lambda e, kvh=kvh, ro=ro: e.memset(qTsz[kvh][ro, 0:512], 0.0), writes=[qTsz[kvh]])
                    P.dve(lambda e, kvh=kvh, r=r: e.tensor_copy(
                        out=qTsz[kvh][r, 0:512].rearrange("p (s g q) -> p g s q", s=16, g=4),
                        in_=qTz[kvh][r, :].rearrange("p (g s q) -> p g s q", g=4, s=16)), reads=[qTz[kvh]], writes=[qTsz[kvh]])
                P.pool(lambda e: e.memset(vcss[:], 0.0), writes=[vcss])
                P.pool(lambda e: e.memset(vcss[:, :, :, 64:65], 1.0), writes=[vcss])
                P.pool(lambda e: e.memset(kcTs[:], 0.0), writes=[kcTs])
                P.pool(lambda e: e.memset(Ys[:], 0.0), writes=[Ys])
                npg = [0]

                def evac_seq(kvh, s):
                    unit_flush()
                    P.act(lambda e: e.activation(
                        out=tmpf[0:65, kvh * 512:(kvh + 1) * 512].rearrange("p (g s q) -> p s g q", g=4, s=16)[:, s, :, :],
                        in_=obank[kvh][0:65, s * 32:(s + 1) * 32].rearrange("p (g q) -> p g q", g=4),
                        func=AF.Copy), reads=[obank[kvh]], writes=[tmpf])

                def gather(s, j, col0):
                    pf = pgf[npg[0] % 2]
                    pbf = pgb[npg[0] % 2]
                    npg[0] += 1
                    P.dma("pool", lambda e: e.indirect_dma_start(
                        out=pf[:], out_offset=None, in_=cache[:, :],
                        in_offset=bass.IndirectOffsetOnAxis(ap=(idxt if col0 == 0 else idxt2)[:, j:j + 1], axis=0)),
                        reads=[idxt, idxt2], writes=[pf], sembuf=pf)
                    P.dve(lambda e: e.tensor_copy(out=pbf[:], in_=pf[:]), reads=[pf], writes=[pbf])
                    return pbf

                for s in range(SB_):
                    qc = (s * 32, 32)
                    ld("sp", ptt, ptt[:], ptab[s:s + 1, :].partition_broadcast(128))
                    P.dve(lambda e: e.tensor_scalar(out=idxt[:], in0=ptt[:], scalar1=256, scalar2=iot[:, 0:1], op0=ALU.mult,
                                                    op1=ALU.add), reads=[ptt, iot], writes=[idxt])
                    P.dve(lambda e: e.tensor_scalar(out=idxt2[:], in0=idxt[:], scalar1=1, scalar2=None, op0=ALU.add),
                          reads=[idxt], writes=[idxt2])
                    for j in range(NPG):
                        pbf = gather(s, j, 0)
                        pY = pb[0]
                        for ty in range(2):
                            for kvh in range(2):
                                for r in range(2):
                                    c0 = ty * 128 + kvh * 64
                                    P.pe(lambda e, ty=ty, kvh=kvh, r=r, c0=c0, pbf=pbf: e.matmul(
                                        pY[r * 64:(r + 1) * 64, (ty * 2 + kvh) * 64:(ty * 2 + kvh + 1) * 64],
                                        lhsT=pbf[:, c0:c0 + 64], rhs=selpt[:, r, :], start=True, stop=True),
                                        reads=[pbf, selpt], writes=[pY])
                        P.act(lambda e, j=j: e.activation(out=Ys[:, :, :, 8 + (j % 8) * 64:8 + (j % 8 + 1) * 64],
                                                          in_=pY[:, 0:256].rearrange("p (a b c) -> p a b c", a=2, b=2),
                                                          func=AF.Copy), reads=[pY], writes=[Ys])
                        if j % 8 == 7:
                            m = j // 8
                            compress(Ys, 64, kcTs, 64 * m, vcss, 64 * m - 1, m == 0)
                            P.dve(lambda e: e.tensor_copy(out=Ys[:, :, :, 0:8], in_=Ys[:, :, :, 512:520]), reads=[Ys], writes=[Ys])
                    for kvh in range(2):
                        r = slice(kvh * 64, kvh * 64 + 64)
                        for jc in range(4):
                            mms = [(kcTs[:, 1 + jc * 128:1 + (jc + 1) * 128], qTsz[kvh][:, s * 32:(s + 1) * 32], [kcTs, qTsz[kvh]], (0, 32)),
                                   (ckct[0:4, jc * 128:(jc + 1) * 128], augq[kvh][0:4, qc[0]:qc[0] + 32], [ckct, augq[kvh]], (0, 32))]
                            if jc == 3:
                                mms.append((identb[:], maskcst[:, :], [identb, maskcst], (0, 32)))
                            unit(mms, 32, vcss[:, jc, kvh, :], [vcss], obank[kvh], qc, jc == 0, jc == 3, keep=PTk[kvh * 4 + jc])
                        evac_seq(kvh, s)
                        pim = pb[kvh]
                        for jc in range(4):
                            P.pe(lambda e, kvh=kvh, jc=jc, pim=pim, qc=qc: e.matmul(pim[:, qc[0]:qc[0] + 32], lhsT=mmst[:, jc, :],
                                                                             rhs=PTk[kvh * 4 + jc][:, :], start=(jc == 0), stop=(jc == 3)),
                                 reads=[mmst, PTk[kvh * 4 + jc]], writes=[pim])
                        P.act(lambda e, kvh=kvh, pim=pim, s=s: e.activation(
                            out=z[:, kvh * 512:(kvh + 1) * 512].rearrange("p (g s q) -> p s g q", g=4, s=16)[:, s, :, :],
                            in_=pim[:, s * 32:(s + 1) * 32].rearrange("p (g q) -> p g q", g=4),
                            func=AF.Copy), reads=[pim], writes=[z])
                branch_epilogue(0, True, True)
                ld("sp", z, z[:, 2048:2560], vmam_s.rearrange("p a b c -> p (a b c)"))
                select_blocks(128, lambda kvh: z[:, 2048 + kvh * 128:2048 + (kvh + 1) * 128],
                              lambda kvh: z[:, 2304 + kvh * 128:2304 + (kvh + 1) * 128], [z], 6, True)
                for kvh in range(2):
                    build_augq(augq[kvh], kvh, 0, True)
                    build_augq(augq2[kvh], kvh, 64, True)
                for s in range(SB_):
                    qc = (s * 32, 32)
                    ld("sp", ptt, ptt[:], ptab[s:s + 1, :].partition_broadcast(128))
                    P.dve(lambda e: e.tensor_scalar(out=idxt[:], in0=ptt[:], scalar1=256, scalar2=iot[:, 0:1], op0=ALU.mult,
                                                    op1=ALU.add), reads=[ptt, iot], writes=[idxt])
                    P.dve(lambda e: e.tensor_scalar(out=idxt2[:], in0=idxt[:], scalar1=1, scalar2=None, op0=ALU.add),
                          reads=[idxt], writes=[idxt2])
                    for j in range(NPG + 1):
                        if j < NPG:
                            pbf = gather(s, j, 256)
                            kt = kTp[j % 2]; vp = vpg[j % 2]
                            P.pe(lambda e, pbf=pbf: e.transpose(out=pT[:, 768:896], in_=pbf[:, 0:128], identity=identb[:]),
                                 reads=[pbf, identb], writes=[pT])
                            P.act(lambda e, kt=kt: e.activation(out=kt[:], in_=pT[:, 768:896], func=AF.Copy), reads=[pT], writes=[kt])
                            P.dve(lambda e, vp=vp, pbf=pbf: e.tensor_copy(out=vp[:, :, 0:64],
                                                                           in_=pbf[:, 128:256].rearrange("p (k d) -> p k d", k=2)),
                                   reads=[pbf], writes=[vp])
                        for kvh in range(2):
                            r = slice(kvh * 64, kvh * 64 + 64)
                            if j < NPG:
                                aq = augq[kvh] if j < 32 else augq2[kvh]
                                mms = [(kt[:, :], qTsz[kvh][:, s * 32:(s + 1) * 32], [kt, qTsz[kvh]], (0, 32)),
                                       (ckt[0:96, j * 128:(j + 1) * 128], aq[0:96, qc[0]:qc[0] + 32], [ckt, aq], (0, 32))]
                                unit(mms, 32, vp[:, kvh, :], [vp], obank[kvh], qc, j == 0, False)
                            else:
                                mms = [(kTn[:, 0, :], qTsz[kvh][:, s * 32:(s + 1) * 32], [kTn, qTsz[kvh]], (0, 32)),
                                       (ckt[0:4, PAST:PAST + 128], augq[kvh][0:4, qc[0]:qc[0] + 32], [ckt, augq[kvh]], (0, 32)),
                                       (identb[:], newmt[:, s, :], [identb, newmt], (0, 32))]
                                unit(mms, 32, vnw[:, 0, kvh, :], [vnw], obank[kvh], qc, False, True)
                                evac_seq(kvh, s)
                branch_epilogue(1, True, False)
                for s in range(SB_):
                    qc = (s * 32, 32)
                    for w in range(5):
                        if w < 4:
                            pf = pgf[npg[0] % 2]; pbf = pgb[npg[0] % 2]
                            npg[0] += 1
                            ld("sp", pf, pf[:], swin[s, w * 128:(w + 1) * 128, :])
                            P.dve(lambda e, pf=pf, pbf=pbf: e.tensor_copy(out=pbf[:], in_=pf[:]), reads=[pf], writes=[pbf])
                            kt = kTp[w % 2]; vp = vpg[w % 2]
                            P.pe(lambda e, pbf=pbf: e.transpose(out=pT[:, 768:896], in_=pbf[:, 0:128], identity=identb[:]),
                                 reads=[pbf, identb], writes=[pT])
                            P.act(lambda e, kt=kt: e.activation(out=kt[:], in_=pT[:, 768:896], func=AF.Copy), reads=[pT], writes=[kt])
                            P.dve(lambda e, vp=vp, pbf=pbf: e.tensor_copy(out=vp[:, :, 0:64],
                                                                           in_=pbf[:, 128:256].rearrange("p (k d) -> p k d", k=2)),
                                   reads=[pbf], writes=[vp])
                        for kvh in range(2):
                            r = slice(kvh * 64, kvh * 64 + 64)
                            if w < 4:
                                c0 = PAST - WBUF + w * 128
                                mms = [(kt[:, :], qTsz[kvh][:, s * 32:(s + 1) * 32], [kt, qTsz[kvh]], (0, 32)),
                                       (ckt[0:4, c0:c0 + 128], augq[kvh][0:4, qc[0]:qc[0] + 32], [ckt, augq[kvh]], (0, 32))]
                                if w == 0:
                                    mms.append((identb[:], wmst[:, :], [identb, wmst], (0, 32)))
                                unit(mms, 32, vp[:, kvh, :], [vp], obank[kvh], qc, w == 0, False)
                            else:
                                mms = [(kTn[:, 1, :], qTsz[kvh][:, s * 32:(s + 1) * 32], [kTn, qTsz[kvh]], (0, 32)),
                                       (ckt[0:4, PAST:PAST + 128], augq[kvh][0:4, qc[0]:qc[0] + 32], [ckt, augq[kvh]], (0, 32)),
                                       (identb[:], newmt[:, s, :], [identb, newmt], (0, 32))]
                                unit(mms, 32, vnw[:, 1, kvh, :], [vnw], obank[kvh], qc, False, True)
                                evac_seq(kvh, s)
                branch_epilogue(2, True, False)
                back(ti, x)
                P.flush()

        with ExitStack() as S2:
          if CFG["ffn"]:
            sb2 = mk(S2)
            wup_bf = sb2("wup_bf", [128, 8, D_FF], BF16)
            wdn_bf = sb2("wdn_bf", [128, 32, D], BF16)
            MOD2 = sb2("MOD2", [128, 3 * D])
            x1t = [sb2("x1t%d" % i, [128, D]) for i in range(2)]
            junk2 = sb2("junk2", [128, D], BF16)
            st2 = sb2("st2", [128, 8])
            tmp2 = sb2("tmp2", [128, D])
            h2bf = sb2("h2bf", [128, D], BF16); h2T = sb2("h2T", [128, D], BF16)
            hr = [sb2("hr%d" % i, [128, 512], BF16) for i in range(2)]
            hsq = sb2("hsq", [128, 32, 128], BF16)
            yt = sb2("yt", [128, D])
            with ExitStack() as S2a0:
                ada_rows(mk(S2a0), 3 * D, 2, 3)
                P.flush()
            with ExitStack() as S2a:
                sba = mk(S2a)
                stg2 = [sba("stg2_%d" % i, [128, 2048]) for i in range(2)]
                for kc in range(16):
                    s_ = stg2[kc % 2]
                    k8, hh = kc // 2, kc % 2
                    ld("sp" if kc % 2 == 0 else "pool", s_, s_[:], w_up[k8 * 128:(k8 + 1) * 128, hh * 2048:(hh + 1) * 2048])
                    if kc % 2 == 0:
                        P.dve(lambda e, s_=s_, k8=k8, hh=hh: e.tensor_copy(out=wup_bf[:, k8, hh * 2048:(hh + 1) * 2048], in_=s_[:]),
                              reads=[s_], writes=[wup_bf])
                    else:
                        P.act(lambda e, s_=s_, k8=k8, hh=hh: e.activation(out=wup_bf[:, k8, hh * 2048:(hh + 1) * 2048], in_=s_[:],
                                                                          func=AF.Copy), reads=[s_], writes=[wup_bf])
                for q4 in range(16):
                    s_ = stg2[q4 % 2]
                    ld("sp" if q4 % 2 == 0 else "pool", s_, s_[:].rearrange("p (a n) -> p a n", a=2),
                       w_down[q4 * 256:(q4 + 1) * 256, :].rearrange("(a p) n -> p a n", p=128))
                    if q4 % 2 == 0:
                        P.dve(lambda e, s_=s_, q4=q4: e.tensor_copy(out=wdn_bf[:, q4 * 2:(q4 + 1) * 2, :].rearrange("p a n -> p (a n)"),
                                                                    in_=s_[:]), reads=[s_], writes=[wdn_bf])
                    else:
                        P.act(lambda e, s_=s_, q4=q4: e.activation(out=wdn_bf[:, q4 * 2:(q4 + 1) * 2, :].rearrange("p a n -> p (a n)"),
                                                                   in_=s_[:], func=AF.Copy), reads=[s_], writes=[wdn_bf])
                P.flush()
            for ti in range(NTILE):
                kind = min(ti // NT_P, 2)
                if ti % NT_P == 0:
                    load_mod(MOD2, 3 * D, kind)
                xx = x1t[ti % 2]
                ld("sp", xx, xx[:], y_all[ti * 128:(ti + 1) * 128, :])
                rms_rstd(xx[:], [xx], junk2[:], junk2, st2, 0, D)
                P.dve(lambda e, xx=xx: e.scalar_tensor_tensor(out=tmp2[:], in0=xx[:], scalar=st2[:, 1:2], in1=MOD2[:, D:2 * D],
                                                              op0=ALU.mult, op1=ALU.mult), reads=[xx, st2, MOD2], writes=[tmp2])
                P.pool(lambda e: e.tensor_tensor(out=h2bf[:], in0=tmp2[:], in1=MOD2[:, 0:D], op=ALU.add),
                       reads=[tmp2, MOD2], writes=[h2bf])
                transpose8(h2bf, h2T)
                for f4 in range(8):
                    pbk = pb[2 + (f4 % 3)]
                    for ff in range(4):
                        fch = f4 * 4 + ff
                        for kc in range(8):
                            P.pe(lambda e, kc=kc, pbk=pbk, ff=ff, fch=fch: e.matmul(
                                pbk[:, ff * 128:(ff + 1) * 128], lhsT=wup_bf[:, kc, fch * 128:(fch + 1) * 128],
                                rhs=h2T[:, kc * 128:(kc + 1) * 128], start=(kc == 0), stop=(kc == 7)),
                                reads=[wup_bf, h2T], writes=[pbk])
                    hrr = hr[f4 % 2]
                    P.act(lambda e, pbk=pbk, hrr=hrr: e.activation(out=hrr[:], in_=pbk[:, :], func=AF.Relu), reads=[pbk], writes=[hrr])
                    if f4 % 2 == 0:
                        P.dve(lambda e, hrr=hrr, f4=f4: e.tensor_tensor(out=hsq[:, f4 * 4:(f4 + 1) * 4, :].rearrange("p a t -> p (a t)"),
                                                                        in0=hrr[:], in1=hrr[:], op=ALU.mult), reads=[hrr], writes=[hsq])
                    else:
                        P.pool(lambda e, hrr=hrr, f4=f4: e.tensor_tensor(out=hsq[:, f4 * 4:(f4 + 1) * 4, :].rearrange("p a t -> p (a t)"),
                                                                         in0=hrr[:], in1=hrr[:], op=ALU.mult), reads=[hrr], writes=[hsq])
                for nh in range(2):
                    pbk = pb[nh]
                    for fch in range(32):
                        P.pe(lambda e, fch=fch, pbk=pbk, nh=nh: e.matmul(pbk[:, :], lhsT=hsq[:, fch, :],
                                                                         rhs=wdn_bf[:, fch, nh * 512:(nh + 1) * 512],
                                                                         start=(fch == 0), stop=(fch == 31)),
                             reads=[hsq, wdn_bf], writes=[pbk])
                    P.act(lambda e, pbk=pbk, nh=nh: e.activation(out=junk2[:, 0:512], in_=pbk[:, :], func=AF.Square,
                                                                 accum_out=st2[:, 2 + nh:3 + nh]), reads=[pbk], writes=[st2, junk2])
                P.dve(lambda e: e.tensor_tensor(out=st2[:, 4:5], in0=st2[:, 2:3], in1=st2[:, 3:4], op=ALU.add), reads=[st2], writes=[st2])
                P.act(lambda e: e.activation(out=st2[:, 5:6], in_=st2[:, 4:5], func=AF.Sqrt, scale=1.0 / D, bias=epsb[:]),
                      reads=[st2, epsb], writes=[st2])
                P.dve(lambda e: e.reciprocal(out=st2[:, 5:6], in_=st2[:, 5:6]), reads=[st2], writes=[st2])
                for nh in range(2):
                    pbk = pb[nh]
                    P.dve(lambda e, pbk=pbk, nh=nh: e.scalar_tensor_tensor(
                        out=tmp2[:, nh * 512:(nh + 1) * 512], in0=pbk[:, :], scalar=st2[:, 5:6],
                        in1=MOD2[:, 2 * D + nh * 512:2 * D + (nh + 1) * 512], op0=ALU.mult, op1=ALU.mult),
                        reads=[pbk, st2, MOD2], writes=[tmp2])
                P.pool(lambda e, xx=xx: e.tensor_tensor(out=yt[:], in0=tmp2[:], in1=xx[:], op=ALU.add), reads=[tmp2, xx], writes=[yt])
                P.dma("sp", lambda e, ti=ti: e.dma_start(out=y_all[ti * 128:(ti + 1) * 128, :], in_=yt[:]),
                      reads=[yt], writes=[dy], sembuf=dy)
            P.flush()
    return nc


_NC = None


def _consts():
    f32 = np.float32
    c = {}
    c["ident_f"] = np.eye(128, dtype=f32)
    sel = np.zeros((3, 18, 128), f32)
    sel[0, 0, :] = 1.0
    sel[1, 1, :] = 1.0
    for m in range(128):
        sel[2, 2 + m // TS, m] = 1.0
    tp = np.arange(128)
    same = [np.ones((128, 128), bool), (tp[:, None] // TS == tp[None, :] // TS)]
    tri = np.zeros((2, 2, 128, 128), f32)
    causm = np.zeros((2, 128, 512), f32)
    for k in range(2):
        tri[k, 0] = ((tp[:, None] > tp[None, :]) & same[k]) * (-1.0 / 16)
        tri[k, 1] = ((tp[:, None] <= tp[None, :]) & same[k]) * (-1.0 / 16)
        causm[k] = np.tile(((tp[:, None] <= tp[None, :]) & same[k]).astype(f32), (1, 4))
    c["tri"] = tri
    c["causm"] = causm
    seqind = np.zeros((2, 128, 16), f32)
    seqind[0, :, 0] = -1.0 / 16
    seqmask = np.zeros((128, 16), f32)
    for s in range(16):
        seqind[1, s * TS:(s + 1) * TS, s] = -1.0 / 16
        seqmask[s * TS:(s + 1) * TS, s] = 1.0
    c["seqind"] = seqind
    c["seqmask"] = seqmask
    ck = np.zeros((96, NKCOL), f32)
    key = np.arange(PAST)
    ck[32 + (key // 64) % 64, key] = 1.0
    ck[0, :] = 1.0
    ck[1, :] = 1.0
    ck[2, :PAST] = key % 128
    ck[3, :PAST] = key - key % 128
    ck[4, :PAST] = 1.0
    m = np.arange(128)
    ck[2, PAST:] = m % TS
    ck[3, PAST:] = PAST
    c["ck"] = ck
    ckc = np.zeros((4, 512), f32)
    cend = 16 * np.arange(512) + 31
    ckc[0] = 1.0
    ckc[1] = 1.0
    ckc[2] = cend % 128
    ckc[3] = cend - cend % 128
    c["ckc"] = ckc
    slope = np.array([2.0 ** -(h + 1) for h in range(8)], f32).reshape(2, 4)
    qab = np.zeros((2, 2, 5, 512), f32)
    for kvh in range(2):
        for g in range(4):
            sl = slope[kvh, g]
            cols = slice(g * 128, (g + 1) * 128)
            qab[0, kvh, 0, cols] = -128.0 * sl
            qab[0, kvh, 1, cols] = -sl * np.arange(128)
            qab[0, kvh, 2, cols] = sl
            qab[0, kvh, 3, cols] = sl
            qab[0, kvh, 4, cols] = -NEG
            for s in range(16):
                cs = slice(s * 32 + g * 8, s * 32 + g * 8 + 8)
                qab[1, kvh, 0, cs] = -128.0 * 64 * sl
                qab[1, kvh, 1, cs] = -sl * np.arange(8)
                qab[1, kvh, 2, cs] = sl
                qab[1, kvh, 3, cs] = sl
                qab[1, kvh, 4, cs] = -NEG
    c["qab"] = qab
    maskc = np.zeros((17, 128, 128), f32)
    for r in range(17):
        maskc[r] = np.where(16 * tp[:, None] + 31 > 128 * r + tp[None, :], -NEG, 0.0)
    c["maskc"] = maskc
    maskcs = np.zeros((128, 32), f32)
    maskcs[127, :] = -NEG
    c["maskcs"] = maskcs

    def mmap(ncb, nsb):
        start = np.arange(ncb) * 16
        bs = np.arange(nsb) * 64
        ov = np.minimum(start[:, None] + 32, bs[None, :] + 64) - np.maximum(start[:, None], bs[None, :])
        return (np.clip(ov, 0, None) / 32).astype(f32)
    mp = np.zeros((256, 64), f32)
    mp[:255] = mmap(255, 64)
    c["mmap_p"] = np.ascontiguousarray(mp.reshape(2, 128, 64).transpose(1, 0, 2))
    ms = np.zeros((512, 128), f32)
    ms[:511] = mmap(511, 129)[:, :128]
    c["mmap_s"] = np.ascontiguousarray(ms.reshape(4, 128, 128).transpose(1, 0, 2))
    vmam_p = np.zeros((NT_P, 128, 2, 2, 64), f32)
    blk = np.arange(64)
    for it in range(NT_P):
        cur = (128 * it + tp) // 64
        valid = blk[None, :] <= cur[:, None]
        forced = valid & ((blk[None, :] == 0) | (blk[None, :] == cur[:, None]) | (blk[None, :] == cur[:, None] - 1))
        vm = (valid & ~forced).astype(f32)
        am = np.where(forced, 1.0e4, np.where(valid, 0.0, -1.0e4)).astype(f32)
        vmam_p[it, :, 0, :, :] = vm[:, None, :]
        vmam_p[it, :, 1, :, :] = am[:, None, :]
    c["vmam_p"] = vmam_p
    vmam_s = np.zeros((128, 2, 2, 128), f32)
    vmam_s[:, 0, :, :] = 1.0
    vmam_s[:, 0, :, 0] = 0.0
    vmam_s[:, 0, :, 127] = 0.0
    vmam_s[:, 1, :, 0] = 1.0e4
    vmam_s[:, 1, :, 127] = 1.0e4
    c["vmam_s"] = vmam_s
    caus4 = np.zeros((2, 128, 512), f32)
    caus4[0] = np.tile(np.where(tp[:, None] > tp[None, :], -NEG, 0.0), (1, 4))
    caus4[1] = np.tile(np.where(tp[:, None] < tp[None, :], -NEG, 0.0), (1, 4))
    c["caus4"] = caus4
    newmask = np.full((128, 16, 32), -NEG, f32)
    for s in range(16):
        for qk in range(TS):
            for g in range(4):
                for q in range(TS):
                    if qk <= q:
                        newmask[s * TS + qk, s, g * 8 + q] = 0.0
    c["newmask"] = newmask
    wm = np.zeros((128, 32), f32)
    for g in range(4):
        for q in range(TS):
            wm[:q, g * 8 + q] = -NEG
    c["wmasks"] = wm
    selp = np.zeros((128, 2, 64), f32)
    for p in range(64):
        for r in range(2):
            selp[2 * p + r, r, p] = 1.0
    c["selp"] = selp
    import ml_dtypes
    for k in ("causm", "ck", "ckc", "qab", "maskc", "maskcs", "mmap_p", "mmap_s", "caus4", "newmask", "wmasks", "selp"):
        c[k] = c[k].astype(ml_dtypes.bfloat16)
    return c


def kernel(x_prompt, x_sample, cache_kv, state_win, state_gla, page_table, c_prompt, c_sample,
           norm_mix_pre, norm_mix_post, norm_ffn_pre, norm_ffn_post, w_ada, b_ada, w_in,
           gla_w_gate, gla_b_gate, gla_norm, cmp_pos, cmp_w1, cmp_b1, cmp_w2, cmp_b2,
           w_out, w_up, w_down):
    global _NC
    f32 = np.float32
    if _NC is None:
        _NC = build_nc()
    nc = _NC
    A = lambda a: np.asarray(a, f32)
    x_prompt = A(x_prompt); x_sample = A(x_sample)
    consts = _consts()
    w1 = A(cmp_w1)[0].reshape(2, 2, 8, 2, 64, 256)
    w1p = np.ascontiguousarray(w1.transpose(3, 4, 0, 1, 2, 5).reshape(128, 2, 2, 8, 256))
    pos = A(cmp_pos)[0].reshape(2, 2, 8, 2, 64)
    posp = pos.transpose(3, 4, 0, 1, 2).reshape(128, 2, 2, 8)
    posrep = np.ascontiguousarray(np.repeat(posp[..., None], 16, axis=-1))
    w2 = A(cmp_w2)[0].reshape(2, 2, 128, 64)
    w2l = np.ascontiguousarray(w2.transpose(2, 0, 1, 3))
    b2 = A(cmp_b2)[0]
    shared = {
        "w_ada": A(w_ada)[0], "b_ada": A(b_ada)[0].reshape(1, -1),
        "gains": np.ascontiguousarray(np.stack([A(norm_mix_pre)[0], A(norm_mix_post)[0], A(norm_ffn_pre)[0], A(norm_ffn_post)[0]])),
        "w_in": A(w_in)[0], "w_out": A(w_out)[0], "w_up": A(w_up)[0], "w_down": A(w_down)[0],
        "wg": A(gla_w_gate)[0], "bg": A(gla_b_gate)[0].reshape(1, -1),
        "gnorm4": np.ascontiguousarray(np.tile(A(gla_norm)[0], 4).reshape(1, 512)),
        "cache": A(cache_kv)[0].reshape(-1, 256),
        "w1p": w1p, "posrep": posrep, "b1r": A(cmp_b1)[0], "w2l": w2l,
        "b2k": np.ascontiguousarray(np.concatenate([b2[0], b2[0]]).reshape(128, 1)), "b2v": b2[1].reshape(1, 64),
    }
    shared.update(consts)
    in_maps = []
    pt = np.asarray(page_table).astype(np.int32)
    for c in range(NCORES):
        xp = x_prompt[PB * c:PB * (c + 1)].reshape(PB * SEQ, D)
        xsm = x_sample[SB_ * c:SB_ * (c + 1)].reshape(SB_ * TS, D)
        cc = np.concatenate([A(c_prompt)[PB * c:PB * (c + 1)], A(c_sample)[SB_ * c:SB_ * (c + 1)]], axis=0)
        cTa = np.ascontiguousarray(cc.T.reshape(8, 128, 18).transpose(1, 0, 2))
        m = dict(shared)
        m.update({
            "xs": np.ascontiguousarray(np.concatenate([xp, xsm], axis=0)),
            "cT": cTa,
            "sgla": np.ascontiguousarray(A(state_gla)[0, SB_ * c:SB_ * (c + 1)]),
            "swin": np.ascontiguousarray(A(state_win)[0, SB_ * c:SB_ * (c + 1)].reshape(SB_, WBUF, 256)),
            "ptab": np.ascontiguousarray(pt[SB_ * c:SB_ * (c + 1)]),
        })
        in_maps.append(m)
    res = run_bass_kernel_spmd(nc, in_maps, core_ids=list(range(NCORES)))
    R = res.results
    NP = PB * SEQ
    y_p = np.concatenate([r["y_all"][:NP].reshape(PB, SEQ, D) for r in R], axis=0)
    y_s = np.concatenate([r["y_all"][NP:].reshape(SB_, TS, D) for r in R], axis=0)
    kv_p = np.concatenate([r["kvr"][:NP].reshape(PB, SEQ, 4, 2, 64) for r in R], axis=0)[None]
    kv_s = np.concatenate([r["kvr"][NP:].reshape(SB_, TS, 4, 2, 64) for r in R], axis=0)[None]
    wp = np.concatenate([r["win_p"].reshape(PB, WBUF, 2, 2, 64) for r in R], axis=0)[None]
    ws = np.concatenate([r["win_s"].reshape(SB_, WBUF, 2, 2, 64) for r in R], axis=0)[None]
    gp = np.concatenate([r["gla_p"] for r in R], axis=0)[None]
    gs = np.concatenate([r["gla_s"] for r in R], axis=0)[None]
    return (y_p.astype(f32), y_s.astype(f32), kv_p.astype(f32), kv_s.astype(f32),
            wp.astype(f32), ws.astype(f32), gp.astype(f32), gs.astype(f32))
```

```python
import numpy as np
from contextlib import ExitStack
import concourse.bass as bass
import concourse.mybir as mybir
from concourse.bass_utils import run_bass_kernel_spmd

F32 = mybir.dt.float32
BF16 = mybir.dt.bfloat16
I32 = mybir.dt.int32
ALU = mybir.AluOpType
AF = mybir.ActivationFunctionType

NCORES = 8
D = 1024
SEQ = 4096
PB = 2
SB_ = 16
TS = 8
NT_P = SEQ // 128
D_IN = 2856
D_FF = 4096
C_ZQ, C_ZK, C_ZV, C_ZA, C_ZR, C_QN, C_KV, C_ZG = 0, 256, 512, 1024, 1040, 1552, 2064, 2832
EPS = 1e-6
PAST = 8192
NPG = 64
WBUF = 512
NEG = 4096.0
NTOK = PB * SEQ + SB_ * TS
NTILE = NTOK // 128
NKCOL = PAST + 128
class _Stop(Exception):
    pass


def chk(k):
    if CFG["stage"] < k:
        raise _Stop()


CFG = {"stage": 99, "pool_pages": 10240, "n_ptiles": NT_P, "sample": True, "ffn": True, "kinds": PB}


class Buf:
    __slots__ = ("name", "t", "last_w", "readers", "dsem", "dcnt")

    def __init__(self, name, t):
        self.name = name
        self.t = t
        self.last_w = None
        self.readers = {}
        self.dsem = None
        self.dcnt = 0

    def __getitem__(self, idx):
        return self.t[idx]


class Op:
    __slots__ = ("eng", "fn", "deps", "signal", "isdma", "buf", "cnt")

    def __init__(self, eng, fn, isdma=False, buf=None):
        self.eng = eng
        self.fn = fn
        self.deps = set()
        self.signal = False
        self.isdma = isdma
        self.buf = buf
        self.cnt = None


class Prog:
    ENGS = ("pe", "dve", "act", "pool", "sp")

    def __init__(self, nc, stack):
        self.nc = nc
        self.stack = stack
        self.ops = []
        self.bufs = []
        self.sems = {e: stack.enter_context(nc.semaphore("s_" + e)) for e in self.ENGS}
        self.cnt = {e: 0 for e in self.ENGS}
        self.waited = {e: {} for e in self.ENGS}
        self.dbufs = []
        self.engmap = {"pe": nc.tensor, "dve": nc.vector, "act": nc.scalar, "pool": nc.gpsimd, "sp": nc.sync}

    def buf(self, name, t):
        b = Buf(name, t)
        self.bufs.append(b)
        return b

    def op(self, eng, fn, reads=(), writes=(), isdma=False, dmabuf=None):
        o = Op(eng, fn, isdma, dmabuf)
        i = len(self.ops)
        for b in reads:
            if b.last_w is not None:
                o.deps.add(b.last_w)
        for b in writes:
            if b.last_w is not None:
                o.deps.add(b.last_w)
            for r in b.readers.values():
                o.deps.add(r)
        rk = ("dma", i) if isdma else eng
        for b in reads:
            b.readers[rk] = i
        for b in writes:
            b.last_w = i
            b.readers = {}
        o.deps.discard(i)
        self.ops.append(o)
        return o

    def pe(self, fn, reads=(), writes=()):
        return self.op("pe", fn, reads, writes)

    def dve(self, fn, reads=(), writes=()):
        return self.op("dve", fn, reads, writes)

    def act(self, fn, reads=(), writes=()):
        return self.op("act", fn, reads, writes)

    def pool(self, fn, reads=(), writes=()):
        return self.op("pool", fn, reads, writes)

    def dma(self, q, fn, reads=(), writes=(), sembuf=None):
        return self.op(q, fn, reads, writes, isdma=True, dmabuf=sembuf)

    def _wait(self, ename, s, v):
        w = self.waited[ename]
        k = id(s)
        if w.get(k, 0) >= v:
            return
        self.engmap[ename].wait_ge(s, v)
        w[k] = v

    def flush(self):
        ops = self.ops
        for o in ops:
            for d in o.deps:
                p = ops[d]
                if p.isdma or not (p.eng == o.eng and p.eng == "pe"):
                    p.signal = True
        last = {}
        for o in ops:
            if not o.isdma:
                last[o.eng] = o
        for o in last.values():
            o.signal = True
        for o in ops:
            if o.isdma:
                b = o.buf
                if b.dsem is None:
                    b.dsem = self.stack.enter_context(self.nc.semaphore("d_" + b.name))
                    self.dbufs.append(b)
                b.dcnt += 16
                o.cnt = (b.dsem, b.dcnt)
            elif o.signal:
                self.cnt[o.eng] += 1
                o.cnt = (self.sems[o.eng], self.cnt[o.eng])
        for o in ops:
            need = {}
            for d in o.deps:
                p = ops[d]
                if p.cnt is None:
                    continue
                if (not p.isdma) and (not o.isdma) and p.eng == "pe" and o.eng == "pe":
                    continue
                s, v = p.cnt
                k = id(s)
                if k not in need or need[k][1] < v:
                    need[k] = (s, v)
            for k, (s, v) in need.items():
                self._wait(o.eng, s, v)
            ins = o.fn(self.engmap[o.eng])
            if o.cnt is not None:
                ins.then_inc(o.cnt[0], 16 if o.isdma else 1)
        for e in self.ENGS:
            for e2 in self.ENGS:
                if e2 != e and self.cnt[e2] > 0:
                    self._wait(e, self.sems[e2], self.cnt[e2])
            for b in self.dbufs:
                self._wait(e, b.dsem, b.dcnt)
        self.ops = []
        for b in self.bufs:
            b.last_w = None
            b.readers = {}


def build_nc():
    nc = bass.Bass("TRN2", target_bir_lowering=False)

    def din(name, shape, dt=F32):
        return nc.dram_tensor(name, list(shape), dt, kind="ExternalInput").ap()

    def dout(name, shape, dt=F32):
        return nc.dram_tensor(name, list(shape), dt, kind="ExternalOutput").ap()

    xs = din("xs", [NTOK, D])
    cT = din("cT", [128, 8, 18])
    w_ada = din("w_ada", [D, 6 * D])
    b_ada = din("b_ada", [1, 6 * D])
    gains = din("gains", [4, D])
    w_in = din("w_in", [D, D_IN])
    w_out = din("w_out", [D, D])
    w_up = din("w_up", [D, D_FF])
    w_down = din("w_down", [D_FF, D])
    wg = din("wg", [16, 256])
    bg = din("bg", [1, 256])
    gnorm4 = din("gnorm4", [1, 512])
    sgla = din("sgla", [SB_, 4, 64, 128])
    swin = din("swin", [SB_, WBUF, 256])
    cache = din("cache", [CFG["pool_pages"] * 128 * 2, 256])
    ptab = din("ptab", [SB_, NPG], I32)
    w1p = din("w1p", [128, 2, 2, 8, 256])
    posrep = din("posrep", [128, 2, 2, 8, 16])
    b1r = din("b1r", [2, 256])
    w2l = din("w2l", [128, 2, 2, 64])
    b2k = din("b2k", [128, 1])
    b2v = din("b2v", [1, 64])
    ident_f = din("ident_f", [128, 128])
    tri = din("tri", [2, 2, 128, 128])
    seqind = din("seqind", [2, 128, 16])
    seqmask = din("seqmask", [128, 16])
    causm = din("causm", [2, 128, 512], BF16)
    ck = din("ck", [96, NKCOL], BF16)
    ckc = din("ckc", [4, 512], BF16)
    qab = din("qab", [2, 2, 5, 512], BF16)
    maskc = din("maskc", [17, 128, 128], BF16)
    maskcs = din("maskcs", [128, 32], BF16)
    mmap_p = din("mmap_p", [128, 2, 64], BF16)
    mmap_s = din("mmap_s", [128, 4, 128], BF16)
    vmam_p = din("vmam_p", [NT_P, 128, 2, 2, 64])
    vmam_s = din("vmam_s", [128, 2, 2, 128])
    caus4 = din("caus4", [2, 128, 512], BF16)
    newmask = din("newmask", [128, 16, 32], BF16)
    wmasks = din("wmasks", [128, 32], BF16)
    selp = din("selp", [128, 2, 64], BF16)

    y_all = dout("y_all", [NTOK, D])
    kvr = dout("kvr", [NTOK, 512])
    win_p = dout("win_p", [PB, WBUF, 256])
    win_s = dout("win_s", [SB_, WBUF, 256])
    gla_p = dout("gla_p", [PB, 4, 64, 128])
    gla_s = dout("gla_s", [SB_, 4, 64, 128])
    modrows = dout("modrows", [18, 6 * D])

    with ExitStack() as G:
        P = Prog(nc, G)

        def mk(stack, suf=""):
            def sb(name, shape, dt=F32):
                return P.buf(name + suf, stack.enter_context(nc.sbuf_tensor(name + suf, list(shape), dt)))
            return sb

        gsb = mk(G)

        def ld(q, dst, dst_ap, src_ap, reads=()):
            P.dma(q, lambda e: e.dma_start(out=dst_ap, in_=src_ap), reads=reads, writes=[dst], sembuf=dst)

        pb = [P.buf("pb%d" % i, G.enter_context(nc.psum_tensor("pb%d" % i, [128, 512], F32))) for i in range(7)]
        pT = P.buf("pT", G.enter_context(nc.psum_tensor("pT", [128, 1024], BF16)))
        dy = P.buf("y_all", y_all); dkvr = P.buf("kvr", kvr); dwinp = P.buf("win_p", win_p)
        dwins = P.buf("win_s", win_s); dglap = P.buf("gla_p", gla_p); dglas = P.buf("gla_s", gla_s)
        dmod = P.buf("modrows", modrows)

        identf = gsb("identf", [128, 128]); identb = gsb("identb", [128, 128], BF16)
        scT = gsb("scT", [128, 8, 18], BF16)
        epsb = gsb("epsb", [128, 1])
        ones1 = gsb("ones1", [1, 128])
        ld("sp", identf, identf[:], ident_f[:, :])
        P.dve(lambda e: e.memset(ones1[:], 1.0), writes=[ones1])
        P.dve(lambda e: e.memset(epsb[:], EPS), writes=[epsb])
        P.dve(lambda e: e.tensor_copy(out=identb[:], in_=identf[:]), reads=[identf], writes=[identb])
        with ExitStack() as S0:
            sb0 = mk(S0)
            cTt = sb0("cTt", [128, 8, 18]); sgm = sb0("sgm", [128, 8, 18])
            ld("sp", cTt, cTt[:], cT[:, :, :])
            P.act(lambda e: e.activation(out=sgm[:], in_=cTt[:], func=AF.Sigmoid), reads=[cTt], writes=[sgm])
            P.dve(lambda e: e.tensor_tensor(out=scT[:], in0=sgm[:], in1=cTt[:], op=ALU.mult),
                  reads=[sgm, cTt], writes=[scT])
            P.flush()

        def ada_rows(sb_, col0, gidx_a, gidx_g):
            adar = sb_("adar%d" % col0, [18, 3 * D])
            wst = sb_("wada_st%d" % col0, [128, 8, 256]); wbf = sb_("wada_bf%d" % col0, [128, 8, 256], BF16)
            bch = sb_("bch%d" % col0, [18, 256]); grow = sb_("grow%d" % col0, [18, D])
            for ncn in range(12):
                c0 = col0 + ncn * 256
                ld("sp", wst, wst[:], w_ada[:, c0:c0 + 256].rearrange("(k p) n -> p k n", p=128))
                ld("pool", bch, bch[:], b_ada[0:1, c0:c0 + 256].partition_broadcast(18))
                P.dve(lambda e: e.tensor_copy(out=wbf[:], in_=wst[:]), reads=[wst], writes=[wbf])
                pbk = pb[ncn % 2]
                for kc in range(8):
                    P.pe(lambda e, kc=kc, pbk=pbk: e.matmul(pbk[0:18, 0:256], lhsT=scT[:, kc, :], rhs=wbf[:, kc, :],
                                                             start=(kc == 0), stop=(kc == 7)),
                         reads=[scT, wbf], writes=[pbk])
                P.dve(lambda e, pbk=pbk, ncn=ncn: e.tensor_tensor(out=adar[:, ncn * 256:(ncn + 1) * 256], in0=pbk[0:18, 0:256],
                                                                  in1=bch[:], op=ALU.add),
                      reads=[pbk, bch], writes=[adar])
            ld("pool", grow, grow[:], gains[gidx_a:gidx_a + 1, :].partition_broadcast(18))
            P.dve(lambda e: e.scalar_tensor_tensor(out=adar[:, D:2 * D], in0=adar[:, D:2 * D], scalar=1.0, in1=grow[:],
                                                   op0=ALU.add, op1=ALU.mult), reads=[adar, grow], writes=[adar])
            ld("pool", grow, grow[:], gains[gidx_g:gidx_g + 1, :].partition_broadcast(18))
            P.dve(lambda e: e.tensor_tensor(out=adar[:, 2 * D:3 * D], in0=adar[:, 2 * D:3 * D], in1=grow[:], op=ALU.mult),
                  reads=[adar, grow], writes=[adar])
            P.dma("sp", lambda e: e.dma_start(out=modrows[:, col0:col0 + 3 * D], in_=adar[:]), reads=[adar], writes=[dmod],
                  sembuf=dmod)

        def load_mod(MODt, col0, kind):
            if kind < 2:
                P.dma("sp", lambda e: e.dma_start(out=MODt[:], in_=modrows[kind:kind + 1, col0:col0 + 3 * D].partition_broadcast(128)),
                      reads=[dmod], writes=[MODt], sembuf=MODt)
            else:
                for s_ in range(SB_):
                    P.dma("sp" if s_ % 2 == 0 else "pool", lambda e, s_=s_: e.dma_start(
                        out=MODt[s_ * TS:(s_ + 1) * TS, :],
                        in_=modrows[2 + s_:3 + s_, col0:col0 + 3 * D].partition_broadcast(TS)),
                        reads=[dmod], writes=[MODt], sembuf=MODt)

        def rms_rstd(src_ap, src_bufs, junk_ap, junk_buf, stb, c0, n):
            P.act(lambda e: e.activation(out=junk_ap, in_=src_ap, func=AF.Square, accum_out=stb[:, c0:c0 + 1]),
                  reads=src_bufs, writes=[stb, junk_buf])
            P.act(lambda e: e.activation(out=stb[:, c0 + 1:c0 + 2], in_=stb[:, c0:c0 + 1], func=AF.Sqrt, scale=1.0 / n, bias=epsb[:]),
                  reads=[stb, epsb], writes=[stb])
            P.dve(lambda e: e.reciprocal(out=stb[:, c0 + 1:c0 + 2], in_=stb[:, c0 + 1:c0 + 2]), reads=[stb], writes=[stb])

        def transpose8(src_bf, dstT, evac="act"):
            for kc in range(8):
                P.pe(lambda e, kc=kc: e.transpose(out=pT[:, kc * 128:(kc + 1) * 128], in_=src_bf[:, kc * 128:(kc + 1) * 128],
                                                  identity=identb[:]), reads=[src_bf, identb], writes=[pT])
            if evac == "act":
                P.act(lambda e: e.activation(out=dstT[:], in_=pT[:], func=AF.Copy), reads=[pT], writes=[dstT])
            else:
                P.dve(lambda e: e.tensor_copy(out=dstT[:], in_=pT[:]), reads=[pT], writes=[dstT])

        with ExitStack() as S1:
            sb1 = mk(S1)
            win_bf = sb1("win_bf", [128, 8, D_IN], BF16)
            wout_bf = sb1("wout_bf", [128, 8, D], BF16)
            w1b = sb1("w1b", [128, 2, 2, 8, 256], BF16)
            w2b = sb1("w2b", [128, 2, 2, 64], BF16)
            w2dup = sb1("w2dup", [128, 2, 128], BF16)
            pbias = sb1("pbias", [128, 2, 256])
            b2kt = sb1("b2kt", [128, 1]); b2vt = sb1("b2vt", [128, 64])
            wgt = sb1("wgt", [16, 256]); bgt = sb1("bgt", [1, 256])
            trit = sb1("trit", [128, 2, 2, 128]); seqi = sb1("seqi", [128, 2, 16]); seqm = sb1("seqm", [128, 16])
            causmt = sb1("causmt", [128, 2, 512], BF16)
            gnb = sb1("gnb", [128, 512])
            ckct = sb1("ckct", [4, 512], BF16)
            selpt = sb1("selpt", [128, 2, 64], BF16)
            ld("sp", ckct, ckct[:], ckc[:, :])
            ld("sp", selpt, selpt[:], selp[:, :, :])
            ld("sp", causmt, causmt[:], causm.rearrange("k p c -> p k c"))
            with ExitStack() as S1a:
                sba = mk(S1a)
                stg = [sba("stg%d" % i, [128, D_IN]) for i in range(2)]
                nst = [0]

                def ldcast(dst, dst_ap, src_ap, shape):
                    s = stg[nst[0] % 2]
                    q = "sp" if nst[0] % 2 == 0 else "pool"
                    np_, nf = shape
                    sap = s[0:np_, 0:nf]
                    ld(q, s, sap, src_ap)
                    if nst[0] % 2 == 0:
                        P.dve(lambda e: e.tensor_copy(out=dst_ap, in_=sap), reads=[s], writes=[dst])
                    else:
                        P.act(lambda e: e.activation(out=dst_ap, in_=sap, func=AF.Copy), reads=[s], writes=[dst])
                    nst[0] += 1

                for kc in range(8):
                    ldcast(win_bf, win_bf[:, kc, :], w_in[kc * 128:(kc + 1) * 128, :], (128, D_IN))
                for kc in range(8):
                    ldcast(wout_bf, wout_bf[:, kc, :], w_out[kc * 128:(kc + 1) * 128, :], (128, D))
                for ty in range(2):
                    for hf in range(2):
                        ldcast(w1b, w1b[:, ty, hf, :, :].rearrange("p a b -> p (a b)"),
                               w1p[:, ty, hf, :, :].rearrange("p a b -> p (a b)"), (128, 2048))
                ldcast(w2b, w2b[:].rearrange("p a b c -> p (a b c)"), w2l.rearrange("p a b c -> p (a b c)"), (128, 256))
                posb = sba("posb", [128, 2, 2, 8, 16], BF16)
                ldcast(posb, posb[:].rearrange("p a b c d -> p (a b c d)"), posrep.rearrange("p a b c d -> p (a b c d)"),
                       (128, 512))
                b1t = sba("b1t", [1, 2, 256]); ones16 = sba("ones16", [1, 16]); pb16 = sba("pb16", [16, 2, 256])
                ld("sp", b1t, b1t[:], b1r.rearrange("(o t) f -> o t f", o=1))
                P.dve(lambda e: e.memset(ones16[:], 1.0), writes=[ones16])
                ld("sp", b2kt, b2kt[:], b2k[:, :])
                ld("sp", b2vt, b2vt[:], b2v[0:1, :].partition_broadcast(128))
                ld("sp", wgt, wgt[:], wg[:, :]); ld("sp", bgt, bgt[:], bg[:, :])
                ld("sp", trit, trit[:], tri.rearrange("k m p t -> p k m t"))
                ld("sp", seqi, seqi[:], seqind.rearrange("k p s -> p k s"))
                ld("sp", seqm, seqm[:], seqmask[:, :])
                ld("sp", gnb, gnb[:], gnorm4[0:1, :].partition_broadcast(128))
                for fc in range(2):
                    for dup in range(2):
                        P.dve(lambda e, fc=fc, dup=dup: e.tensor_copy(out=w2dup[:, fc, dup * 64:(dup + 1) * 64],
                                                                      in_=w2b[:, 0, fc, :]), reads=[w2b], writes=[w2dup])
                for ty in range(2):
                    pbk = pb[2 + ty]
                    n = 0
                    for hf in range(2):
                        for j2 in range(8):
                            P.pe(lambda e, ty=ty, hf=hf, j2=j2, pbk=pbk, n=n: e.matmul(
                                pbk[0:16, 0:256], lhsT=posb[:, ty, hf, j2, :], rhs=w1b[:, ty, hf, j2, :],
                                start=(n == 0), stop=False), reads=[posb, w1b], writes=[pbk])
                            n += 1
                    P.pe(lambda e, ty=ty, pbk=pbk: e.matmul(pbk[0:16, 0:256], lhsT=ones16[:], rhs=b1t[:, ty, :],
                                                            start=False, stop=True), reads=[ones16, b1t], writes=[pbk])
                    P.act(lambda e, ty=ty, pbk=pbk: e.activation(out=pb16[0:16, ty, :], in_=pbk[0:16, 0:256], func=AF.Copy),
                          reads=[pbk], writes=[pb16])
                    P.pe(lambda e, ty=ty, pbk=pbk: e.matmul(pbk[:, 256:512], lhsT=ones1[0:1, :], rhs=pb16[0:1, ty, :],
                                                            start=True, stop=True), reads=[ones1, pb16], writes=[pbk])
                    P.act(lambda e, ty=ty, pbk=pbk: e.activation(out=pbias[:, ty, :], in_=pbk[:, 256:512], func=AF.Copy),
                          reads=[pbk], writes=[pbias])
                ada_rows(sba, 0, 0, 1)
                P.flush()

            MOD = sb1("MOD", [128, 3 * D])
            xt = [sb1("xt%d" % i, [128, D]) for i in range(2)]
            junkb = sb1("junkb", [128, 512], BF16)
            st4 = sb1("st4", [128, 16])
            tmpf = sb1("tmpf", [128, D])
            hbf = sb1("hbf", [128, D], BF16)
            hT = sb1("hT", [128, D], BF16)
            z = sb1("z", [128, D_IN])
            zaT = sb1("zaT", [16, 128])
            Lt = sb1("Lt", [128, 256]); Eout = sb1("Eout", [128, 256])
            EqT = sb1("EqT", [128, 256]); EkT = sb1("EkT", [128, 256])
            qinT = sb1("qinT", [128, 256], BF16)
            qz = [sb1("qz%d" % k, [128, 256], BF16) for k in range(2)]
            kz = [sb1("kz%d" % k, [128, 256], BF16) for k in range(2)]
            ATs = sb1("ATs", [128, 512], BF16)
            kout = sb1("kout", [128, 256], BF16); vbf = sb1("vbf", [128, 512], BF16)
            koutm = sb1("koutm", [128, 256], BF16)
            dec = sb1("dec", [128, 2, 16])
            Sf = sb1("Sf", [128, 2, 128]); Sbf = sb1("Sbf", [128, 2, 128], BF16)
            SX = {}
            szr = sb1("szr", [128, 512])
            obf = sb1("obf", [128, D], BF16)
            qnbf = sb1("qnbf", [128, 512], BF16); kvbf = sb1("kvbf", [128, 768], BF16)
            qTz = [sb1("qTz%d" % k, [128, 512], BF16) for k in range(2)]
            augq = [sb1("augq%d" % k, [96, 512], BF16) for k in range(2)]
            qb0 = sb1("qb0", [1, 2, 512], BF16)
            PTs = [sb1("PT%d" % i, [128, 512], BF16) for i in range(2)]
            oacc = sb1("oacc", [128, 512])
            sg = sb1("sg", [128, 24]); coef = sb1("coef", [128, 8]); rcs = sb1("rcs", [128, 8])
            impn = sb1("impn", [128, 2, 128])
            score = sb1("score", [128, 2, 128]); sc2 = EqT
            m8 = sb1("m8", [128, 16]); selpos = sb1("selpos", [128, 2, 160], BF16)
            vma = sb1("vma", [128, 2, 2, 64])
            hid = Eout; hidb = sb1("hidb", [128, 256], BF16)
            hidT = sb1("hidT", [128, 2, 128], BF16)
            vcn = sb1("vcn", [128, 64], BF16)

            for k_ in range(2):
                P.pool(lambda e, k_=k_: e.memset(qTz[k_][:], 0.0), writes=[qTz[k_]])
                P.pool(lambda e, k_=k_: e.memset(qz[k_][:], 0.0), writes=[qz[k_]])
                P.pool(lambda e, k_=k_: e.memset(kz[k_][:], 0.0), writes=[kz[k_]])
            P.pool(lambda e: e.memset(selpos[:], 0.0), writes=[selpos])
            sbank = [pb[2], pb[3], pb[4]]
            obank = [pb[5], pb[6]]
            ucount = [0]

            pending = [None]

            def unit_flush():
                if pending[0] is not None:
                    f = pending[0]
                    pending[0] = None
                    f()

            def unit(mms, nq, vT_ap, vbufs, ob, ocols, first, last, keep=None):
                sbk = sbank[ucount[0] % 3]
                pt = keep if keep is not None else PTs[ucount[0] % 2]
                ucount[0] += 1
                n = len(mms)
                for mi, (l_ap, r_ap, rd, cols) in enumerate(mms):
                    c0, cn = cols
                    P.pe(lambda e, l_ap=l_ap, r_ap=r_ap, mi=mi, c0=c0, cn=cn: e.matmul(
                        sbk[:, c0:c0 + cn], lhsT=l_ap, rhs=r_ap, start=(mi == 0), stop=(mi == n - 1)),
                        reads=rd, writes=[sbk])
                unit_flush()
                P.act(lambda e: e.activation(out=pt[:, 0:nq], in_=sbk[:, 0:nq], func=AF.Exp), reads=[sbk], writes=[pt])

                def pv():
                    P.pe(lambda e: e.matmul(ob[0:65, ocols[0]:ocols[0] + nq], lhsT=vT_ap, rhs=pt[:, 0:nq], start=first, stop=last),
                         reads=[pt] + vbufs, writes=[ob])
                pending[0] = pv
                return pt

            def branch_epilogue(br, samp, first_branch):
                unit_flush()
                for kvh in range(2):
                    if not samp:
                        P.act(lambda e, kvh=kvh: e.activation(out=tmpf[0:65, kvh * 512:(kvh + 1) * 512], in_=obank[kvh][0:65, :],
                                                              func=AF.Copy), reads=[obank[kvh]], writes=[tmpf])
                for kvh in range(2):
                    ob = pb[kvh]
                    for g in range(4):
                        src = tmpf[0:65, kvh * 512 + g * 128:kvh * 512 + (g + 1) * 128]
                        P.pe(lambda e, src=src, g=g, ob=ob: e.transpose(out=ob[:, g * 65:(g + 1) * 65], in_=src,
                                                                         identity=identf[0:65, 0:65]),
                             reads=[tmpf, identf], writes=[ob])
                    ov = ob[:, 0:260].rearrange("p (g c) -> p g c", g=4)
                    P.dve(lambda e, ov=ov, kvh=kvh: e.tensor_scalar(out=rcs[:, kvh * 4:(kvh + 1) * 4], in0=ov[:, :, 64],
                                                                    scalar1=1e-30, scalar2=None, op0=ALU.max),
                          reads=[ob], writes=[rcs])
                P.dve(lambda e: e.reciprocal(out=rcs[:], in_=rcs[:]), reads=[rcs], writes=[rcs])
                sgv = sg[:].rearrange("p (h b) -> p h b", b=3)
                P.dve(lambda e: e.tensor_tensor(out=coef[:], in0=rcs[:], in1=sgv[:, :, br], op=ALU.mult),
                      reads=[rcs, sg], writes=[coef])
                for kvh in range(2):
                    ob = pb[kvh]
                    for g in range(4):
                        h = kvh * 4 + g
                        if first_branch:
                            P.dve(lambda e, h=h, g=g, ob=ob: e.tensor_scalar(out=oacc[:, h * 64:(h + 1) * 64],
                                                                             in0=ob[:, g * 65:g * 65 + 64],
                                                                             scalar1=coef[:, h:h + 1], scalar2=None, op0=ALU.mult),
                                  reads=[ob, coef], writes=[oacc])
                        else:
                            P.dve(lambda e, h=h, g=g, ob=ob: e.scalar_tensor_tensor(
                                out=oacc[:, h * 64:(h + 1) * 64], in0=ob[:, g * 65:g * 65 + 64], scalar=coef[:, h:h + 1],
                                in1=oacc[:, h * 64:(h + 1) * 64], op0=ALU.mult, op1=ALU.add),
                                reads=[ob, coef, oacc], writes=[oacc])

            def select_blocks(ns, vm_ap, am_ap, vmbufs, kth, samp):
                for kvh in range(2):
                    ob = obank[kvh]
                    for g in range(4):
                        src = z[0:ns, kvh * 512 + g * 128:kvh * 512 + (g + 1) * 128]
                        P.pe(lambda e, src=src, g=g, ob=ob: e.transpose(out=ob[:, g * 128:g * 128 + ns], in_=src,
                                                                         identity=identf[0:ns, 0:ns]),
                             reads=[z, identf], writes=[ob])
                    for g in range(4):
                        h = kvh * 4 + g
                        if g == 0:
                            P.dve(lambda e, h=h, g=g, ob=ob, kvh=kvh: e.tensor_scalar(
                                out=impn[:, kvh, 0:ns], in0=ob[:, g * 128:g * 128 + ns], scalar1=rcs[:, h:h + 1], scalar2=None,
                                op0=ALU.mult), reads=[ob, rcs], writes=[impn])
                        else:
                            P.dve(lambda e, h=h, g=g, ob=ob, kvh=kvh: e.scalar_tensor_tensor(
                                out=impn[:, kvh, 0:ns], in0=ob[:, g * 128:g * 128 + ns], scalar=rcs[:, h:h + 1],
                                in1=impn[:, kvh, 0:ns], op0=ALU.mult, op1=ALU.add), reads=[ob, rcs, impn], writes=[impn])
                for kvh in range(2):
                    P.dve(lambda e, kvh=kvh: e.tensor_tensor(out=score[:, kvh, 0:ns], in0=impn[:, kvh, 0:ns], in1=vm_ap(kvh),
                                                             op=ALU.mult), reads=[impn] + vmbufs, writes=[score])
                    P.dve(lambda e, kvh=kvh: e.tensor_tensor(out=score[:, kvh, 0:ns], in0=score[:, kvh, 0:ns], in1=am_ap(kvh),
                                                             op=ALU.add), reads=[score] + vmbufs, writes=[score])
                    P.dve(lambda e, kvh=kvh: e.max(out=m8[:, 0:8], in_=score[:, kvh, 0:ns]), reads=[score], writes=[m8])
                    P.dve(lambda e, kvh=kvh: e.match_replace(out=sc2[:, 0:ns], in_to_replace=m8[:, 0:8],
                                                             in_values=score[:, kvh, 0:ns], imm_value=-3.0e4),
                          reads=[score, m8], writes=[sc2])
                    P.dve(lambda e: e.max(out=m8[:, 8:16], in_=sc2[:, 0:ns]), reads=[sc2], writes=[m8])
                    P.dve(lambda e, kvh=kvh: e.tensor_scalar(out=selpos[:, kvh, 32:32 + ns], in0=score[:, kvh, 0:ns],
                                                             scalar1=m8[:, 8 + kth:9 + kth], scalar2=NEG, op0=ALU.is_ge,
                                                             op1=ALU.mult), reads=[score, m8], writes=[selpos])

            def build_augq(aq, kvh, blk0, samp):
                P.pe(lambda e: e.transpose(out=pT[0:96, 0:128], in_=selpos[:, kvh, blk0:blk0 + 96], identity=identb[:]),
                     reads=[selpos, identb], writes=[pT])
                for (p0, p1) in ((32, 64), (64, 96)):
                    if not samp:
                        for g in range(4):
                            P.dve(lambda e, g=g, p0=p0, p1=p1: e.tensor_copy(out=aq[p0:p1, g * 128:(g + 1) * 128], in_=pT[p0:p1, 0:128]),
                                  reads=[pT], writes=[aq])
                    else:
                        av = aq[p0:p1, :].rearrange("p (s g q) -> p s g q", s=16, g=4)
                        pv = pT[p0:p1, 0:128].rearrange("p (s q) -> p s q", s=16)
                        for g in range(4):
                            P.dve(lambda e, g=g, av=av, pv=pv: e.tensor_copy(out=av[:, :, g, :], in_=pv), reads=[pT], writes=[aq])

            def compress(Y, nb, kcT, kc_col0, vcs, c0, skip_first):
                R1 = 32 if nb <= 32 else 64
                NR = R1 + nb
                for ty in range(2):
                    pbk = pb[ty]
                    for kvh in range(2):
                        n = 0
                        for hf in range(2):
                            for j2 in range(8):
                                base = 8 * hf + j2
                                lap = Y[:, ty, kvh, base:base + 8 * (nb - 1) + 1:8]
                                P.pe(lambda e, lap=lap, ty=ty, hf=hf, j2=j2, pbk=pbk, n=n, kvh=kvh: e.matmul(
                                    pbk[kvh * R1:kvh * R1 + nb, 0:256], lhsT=lap, rhs=w1b[:, ty, hf, j2, :],
                                    start=(n == 0), stop=(n == 15)), reads=[Y, w1b], writes=[pbk])
                                n += 1
                    P.dve(lambda e, ty=ty, pbk=pbk: e.tensor_tensor(out=hid[0:NR, :], in0=pbk[0:NR, 0:256],
                                                                    in1=pbias[0:NR, ty, :], op=ALU.add),
                          reads=[pbk, pbias], writes=[hid])
                    P.act(lambda e: e.activation(out=hidb[0:NR, :], in_=hid[0:NR, :], func=AF.Silu),
                          reads=[hid], writes=[hidb])
                    for fc in range(2):
                        P.pe(lambda e, fc=fc: e.transpose(out=pT[:, fc * 128:fc * 128 + NR],
                                                          in_=hidb[0:NR, fc * 128:(fc + 1) * 128],
                                                          identity=identb[0:NR, 0:NR]),
                             reads=[hidb, identb], writes=[pT])
                    P.act(lambda e: e.activation(out=hidT[:, :, 0:NR],
                                                 in_=pT[:, 0:256].rearrange("p (a b) -> p a b", a=2)[:, :, 0:NR],
                                                 func=AF.Copy), reads=[pT], writes=[hidT])
                    if ty == 0:
                        pk = pb[1]
                        for fc in range(2):
                            P.pe(lambda e, fc=fc, pk=pk: e.matmul(pk[:, 256:256 + NR], lhsT=w2dup[:, fc, :],
                                                                   rhs=hidT[:, fc, 0:NR], start=(fc == 0), stop=(fc == 1)),
                                 reads=[w2dup, hidT], writes=[pk])
                        for kvh in range(2):
                            r = slice(kvh * 64, kvh * 64 + 64)
                            P.act(lambda e, kvh=kvh, r=r, pk=pk: e.activation(
                                out=kcT[r, kc_col0:kc_col0 + nb], in_=pk[r, 256 + kvh * R1:256 + kvh * R1 + nb],
                                func=AF.Identity, bias=b2kt[r, 0:1]), reads=[pk, b2kt], writes=[kcT])
                    else:
                        pk = pb[0]
                        for fc in range(2):
                            P.pe(lambda e, fc=fc, pk=pk: e.matmul(pk[0:NR, 256:320], lhsT=hidT[:, fc, 0:NR],
                                                                   rhs=w2b[:, 1, fc, :], start=(fc == 0), stop=(fc == 1)),
                                 reads=[w2b, hidT], writes=[pk])
                        P.dve(lambda e, pk=pk: e.tensor_tensor(out=vcn[0:NR, :], in0=pk[0:NR, 256:320],
                                                               in1=b2vt[0:NR, :], op=ALU.add), reads=[pk, b2vt], writes=[vcn])
                        for kvh in range(2):
                            r0 = 1 if skip_first else 0
                            c = c0 + r0
                            while c < c0 + nb:
                                ce = min(c0 + nb, (c // 128 + 1) * 128)
                                P.dma("sp", lambda e, kvh=kvh, c=c, ce=ce: e.dma_start(
                                    out=vcs[c % 128:c % 128 + (ce - c), c // 128, kvh, 0:64],
                                    in_=vcn[kvh * R1 + (c - c0):kvh * R1 + (ce - c0), :]), reads=[vcn], writes=[vcs], sembuf=vcs)
                                c = ce

            def front(ti, x, kind):
                ld("sp", x, x[:], xs[ti * 128:(ti + 1) * 128, :])
                rms_rstd(x[:], [x], hbf[:], hbf, st4, 0, D)
                P.dve(lambda e: e.scalar_tensor_tensor(out=tmpf[:], in0=x[:], scalar=st4[:, 1:2], in1=MOD[:, D:2 * D],
                                                       op0=ALU.mult, op1=ALU.mult), reads=[x, st4, MOD], writes=[tmpf])
                P.dve(lambda e: e.tensor_tensor(out=hbf[:], in0=tmpf[:], in1=MOD[:, 0:D], op=ALU.add),
                      reads=[tmpf, MOD], writes=[hbf])
                transpose8(hbf, hT)
                zch = [(0, 512), (512, 512), (1024, 512), (1536, 512), (2048, 512), (2560, 296)]
                for ci, (c0, cn) in enumerate(zch):
                    pbk = pb[ci % 2]
                    for kc in range(8):
                        P.pe(lambda e, kc=kc, pbk=pbk, c0=c0, cn=cn: e.matmul(
                            pbk[:, 0:cn], lhsT=hT[:, kc * 128:(kc + 1) * 128], rhs=win_bf[:, kc, c0:c0 + cn],
                            start=(kc == 0), stop=(kc == 7)), reads=[hT, win_bf], writes=[pbk])
                    if ci % 2 == 0:
                        P.dve(lambda e, pbk=pbk, c0=c0, cn=cn: e.tensor_copy(out=z[:, c0:c0 + cn], in_=pbk[:, 0:cn]),
                              reads=[pbk], writes=[z])
                    else:
                        P.act(lambda e, pbk=pbk, c0=c0, cn=cn: e.activation(out=z[:, c0:c0 + cn], in_=pbk[:, 0:cn], func=AF.Copy),
                              reads=[pbk], writes=[z])
                P.dma("sp", lambda e: e.dma_start(out=kvr[ti * 128:(ti + 1) * 128, :], in_=z[:, C_KV:C_KV + 512]),
                      reads=[z], writes=[dkvr], sembuf=dkvr)
                P.act(lambda e: e.activation(out=qnbf[:].rearrange("p (g k d) -> p k g d", g=4, k=2),
                                             in_=z[:, C_QN:C_QN + 512].rearrange("p (k g d) -> p k g d", k=2, g=4),
                                             func=AF.Copy, scale=0.125), reads=[z], writes=[qnbf])
                P.pool(lambda e: e.tensor_copy(out=kvbf[:], in_=z[:, C_KV:C_KV + 768]), reads=[z], writes=[kvbf])
                P.act(lambda e: e.activation(out=sg[:], in_=z[:, C_ZG:C_ZG + 24], func=AF.Sigmoid), reads=[z], writes=[sg])
                for g in range(4):
                    P.pe(lambda e, g=g: e.transpose(out=pT[:, g * 128:(g + 1) * 128], in_=qnbf[:, g * 128:(g + 1) * 128], identity=identb[:]),
                         reads=[qnbf, identb], writes=[pT])
                for kvh in range(2):
                    r = slice(kvh * 64, kvh * 64 + 64)
                    P.dve(lambda e, kvh=kvh, r=r: e.tensor_copy(out=qTz[kvh][r, :], in_=pT[r, 0:512]), reads=[pT], writes=[qTz[kvh]])

            def gla(sample, it, kind):
                tk = 1 if sample else 0
                pa, pb_, pc, pd = pb[2], pb[3], pb[4], pb[5]
                P.pe(lambda e: e.transpose(out=pa[0:16, 0:128], in_=z[:, C_ZA:C_ZA + 16], identity=identf[:]),
                     reads=[z, identf], writes=[pa])
                P.act(lambda e: e.activation(out=zaT[:], in_=pa[0:16, 0:128], func=AF.Copy), reads=[pa], writes=[zaT])
                P.pe(lambda e: e.matmul(pb_[:, 0:256], lhsT=zaT[:], rhs=wgt[:], start=True, stop=False),
                     reads=[zaT, wgt], writes=[pb_])
                P.pe(lambda e: e.matmul(pb_[:, 0:256], lhsT=ones1[:], rhs=bgt[:], start=False, stop=True),
                     reads=[ones1, bgt], writes=[pb_])
                P.act(lambda e: e.activation(out=Lt[:], in_=pb_[:, 0:256], func=AF.Exp, scale=-1.0), reads=[pb_], writes=[Lt])
                P.act(lambda e: e.activation(out=Lt[:], in_=Lt[:], func=AF.Ln, bias=1.0), reads=[Lt], writes=[Lt])
                P.pe(lambda e: e.matmul(pc[:, 0:256], lhsT=trit[:, tk, 0, :], rhs=Lt[:], start=True, stop=True),
                     reads=[trit, Lt], writes=[pc])
                for fc in range(2):
                    P.pe(lambda e, fc=fc: e.matmul(pd[:, fc * 16:fc * 16 + 16], lhsT=Lt[:, fc * 128:(fc + 1) * 128],
                                                   rhs=seqi[:, tk, :], start=True, stop=True), reads=[Lt, seqi], writes=[pd])
                    P.pe(lambda e, fc=fc: e.matmul(pd[:, 128 + fc * 128:256 + fc * 128], lhsT=Lt[:, fc * 128:(fc + 1) * 128],
                                                   rhs=trit[:, tk, 1, :], start=True, stop=True), reads=[Lt, trit], writes=[pd])
                P.act(lambda e: e.activation(out=Eout[:], in_=pc[:, 0:256], func=AF.Exp), reads=[pc], writes=[Eout])
                P.act(lambda e: e.activation(out=dec[:].rearrange("p a b -> p (a b)"), in_=pd[:, 0:32], func=AF.Exp),
                      reads=[pd], writes=[dec])
                P.act(lambda e: e.activation(out=EqT[:], in_=pd[:, 128:384], func=AF.Exp), reads=[pd], writes=[EqT])
                P.act(lambda e: e.activation(out=EkT[:], in_=pd[:, 128:384], func=AF.Exp, scale=-1.0), reads=[pd], writes=[EkT])
                P.dve(lambda e: e.tensor_tensor(out=kout[:], in0=z[:, C_ZK:C_ZK + 256], in1=Eout[:], op=ALU.mult),
                      reads=[z, Eout], writes=[kout])
                P.pool(lambda e: e.tensor_copy(out=vbf[:], in_=z[:, C_ZV:C_ZV + 512]), reads=[z], writes=[vbf])
                for a in range(4):
                    c0 = (C_ZQ if a < 2 else C_ZK) + (a % 2) * 128
                    P.pe(lambda e, a=a, c0=c0: e.transpose(out=pa[:, a * 128:(a + 1) * 128], in_=z[:, c0:c0 + 128],
                                                           identity=identf[:]), reads=[z, identf], writes=[pa])
                P.dve(lambda e: e.scalar_tensor_tensor(out=qinT[:], in0=pa[:, 0:256], scalar=0.125, in1=EqT[:],
                                                       op0=ALU.mult, op1=ALU.mult), reads=[pa, EqT], writes=[qinT])
                for hh in range(2):
                    r = slice(hh * 64, hh * 64 + 64)
                    P.dve(lambda e, hh=hh, r=r: e.tensor_tensor(out=kz[hh][r, :], in0=pa[r, 256:512], in1=EkT[r, :], op=ALU.mult),
                          reads=[pa, EkT], writes=[kz[hh]])
                    P.pool(lambda e, hh=hh, r=r: e.tensor_copy(out=qz[hh][r, :], in_=qinT[r, :]), reads=[qinT], writes=[qz[hh]])
                for h in range(4):
                    r = slice((h % 2) * 64, (h % 2) * 64 + 64)
                    fc = h // 2
                    P.pe(lambda e, h=h, r=r, fc=fc: e.matmul(pb_[:, h * 128:(h + 1) * 128], lhsT=kz[h % 2][:, fc * 128:(fc + 1) * 128],
                                                             rhs=qinT[:, fc * 128:(fc + 1) * 128], start=True, stop=True),
                         reads=[kz[h % 2], qinT], writes=[pb_])
                P.dve(lambda e: e.tensor_tensor(out=ATs[:], in0=pb_[:, :], in1=causmt[:, tk, :], op=ALU.mult),
                      reads=[pb_, causmt], writes=[ATs])
                po = pc
                if not sample:
                    if it == 0:
                        P.dve(lambda e: e.memset(Sf[:], 0.0), writes=[Sf])
                        P.dve(lambda e: e.memset(Sbf[:], 0.0), writes=[Sbf])
                    for h in range(4):
                        r = slice((h % 2) * 64, (h % 2) * 64 + 64)
                        fc = h // 2
                        P.pe(lambda e, h=h, r=r, fc=fc: e.matmul(po[:, h * 128:(h + 1) * 128], lhsT=qz[h % 2][:, fc * 128:(fc + 1) * 128],
                                                                 rhs=Sbf[:, fc, :], start=True, stop=False),
                             reads=[qz[h % 2], Sbf], writes=[po])
                        P.pe(lambda e, h=h: e.matmul(po[:, h * 128:(h + 1) * 128], lhsT=ATs[:, h * 128:(h + 1) * 128],
                                                     rhs=vbf[:, h * 128:(h + 1) * 128], start=False, stop=True),
                             reads=[ATs, vbf], writes=[po])
                else:
                    for s in range(SB_):
                        S = SX["Ssm"][s % 2]
                        ld("sp", S, S[:], sgla[s].rearrange("(fc hh) k v -> (hh k) fc v", hh=2))
                        P.dve(lambda e, s=s, S=S: e.tensor_copy(out=SX["S0bf"][:, s, :, :], in_=S[:]), reads=[S], writes=[SX["S0bf"]])
                    for h in range(4):
                        r = slice((h % 2) * 64, (h % 2) * 64 + 64)
                        fc = h // 2
                        P.pe(lambda e, h=h: e.matmul(po[:, h * 128:(h + 1) * 128], lhsT=vbf[:, h * 128:(h + 1) * 128],
                                                     rhs=ATs[:, h * 128:(h + 1) * 128], start=True, stop=False),
                             reads=[ATs, vbf], writes=[po])
                        for s in range(SB_):
                            P.pe(lambda e, h=h, r=r, fc=fc, s=s: e.matmul(
                                po[:, h * 128 + s * TS:h * 128 + (s + 1) * TS], lhsT=SX["S0bf"][:, s, fc, :],
                                rhs=qz[h % 2][:, fc * 128 + s * TS:fc * 128 + (s + 1) * TS], start=False, stop=(s == SB_ - 1)),
                                reads=[SX["S0bf"], qz[h % 2]], writes=[po])
                    P.act(lambda e: e.activation(out=SX["ogT"][:, 0:512], in_=po[:, :], func=AF.Copy), reads=[po], writes=[SX["ogT"]])
                    for h in range(4):
                        P.pe(lambda e, h=h: e.transpose(out=po[:, h * 128:(h + 1) * 128], in_=SX["ogT"][:, h * 128:(h + 1) * 128],
                                                        identity=identf[:]), reads=[SX["ogT"], identf], writes=[po])
                for h in range(4):
                    P.act(lambda e, h=h: e.activation(out=junkb[:, 0:128], in_=po[:, h * 128:(h + 1) * 128], func=AF.Square,
                                                      accum_out=st4[:, 4 + h:5 + h]), reads=[po], writes=[st4, junkb])
                P.act(lambda e: e.activation(out=st4[:, 8:12], in_=st4[:, 4:8], func=AF.Sqrt, scale=1.0 / 128, bias=epsb[:]),
                      reads=[st4, epsb], writes=[st4])
                P.dve(lambda e: e.reciprocal(out=st4[:, 8:12], in_=st4[:, 8:12]), reads=[st4], writes=[st4])
                P.act(lambda e: e.activation(out=szr[:], in_=z[:, C_ZR:C_ZR + 512], func=AF.Sigmoid), reads=[z], writes=[szr])
                P.dve(lambda e: e.tensor_tensor(out=szr[:], in0=szr[:], in1=z[:, C_ZR:C_ZR + 512], op=ALU.mult),
                      reads=[szr, z], writes=[szr])
                P.pool(lambda e: e.tensor_tensor(out=szr[:], in0=szr[:], in1=gnb[:], op=ALU.mult), reads=[szr, gnb], writes=[szr])
                for h in range(4):
                    P.dve(lambda e, h=h: e.scalar_tensor_tensor(out=obf[:, h * 128:(h + 1) * 128], in0=po[:, h * 128:(h + 1) * 128],
                                                                scalar=st4[:, 8 + h:9 + h], in1=szr[:, h * 128:(h + 1) * 128],
                                                                op0=ALU.mult, op1=ALU.mult), reads=[po, st4, szr], writes=[obf])

                def state_update(S, kl, si, pu):
                    for fc in range(2):
                        P.pe(lambda e, fc=fc: e.matmul(pu[:, fc * 256:(fc + 1) * 256], lhsT=kl[:, fc * 128:(fc + 1) * 128],
                                                       rhs=vbf[:, fc * 256:(fc + 1) * 256], start=True, stop=True),
                             reads=[kl, vbf], writes=[pu])
                    for fc in range(2):
                        for hh in range(2):
                            r = slice(hh * 64, hh * 64 + 64)
                            P.dve(lambda e, fc=fc, hh=hh, r=r: e.scalar_tensor_tensor(
                                out=S[r, fc, :], in0=S[r, fc, :], scalar=dec[r, fc, si:si + 1],
                                in1=pu[r, fc * 256 + hh * 128:fc * 256 + hh * 128 + 128], op0=ALU.mult, op1=ALU.add),
                                reads=[S, dec, pu], writes=[S])

                if not sample:
                    state_update(Sf, kout, 0, pb[6])
                    P.pool(lambda e: e.tensor_copy(out=Sbf[:], in_=Sf[:]), reads=[Sf], writes=[Sbf])
                    if it == NT_P - 1:
                        P.dma("sp", lambda e: e.dma_start(out=gla_p[kind].rearrange("(fc hh) k v -> (hh k) fc v", hh=2),
                                                          in_=Sf[:]), reads=[Sf], writes=[dglap], sembuf=dglap)
                else:
                    for s in range(SB_):
                        S = SX["Ssm"][s % 2]
                        ld("sp", S, S[:], sgla[s].rearrange("(fc hh) k v -> (hh k) fc v", hh=2))
                        P.dve(lambda e, s=s: e.tensor_scalar(out=koutm[:], in0=kout[:], scalar1=seqm[:, s:s + 1], scalar2=None,
                                                             op0=ALU.mult), reads=[kout, seqm], writes=[koutm])
                        state_update(S, koutm, s, pb[5 + (s % 2)])
                        P.dma("sp", lambda e, s=s, S=S: e.dma_start(
                            out=gla_s[s].rearrange("(fc hh) k v -> (hh k) fc v", hh=2), in_=S[:]),
                            reads=[S], writes=[dglas], sembuf=dglas)

            def back(ti, x):
                P.dve(lambda e: e.tensor_copy(out=obf[:, 512:1024], in_=oacc[:]), reads=[oacc], writes=[obf])
                transpose8(obf, hT, evac="dve")
                for nh in range(2):
                    pbk = pb[nh]
                    for kc in range(8):
                        P.pe(lambda e, kc=kc, pbk=pbk, nh=nh: e.matmul(pbk[:, :], lhsT=hT[:, kc * 128:(kc + 1) * 128],
                                                                       rhs=wout_bf[:, kc, nh * 512:(nh + 1) * 512],
                                                                       start=(kc == 0), stop=(kc == 7)),
                             reads=[hT, wout_bf], writes=[pbk])
                    P.act(lambda e, pbk=pbk, nh=nh: e.activation(out=junkb[:, 0:512], in_=pbk[:, :], func=AF.Square,
                                                                 accum_out=st4[:, 12 + nh:13 + nh]), reads=[pbk], writes=[st4, junkb])
                P.dve(lambda e: e.tensor_tensor(out=st4[:, 14:15], in0=st4[:, 12:13], in1=st4[:, 13:14], op=ALU.add),
                      reads=[st4], writes=[st4])
                P.act(lambda e: e.activation(out=st4[:, 15:16], in_=st4[:, 14:15], func=AF.Sqrt, scale=1.0 / D, bias=epsb[:]),
                      reads=[st4, epsb], writes=[st4])
                P.dve(lambda e: e.reciprocal(out=st4[:, 15:16], in_=st4[:, 15:16]), reads=[st4], writes=[st4])
                for nh in range(2):
                    pbk = pb[nh]
                    P.dve(lambda e, pbk=pbk, nh=nh: e.scalar_tensor_tensor(
                        out=tmpf[:, nh * 512:(nh + 1) * 512], in0=pbk[:, :], scalar=st4[:, 15:16],
                        in1=MOD[:, 2 * D + nh * 512:2 * D + (nh + 1) * 512], op0=ALU.mult, op1=ALU.mult),
                        reads=[pbk, st4, MOD], writes=[tmpf])
                P.pool(lambda e: e.tensor_tensor(out=tmpf[:], in0=tmpf[:], in1=x[:], op=ALU.add), reads=[tmpf, x], writes=[tmpf])
                P.dma("sp", lambda e: e.dma_start(out=y_all[ti * 128:(ti + 1) * 128, :], in_=tmpf[:]),
                      reads=[tmpf], writes=[dy], sembuf=dy)

            for kind in range(CFG["kinds"]):
                with ExitStack() as SP:
                    sbp = mk(SP, "_p%d" % kind)
                    kslcT = sbp("kslcT", [128, SEQ], BF16); kwinT = sbp("kwinT", [128, 8 * 128], BF16)
                    Yp = sbp("Yp", [128, 2, 2, 8 + 64], BF16)
                    vslc = sbp("vslc", [128, NT_P, 2, 65], BF16); vwin = sbp("vwin", [128, 8, 2, 65], BF16)
                    ckt = sbp("ckt", [96, SEQ], BF16)
                    maskct = sbp("maskct", [128, 17, 128], BF16); mmpt = sbp("mmpt", [128, 2, 64], BF16)
                    caus4t = sbp("caus4t", [128, 2, 512], BF16)
                    ld("sp", ckt, ckt[:], ck[:, 0:SEQ])
                    ld("sp", maskct, maskct[:], maskc.rearrange("r p c -> p r c"))
                    ld("sp", mmpt, mmpt[:], mmap_p[:, :, :])
                    ld("sp", caus4t, caus4t[:], caus4.rearrange("k p c -> p k c"))
                    kcT = sbp("kcT", [128, 264], BF16); vcs = sbp("vcs", [128, 2, 2, 65], BF16)
                    PTc = [sbp("PTc%d" % i, [128, 512], BF16) for i in range(4)]
                    load_mod(MOD, 0, kind)
                    for kvh in range(2):
                        P.pool(lambda e, kvh=kvh: e.memset(augq[kvh][:], 0.0), writes=[augq[kvh]])
                        ld("sp", augq[kvh], augq[kvh][0:5, :], qab[0, kvh])
                        ld("sp", qb0, qb0[0:1, kvh, :], qab[0, kvh, 0:1, :])
                    P.pool(lambda e: e.memset(Yp[:], 0.0), writes=[Yp])
                    P.pool(lambda e: e.memset(kcT[:], 0.0), writes=[kcT])
                    P.pool(lambda e: e.memset(vcs[:], 0.0), writes=[vcs])
                    P.pool(lambda e: e.memset(vcs[:, :, :, 64:65], 1.0), writes=[vcs])
                    P.pool(lambda e: e.memset(vslc[:, :, :, 64:65], 1.0), writes=[vslc])
                    P.pool(lambda e: e.memset(vwin[:, :, :, 64:65], 1.0), writes=[vwin])
                    for it in range(CFG["n_ptiles"]):
                      try:
                          ti = kind * NT_P + it
                          x = xt[ti % 2]
                          front(ti, x, kind)
                          if it >= NT_P - 4:
                              r0 = (it - (NT_P - 4)) * 128
                              P.dma("sp", lambda e, r0=r0: e.dma_start(out=win_p[kind, r0:r0 + 128, :],
                                                                       in_=z[:, C_KV + 512:C_KV + 768]),
                                    reads=[z], writes=[dwinp], sembuf=dwinp)
                          chk(1)
                          gla(False, it, kind)
                          chk(2)
                          t0 = it * 128
                          P.pe(lambda e: e.transpose(out=pT[:, 512:640], in_=kvbf[:, 256:384], identity=identb[:]),
                               reads=[kvbf, identb], writes=[pT])
                          P.pe(lambda e: e.transpose(out=pT[:, 640:768], in_=kvbf[:, 512:640], identity=identb[:]),
                               reads=[kvbf, identb], writes=[pT])
                          P.dve(lambda e, t0=t0: e.tensor_copy(out=kslcT[:, t0:t0 + 128], in_=pT[:, 512:640]), reads=[pT], writes=[kslcT])
                          P.dve(lambda e, it=it: e.tensor_copy(out=kwinT[:, (it % 8) * 128:(it % 8 + 1) * 128], in_=pT[:, 640:768]), reads=[pT], writes=[kwinT])
                          P.pool(lambda e, it=it: e.tensor_copy(out=vslc[:, it, :, 0:64],
                                                                in_=kvbf[:, 384:512].rearrange("p (k d) -> p k d", k=2)),
                                 reads=[kvbf], writes=[vslc])
                          P.pool(lambda e, it=it: e.tensor_copy(out=vwin[:, it % 8, :, 0:64],
                                                                in_=kvbf[:, 640:768].rearrange("p (k d) -> p k d", k=2)),
                                 reads=[kvbf], writes=[vwin])
                          pY = pb[0]
                          for ty in range(2):
                              for kvh in range(2):
                                  for r in range(2):
                                      c0 = ty * 128 + kvh * 64
                                      P.pe(lambda e, ty=ty, kvh=kvh, r=r, c0=c0: e.matmul(
                                          pY[r * 64:(r + 1) * 64, (ty * 2 + kvh) * 64:(ty * 2 + kvh + 1) * 64],
                                          lhsT=kvbf[:, c0:c0 + 64], rhs=selpt[:, r, :], start=True, stop=True),
                                          reads=[kvbf, selpt], writes=[pY])
                          if it > 0:
                              P.dve(lambda e: e.tensor_copy(out=Yp[:, :, :, 0:8], in_=Yp[:, :, :, 64:72]), reads=[Yp], writes=[Yp])
                          P.act(lambda e, it=it: e.activation(out=Yp[:, :, :, 8:72],
                                                              in_=pY[:, 0:256].rearrange("p (a b c) -> p a b c", a=2, b=2),
                                                              func=AF.Copy), reads=[pY], writes=[Yp])
                          chk(3)
                          cb0 = 8 * it - 1
                          compress(Yp, 8, kcT, 1 + cb0, vcs, cb0, it == 0)
                          for kvh in range(2):
                              P.dve(lambda e, kvh=kvh, it=it: e.tensor_scalar(out=augq[kvh][0:1, :], in0=qb0[0:1, kvh, :],
                                                                              scalar1=float(it), scalar2=None, op0=ALU.mult),
                                    reads=[qb0], writes=[augq[kvh]])
                          ld("pool", vma, vma[:], vmam_p[it])
                          chk(4)
                          jl = (8 * it + 6) // 128
                          for kvh in range(2):
                              r = slice(kvh * 64, kvh * 64 + 64)
                              for jc in range(jl + 1):
                                  mms = [(kcT[:, 1 + jc * 128:1 + (jc + 1) * 128], qTz[kvh][:, :], [kcT, qTz[kvh]], (0, 512)),
                                         (ckct[0:4, jc * 128:(jc + 1) * 128], augq[kvh][0:4, :], [ckct, augq[kvh]], (0, 512))]
                                  rr = it - 16 * jc
                                  if rr <= 16:
                                      for g in range(4):
                                          mms.append((identb[:], maskct[:, rr, :], [identb, maskct], (g * 128, 128)))
                                  unit(mms, 512, vcs[:, jc, kvh, :], [vcs], obank[kvh], (0, 512), jc == 0, jc == jl,
                                       keep=PTc[kvh * 2 + jc])
                          for kvh in range(2):
                              pim = pb[kvh]
                              for jc in range(jl + 1):
                                  P.pe(lambda e, kvh=kvh, jc=jc, pim=pim: e.matmul(pim[0:64, :], lhsT=mmpt[:, jc, :],
                                                                                   rhs=PTc[kvh * 2 + jc][:, :], start=(jc == 0),
                                                                                   stop=(jc == jl)),
                                       reads=[mmpt, PTc[kvh * 2 + jc]], writes=[pim])
                              P.act(lambda e, kvh=kvh, pim=pim: e.activation(out=z[0:64, kvh * 512:(kvh + 1) * 512], in_=pim[0:64, :], func=AF.Copy),
                                    reads=[pim], writes=[z])
                          chk(5)
                          branch_epilogue(0, False, True)
                          chk(6)
                          select_blocks(64, lambda kvh: vma[:, 0, kvh, :], lambda kvh: vma[:, 1, kvh, :], [vma], 7, False)
                          for kvh in range(2):
                              build_augq(augq[kvh], kvh, 0, False)
                          chk(7)
                          for kvh in range(2):
                              r = slice(kvh * 64, kvh * 64 + 64)
                              for j in range(it + 1):
                                  mms = [(kslcT[:, j * 128:(j + 1) * 128], qTz[kvh][:, :], [kslcT, qTz[kvh]], (0, 512)),
                                         (ckt[0:96, j * 128:(j + 1) * 128], augq[kvh][0:96, :], [ckt, augq[kvh]], (0, 512))]
                                  if j == it:
                                      mms.append((identb[:], caus4t[:, 0, :], [identb, caus4t], (0, 512)))
                                  unit(mms, 512, vslc[:, j, kvh, :], [vslc], obank[kvh], (0, 512), j == 0, j == it)
                          branch_epilogue(1, False, False)
                          chk(8)
                          j0 = max(0, it - 4)
                          for kvh in range(2):
                              r = slice(kvh * 64, kvh * 64 + 64)
                              for j in range(j0, it + 1):
                                  mms = [(kwinT[:, (j % 8) * 128:(j % 8 + 1) * 128], qTz[kvh][:, :], [kwinT, qTz[kvh]], (0, 512)),
                                         (ckt[0:4, j * 128:(j + 1) * 128], augq[kvh][0:4, :], [ckt, augq[kvh]], (0, 512))]
                                  if j == it:
                                      mms.append((identb[:], caus4t[:, 0, :], [identb, caus4t], (0, 512)))
                                  if j == it - 4:
                                      mms.append((identb[:], caus4t[:, 1, :], [identb, caus4t], (0, 512)))
                                  unit(mms, 512, vwin[:, j % 8, kvh, :], [vwin], obank[kvh], (0, 512), j == j0, j == it)
                          branch_epilogue(2, False, False)
                          chk(9)
                          back(ti, x)
                      except _Stop:
                          pass
                    P.flush()

            with ExitStack() as SS:
              if CFG["sample"]:
                sbs = mk(SS)
                Ys = sbs("Ys", [128, 2, 2, 8 + 512], BF16)
                kcTs = sbs("kcTs", [128, 520], BF16); vcss = sbs("vcss", [128, 4, 2, 65], BF16)
                ckt = sbs("ckt", [96, NKCOL], BF16)
                maskcst = sbs("maskcst", [128, 32], BF16); mmst = sbs("mmst", [128, 4, 128], BF16)
                newmt = sbs("newmt", [128, 16, 32], BF16); wmst = sbs("wmst", [128, 32], BF16)
                augq2 = [sbs("augq2_%d" % k, [96, 512], BF16) for k in range(2)]
                SX["Ssm"] = [Sf, sbs("Ssm1", [128, 2, 128])]
                SX["S0bf"] = sbs("S0bf", [128, 16, 2, 128], BF16)
                SX["ogT"] = tmpf
                ld("sp", ckt, ckt[:], ck[:, :])
                ld("sp", maskcst, maskcst[:], maskcs[:, :])
                ld("sp", mmst, mmst[:], mmap_s[:, :, :])
                ld("sp", newmt, newmt[:], newmask[:, :, :])
                ld("sp", wmst, wmst[:], wmasks[:, :])
                pgf = [sbs("pgf%d" % i, [128, 256]) for i in range(2)]
                pgb = [sbs("pgb%d" % i, [128, 256], BF16) for i in range(2)]
                kTp = [sbs("kTp%d" % i, [128, 128], BF16) for i in range(2)]
                vpg = [sbs("vpg%d" % i, [128, 2, 65], BF16) for i in range(2)]
                ptt = sbs("ptt", [128, NPG], I32); idxt = sbs("idxt", [128, NPG], I32); iot = sbs("iot", [128, 1], I32)
                idxt2 = sbs("idxt2", [128, NPG], I32)
                kTn = sbs("kTn", [128, 2, 128], BF16)
                vnw = sbs("vnw", [128, 2, 2, 65], BF16)
                PTk = [sbs("PTk%d" % i, [128, 32], BF16) for i in range(8)]
                ti = PB * NT_P
                x = xt[ti % 2]
                load_mod(MOD, 0, 2)
                for kvh in range(2):
                    P.pool(lambda e, kvh=kvh: e.memset(augq[kvh][:], 0.0), writes=[augq[kvh]])
                    P.pool(lambda e, kvh=kvh: e.memset(augq2[kvh][:], 0.0), writes=[augq2[kvh]])
                    ld("sp", augq[kvh], augq[kvh][0:5, :], qab[1, kvh])
                    ld("sp", augq2[kvh], augq2[kvh][0:5, :], qab[1, kvh])
                front(ti, x, 2)
                for s in range(SB_):
                    P.dma("sp", lambda e, s=s: e.dma_start(out=win_s[s, WBUF - TS:WBUF, :],
                                                           in_=z[s * TS:(s + 1) * TS, C_KV + 512:C_KV + 768]),
                          reads=[z], writes=[dwins], sembuf=dwins)
                P.dma("pool", lambda e: e.dma_start(out=win_s[:, 0:WBUF - TS, :].rearrange("s t f -> s (t f)"),
                                                    in_=swin[:, TS:WBUF, :].rearrange("s t f -> s (t f)")),
                      writes=[dwins], sembuf=dwins)
                gla(True, 0, 2)
                P.pe(lambda e: e.transpose(out=pT[:, 512:640], in_=kvbf[:, 256:384], identity=identb[:]),
                     reads=[kvbf, identb], writes=[pT])
                P.pe(lambda e: e.transpose(out=pT[:, 640:768], in_=kvbf[:, 512:640], identity=identb[:]),
                     reads=[kvbf, identb], writes=[pT])
                P.dve(lambda e: e.tensor_copy(out=kTn[:].rearrange("p a b -> p (a b)"), in_=pT[:, 512:768]), reads=[pT], writes=[kTn])
                P.pool(lambda e: e.memset(vnw[:, :, :, 64:65], 1.0), writes=[vnw])
                P.pool(lambda e: e.tensor_copy(out=vnw[:, 0, :, 0:64], in_=kvbf[:, 384:512].rearrange("p (k d) -> p k d", k=2)),
                       reads=[kvbf], writes=[vnw])
                P.pool(lambda e: e.tensor_copy(out=vnw[:, 1, :, 0:64], in_=kvbf[:, 640:768].rearrange("p (k d) -> p k d", k=2)),
                       reads=[kvbf], writes=[vnw])
                P.pool(lambda e: e.iota(iot[:], pattern=[[0, 1]], base=0, channel_multiplier=2), writes=[iot])
                for kvh in range(2):
                    P.pool(lambda e, kvh=kvh: e.memset(vpg[kvh][:, :, 64:65], 1.0), writes=[vpg[kvh]])
                qTsz = [qnbf, kvbf]
                for kvh in range(2):
                    r = slice(kvh * 64, kvh * 64 + 64)
                    ro = slice((1 - kvh) * 64, (1 - kvh) * 64 + 64)
                    P.dve(lambda e, kvh=kvh, ro=ro: e.memset(qTsz[kvh][ro, 0:512], 0.0), writes=[qTsz[kvh]])
                    P.dve(lambda e, kvh=kvh, r=r: e.tensor_copy(
                        out=qTsz[kvh][r, 0:512].rearrange("p (s g q) -> p g s q", s=16, g=4),
                        in_=qTz[kvh][r, :].rearrange("p (g s q) -> p g s q", g=4, s=16)), reads=[qTz[kvh]], writes=[qTsz[kvh]])
                P.pool(lambda e: e.memset(vcss[:], 0.0), writes=[vcss])
                P.pool(lambda e: e.memset(vcss[:, :, :, 64:65], 1.0), writes=[vcss])
                P.pool(lambda e: e.memset(kcTs[:], 0.0), writes=[kcTs])
                P.pool(lambda e: e.memset(Ys[:], 0.0), writes=[Ys])
                npg = [0]

                def evac_seq(kvh, s):
                    unit_flush()
                    P.act(lambda e: e.activation(
                        out=tmpf[0:65, kvh * 512:(kvh + 1) * 512].rearrange("p (g s q) -> p s g q", g=4, s=16)[:, s, :, :],
                        in_=obank[kvh][0:65, s * 32:(s + 1) * 32].rearrange("p (g q) -> p g q", g=4),
                        func=AF.Copy), reads=[obank[kvh]], writes=[tmpf])

                def gather(s, j, col0):
                    pf = pgf[npg[0] % 2]
                    pbf = pgb[npg[0] % 2]
                    npg[0] += 1
                    P.dma("pool", lambda e: e.indirect_dma_start(
                        out=pf[:], out_offset=None, in_=cache[:, :],
                        in_offset=bass.IndirectOffsetOnAxis(ap=(idxt if col0 == 0 else idxt2)[:, j:j + 1], axis=0)),
                        reads=[idxt, idxt2], writes=[pf], sembuf=pf)
                    P.dve(lambda e: e.tensor_copy(out=pbf[:], in_=pf[:]), reads=[pf], writes=[pbf])
                    return pbf

                for s in range(SB_):
                    qc = (s * 32, 32)
                    ld("sp", ptt, ptt[:], ptab[s:s + 1, :].partition_broadcast(128))
                    P.dve(lambda e: e.tensor_scalar(out=idxt[:], in0=ptt[:], scalar1=256, scalar2=iot[:, 0:1], op0=ALU.mult,
                                                    op1=ALU.add), reads=[ptt, iot], writes=[idxt])
                    P.dve(lambda e: e.tensor_scalar(out=idxt2[:], in0=idxt[:], scalar1=1, scalar2=None, op0=ALU.add),
                          reads=[idxt], writes=[idxt2])
                    for j in range(NPG):
                        pbf = gather(s, j, 0)
                        pY = pb[0]
                        for ty in range(2):
                            for kvh in range(2):
                                for r in range(2):
                                    c0 = ty * 128 + kvh * 64
                                    P.pe(lambda e, ty=ty, kvh=kvh, r=r, c0=c0, pbf=pbf: e.matmul(
                                        pY[r * 64:(r + 1) * 64, (ty * 2 + kvh) * 64:(ty * 2 + kvh + 1) * 64],
                                        lhsT=pbf[:, c0:c0 + 64], rhs=selpt[:, r, :], start=True, stop=True),
                                        reads=[pbf, selpt], writes=[pY])
                        P.act(lambda e, j=j: e.activation(out=Ys[:, :, :, 8 + (j % 8) * 64:8 + (j % 8 + 1) * 64],
                                                          in_=pY[:, 0:256].rearrange("p (a b c) -> p a b c", a=2, b=2),
                                                          func=AF.Copy), reads=[pY], writes=[Ys])
                        if j % 8 == 7:
                            m = j // 8
                            compress(Ys, 64, kcTs, 64 * m, vcss, 64 * m - 1, m == 0)
                            P.dve(lambda e: e.tensor_copy(out=Ys[:, :, :, 0:8], in_=Ys[:, :, :, 512:520]), reads=[Ys], writes=[Ys])
                    for kvh in range(2):
                        r = slice(kvh * 64, kvh * 64 + 64)
                        for jc in range(4):
                            mms = [(kcTs[:, 1 + jc * 128:1 + (jc + 1) * 128], qTsz[kvh][:, s * 32:(s + 1) * 32], [kcTs, qTsz[kvh]], (0, 32)),
                                   (ckct[0:4, jc * 128:(jc + 1) * 128], augq[kvh][0:4, qc[0]:qc[0] + 32], [ckct, augq[kvh]], (0, 32))]
                            if jc == 3:
                                mms.append((identb[:], maskcst[:, :], [identb, maskcst], (0, 32)))
                            unit(mms, 32, vcss[:, jc, kvh, :], [vcss], obank[kvh], qc, jc == 0, jc == 3, keep=PTk[kvh * 4 + jc])
                        evac_seq(kvh, s)
                        pim = pb[kvh]
                        for jc in range(4):
                            P.pe(lambda e, kvh=kvh, jc=jc, pim=pim, qc=qc: e.matmul(pim[:, qc[0]:qc[0] + 32], lhsT=mmst[:, jc, :],
                                                                             rhs=PTk[kvh * 4 + jc][:, :], start=(jc == 0), stop=(jc == 3)),
                                 reads=[mmst, PTk[kvh * 4 + jc]], writes=[pim])
                        P.act(lambda e, kvh=kvh, pim=pim, s=s: e.activation(
                            out=z[:, kvh * 512:(kvh + 1) * 512].rearrange("p (g s q) -> p s g q", g=4, s=16)[:, s, :, :],
                            in_=pim[:, s * 32:(s + 1) * 32].rearrange("p (g q) -> p g q", g=4),
                            func=AF.Copy), reads=[pim], writes=[z])
                branch_epilogue(0, True, True)
                ld("sp", z, z[:, 2048:2560], vmam_s.rearrange("p a b c -> p (a b c)"))
                select_blocks(128, lambda kvh: z[:, 2048 + kvh * 128:2048 + (kvh + 1) * 128],
                              lambda kvh: z[:, 2304 + kvh * 128:2304 + (kvh + 1) * 128], [z], 6, True)
                for kvh in range(2):
                    build_augq(augq[kvh], kvh, 0, True)
                    build_augq(augq2[kvh], kvh, 64, True)
                for s in range(SB_):
                    qc = (s * 32, 32)
                    ld("sp", ptt, ptt[:], ptab[s:s + 1, :].partition_broadcast(128))
                    P.dve(lambda e: e.tensor_scalar(out=idxt[:], in0=ptt[:], scalar1=256, scalar2=iot[:, 0:1], op0=ALU.mult,
                                                    op1=ALU.add), reads=[ptt, iot], writes=[idxt])
                    P.dve(lambda e: e.tensor_scalar(out=idxt2[:], in0=idxt[:], scalar1=1, scalar2=None, op0=ALU.add),
                          reads=[idxt], writes=[idxt2])
                    for j in range(NPG + 1):
                        if j < NPG:
                            pbf = gather(s, j, 256)
                            kt = kTp[j % 2]; vp = vpg[j % 2]
                            P.pe(lambda e, pbf=pbf: e.transpose(out=pT[:, 768:896], in_=pbf[:, 0:128], identity=identb[:]),
                                 reads=[pbf, identb], writes=[pT])
                            P.act(lambda e, kt=kt: e.activation(out=kt[:], in_=pT[:, 768:896], func=AF.Copy), reads=[pT], writes=[kt])
                            P.dve(lambda e, vp=vp, pbf=pbf: e.tensor_copy(out=vp[:, :, 0:64],
                                                                           in_=pbf[:, 128:256].rearrange("p (k d) -> p k d", k=2)),
                                   reads=[pbf], writes=[vp])
                        for kvh in range(2):
                            r = slice(kvh * 64, kvh * 64 + 64)
                            if j < NPG:
                                aq = augq[kvh] if j < 32 else augq2[kvh]
                                mms = [(kt[:, :], qTsz[kvh][:, s * 32:(s + 1) * 32], [kt, qTsz[kvh]], (0, 32)),
                                       (ckt[0:96, j * 128:(j + 1) * 128], aq[0:96, qc[0]:qc[0] + 32], [ckt, aq], (0, 32))]
                                unit(mms, 32, vp[:, kvh, :], [vp], obank[kvh], qc, j == 0, False)
                            else:
                                mms = [(kTn[:, 0, :], qTsz[kvh][:, s * 32:(s + 1) * 32], [kTn, qTsz[kvh]], (0, 32)),
                                       (ckt[0:4, PAST:PAST + 128], augq[kvh][0:4, qc[0]:qc[0] + 32], [ckt, augq[kvh]], (0, 32)),
                                       (identb[:], newmt[:, s, :], [identb, newmt], (0, 32))]
                                unit(mms, 32, vnw[:, 0, kvh, :], [vnw], obank[kvh], qc, False, True)
                                evac_seq(kvh, s)
                branch_epilogue(1, True, False)
                for s in range(SB_):
                    qc = (s * 32, 32)
                    for w in range(5):
                        if w < 4:
                            pf = pgf[npg[0] % 2]; pbf = pgb[npg[0] % 2]
                            npg[0] += 1
                            ld("sp", pf, pf[:], swin[s, w * 128:(w + 1) * 128, :])
                            P.dve(lambda e, pf=pf, pbf=pbf: e.tensor_copy(out=pbf[:], in_=pf[:]), reads=[pf], writes=[pbf])
                            kt = kTp[w % 2]; vp = vpg[w % 2]
                            P.pe(lambda e, pbf=pbf: e.transpose(out=pT[:, 768:896], in_=pbf[:, 0:128], identity=identb[:]),
                                 reads=[pbf, identb], writes=[pT])
                            P.act(lambda e, kt=kt: e.activation(out=kt[:], in_=pT[:, 768:896], func=AF.Copy), reads=[pT], writes=[kt])
                            P.dve(lambda e, vp=vp, pbf=pbf: e.tensor_copy(out=vp[:, :, 0:64],
                                                                           in_=pbf[:, 128:256].rearrange("p (k d) -> p k d", k=2)),
                                   reads=[pbf], writes=[vp])
                        for kvh in range(2):
                            r = slice(kvh * 64, kvh * 64 + 64)
                            if w < 4:
                                c0 = PAST - WBUF + w * 128
                                mms = [(kt[:, :], qTsz[kvh][:, s * 32:(s + 1) * 32], [kt, qTsz[kvh]], (0, 32)),
                                       (ckt[0:4, c0:c0 + 128], augq[kvh][0:4, qc[0]:qc[0] + 32], [ckt, augq[kvh]], (0, 32))]
                                if w == 0:
                                    mms.append((identb[:], wmst[:, :], [identb, wmst], (0, 32)))
                                unit(mms, 32, vp[:, kvh, :], [vp], obank[kvh], qc, w == 0, False)
                            else:
                                mms = [(kTn[:, 1, :], qTsz[kvh][:, s * 32:(s + 1) * 32], [kTn, qTsz[kvh]], (0, 32)),
                                       (ckt[0:4, PAST:PAST + 128], augq[kvh][0:4, qc[0]:qc[0] + 32], [ckt, augq[kvh]], (0, 32)),
                                       (identb[:], newmt[:, s, :], [identb, newmt], (0, 32))]
                                unit(mms, 32, vnw[:, 1, kvh, :], [vnw], obank[kvh], qc, False, True)
                                evac_seq(kvh, s)
                branch_epilogue(2, True, False)
                back(ti, x)
                P.flush()

        with ExitStack() as S2:
          if CFG["ffn"]:
            sb2 = mk(S2)
            wup_bf = sb2("wup_bf", [128, 8, D_FF], BF16)
            wdn_bf = sb2("wdn_bf", [128, 32, D], BF16)
            MOD2 = sb2("MOD2", [128, 3 * D])
            x1t = [sb2("x1t%d" % i, [128, D]) for i in range(2)]
            junk2 = sb2("junk2", [128, D], BF16)
            st2 = sb2("st2", [128, 8])
            tmp2 = sb2("tmp2", [128, D])
            h2bf = sb2("h2bf", [128, D], BF16); h2T = sb2("h2T", [128, D], BF16)
            hr = [sb2("hr%d" % i, [128, 512], BF16) for i in range(2)]
            hsq = sb2("hsq", [128, 32, 128], BF16)
            yt = sb2("yt", [128, D])
            with ExitStack() as S2a0:
                ada_rows(mk(S2a0), 3 * D, 2, 3)
                P.flush()
            with ExitStack() as S2a:
                sba = mk(S2a)
                stg2 = [sba("stg2_%d" % i, [128, 2048]) for i in range(2)]
                for kc in range(16):
                    s_ = stg2[kc % 2]
                    k8, hh = kc // 2, kc % 2
                    ld("sp" if kc % 2 == 0 else "pool", s_, s_[:], w_up[k8 * 128:(k8 + 1) * 128, hh * 2048:(hh + 1) * 2048])
                    if kc % 2 == 0:
                        P.dve(lambda e, s_=s_, k8=k8, hh=hh: e.tensor_copy(out=wup_bf[:, k8, hh * 2048:(hh + 1) * 2048], in_=s_[:]),
                              reads=[s_], writes=[wup_bf])
                    else:
                        P.act(lambda e, s_=s_, k8=k8, hh=hh: e.activation(out=wup_bf[:, k8, hh * 2048:(hh + 1) * 2048], in_=s_[:],
                                                                          func=AF.Copy), reads=[s_], writes=[wup_bf])
                for q4 in range(16):
                    s_ = stg2[q4 % 2]
                    ld("sp" if q4 % 2 == 0 else "pool", s_, s_[:].rearrange("p (a n) -> p a n", a=2),
                       w_down[q4 * 256:(q4 + 1) * 256, :].rearrange("(a p) n -> p a n", p=128))
                    if q4 % 2 == 0:
                        P.dve(lambda e, s_=s_, q4=q4: e.tensor_copy(out=wdn_bf[:, q4 * 2:(q4 + 1) * 2, :].rearrange("p a n -> p (a n)"),
                                                                    in_=s_[:]), reads=[s_], writes=[wdn_bf])
                    else:
                        P.act(lambda e, s_=s_, q4=q4: e.activation(out=wdn_bf[:, q4 * 2:(q4 + 1) * 2, :].rearrange("p a n -> p (a n)"),
                                                                   in_=s_[:], func=AF.Copy), reads=[s_], writes=[wdn_bf])
                P.flush()
            for ti in range(NTILE):
                kind = min(ti // NT_P, 2)
                if ti % NT_P == 0:
                    load_mod(MOD2, 3 * D, kind)
                xx = x1t[ti % 2]
                ld("sp", xx, xx[:], y_all[ti * 128:(ti + 1) * 128, :])
                rms_rstd(xx[:], [xx], junk2[:], junk2, st2, 0, D)
                P.dve(lambda e, xx=xx: e.scalar_tensor_tensor(out=tmp2[:], in0=xx[:], scalar=st2[:, 1:2], in1=MOD2[:, D:2 * D],
                                                              op0=ALU.mult, op1=ALU.mult), reads=[xx, st2, MOD2], writes=[tmp2])
                P.pool(lambda e: e.tensor_tensor(out=h2bf[:], in0=tmp2[:], in1=MOD2[:, 0:D], op=ALU.add),
                       reads=[tmp2, MOD2], writes=[h2bf])
                transpose8(h2bf, h2T)
                for f4 in range(8):
                    pbk = pb[2 + (f4 % 3)]
                    for ff in range(4):
                        fch = f4 * 4 + ff
                        for kc in range(8):
                            P.pe(lambda e, kc=kc, pbk=pbk, ff=ff, fch=fch: e.matmul(
                                pbk[:, ff * 128:(ff + 1) * 128], lhsT=wup_bf[:, kc, fch * 128:(fch + 1) * 128],
                                rhs=h2T[:, kc * 128:(kc + 1) * 128], start=(kc == 0), stop=(kc == 7)),
                                reads=[wup_bf, h2T], writes=[pbk])
                    hrr = hr[f4 % 2]
                    P.act(lambda e, pbk=pbk, hrr=hrr: e.activation(out=hrr[:], in_=pbk[:, :], func=AF.Relu), reads=[pbk], writes=[hrr])
                    if f4 % 2 == 0:
                        P.dve(lambda e, hrr=hrr, f4=f4: e.tensor_tensor(out=hsq[:, f4 * 4:(f4 + 1) * 4, :].rearrange("p a t -> p (a t)"),
                                                                        in0=hrr[:], in1=hrr[:], op=ALU.mult), reads=[hrr], writes=[hsq])
                    else:
                        P.pool(lambda e, hrr=hrr, f4=f4: e.tensor_tensor(out=hsq[:, f4 * 4:(f4 + 1) * 4, :].rearrange("p a t -> p (a t)"),
                                                                         in0=hrr[:], in1=hrr[:], op=ALU.mult), reads=[hrr], writes=[hsq])
                for nh in range(2):
                    pbk = pb[nh]
                    for fch in range(32):
                        P.pe(lambda e, fch=fch, pbk=pbk, nh=nh: e.matmul(pbk[:, :], lhsT=hsq[:, fch, :],
                                                                         rhs=wdn_bf[:, fch, nh * 512:(nh + 1) * 512],
                                                                         start=(fch == 0), stop=(fch == 31)),
                             reads=[hsq, wdn_bf], writes=[pbk])
                    P.act(lambda e, pbk=pbk, nh=nh: e.activation(out=junk2[:, 0:512], in_=pbk[:, :], func=AF.Square,
                                                                 accum_out=st2[:, 2 + nh:3 + nh]), reads=[pbk], writes=[st2, junk2])
                P.dve(lambda e: e.tensor_tensor(out=st2[:, 4:5], in0=st2[:, 2:3], in1=st2[:, 3:4], op=ALU.add), reads=[st2], writes=[st2])
                P.act(lambda e: e.activation(out=st2[:, 5:6], in_=st2[:, 4:5], func=AF.Sqrt, scale=1.0 / D, bias=epsb[:]),
                      reads=[st2, epsb], writes=[st2])
                P.dve(lambda e: e.reciprocal(out=st2[:, 5:6], in_=st2[:, 5:6]), reads=[st2], writes=[st2])
                for nh in range(2):
                    pbk = pb[nh]
                    P.dve(lambda e, pbk=pbk, nh=nh: e.scalar_tensor_tensor(
                        out=tmp2[:, nh * 512:(nh + 1) * 512], in0=pbk[:, :], scalar=st2[:, 5:6],
                        in1=MOD2[:, 2 * D + nh * 512:2 * D + (nh + 1) * 512], op0=ALU.mult, op1=ALU.mult),
                        reads=[pbk, st2, MOD2], writes=[tmp2])
                P.pool(lambda e, xx=xx: e.tensor_tensor(out=yt[:], in0=tmp2[:], in1=xx[:], op=ALU.add), reads=[tmp2, xx], writes=[yt])
                P.dma("sp", lambda e, ti=ti: e.dma_start(out=y_all[ti * 128:(ti + 1) * 128, :], in_=yt[:]),
                      reads=[yt], writes=[dy], sembuf=dy)
            P.flush()
    return nc


_NC = None


def _consts():
    f32 = np.float32
    c = {}
    c["ident_f"] = np.eye(128, dtype=f32)
    sel = np.zeros((3, 18, 128), f32)
    sel[0, 0, :] = 1.0
    sel[1, 1, :] = 1.0
    for m in range(128):
        sel[2, 2 + m // TS, m] = 1.0
    tp = np.arange(128)
    same = [np.ones((128, 128), bool), (tp[:, None] // TS == tp[None, :] // TS)]
    tri = np.zeros((2, 2, 128, 128), f32)
    causm = np.zeros((2, 128, 512), f32)
    for k in range(2):
        tri[k, 0] = ((tp[:, None] > tp[None, :]) & same[k]) * (-1.0 / 16)
        tri[k, 1] = ((tp[:, None] <= tp[None, :]) & same[k]) * (-1.0 / 16)
        causm[k] = np.tile(((tp[:, None] <= tp[None, :]) & same[k]).astype(f32), (1, 4))
    c["tri"] = tri
    c["causm"] = causm
    seqind = np.zeros((2, 128, 16), f32)
    seqind[0, :, 0] = -1.0 / 16
    seqmask = np.zeros((128, 16), f32)
    for s in range(16):
        seqind[1, s * TS:(s + 1) * TS, s] = -1.0 / 16
        seqmask[s * TS:(s + 1) * TS, s] = 1.0
    c["seqind"] = seqind
    c["seqmask"] = seqmask
    ck = np.zeros((96, NKCOL), f32)
    key = np.arange(PAST)
    ck[32 + (key // 64) % 64, key] = 1.0
    ck[0, :] = 1.0
    ck[1, :] = 1.0
    ck[2, :PAST] = key % 128
    ck[3, :PAST] = key - key % 128
    ck[4, :PAST] = 1.0
    m = np.arange(128)
    ck[2, PAST:] = m % TS
    ck[3, PAST:] = PAST
    c["ck"] = ck
    ckc = np.zeros((4, 512), f32)
    cend = 16 * np.arange(512) + 31
    ckc[0] = 1.0
    ckc[1] = 1.0
    ckc[2] = cend % 128
    ckc[3] = cend - cend % 128
    c["ckc"] = ckc
    slope = np.array([2.0 ** -(h + 1) for h in range(8)], f32).reshape(2, 4)
    qab = np.zeros((2, 2, 5, 512), f32)
    for kvh in range(2):
        for g in range(4):
            sl = slope[kvh, g]
            cols = slice(g * 128, (g + 1) * 128)
            qab[0, kvh, 0, cols] = -128.0 * sl
            qab[0, kvh, 1, cols] = -sl * np.arange(128)
            qab[0, kvh, 2, cols] = sl
            qab[0, kvh, 3, cols] = sl
            qab[0, kvh, 4, cols] = -NEG
            for s in range(16):
                cs = slice(s * 32 + g * 8, s * 32 + g * 8 + 8)
                qab[1, kvh, 0, cs] = -128.0 * 64 * sl
                qab[1, kvh, 1, cs] = -sl * np.arange(8)
                qab[1, kvh, 2, cs] = sl
                qab[1, kvh, 3, cs] = sl
                qab[1, kvh, 4, cs] = -NEG
    c["qab"] = qab
    maskc = np.zeros((17, 128, 128), f32)
    for r in range(17):
        maskc[r] = np.where(16 * tp[:, None] + 31 > 128 * r + tp[None, :], -NEG, 0.0)
    c["maskc"] = maskc
    maskcs = np.zeros((128, 32), f32)
    maskcs[127, :] = -NEG
    c["maskcs"] = maskcs

    def mmap(ncb, nsb):
        start = np.arange(ncb) * 16
        bs = np.arange(nsb) * 64
        ov = np.minimum(start[:, None] + 32, bs[None, :] + 64) - np.maximum(start[:, None], bs[None, :])
        return (np.clip(ov, 0, None) / 32).astype(f32)
    mp = np.zeros((256, 64), f32)
    mp[:255] = mmap(255, 64)
    c["mmap_p"] = np.ascontiguousarray(mp.reshape(2, 128, 64).transpose(1, 0, 2))
    ms = np.zeros((512, 128), f32)
    ms[:511] = mmap(511, 129)[:, :128]
    c["mmap_s"] = np.ascontiguousarray(ms.reshape(4, 128, 128).transpose(1, 0, 2))
    vmam_p = np.zeros((NT_P, 128, 2, 2, 64), f32)
    blk = np.arange(64)
    for it in range(NT_P):
        cur = (128 * it + tp) // 64
        valid = blk[None, :] <= cur[:, None]
        forced = valid & ((blk[None, :] == 0) | (blk[None, :] == cur[:, None]) | (blk[None, :] == cur[:, None] - 1))
        vm = (valid & ~forced).astype(f32)
        am = np.where(forced, 1.0e4, np.where(valid, 0.0, -1.0e4)).astype(f32)
        vmam_p[it, :, 0, :, :] = vm[:, None, :]
        vmam_p[it, :, 1, :, :] = am[:, None, :]
    c["vmam_p"] = vmam_p
    vmam_s = np.zeros((128, 2, 2, 128), f32)
    vmam_s[:, 0, :, :] = 1.0
    vmam_s[:, 0, :, 0] = 0.0
    vmam_s[:, 0, :, 127] = 0.0
    vmam_s[:, 1, :, 0] = 1.0e4
    vmam_s[:, 1, :, 127] = 1.0e4
    c["vmam_s"] = vmam_s
    caus4 = np.zeros((2, 128, 512), f32)
    caus4[0] = np.tile(np.where(tp[:, None] > tp[None, :], -NEG, 0.0), (1, 4))
    caus4[1] = np.tile(np.where(tp[:, None] < tp[None, :], -NEG, 0.0), (1, 4))
    c["caus4"] = caus4
    newmask = np.full((128, 16, 32), -NEG, f32)
    for s in range(16):
        for qk in range(TS):
            for g in range(4):
                for q in range(TS):
                    if qk <= q:
                        newmask[s * TS + qk, s, g * 8 + q] = 0.0
    c["newmask"] = newmask
    wm = np.zeros((128, 32), f32)
    for g in range(4):
        for q in range(TS):
            wm[:q, g * 8 + q] = -NEG
    c["wmasks"] = wm
    selp = np.zeros((128, 2, 64), f32)
    for p in range(64):
        for r in range(2):
            selp[2 * p + r, r, p] = 1.0
    c["selp"] = selp
    import ml_dtypes
    for k in ("causm", "ck", "ckc", "qab", "maskc", "maskcs", "mmap_p", "mmap_s", "caus4", "newmask", "wmasks", "selp"):
        c[k] = c[k].astype(ml_dtypes.bfloat16)
    return c


def kernel(x_prompt, x_sample, cache_kv, state_win, state_gla, page_table, c_prompt, c_sample,
           norm_mix_pre, norm_mix_post, norm_ffn_pre, norm_ffn_post, w_ada, b_ada, w_in,
           gla_w_gate, gla_b_gate, gla_norm, cmp_pos, cmp_w1, cmp_b1, cmp_w2, cmp_b2,
           w_out, w_up, w_down):
    global _NC
    f32 = np.float32
    if _NC is None:
        _NC = build_nc()
    nc = _NC
    A = lambda a: np.asarray(a, f32)
    x_prompt = A(x_prompt); x_sample = A(x_sample)
    consts = _consts()
    w1 = A(cmp_w1)[0].reshape(2, 2, 8, 2, 64, 256)
    w1p = np.ascontiguousarray(w1.transpose(3, 4, 0, 1, 2, 5).reshape(128, 2, 2, 8, 256))
    pos = A(cmp_pos)[0].reshape(2, 2, 8, 2, 64)
    posp = pos.transpose(3, 4, 0, 1, 2).reshape(128, 2, 2, 8)
    posrep = np.ascontiguousarray(np.repeat(posp[..., None], 16, axis=-1))
    w2 = A(cmp_w2)[0].reshape(2, 2, 128, 64)
    w2l = np.ascontiguousarray(w2.transpose(2, 0, 1, 3))
    b2 = A(cmp_b2)[0]
    shared = {
        "w_ada": A(w_ada)[0], "b_ada": A(b_ada)[0].reshape(1, -1),
        "gains": np.ascontiguousarray(np.stack([A(norm_mix_pre)[0], A(norm_mix_post)[0], A(norm_ffn_pre)[0], A(norm_ffn_post)[0]])),
        "w_in": A(w_in)[0], "w_out": A(w_out)[0], "w_up": A(w_up)[0], "w_down": A(w_down)[0],
        "wg": A(gla_w_gate)[0], "bg": A(gla_b_gate)[0].reshape(1, -1),
        "gnorm4": np.ascontiguousarray(np.tile(A(gla_norm)[0], 4).reshape(1, 512)),
        "cache": A(cache_kv)[0].reshape(-1, 256),
        "w1p": w1p, "posrep": posrep, "b1r": A(cmp_b1)[0], "w2l": w2l,
        "b2k": np.ascontiguousarray(np.concatenate([b2[0], b2[0]]).reshape(128, 1)), "b2v": b2[1].reshape(1, 64),
    }
    shared.update(consts)
    in_maps = []
    pt = np.asarray(page_table).astype(np.int32)
    for c in range(NCORES):
        xp = x_prompt[PB * c:PB * (c + 1)].reshape(PB * SEQ, D)
        xsm = x_sample[SB_ * c:SB_ * (c + 1)].reshape(SB_ * TS, D)
        cc = np.concatenate([A(c_prompt)[PB * c:PB * (c + 1)], A(c_sample)[SB_ * c:SB_ * (c + 1)]], axis=0)
        cTa = np.ascontiguousarray(cc.T.reshape(8, 128, 18).transpose(1, 0, 2))
        m = dict(shared)
        m.update({
            "xs": np.ascontiguousarray(np.concatenate([xp, xsm], axis=0)),
            "cT": cTa,
            "sgla": np.ascontiguousarray(A(state_gla)[0, SB_ * c:SB_ * (c + 1)]),
            "swin": np.ascontiguousarray(A(state_win)[0, SB_ * c:SB_ * (c + 1)].reshape(SB_, WBUF, 256)),
            "ptab": np.ascontiguousarray(pt[SB_ * c:SB_ * (c + 1)]),
        })
        in_maps.append(m)
    res = run_bass_kernel_spmd(nc, in_maps, core_ids=list(range(NCORES)))
    R = res.results
    NP = PB * SEQ
    y_p = np.concatenate([r["y_all"][:NP].reshape(PB, SEQ, D) for r in R], axis=0)
    y_s = np.concatenate([r["y_all"][NP:].reshape(SB_, TS, D) for r in R], axis=0)
    kv_p = np.concatenate([r["kvr"][:NP].reshape(PB, SEQ, 4, 2, 64) for r in R], axis=0)[None]
    kv_s = np.concatenate([r["kvr"][NP:].reshape(SB_, TS, 4, 2, 64) for r in R], axis=0)[None]
    wp = np.concatenate([r["win_p"].reshape(PB, WBUF, 2, 2, 64) for r in R], axis=0)[None]
    ws = np.concatenate([r["win_s"].reshape(SB_, WBUF, 2, 2, 64) for r in R], axis=0)[None]
    gp = np.concatenate([r["gla_p"] for r in R], axis=0)[None]
    gs = np.concatenate([r["gla_s"] for r in R], axis=0)[None]
    return (y_p.astype(f32), y_s.astype(f32), kv_p.astype(f32), kv_s.astype(f32),
            wp.astype(f32), ws.astype(f32), gp.astype(f32), gs.astype(f32))
```

```python
import numpy as np
from contextlib import ExitStack
import concourse.bass as bass
import concourse.mybir as mybir
from concourse.bass_utils import run_bass_kernel_spmd

F32 = mybir.dt.float32
BF16 = mybir.dt.bfloat16
I32 = mybir.dt.int32
ALU = mybir.AluOpType
AF = mybir.ActivationFunctionType

NCORES = 8
D = 1024
SEQ = 4096
PB = 2
SB_ = 16
TS = 8
NT_P = SEQ // 128
D_IN = 2856
D_FF = 4096
C_ZQ, C_ZK, C_ZV, C_ZA, C_ZR, C_QN, C_KV, C_ZG = 0, 256, 512, 1024, 1040, 1552, 2064, 2832
EPS = 1e-6
PAST = 8192
NPG = 64
WBUF = 512
NEG = 4096.0
NTOK = PB * SEQ + SB_ * TS
NTILE = NTOK // 128
NKCOL = PAST + 128
class _Stop(Exception):
    pass


def chk(k):
    if CFG["stage"] < k:
        raise _Stop()


CFG = {"stage": 99, "pool_pages": 10240, "n_ptiles": NT_P, "sample": True, "ffn": True, "kinds": PB}


class Buf:
    __slots__ = ("name", "t", "last_w", "readers", "dsem", "dcnt")

    def __init__(self, name, t):
        self.name = name
        self.t = t
        self.last_w = None
        self.readers = {}
        self.dsem = None
        self.dcnt = 0

    def __getitem__(self, idx):
        return self.t[idx]


class Op:
    __slots__ = ("eng", "fn", "deps", "signal", "isdma", "buf", "cnt")

    def __init__(self, eng, fn, isdma=False, buf=None):
        self.eng = eng
        self.fn = fn
        self.deps = set()
        self.signal = False
        self.isdma = isdma
        self.buf = buf
        self.cnt = None


class Prog:
    ENGS = ("pe", "dve", "act", "pool", "sp")

    def __init__(self, nc, stack):
        self.nc = nc
        self.stack = stack
        self.ops = []
        self.bufs = []
        self.sems = {e: stack.enter_context(nc.semaphore("s_" + e)) for e in self.ENGS}
        self.cnt = {e: 0 for e in self.ENGS}
        self.waited = {e: {} for e in self.ENGS}
        self.dbufs = []
        self.engmap = {"pe": nc.tensor, "dve": nc.vector, "act": nc.scalar, "pool": nc.gpsimd, "sp": nc.sync}

    def buf(self, name, t):
        b = Buf(name, t)
        self.bufs.append(b)
        return b

    def op(self, eng, fn, reads=(), writes=(), isdma=False, dmabuf=None):
        o = Op(eng, fn, isdma, dmabuf)
        i = len(self.ops)
        for b in reads:
            if b.last_w is not None:
                o.deps.add(b.last_w)
        for b in writes:
            if b.last_w is not None:
                o.deps.add(b.last_w)
            for r in b.readers.values():
                o.deps.add(r)
        rk = ("dma", i) if isdma else eng
        for b in reads:
            b.readers[rk] = i
        for b in writes:
            b.last_w = i
            b.readers = {}
        o.deps.discard(i)
        self.ops.append(o)
        return o

    def pe(self, fn, reads=(), writes=()):
        return self.op("pe", fn, reads, writes)

    def dve(self, fn, reads=(), writes=()):
        return self.op("dve", fn, reads, writes)

    def act(self, fn, reads=(), writes=()):
        return self.op("act", fn, reads, writes)

    def pool(self, fn, reads=(), writes=()):
        return self.op("pool", fn, reads, writes)

    def dma(self, q, fn, reads=(), writes=(), sembuf=None):
        return self.op(q, fn, reads, writes, isdma=True, dmabuf=sembuf)

    def _wait(self, ename, s, v):
        w = self.waited[ename]
        k = id(s)
        if w.get(k, 0) >= v:
            return
        self.engmap[ename].wait_ge(s, v)
        w[k] = v

    def flush(self):
        ops = self.ops
        for o in ops:
            for d in o.deps:
                p = ops[d]
                if p.isdma or not (p.eng == o.eng and p.eng == "pe"):
                    p.signal = True
        last = {}
        for o in ops:
            if not o.isdma:
                last[o.eng] = o
        for o in last.values():
            o.signal = True
        for o in ops:
            if o.isdma:
                b = o.buf
                if b.dsem is None:
                    b.dsem = self.stack.enter_context(self.nc.semaphore("d_" + b.name))
                    self.dbufs.append(b)
                b.dcnt += 16
                o.cnt = (b.dsem, b.dcnt)
            elif o.signal:
                self.cnt[o.eng] += 1
                o.cnt = (self.sems[o.eng], self.cnt[o.eng])
        for o in ops:
            need = {}
            for d in o.deps:
                p = ops[d]
                if p.cnt is None:
                    continue
                if (not p.isdma) and (not o.isdma) and p.eng == "pe" and o.eng == "pe":
                    continue
                s, v = p.cnt
                k = id(s)
                if k not in need or need[k][1] < v:
                    need[k] = (s, v)
            for k, (s, v) in need.items():
                self._wait(o.eng, s, v)
            ins = o.fn(self.engmap[o.eng])
            if o.cnt is not None:
                ins.then_inc(o.cnt[0], 16 if o.isdma else 1)
        for e in self.ENGS:
            for e2 in self.ENGS:
                if e2 != e and self.cnt[e2] > 0:
                    self._wait(e, self.sems[e2], self.cnt[e2])
            for b in self.dbufs:
                self._wait(e, b.dsem, b.dcnt)
        self.ops = []
        for b in self.bufs:
            b.last_w = None
            b.readers = {}


def build_nc():
    nc = bass.Bass("TRN2", target_bir_lowering=False)

    def din(name, shape, dt=F32):
        return nc.dram_tensor(name, list(shape), dt, kind="ExternalInput").ap()

    def dout(name, shape, dt=F32):
        return nc.dram_tensor(name, list(shape), dt, kind="ExternalOutput").ap()

    xs = din("xs", [NTOK, D])
    cT = din("cT", [128, 8, 18])
    w_ada = din("w_ada", [D, 6 * D])
    b_ada = din("b_ada", [1, 6 * D])
    gains = din("gains", [4, D])
    w_in = din("w_in", [D, D_IN])
    w_out = din("w_out", [D, D])
    w_up = din("w_up", [D, D_FF])
    w_down = din("w_down", [D_FF, D])
    wg = din("wg", [16, 256])
    bg = din("bg", [1, 256])
    gnorm4 = din("gnorm4", [1, 512])
    sgla = din("sgla", [SB_, 4, 64, 128])
    swin = din("swin", [SB_, WBUF, 256])
    cache = din("cache", [CFG["pool_pages"] * 128 * 2, 256])
    ptab = din("ptab", [SB_, NPG], I32)
    w1p = din("w1p", [128, 2, 2, 8, 256])
    posrep = din("posrep", [128, 2, 2, 8, 16])
    b1r = din("b1r", [2, 256])
    w2l = din("w2l", [128, 2, 2, 64])
    b2k = din("b2k", [128, 1])
    b2v = din("b2v", [1, 64])
    ident_f = din("ident_f", [128, 128])
    tri = din("tri", [2, 2, 128, 128])
    seqind = din("seqind", [2, 128, 16])
    seqmask = din("seqmask", [128, 16])
    causm = din("causm", [2, 128, 512], BF16)
    ck = din("ck", [96, NKCOL], BF16)
    ckc = din("ckc", [4, 512], BF16)
    qab = din("qab", [2, 2, 5, 512], BF16)
    maskc = din("maskc", [17, 128, 128], BF16)
    maskcs = din("maskcs", [128, 32], BF16)
    mmap_p = din("mmap_p", [128, 2, 64], BF16)
    mmap_s = din("mmap_s", [128, 4, 128], BF16)
    vmam_p = din("vmam_p", [NT_P, 128, 2, 2, 64])
    vmam_s = din("vmam_s", [128, 2, 2, 128])
    caus4 = din("caus4", [2, 128, 512], BF16)
    newmask = din("newmask", [128, 16, 32], BF16)
    wmasks = din("wmasks", [128, 32], BF16)
    selp = din("selp", [128, 2, 64], BF16)

    y_all = dout("y_all", [NTOK, D])
    kvr = dout("kvr", [NTOK, 512])
    win_p = dout("win_p", [PB, WBUF, 256])
    win_s = dout("win_s", [SB_, WBUF, 256])
    gla_p = dout("gla_p", [PB, 4, 64, 128])
    gla_s = dout("gla_s", [SB_, 4, 64, 128])
    modrows = dout("modrows", [18, 6 * D])

    with ExitStack() as G:
        P = Prog(nc, G)

        def mk(stack, suf=""):
            def sb(name, shape, dt=F32):
                return P.buf(name + suf, stack.enter_context(nc.sbuf_tensor(name + suf, list(shape), dt)))
            return sb

        gsb = mk(G)

        def ld(q, dst, dst_ap, src_ap, reads=()):
            P.dma(q, lambda e: e.dma_start(out=dst_ap, in_=src_ap), reads=reads, writes=[dst], sembuf=dst)

        pb = [P.buf("pb%d" % i, G.enter_context(nc.psum_tensor("pb%d" % i, [128, 512], F32))) for i in range(7)]
        pT = P.buf("pT", G.enter_context(nc.psum_tensor("pT", [128, 1024], BF16)))
        dy = P.buf("y_all", y_all); dkvr = P.buf("kvr", kvr); dwinp = P.buf("win_p", win_p)
        dwins = P.buf("win_s", win_s); dglap = P.buf("gla_p", gla_p); dglas = P.buf("gla_s", gla_s)
        dmod = P.buf("modrows", modrows)

        identf = gsb("identf", [128, 128]); identb = gsb("identb", [128, 128], BF16)
        scT = gsb("scT", [128, 8, 18], BF16)
        epsb = gsb("epsb", [128, 1])
        ones1 = gsb("ones1", [1, 128])
        ld("sp", identf, identf[:], ident_f[:, :])
        P.dve(lambda e: e.memset(ones1[:], 1.0), writes=[ones1])
        P.dve(lambda e: e.memset(epsb[:], EPS), writes=[epsb])
        P.dve(lambda e: e.tensor_copy(out=identb[:], in_=identf[:]), reads=[identf], writes=[identb])
        with ExitStack() as S0:
            sb0 = mk(S0)
            cTt = sb0("cTt", [128, 8, 18]); sgm = sb0("sgm", [128, 8, 18])
            ld("sp", cTt, cTt[:], cT[:, :, :])
            P.act(lambda e: e.activation(out=sgm[:], in_=cTt[:], func=AF.Sigmoid), reads=[cTt], writes=[sgm])
            P.dve(lambda e: e.tensor_tensor(out=scT[:], in0=sgm[:], in1=cTt[:], op=ALU.mult),
                  reads=[sgm, cTt], writes=[scT])
            P.flush()

        def ada_rows(sb_, col0, gidx_a, gidx_g):
            adar = sb_("adar%d" % col0, [18, 3 * D])
            wst = sb_("wada_st%d" % col0, [128, 8, 256]); wbf = sb_("wada_bf%d" % col0, [128, 8, 256], BF16)
            bch = sb_("bch%d" % col0, [18, 256]); grow = sb_("grow%d" % col0, [18, D])
            for ncn in range(12):
                c0 = col0 + ncn * 256
                ld("sp", wst, wst[:], w_ada[:, c0:c0 + 256].rearrange("(k p) n -> p k n", p=128))
                ld("pool", bch, bch[:], b_ada[0:1, c0:c0 + 256].partition_broadcast(18))
                P.dve(lambda e: e.tensor_copy(out=wbf[:], in_=wst[:]), reads=[wst], writes=[wbf])
                pbk = pb[ncn % 2]
                for kc in range(8):
                    P.pe(lambda e, kc=kc, pbk=pbk: e.matmul(pbk[0:18, 0:256], lhsT=scT[:, kc, :], rhs=wbf[:, kc, :],
                                                             start=(kc == 0), stop=(kc == 7)),
                         reads=[scT, wbf], writes=[pbk])
                P.dve(lambda e, pbk=pbk, ncn=ncn: e.tensor_tensor(out=adar[:, ncn * 256:(ncn + 1) * 256], in0=pbk[0:18, 0:256],
                                                                  in1=bch[:], op=ALU.add),
                      reads=[pbk, bch], writes=[adar])
            ld("pool", grow, grow[:], gains[gidx_a:gidx_a + 1, :].partition_broadcast(18))
            P.dve(lambda e: e.scalar_tensor_tensor(out=adar[:, D:2 * D], in0=adar[:, D:2 * D], scalar=1.0, in1=grow[:],
                                                   op0=ALU.add, op1=ALU.mult), reads=[adar, grow], writes=[adar])
            ld("pool", grow, grow[:], gains[gidx_g:gidx_g + 1, :].partition_broadcast(18))
            P.dve(lambda e: e.tensor_tensor(out=adar[:, 2 * D:3 * D], in0=adar[:, 2 * D:3 * D], in1=grow[:], op=ALU.mult),
                  reads=[adar, grow], writes=[adar])
            P.dma("sp", lambda e: e.dma_start(out=modrows[:, col0:col0 + 3 * D], in_=adar[:]), reads=[adar], writes=[dmod],
                  sembuf=dmod)

        def load_mod(MODt, col0, kind):
            if kind < 2:
                P.dma("sp", lambda e: e.dma_start(out=MODt[:], in_=modrows[kind:kind + 1, col0:col0 + 3 * D].partition_broadcast(128)),
                      reads=[dmod], writes=[MODt], sembuf=MODt)
            else:
                for s_ in range(SB_):
                    P.dma("sp" if s_ % 2 == 0 else "pool", lambda e, s_=s_: e.dma_start(
                        out=MODt[s_ * TS:(s_ + 1) * TS, :],
                        in_=modrows[2 + s_:3 + s_, col0:col0 + 3 * D].partition_broadcast(TS)),
                        reads=[dmod], writes=[MODt], sembuf=MODt)

        def rms_rstd(src_ap, src_bufs, junk_ap, junk_buf, stb, c0, n):
            P.act(lambda e: e.activation(out=junk_ap, in_=src_ap, func=AF.Square, accum_out=stb[:, c0:c0 + 1]),
                  reads=src_bufs, writes=[stb, junk_buf])
            P.act(lambda e: e.activation(out=stb[:, c0 + 1:c0 + 2], in_=stb[:, c0:c0 + 1], func=AF.Sqrt, scale=1.0 / n, bias=epsb[:]),
                  reads=[stb, epsb], writes=[stb])
            P.dve(lambda e: e.reciprocal(out=stb[:, c0 + 1:c0 + 2], in_=stb[:, c0 + 1:c0 + 2]), reads=[stb], writes=[stb])

        def transpose8(src_bf, dstT, evac="act"):
            for kc in range(8):
                P.pe(lambda e, kc=kc: e.transpose(out=pT[:, kc * 128:(kc + 1) * 128], in_=src_bf[:, kc * 128:(kc + 1) * 128],
                                                  identity=identb[:]), reads=[src_bf, identb], writes=[pT])
            if evac == "act":
                P.act(lambda e: e.activation(out=dstT[:], in_=pT[:], func=AF.Copy), reads=[pT], writes=[dstT])
            else:
                P.dve(lambda e: e.tensor_copy(out=dstT[:], in_=pT[:]), reads=[pT], writes=[dstT])

        with ExitStack() as S1:
            sb1 = mk(S1)
            win_bf = sb1("win_bf", [128, 8, D_IN], BF16)
            wout_bf = sb1("wout_bf", [128, 8, D], BF16)
            w1b = sb1("w1b", [128, 2, 2, 8, 256], BF16)
            w2b = sb1("w2b", [128, 2, 2, 64], BF16)
            w2dup = sb1("w2dup", [128, 2, 128], BF16)
            pbias = sb1("pbias", [128, 2, 256])
            b2kt = sb1("b2kt", [128, 1]); b2vt = sb1("b2vt", [128, 64])
            wgt = sb1("wgt", [16, 256]); bgt = sb1("bgt", [1, 256])
            trit = sb1("trit", [128, 2, 2, 128]); seqi = sb1("seqi", [128, 2, 16]); seqm = sb1("seqm", [128, 16])
            causmt = sb1("causmt", [128, 2, 512], BF16)
            gnb = sb1("gnb", [128, 512])
            ckct = sb1("ckct", [4, 512], BF16)
            selpt = sb1("selpt", [128, 2, 64], BF16)
            ld("sp", ckct, ckct[:], ckc[:, :])
            ld("sp", selpt, selpt[:], selp[:, :, :])
            ld("sp", causmt, causmt[:], causm.rearrange("k p c -> p k c"))
            with ExitStack() as S1a:
                sba = mk(S1a)
                stg = [sba("stg%d" % i, [128, D_IN]) for i in range(2)]
                nst = [0]

                def ldcast(dst, dst_ap, src_ap, shape):
                    s = stg[nst[0] % 2]
                    q = "sp" if nst[0] % 2 == 0 else "pool"
                    np_, nf = shape
                    sap = s[0:np_, 0:nf]
                    ld(q, s, sap, src_ap)
                    if nst[0] % 2 == 0:
                        P.dve(lambda e: e.tensor_copy(out=dst_ap, in_=sap), reads=[s], writes=[dst])
                    else:
                        P.act(lambda e: e.activation(out=dst_ap, in_=sap, func=AF.Copy), reads=[s], writes=[dst])
                    nst[0] += 1

                for kc in range(8):
                    ldcast(win_bf, win_bf[:, kc, :], w_in[kc * 128:(kc + 1) * 128, :], (128, D_IN))
                for kc in range(8):
                    ldcast(wout_bf, wout_bf[:, kc, :], w_out[kc * 128:(kc + 1) * 128, :], (128, D))
                for ty in range(2):
                    for hf in range(2):
                        ldcast(w1b, w1b[:, ty, hf, :, :].rearrange("p a b -> p (a b)"),
                               w1p[:, ty, hf, :, :].rearrange("p a b -> p (a b)"), (128, 2048))
                ldcast(w2b, w2b[:].rearrange("p a b c -> p (a b c)"), w2l.rearrange("p a b c -> p (a b c)"), (128, 256))
                posb = sba("posb", [128, 2, 2, 8, 16], BF16)
                ldcast(posb, posb[:].rearrange("p a b c d -> p (a b c d)"), posrep.rearrange("p a b c d -> p (a b c d)"),
                       (128, 512))
                b1t = sba("b1t", [1, 2, 256]); ones16 = sba("ones16", [1, 16]); pb16 = sba("pb16", [16, 2, 256])
                ld("sp", b1t, b1t[:], b1r.rearrange("(o t) f -> o t f", o=1))
                P.dve(lambda e: e.memset(ones16[:], 1.0), writes=[ones16])
                ld("sp", b2kt, b2kt[:], b2k[:, :])
                ld("sp", b2vt, b2vt[:], b2v[0:1, :].partition_broadcast(128))
                ld("sp", wgt, wgt[:], wg[:, :]); ld("sp", bgt, bgt[:], bg[:, :])
                ld("sp", trit, trit[:], tri.rearrange("k m p t -> p k m t"))
                ld("sp", seqi, seqi[:], seqind.rearrange("k p s -> p k s"))
                ld("sp", seqm, seqm[:], seqmask[:, :])
                ld("sp", gnb, gnb[:], gnorm4[0:1, :].partition_broadcast(128))
                for fc in range(2):
                    for dup in range(2):
                        P.dve(lambda e, fc=fc, dup=dup: e.tensor_copy(out=w2dup[:, fc, dup * 64:(dup + 1) * 64],
                                                                      in_=w2b[:, 0, fc, :]), reads=[w2b], writes=[w2dup])
                for ty in range(2):
                    pbk = pb[2 + ty]
                    n = 0
                    for hf in range(2):
                        for j2 in range(8):
                            P.pe(lambda e, ty=ty, hf=hf, j2=j2, pbk=pbk, n=n: e.matmul(
                                pbk[0:16, 0:256], lhsT=posb[:, ty, hf, j2, :], rhs=w1b[:, ty, hf, j2, :],
                                start=(n == 0), stop=False), reads=[posb, w1b], writes=[pbk])
                            n += 1
                    P.pe(lambda e, ty=ty, pbk=pbk: e.matmul(pbk[0:16, 0:256], lhsT=ones16[:], rhs=b1t[:, ty, :],
                                                            start=False, stop=True), reads=[ones16, b1t], writes=[pbk])
                    P.act(lambda e, ty=ty, pbk=pbk: e.activation(out=pb16[0:16, ty, :], in_=pbk[0:16, 0:256], func=AF.Copy),
                          reads=[pbk], writes=[pb16])
                    P.pe(lambda e, ty=ty, pbk=pbk: e.matmul(pbk[:, 256:512], lhsT=ones1[0:1, :], rhs=pb16[0:1, ty, :],
                                                            start=True, stop=True), reads=[ones1, pb16], writes=[pbk])
                    P.act(lambda e, ty=ty, pbk=pbk: e.activation(out=pbias[:, ty, :], in_=pbk[:, 256:512], func=AF.Copy),
                          reads=[pbk], writes=[pbias])
                ada_rows(sba, 0, 0, 1)
                P.flush()

            MOD = sb1("MOD", [128, 3 * D])
            xt = [sb1("xt%d" % i, [128, D]) for i in range(2)]
            junkb = sb1("junkb", [128, 512], BF16)
            st4 = sb1("st4", [128, 16])
            tmpf = sb1("tmpf", [128, D])
            hbf = sb1("hbf", [128, D], BF16)
            hT = sb1("hT", [128, D], BF16)
            z = sb1("z", [128, D_IN])
            zaT = sb1("zaT", [16, 128])
            Lt = sb1("Lt", [128, 256]); Eout = sb1("Eout", [128, 256])
            EqT = sb1("EqT", [128, 256]); EkT = sb1("EkT", [128, 256])
            qinT = sb1("qinT", [128, 256], BF16)
            qz = [sb1("qz%d" % k, [128, 256], BF16) for k in range(2)]
            kz = [sb1("kz%d" % k, [128, 256], BF16) for k in range(2)]
            ATs = sb1("ATs", [128, 512], BF16)
            kout = sb1("kout", [128, 256], BF16); vbf = sb1("vbf", [128, 512], BF16)
            koutm = sb1("koutm", [128, 256], BF16)
            dec = sb1("dec", [128, 2, 16])
            Sf = sb1("Sf", [128, 2, 128]); Sbf = sb1("Sbf", [128, 2, 128], BF16)
            SX = {}
            szr = sb1("szr", [128, 512])
            obf = sb1("obf", [128, D], BF16)
            qnbf = sb1("qnbf", [128, 512], BF16); kvbf = sb1("kvbf", [128, 768], BF16)
            qTz = [sb1("qTz%d" % k, [128, 512], BF16) for k in range(2)]
            augq = [sb1("augq%d" % k, [96, 512], BF16) for k in range(2)]
            qb0 = sb1("qb0", [1, 2, 512], BF16)
            PTs = [sb1("PT%d" % i, [128, 512], BF16) for i in range(2)]
            oacc = sb1("oacc", [128, 512])
            sg = sb1("sg", [128, 24]); coef = sb1("coef", [128, 8]); rcs = sb1("rcs", [128, 8])
            impn = sb1("impn", [128, 2, 128])
            score = sb1("score", [128, 2, 128]); sc2 = EqT
            m8 = sb1("m8", [128, 16]); selpos = sb1("selpos", [128, 2, 160], BF16)
            vma = sb1("vma", [128, 2, 2, 64])
            hid = Eout; hidb = sb1("hidb", [128, 256], BF16)
            hidT = sb1("hidT", [128, 2, 128], BF16)
            vcn = sb1("vcn", [128, 64], BF16)

            for k_ in range(2):
                P.pool(lambda e, k_=k_: e.memset(qTz[k_][:], 0.0), writes=[qTz[k_]])
                P.pool(lambda e, k_=k_: e.memset(qz[k_][:], 0.0), writes=[qz[k_]])
                P.pool(lambda e, k_=k_: e.memset(kz[k_][:], 0.0), writes=[kz[k_]])
            P.pool(lambda e: e.memset(selpos[:], 0.0), writes=[selpos])
            sbank = [pb[2], pb[3], pb[4]]
            obank = [pb[5], pb[6]]
            ucount = [0]

            pending = [None]

            def unit_flush():
                if pending[0] is not None:
                    f = pending[0]
                    pending[0] = None
                    f()

            def unit(mms, nq, vT_ap, vbufs, ob, ocols, first, last, keep=None):
                sbk = sbank[ucount[0] % 3]
                pt = keep if keep is not None else PTs[ucount[0] % 2]
                ucount[0] += 1
                n = len(mms)
                for mi, (l_ap, r_ap, rd, cols) in enumerate(mms):
                    c0, cn = cols
                    P.pe(lambda e, l_ap=l_ap, r_ap=r_ap, mi=mi, c0=c0, cn=cn: e.matmul(
                        sbk[:, c0:c0 + cn], lhsT=l_ap, rhs=r_ap, start=(mi == 0), stop=(mi == n - 1)),
                        reads=rd, writes=[sbk])
                unit_flush()
                P.act(lambda e: e.activation(out=pt[:, 0:nq], in_=sbk[:, 0:nq], func=AF.Exp), reads=[sbk], writes=[pt])

                def pv():
                    P.pe(lambda e: e.matmul(ob[0:65, ocols[0]:ocols[0] + nq], lhsT=vT_ap, rhs=pt[:, 0:nq], start=first, stop=last),
                         reads=[pt] + vbufs, writes=[ob])
                pending[0] = pv
                return pt

            def branch_epilogue(br, samp, first_branch):
                unit_flush()
                for kvh in range(2):
                    if not samp and kvh == 0:
                        P.act(lambda e, kvh=kvh: e.activation(out=tmpf[0:65, kvh * 512:(kvh + 1) * 512], in_=obank[kvh][0:65, :],
                                                              func=AF.Copy), reads=[obank[kvh]], writes=[tmpf])
                    elif not samp:
                        P.dve(lambda e, kvh=kvh: e.tensor_copy(out=tmpf[0:65, kvh * 512:(kvh + 1) * 512], in_=obank[kvh][0:65, :]),
                              reads=[obank[kvh]], writes=[tmpf])
                for kvh in range(2):
                    ob = pb[kvh]
                    for g in range(4):
                        src = tmpf[0:65, kvh * 512 + g * 128:kvh * 512 + (g + 1) * 128]
                        P.pe(lambda e, src=src, g=g, ob=ob: e.transpose(out=ob[:, g * 65:(g + 1) * 65], in_=src,
                                                                         identity=identf[0:65, 0:65]),
                             reads=[tmpf, identf], writes=[ob])
                    ov = ob[:, 0:260].rearrange("p (g c) -> p g c", g=4)
                    P.dve(lambda e, ov=ov, kvh=kvh: e.tensor_scalar(out=rcs[:, kvh * 4:(kvh + 1) * 4], in0=ov[:, :, 64],
                                                                    scalar1=1e-30, scalar2=None, op0=ALU.max),
                          reads=[ob], writes=[rcs])
                P.dve(lambda e: e.reciprocal(out=rcs[:], in_=rcs[:]), reads=[rcs], writes=[rcs])
                sgv = sg[:].rearrange("p (h b) -> p h b", b=3)
                P.dve(lambda e: e.tensor_tensor(out=coef[:], in0=rcs[:], in1=sgv[:, :, br], op=ALU.mult),
                      reads=[rcs, sg], writes=[coef])
                for kvh in range(2):
                    ob = pb[kvh]
                    for g in range(4):
                        h = kvh * 4 + g
                        if first_branch:
                            P.dve(lambda e, h=h, g=g, ob=ob: e.tensor_scalar(out=oacc[:, h * 64:(h + 1) * 64],
                                                                             in0=ob[:, g * 65:g * 65 + 64],
                                                                             scalar1=coef[:, h:h + 1], scalar2=None, op0=ALU.mult),
                                  reads=[ob, coef], writes=[oacc])
                        else:
                            P.dve(lambda e, h=h, g=g, ob=ob: e.scalar_tensor_tensor(
                                out=oacc[:, h * 64:(h + 1) * 64], in0=ob[:, g * 65:g * 65 + 64], scalar=coef[:, h:h + 1],
                                in1=oacc[:, h * 64:(h + 1) * 64], op0=ALU.mult, op1=ALU.add),
                                reads=[ob, coef, oacc], writes=[oacc])

            def select_blocks(ns, vm_ap, am_ap, vmbufs, kth, samp):
                for kvh in range(2):
                    ob = obank[kvh]
                    for g in range(4):
                        src = z[0:ns, kvh * 512 + g * 128:kvh * 512 + (g + 1) * 128]
                        P.pe(lambda e, src=src, g=g, ob=ob: e.transpose(out=ob[:, g * 128:g * 128 + ns], in_=src,
                                                                         identity=identf[0:ns, 0:ns]),
                             reads=[z, identf], writes=[ob])
                    for g in range(4):
                        h = kvh * 4 + g
                        if g == 0:
                            P.dve(lambda e, h=h, g=g, ob=ob, kvh=kvh: e.tensor_scalar(
                                out=impn[:, kvh, 0:ns], in0=ob[:, g * 128:g * 128 + ns], scalar1=rcs[:, h:h + 1], scalar2=None,
                                op0=ALU.mult), reads=[ob, rcs], writes=[impn])
                        else:
                            P.dve(lambda e, h=h, g=g, ob=ob, kvh=kvh: e.scalar_tensor_tensor(
                                out=impn[:, kvh, 0:ns], in0=ob[:, g * 128:g * 128 + ns], scalar=rcs[:, h:h + 1],
                                in1=impn[:, kvh, 0:ns], op0=ALU.mult, op1=ALU.add), reads=[ob, rcs, impn], writes=[impn])
                for kvh in range(2):
                    P.dve(lambda e, kvh=kvh: e.tensor_tensor(out=score[:, kvh, 0:ns], in0=impn[:, kvh, 0:ns], in1=vm_ap(kvh),
                                                             op=ALU.mult), reads=[impn] + vmbufs, writes=[score])
                    P.dve(lambda e, kvh=kvh: e.tensor_tensor(out=score[:, kvh, 0:ns], in0=score[:, kvh, 0:ns], in1=am_ap(kvh),
                                                             op=ALU.add), reads=[score] + vmbufs, writes=[score])
                    P.dve(lambda e, kvh=kvh: e.max(out=m8[:, 0:8], in_=score[:, kvh, 0:ns]), reads=[score], writes=[m8])
                    P.dve(lambda e, kvh=kvh: e.match_replace(out=sc2[:, 0:ns], in_to_replace=m8[:, 0:8],
                                                             in_values=score[:, kvh, 0:ns], imm_value=-3.0e4),
                          reads=[score, m8], writes=[sc2])
                    P.dve(lambda e: e.max(out=m8[:, 8:16], in_=sc2[:, 0:ns]), reads=[sc2], writes=[m8])
                    P.dve(lambda e, kvh=kvh: e.tensor_scalar(out=selpos[:, kvh, 32:32 + ns], in0=score[:, kvh, 0:ns],
                                                             scalar1=m8[:, 8 + kth:9 + kth], scalar2=NEG, op0=ALU.is_ge,
                                                             op1=ALU.mult), reads=[score, m8], writes=[selpos])

            def build_augq(aq, kvh, blk0, samp):
                P.pe(lambda e: e.transpose(out=pT[0:96, 0:128], in_=selpos[:, kvh, blk0:blk0 + 96], identity=identb[:]),
                     reads=[selpos, identb], writes=[pT])
                for (p0, p1) in ((32, 64), (64, 96)):
                    if not samp:
                        for g in range(4):
                            P.dve(lambda e, g=g, p0=p0, p1=p1: e.tensor_copy(out=aq[p0:p1, g * 128:(g + 1) * 128], in_=pT[p0:p1, 0:128]),
                                  reads=[pT], writes=[aq])
                    else:
                        av = aq[p0:p1, :].rearrange("p (s g q) -> p s g q", s=16, g=4)
                        pv = pT[p0:p1, 0:128].rearrange("p (s q) -> p s q", s=16)
                        for g in range(4):
                            P.dve(lambda e, g=g, av=av, pv=pv: e.tensor_copy(out=av[:, :, g, :], in_=pv), reads=[pT], writes=[aq])

            def compress(Y, nb, kcT, kc_col0, vcs, c0, skip_first):
                R1 = 32 if nb <= 32 else 64
                NR = R1 + nb
                for ty in range(2):
                    pbk = pb[ty]
                    for kvh in range(2):
                        n = 0
                        for hf in range(2):
                            for j2 in range(8):
                                base = 8 * hf + j2
                                lap = Y[:, ty, kvh, base:base + 8 * (nb - 1) + 1:8]
                                P.pe(lambda e, lap=lap, ty=ty, hf=hf, j2=j2, pbk=pbk, n=n, kvh=kvh: e.matmul(
                                    pbk[kvh * R1:kvh * R1 + nb, 0:256], lhsT=lap, rhs=w1b[:, ty, hf, j2, :],
                                    start=(n == 0), stop=(n == 15)), reads=[Y, w1b], writes=[pbk])
                                n += 1
                    P.dve(lambda e, ty=ty, pbk=pbk: e.tensor_tensor(out=hid[0:NR, :], in0=pbk[0:NR, 0:256],
                                                                    in1=pbias[0:NR, ty, :], op=ALU.add),
                          reads=[pbk, pbias], writes=[hid])
                    P.act(lambda e: e.activation(out=hidb[0:NR, :], in_=hid[0:NR, :], func=AF.Silu),
                          reads=[hid], writes=[hidb])
                    for fc in range(2):
                        P.pe(lambda e, fc=fc: e.transpose(out=pT[:, fc * 128:fc * 128 + NR],
                                                          in_=hidb[0:NR, fc * 128:(fc + 1) * 128],
                                                          identity=identb[0:NR, 0:NR]),
                             reads=[hidb, identb], writes=[pT])
                    P.act(lambda e: e.activation(out=hidT[:, :, 0:NR],
                                                 in_=pT[:, 0:256].rearrange("p (a b) -> p a b", a=2)[:, :, 0:NR],
                                                 func=AF.Copy), reads=[pT], writes=[hidT])
                    if ty == 0:
                        pk = pb[1]
                        for fc in range(2):
                            P.pe(lambda e, fc=fc, pk=pk: e.matmul(pk[:, 256:256 + NR], lhsT=w2dup[:, fc, :],
                                                                   rhs=hidT[:, fc, 0:NR], start=(fc == 0), stop=(fc == 1)),
                                 reads=[w2dup, hidT], writes=[pk])
                        for kvh in range(2):
                            r = slice(kvh * 64, kvh * 64 + 64)
                            P.act(lambda e, kvh=kvh, r=r, pk=pk: e.activation(
                                out=kcT[r, kc_col0:kc_col0 + nb], in_=pk[r, 256 + kvh * R1:256 + kvh * R1 + nb],
                                func=AF.Identity, bias=b2kt[r, 0:1]), reads=[pk, b2kt], writes=[kcT])
                    else:
                        pk = pb[0]
                        for fc in range(2):
                            P.pe(lambda e, fc=fc, pk=pk: e.matmul(pk[0:NR, 256:320], lhsT=hidT[:, fc, 0:NR],
                                                                   rhs=w2b[:, 1, fc, :], start=(fc == 0), stop=(fc == 1)),
                                 reads=[w2b, hidT], writes=[pk])
                        P.dve(lambda e, pk=pk: e.tensor_tensor(out=vcn[0:NR, :], in0=pk[0:NR, 256:320],
                                                               in1=b2vt[0:NR, :], op=ALU.add), reads=[pk, b2vt], writes=[vcn])
                        for kvh in range(2):
                            r0 = 1 if skip_first else 0
                            c = c0 + r0
                            while c < c0 + nb:
                                ce = min(c0 + nb, (c // 128 + 1) * 128)
                                P.dma("sp", lambda e, kvh=kvh, c=c, ce=ce: e.dma_start(
                                    out=vcs[c % 128:c % 128 + (ce - c), c // 128, kvh, 0:64],
                                    in_=vcn[kvh * R1 + (c - c0):kvh * R1 + (ce - c0), :]), reads=[vcn], writes=[vcs], sembuf=vcs)
                                c = ce

            def front(ti, x, kind):
                ld("sp", x, x[:], xs[ti * 128:(ti + 1) * 128, :])
                rms_rstd(x[:], [x], hbf[:], hbf, st4, 0, D)
                P.dve(lambda e: e.scalar_tensor_tensor(out=tmpf[:], in0=x[:], scalar=st4[:, 1:2], in1=MOD[:, D:2 * D],
                                                       op0=ALU.mult, op1=ALU.mult), reads=[x, st4, MOD], writes=[tmpf])
                P.dve(lambda e: e.tensor_tensor(out=hbf[:], in0=tmpf[:], in1=MOD[:, 0:D], op=ALU.add),
                      reads=[tmpf, MOD], writes=[hbf])
                transpose8(hbf, hT)
                zch = [(0, 512), (512, 512), (1024, 512), (1536, 512), (2048, 512), (2560, 296)]
                for ci, (c0, cn) in enumerate(zch):
                    pbk = pb[ci % 2]
                    for kc in range(8):
                        P.pe(lambda e, kc=kc, pbk=pbk, c0=c0, cn=cn: e.matmul(
                            pbk[:, 0:cn], lhsT=hT[:, kc * 128:(kc + 1) * 128], rhs=win_bf[:, kc, c0:c0 + cn],
                            start=(kc == 0), stop=(kc == 7)), reads=[hT, win_bf], writes=[pbk])
                    if ci % 2 == 0:
                        P.dve(lambda e, pbk=pbk, c0=c0, cn=cn: e.tensor_copy(out=z[:, c0:c0 + cn], in_=pbk[:, 0:cn]),
                              reads=[pbk], writes=[z])
                    else:
                        P.act(lambda e, pbk=pbk, c0=c0, cn=cn: e.activation(out=z[:, c0:c0 + cn], in_=pbk[:, 0:cn], func=AF.Copy),
                              reads=[pbk], writes=[z])
                P.dma("sp", lambda e: e.dma_start(out=kvr[ti * 128:(ti + 1) * 128, :], in_=z[:, C_KV:C_KV + 512]),
                      reads=[z], writes=[dkvr], sembuf=dkvr)
                P.act(lambda e: e.activation(out=qnbf[:].rearrange("p (g k d) -> p k g d", g=4, k=2),
                                             in_=z[:, C_QN:C_QN + 512].rearrange("p (k g d) -> p k g d", k=2, g=4),
                                             func=AF.Copy, scale=0.125), reads=[z], writes=[qnbf])
                P.pool(lambda e: e.tensor_copy(out=kvbf[:], in_=z[:, C_KV:C_KV + 768]), reads=[z], writes=[kvbf])
                P.act(lambda e: e.activation(out=sg[:], in_=z[:, C_ZG:C_ZG + 24], func=AF.Sigmoid), reads=[z], writes=[sg])
                for g in range(4):
                    P.pe(lambda e, g=g: e.transpose(out=pT[:, g * 128:(g + 1) * 128], in_=qnbf[:, g * 128:(g + 1) * 128], identity=identb[:]),
                         reads=[qnbf, identb], writes=[pT])
                for kvh in range(2):
                    r = slice(kvh * 64, kvh * 64 + 64)
                    P.dve(lambda e, kvh=kvh, r=r: e.tensor_copy(out=qTz[kvh][r, :], in_=pT[r, 0:512]), reads=[pT], writes=[qTz[kvh]])

            def gla(sample, it, kind):
                tk = 1 if sample else 0
                pa, pb_, pc, pd = pb[2], pb[3], pb[4], pb[5]
                P.pe(lambda e: e.transpose(out=pa[0:16, 0:128], in_=z[:, C_ZA:C_ZA + 16], identity=identf[:]),
                     reads=[z, identf], writes=[pa])
                P.act(lambda e: e.activation(out=zaT[:], in_=pa[0:16, 0:128], func=AF.Copy), reads=[pa], writes=[zaT])
                P.pe(lambda e: e.matmul(pb_[:, 0:256], lhsT=zaT[:], rhs=wgt[:], start=True, stop=False),
                     reads=[zaT, wgt], writes=[pb_])
                P.pe(lambda e: e.matmul(pb_[:, 0:256], lhsT=ones1[:], rhs=bgt[:], start=False, stop=True),
                     reads=[ones1, bgt], writes=[pb_])
                P.act(lambda e: e.activation(out=Lt[:], in_=pb_[:, 0:256], func=AF.Exp, scale=-1.0), reads=[pb_], writes=[Lt])
                P.act(lambda e: e.activation(out=Lt[:], in_=Lt[:], func=AF.Ln, bias=1.0), reads=[Lt], writes=[Lt])
                P.pe(lambda e: e.matmul(pc[:, 0:256], lhsT=trit[:, tk, 0, :], rhs=Lt[:], start=True, stop=True),
                     reads=[trit, Lt], writes=[pc])
                for fc in range(2):
                    P.pe(lambda e, fc=fc: e.matmul(pd[:, fc * 16:fc * 16 + 16], lhsT=Lt[:, fc * 128:(fc + 1) * 128],
                                                   rhs=seqi[:, tk, :], start=True, stop=True), reads=[Lt, seqi], writes=[pd])
                    P.pe(lambda e, fc=fc: e.matmul(pd[:, 128 + fc * 128:256 + fc * 128], lhsT=Lt[:, fc * 128:(fc + 1) * 128],
                                                   rhs=trit[:, tk, 1, :], start=True, stop=True), reads=[Lt, trit], writes=[pd])
                P.act(lambda e: e.activation(out=Eout[:], in_=pc[:, 0:256], func=AF.Exp), reads=[pc], writes=[Eout])
                P.act(lambda e: e.activation(out=dec[:].rearrange("p a b -> p (a b)"), in_=pd[:, 0:32], func=AF.Exp),
                      reads=[pd], writes=[dec])
                P.act(lambda e: e.activation(out=EqT[:], in_=pd[:, 128:384], func=AF.Exp), reads=[pd], writes=[EqT])
                P.act(lambda e: e.activation(out=EkT[:], in_=pd[:, 128:384], func=AF.Exp, scale=-1.0), reads=[pd], writes=[EkT])
                P.dve(lambda e: e.tensor_tensor(out=kout[:], in0=z[:, C_ZK:C_ZK + 256], in1=Eout[:], op=ALU.mult),
                      reads=[z, Eout], writes=[kout])
                P.pool(lambda e: e.tensor_copy(out=vbf[:], in_=z[:, C_ZV:C_ZV + 512]), reads=[z], writes=[vbf])
                for a in range(4):
                    c0 = (C_ZQ if a < 2 else C_ZK) + (a % 2) * 128
                    P.pe(lambda e, a=a, c0=c0: e.transpose(out=pa[:, a * 128:(a + 1) * 128], in_=z[:, c0:c0 + 128],
                                                           identity=identf[:]), reads=[z, identf], writes=[pa])
                P.dve(lambda e: e.scalar_tensor_tensor(out=qinT[:], in0=pa[:, 0:256], scalar=0.125, in1=EqT[:],
                                                       op0=ALU.mult, op1=ALU.mult), reads=[pa, EqT], writes=[qinT])
                for hh in range(2):
                    r = slice(hh * 64, hh * 64 + 64)
                    P.dve(lambda e, hh=hh, r=r: e.tensor_tensor(out=kz[hh][r, :], in0=pa[r, 256:512], in1=EkT[r, :], op=ALU.mult),
                          reads=[pa, EkT], writes=[kz[hh]])
                    P.pool(lambda e, hh=hh, r=r: e.tensor_copy(out=qz[hh][r, :], in_=qinT[r, :]), reads=[qinT], writes=[qz[hh]])
                for h in range(4):
                    r = slice((h % 2) * 64, (h % 2) * 64 + 64)
                    fc = h // 2
                    P.pe(lambda e, h=h, r=r, fc=fc: e.matmul(pb_[:, h * 128:(h + 1) * 128], lhsT=kz[h % 2][:, fc * 128:(fc + 1) * 128],
                                                             rhs=qinT[:, fc * 128:(fc + 1) * 128], start=True, stop=True),
                         reads=[kz[h % 2], qinT], writes=[pb_])
                P.dve(lambda e: e.tensor_tensor(out=ATs[:], in0=pb_[:, :], in1=causmt[:, tk, :], op=ALU.mult),
                      reads=[pb_, causmt], writes=[ATs])
                po = pc
                if not sample:
                    if it == 0:
                        P.dve(lambda e: e.memset(Sf[:], 0.0), writes=[Sf])
                        P.dve(lambda e: e.memset(Sbf[:], 0.0), writes=[Sbf])
                    for h in range(4):
                        r = slice((h % 2) * 64, (h % 2) * 64 + 64)
                        fc = h // 2
                        P.pe(lambda e, h=h, r=r, fc=fc: e.matmul(po[:, h * 128:(h + 1) * 128], lhsT=qz[h % 2][:, fc * 128:(fc + 1) * 128],
                                                                 rhs=Sbf[:, fc, :], start=True, stop=False),
                             reads=[qz[h % 2], Sbf], writes=[po])
                        P.pe(lambda e, h=h: e.matmul(po[:, h * 128:(h + 1) * 128], lhsT=ATs[:, h * 128:(h + 1) * 128],
                                                     rhs=vbf[:, h * 128:(h + 1) * 128], start=False, stop=True),
                             reads=[ATs, vbf], writes=[po])
                else:
                    for s in range(SB_):
                        S = SX["Ssm"][s % 2]
                        ld("sp", S, S[:], sgla[s].rearrange("(fc hh) k v -> (hh k) fc v", hh=2))
                        P.dve(lambda e, s=s, S=S: e.tensor_copy(out=SX["S0bf"][:, s, :, :], in_=S[:]), reads=[S], writes=[SX["S0bf"]])
                    for h in range(4):
                        r = slice((h % 2) * 64, (h % 2) * 64 + 64)
                        fc = h // 2
                        P.pe(lambda e, h=h: e.matmul(po[:, h * 128:(h + 1) * 128], lhsT=vbf[:, h * 128:(h + 1) * 128],
                                                     rhs=ATs[:, h * 128:(h + 1) * 128], start=True, stop=False),
                             reads=[ATs, vbf], writes=[po])
                        for s in range(SB_):
                            P.pe(lambda e, h=h, r=r, fc=fc, s=s: e.matmul(
                                po[:, h * 128 + s * TS:h * 128 + (s + 1) * TS], lhsT=SX["S0bf"][:, s, fc, :],
                                rhs=qz[h % 2][:, fc * 128 + s * TS:fc * 128 + (s + 1) * TS], start=False, stop=(s == SB_ - 1)),
                                reads=[SX["S0bf"], qz[h % 2]], writes=[po])
                    P.act(lambda e: e.activation(out=SX["ogT"][:, 0:512], in_=po[:, :], func=AF.Copy), reads=[po], writes=[SX["ogT"]])
                    for h in range(4):
                        P.pe(lambda e, h=h: e.transpose(out=po[:, h * 128:(h + 1) * 128], in_=SX["ogT"][:, h * 128:(h + 1) * 128],
                                                        identity=identf[:]), reads=[SX["ogT"], identf], writes=[po])
                for h in range(4):
                    P.act(lambda e, h=h: e.activation(out=junkb[:, 0:128], in_=po[:, h * 128:(h + 1) * 128], func=AF.Square,
                                                      accum_out=st4[:, 4 + h:5 + h]), reads=[po], writes=[st4, junkb])
                P.act(lambda e: e.activation(out=st4[:, 8:12], in_=st4[:, 4:8], func=AF.Sqrt, scale=1.0 / 128, bias=epsb[:]),
                      reads=[st4, epsb], writes=[st4])
                P.dve(lambda e: e.reciprocal(out=st4[:, 8:12], in_=st4[:, 8:12]), reads=[st4], writes=[st4])
                P.act(lambda e: e.activation(out=szr[:], in_=z[:, C_ZR:C_ZR + 512], func=AF.Sigmoid), reads=[z], writes=[szr])
                P.dve(lambda e: e.tensor_tensor(out=szr[:], in0=szr[:], in1=z[:, C_ZR:C_ZR + 512], op=ALU.mult),
                      reads=[szr, z], writes=[szr])
                P.pool(lambda e: e.tensor_tensor(out=szr[:], in0=szr[:], in1=gnb[:], op=ALU.mult), reads=[szr, gnb], writes=[szr])
                for h in range(4):
                    P.dve(lambda e, h=h: e.scalar_tensor_tensor(out=obf[:, h * 128:(h + 1) * 128], in0=po[:, h * 128:(h + 1) * 128],
                                                                scalar=st4[:, 8 + h:9 + h], in1=szr[:, h * 128:(h + 1) * 128],
                                                                op0=ALU.mult, op1=ALU.mult), reads=[po, st4, szr], writes=[obf])

                def state_update(S, kl, si, pu):
                    for fc in range(2):
                        P.pe(lambda e, fc=fc: e.matmul(pu[:, fc * 256:(fc + 1) * 256], lhsT=kl[:, fc * 128:(fc + 1) * 128],
                                                       rhs=vbf[:, fc * 256:(fc + 1) * 256], start=True, stop=True),
                             reads=[kl, vbf], writes=[pu])
                    for fc in range(2):
                        for hh in range(2):
                            r = slice(hh * 64, hh * 64 + 64)
                            P.dve(lambda e, fc=fc, hh=hh, r=r: e.scalar_tensor_tensor(
                                out=S[r, fc, :], in0=S[r, fc, :], scalar=dec[r, fc, si:si + 1],
                                in1=pu[r, fc * 256 + hh * 128:fc * 256 + hh * 128 + 128], op0=ALU.mult, op1=ALU.add),
                                reads=[S, dec, pu], writes=[S])

                if not sample:
                    state_update(Sf, kout, 0, pb[6])
                    P.pool(lambda e: e.tensor_copy(out=Sbf[:], in_=Sf[:]), reads=[Sf], writes=[Sbf])
                    if it == NT_P - 1:
                        P.dma("sp", lambda e: e.dma_start(out=gla_p[kind].rearrange("(fc hh) k v -> (hh k) fc v", hh=2),
                                                          in_=Sf[:]), reads=[Sf], writes=[dglap], sembuf=dglap)
                else:
                    for s in range(SB_):
                        S = SX["Ssm"][s % 2]
                        ld("sp", S, S[:], sgla[s].rearrange("(fc hh) k v -> (hh k) fc v", hh=2))
                        P.dve(lambda e, s=s: e.tensor_scalar(out=koutm[:], in0=kout[:], scalar1=seqm[:, s:s + 1], scalar2=None,
                                                             op0=ALU.mult), reads=[kout, seqm], writes=[koutm])
                        state_update(S, koutm, s, pb[5 + (s % 2)])
                        P.dma("sp", lambda e, s=s, S=S: e.dma_start(
                            out=gla_s[s].rearrange("(fc hh) k v -> (hh k) fc v", hh=2), in_=S[:]),
                            reads=[S], writes=[dglas], sembuf=dglas)

            def back(ti, x):
                P.dve(lambda e: e.tensor_copy(out=obf[:, 512:1024], in_=oacc[:]), reads=[oacc], writes=[obf])
                transpose8(obf, hT, evac="dve")
                for nh in range(2):
                    pbk = pb[nh]
                    for kc in range(8):
                        P.pe(lambda e, kc=kc, pbk=pbk, nh=nh: e.matmul(pbk[:, :], lhsT=hT[:, kc * 128:(kc + 1) * 128],
                                                                       rhs=wout_bf[:, kc, nh * 512:(nh + 1) * 512],
                                                                       start=(kc == 0), stop=(kc == 7)),
                             reads=[hT, wout_bf], writes=[pbk])
                    P.act(lambda e, pbk=pbk, nh=nh: e.activation(out=junkb[:, 0:512], in_=pbk[:, :], func=AF.Square,
                                                                 accum_out=st4[:, 12 + nh:13 + nh]), reads=[pbk], writes=[st4, junkb])
                P.dve(lambda e: e.tensor_tensor(out=st4[:, 14:15], in0=st4[:, 12:13], in1=st4[:, 13:14], op=ALU.add),
                      reads=[st4], writes=[st4])
                P.act(lambda e: e.activation(out=st4[:, 15:16], in_=st4[:, 14:15], func=AF.Sqrt, scale=1.0 / D, bias=epsb[:]),
                      reads=[st4, epsb], writes=[st4])
                P.dve(lambda e: e.reciprocal(out=st4[:, 15:16], in_=st4[:, 15:16]), reads=[st4], writes=[st4])
                for nh in range(2):
                    pbk = pb[nh]
                    P.dve(lambda e, pbk=pbk, nh=nh: e.scalar_tensor_tensor(
                        out=tmpf[:, nh * 512:(nh + 1) * 512], in0=pbk[:, :], scalar=st4[:, 15:16],
                        in1=MOD[:, 2 * D + nh * 512:2 * D + (nh + 1) * 512], op0=ALU.mult, op1=ALU.mult),
                        reads=[pbk, st4, MOD], writes=[tmpf])
                P.pool(lambda e: e.tensor_tensor(out=tmpf[:], in0=tmpf[:], in1=x[:], op=ALU.add), reads=[tmpf, x], writes=[tmpf])
                P.dma("sp", lambda e: e.dma_start(out=y_all[ti * 128:(ti + 1) * 128, :], in_=tmpf[:]),
                      reads=[tmpf], writes=[dy], sembuf=dy)

            for kind in range(CFG["kinds"]):
                with ExitStack() as SP:
                    sbp = mk(SP, "_p%d" % kind)
                    kslcT = sbp("kslcT", [128, SEQ], BF16); kwinT = sbp("kwinT", [128, 8 * 128], BF16)
                    Yp = sbp("Yp", [128, 2, 2, 8 + 64], BF16)
                    vslc = sbp("vslc", [128, NT_P, 2, 65], BF16); vwin = sbp("vwin", [128, 8, 2, 65], BF16)
                    ckt = sbp("ckt", [96, SEQ], BF16)
                    maskct = sbp("maskct", [128, 17, 128], BF16); mmpt = sbp("mmpt", [128, 2, 64], BF16)
                    caus4t = sbp("caus4t", [128, 2, 512], BF16)
                    ld("sp", ckt, ckt[:], ck[:, 0:SEQ])
                    ld("sp", maskct, maskct[:], maskc.rearrange("r p c -> p r c"))
                    ld("sp", mmpt, mmpt[:], mmap_p[:, :, :])
                    ld("sp", caus4t, caus4t[:], caus4.rearrange("k p c -> p k c"))
                    kcT = sbp("kcT", [128, 264], BF16); vcs = sbp("vcs", [128, 2, 2, 65], BF16)
                    PTc = [sbp("PTc%d" % i, [128, 512], BF16) for i in range(4)]
                    load_mod(MOD, 0, kind)
                    for kvh in range(2):
                        P.pool(lambda e, kvh=kvh: e.memset(augq[kvh][:], 0.0), writes=[augq[kvh]])
                        ld("sp", augq[kvh], augq[kvh][0:5, :], qab[0, kvh])
                        ld("sp", qb0, qb0[0:1, kvh, :], qab[0, kvh, 0:1, :])
                    P.pool(lambda e: e.memset(Yp[:], 0.0), writes=[Yp])
                    P.pool(lambda e: e.memset(kcT[:], 0.0), writes=[kcT])
                    P.pool(lambda e: e.memset(vcs[:], 0.0), writes=[vcs])
                    P.pool(lambda e: e.memset(vcs[:, :, :, 64:65], 1.0), writes=[vcs])
                    P.pool(lambda e: e.memset(vslc[:, :, :, 64:65], 1.0), writes=[vslc])
                    P.pool(lambda e: e.memset(vwin[:, :, :, 64:65], 1.0), writes=[vwin])
                    for it in range(CFG["n_ptiles"]):
                      try:
                          ti = kind * NT_P + it
                          x = xt[ti % 2]
                          front(ti, x, kind)
                          if it >= NT_P - 4:
                              r0 = (it - (NT_P - 4)) * 128
                              P.dma("sp", lambda e, r0=r0: e.dma_start(out=win_p[kind, r0:r0 + 128, :],
                                                                       in_=z[:, C_KV + 512:C_KV + 768]),
                                    reads=[z], writes=[dwinp], sembuf=dwinp)
                          chk(1)
                          gla(False, it, kind)
                          chk(2)
                          t0 = it * 128
                          P.pe(lambda e: e.transpose(out=pT[:, 512:640], in_=kvbf[:, 256:384], identity=identb[:]),
                               reads=[kvbf, identb], writes=[pT])
                          P.pe(lambda e: e.transpose(out=pT[:, 640:768], in_=kvbf[:, 512:640], identity=identb[:]),
                               reads=[kvbf, identb], writes=[pT])
                          P.dve(lambda e, t0=t0: e.tensor_copy(out=kslcT[:, t0:t0 + 128], in_=pT[:, 512:640]), reads=[pT], writes=[kslcT])
                          P.dve(lambda e, it=it: e.tensor_copy(out=kwinT[:, (it % 8) * 128:(it % 8 + 1) * 128], in_=pT[:, 640:768]), reads=[pT], writes=[kwinT])
                          P.pool(lambda e, it=it: e.tensor_copy(out=vslc[:, it, :, 0:64],
                                                                in_=kvbf[:, 384:512].rearrange("p (k d) -> p k d", k=2)),
                                 reads=[kvbf], writes=[vslc])
                          P.pool(lambda e, it=it: e.tensor_copy(out=vwin[:, it % 8, :, 0:64],
                                                                in_=kvbf[:, 640:768].rearrange("p (k d) -> p k d", k=2)),
                                 reads=[kvbf], writes=[vwin])
                          pY = pb[0]
                          for ty in range(2):
                              for kvh in range(2):
                                  for r in range(2):
                                      c0 = ty * 128 + kvh * 64
                                      P.pe(lambda e, ty=ty, kvh=kvh, r=r, c0=c0: e.matmul(
                                          pY[r * 64:(r + 1) * 64, (ty * 2 + kvh) * 64:(ty * 2 + kvh + 1) * 64],
                                          lhsT=kvbf[:, c0:c0 + 64], rhs=selpt[:, r, :], start=True, stop=True),
                                          reads=[kvbf, selpt], writes=[pY])
                          if it > 0:
                              P.dve(lambda e: e.tensor_copy(out=Yp[:, :, :, 0:8], in_=Yp[:, :, :, 64:72]), reads=[Yp], writes=[Yp])
                          P.act(lambda e, it=it: e.activation(out=Yp[:, :, :, 8:72],
                                                              in_=pY[:, 0:256].rearrange("p (a b c) -> p a b c", a=2, b=2),
                                                              func=AF.Copy), reads=[pY], writes=[Yp])
                          chk(3)
                          cb0 = 8 * it - 1
                          compress(Yp, 8, kcT, 1 + cb0, vcs, cb0, it == 0)
                          for kvh in range(2):
                              P.dve(lambda e, kvh=kvh, it=it: e.tensor_scalar(out=augq[kvh][0:1, :], in0=qb0[0:1, kvh, :],
                                                                              scalar1=float(it), scalar2=None, op0=ALU.mult),
                                    reads=[qb0], writes=[augq[kvh]])
                          ld("pool", vma, vma[:], vmam_p[it])
                          chk(4)
                          jl = (8 * it + 6) // 128
                          for kvh in range(2):
                              r = slice(kvh * 64, kvh * 64 + 64)
                              for jc in range(jl + 1):
                                  mms = [(kcT[:, 1 + jc * 128:1 + (jc + 1) * 128], qTz[kvh][:, :], [kcT, qTz[kvh]], (0, 512)),
                                         (ckct[0:4, jc * 128:(jc + 1) * 128], augq[kvh][0:4, :], [ckct, augq[kvh]], (0, 512))]
                                  rr = it - 16 * jc
                                  if rr <= 16:
                                      for g in range(4):
                                          mms.append((identb[:], maskct[:, rr, :], [identb, maskct], (g * 128, 128)))
                                  unit(mms, 512, vcs[:, jc, kvh, :], [vcs], obank[kvh], (0, 512), jc == 0, jc == jl,
                                       keep=PTc[kvh * 2 + jc])
                          for kvh in range(2):
                              pim = pb[kvh]
                              for jc in range(jl + 1):
                                  P.pe(lambda e, kvh=kvh, jc=jc, pim=pim: e.matmul(pim[0:64, :], lhsT=mmpt[:, jc, :],
                                                                                   rhs=PTc[kvh * 2 + jc][:, :], start=(jc == 0),
                                                                                   stop=(jc == jl)),
                                       reads=[mmpt, PTc[kvh * 2 + jc]], writes=[pim])
                              P.act(lambda e, kvh=kvh, pim=pim: e.activation(out=z[0:64, kvh * 512:(kvh + 1) * 512], in_=pim[0:64, :], func=AF.Copy),
                                    reads=[pim], writes=[z])
                          chk(5)
                          branch_epilogue(0, False, True)
                          chk(6)
                          select_blocks(64, lambda kvh: vma[:, 0, kvh, :], lambda kvh: vma[:, 1, kvh, :], [vma], 7, False)
                          for kvh in range(2):
                              build_augq(augq[kvh], kvh, 0, False)
                          chk(7)
                          for kvh in range(2):
                              r = slice(kvh * 64, kvh * 64 + 64)
                              for j in range(it + 1):
                                  mms = [(kslcT[:, j * 128:(j + 1) * 128], qTz[kvh][:, :], [kslcT, qTz[kvh]], (0, 512)),
                                         (ckt[0:96, j * 128:(j + 1) * 128], augq[kvh][0:96, :], [ckt, augq[kvh]], (0, 512))]
                                  if j == it:
                                      mms.append((identb[:], caus4t[:, 0, :], [identb, caus4t], (0, 512)))
                                  unit(mms, 512, vslc[:, j, kvh, :], [vslc], obank[kvh], (0, 512), j == 0, j == it)
                          branch_epilogue(1, False, False)
                          chk(8)
                          j0 = max(0, it - 4)
                          for kvh in range(2):
                              r = slice(kvh * 64, kvh * 64 + 64)
                              for j in range(j0, it + 1):
                                  mms = [(kwinT[:, (j % 8) * 128:(j % 8 + 1) * 128], qTz[kvh][:, :], [kwinT, qTz[kvh]], (0, 512)),
                                         (ckt[0:4, j * 128:(j + 1) * 128], augq[kvh][0:4, :], [ckt, augq[kvh]], (0, 512))]
                                  if j == it:
                                      mms.append((identb[:], caus4t[:, 0, :], [identb, caus4t], (0, 512)))
                                  if j == it - 4:
                                      mms.append((identb[:], caus4t[:, 1, :], [identb, caus4t], (0, 512)))
                                  unit(mms, 512, vwin[:, j % 8, kvh, :], [vwin], obank[kvh], (0, 512), j == j0, j == it)
                          branch_epilogue(2, False, False)
                          chk(9)
                          back(ti, x)
                      except _Stop:
                          pass
                    P.flush()

            with ExitStack() as SS:
              if CFG["sample"]:
                sbs = mk(SS)
                Ys = sbs("Ys", [128, 2, 2, 8 + 512], BF16)
                kcTs = sbs("kcTs", [128, 520], BF16); vcss = sbs("vcss", [128, 4, 2, 65], BF16)
                ckt = sbs("ckt", [96, NKCOL], BF16)
                maskcst = sbs("maskcst", [128, 32], BF16); mmst = sbs("mmst", [128, 4, 128], BF16)
                newmt = sbs("newmt", [128, 16, 32], BF16); wmst = sbs("wmst", [128, 32], BF16)
                augq2 = [sbs("augq2_%d" % k, [96, 512], BF16) for k in range(2)]
                SX["Ssm"] = [Sf, sbs("Ssm1", [128, 2, 128])]
                SX["S0bf"] = sbs("S0bf", [128, 16, 2, 128], BF16)
                SX["ogT"] = tmpf
                ld("sp", ckt, ckt[:], ck[:, :])
                ld("sp", maskcst, maskcst[:], maskcs[:, :])
                ld("sp", mmst, mmst[:], mmap_s[:, :, :])
                ld("sp", newmt, newmt[:], newmask[:, :, :])
                ld("sp", wmst, wmst[:], wmasks[:, :])
                pgf = [sbs("pgf%d" % i, [128, 256]) for i in range(2)]
                pgb = [sbs("pgb%d" % i, [128, 256], BF16) for i in range(2)]
                kTp = [sbs("kTp%d" % i, [128, 128], BF16) for i in range(2)]
                vpg = [sbs("vpg%d" % i, [128, 2, 65], BF16) for i in range(2)]
                ptt = sbs("ptt", [128, NPG], I32); idxt = sbs("idxt", [128, NPG], I32); iot = sbs("iot", [128, 1], I32)
                idxt2 = sbs("idxt2", [128, NPG], I32)
                kTn = sbs("kTn", [128, 2, 128], BF16)
                vnw = sbs("vnw", [128, 2, 2, 65], BF16)
                PTk = [sbs("PTk%d" % i, [128, 32], BF16) for i in range(8)]
                ti = PB * NT_P
                x = xt[ti % 2]
                load_mod(MOD, 0, 2)
                for kvh in range(2):
                    P.pool(lambda e, kvh=kvh: e.memset(augq[kvh][:], 0.0), writes=[augq[kvh]])
                    P.pool(lambda e, kvh=kvh: e.memset(augq2[kvh][:], 0.0), writes=[augq2[kvh]])
                    ld("sp", augq[kvh], augq[kvh][0:5, :], qab[1, kvh])
                    ld("sp", augq2[kvh], augq2[kvh][0:5, :], qab[1, kvh])
                front(ti, x, 2)
                for s in range(SB_):
                    P.dma("sp", lambda e, s=s: e.dma_start(out=win_s[s, WBUF - TS:WBUF, :],
                                                           in_=z[s * TS:(s + 1) * TS, C_KV + 512:C_KV + 768]),
                          reads=[z], writes=[dwins], sembuf=dwins)
                P.dma("pool", lambda e: e.dma_start(out=win_s[:, 0:WBUF - TS, :].rearrange("s t f -> s (t f)"),
                                                    in_=swin[:, TS:WBUF, :].rearrange("s t f -> s (t f)")),
                      writes=[dwins], sembuf=dwins)
                gla(True, 0, 2)
                P.pe(lambda e: e.transpose(out=pT[:, 512:640], in_=kvbf[:, 256:384], identity=identb[:]),
                     reads=[kvbf, identb], writes=[pT])
                P.pe(lambda e: e.transpose(out=pT[:, 640:768], in_=kvbf[:, 512:640], identity=identb[:]),
                     reads=[kvbf, identb], writes=[pT])
                P.dve(lambda e: e.tensor_copy(out=kTn[:].rearrange("p a b -> p (a b)"), in_=pT[:, 512:768]), reads=[pT], writes=[kTn])
                P.pool(lambda e: e.memset(vnw[:, :, :, 64:65], 1.0), writes=[vnw])
                P.pool(lambda e: e.tensor_copy(out=vnw[:, 0, :, 0:64], in_=kvbf[:, 384:512].rearrange("p (k d) -> p k d", k=2)),
                       reads=[kvbf], writes=[vnw])
                P.pool(lambda e: e.tensor_copy(out=vnw[:, 1, :, 0:64], in_=kvbf[:, 640:768].rearrange("p (k d) -> p k d", k=2)),
                       reads=[kvbf], writes=[vnw])
                P.pool(lambda e: e.iota(iot[:], pattern=[[0, 1]], base=0, channel_multiplier=2), writes=[iot])
                for kvh in range(2):
                    P.pool(lambda e, kvh=kvh: e.memset(vpg[kvh][:, :, 64:65], 1.0), writes=[vpg[kvh]])
                qTsz = [qnbf, kvbf]
                for kvh in range(2):
                    r = slice(kvh * 64, kvh * 64 + 64)
                    ro = slice((1 - kvh) * 64, (1 - kvh) * 64 + 64)
                    P.dve(lambda e, kvh=kvh, ro=ro: e.memset(qTsz[kvh][ro, 0:512], 0.0), writes=[qTsz[kvh]])
                    P.dve(lambda e, kvh=kvh, r=r: e.tensor_copy(
                        out=qTsz[kvh][r, 0:512].rearrange("p (s g q) -> p g s q", s=16, g=4),
                        in_=qTz[kvh][r, :].rearrange("p (g s q) -> p g s q", g=4, s=16)), reads=[qTz[kvh]], writes=[qTsz[kvh]])
                P.pool(lambda e: e.memset(vcss[:], 0.0), writes=[vcss])
                P.pool(lambda e: e.memset(vcss[:, :, :, 64:65], 1.0), writes=[vcss])
                P.pool(lambda e: e.memset(kcTs[:], 0.0), writes=[kcTs])
                P.pool(lambda e: e.memset(Ys[:], 0.0), writes=[Ys])
                npg = [0]

                def evac_seq(kvh, s):
                    unit_flush()
                    P.act(lambda e: e.activation(
                        out=tmpf[0:65, kvh * 512:(kvh + 1) * 512].rearrange("p (g s q) -> p s g q", g=4, s=16)[:, s, :, :],
                        in_=obank[kvh][0:65, s * 32:(s + 1) * 32].rearrange("p (g q) -> p g q", g=4),
                        func=AF.Copy), reads=[obank[kvh]], writes=[tmpf])

                def gather(s, j, col0):
                    pf = pgf[npg[0] % 2]
                    pbf = pgb[npg[0] % 2]
                    npg[0] += 1
                    P.dma("pool", lambda e: e.indirect_dma_start(
                        out=pf[:], out_offset=None, in_=cache[:, :],
                        in_offset=bass.IndirectOffsetOnAxis(ap=(idxt if col0 == 0 else idxt2)[:, j:j + 1], axis=0)),
                        reads=[idxt, idxt2], writes=[pf], sembuf=pf)
                    P.dve(lambda e: e.tensor_copy(out=pbf[:], in_=pf[:]), reads=[pf], writes=[pbf])
                    return pbf

                for s in range(SB_):
                    qc = (s * 32, 32)
                    ld("sp", ptt, ptt[:], ptab[s:s + 1, :].partition_broadcast(128))
                    P.dve(lambda e: e.tensor_scalar(out=idxt[:], in0=ptt[:], scalar1=256, scalar2=iot[:, 0:1], op0=ALU.mult,
                                                    op1=ALU.add), reads=[ptt, iot], writes=[idxt])
                    P.dve(lambda e: e.tensor_scalar(out=idxt2[:], in0=idxt[:], scalar1=1, scalar2=None, op0=ALU.add),
                          reads=[idxt], writes=[idxt2])
                    for j in range(NPG):
                        pbf = gather(s, j, 0)
                        pY = pb[2 + (j % 2)]
                        for ty in range(2):
                            for kvh in range(2):
                                for r in range(2):
                                    c0 = ty * 128 + kvh * 64
                                    P.pe(lambda e, ty=ty, kvh=kvh, r=r, c0=c0, pbf=pbf, pY=pY: e.matmul(
                                        pY[r * 64:(r + 1) * 64, (ty * 2 + kvh) * 64:(ty * 2 + kvh + 1) * 64],
                                        lhsT=pbf[:, c0:c0 + 64], rhs=selpt[:, r, :], start=True, stop=True),
                                        reads=[pbf, selpt], writes=[pY])
                        P.act(lambda e, j=j, pY=pY: e.activation(out=Ys[:, :, :, 8 + (j % 8) * 64:8 + (j % 8 + 1) * 64],
                                                          in_=pY[:, 0:256].rearrange("p (a b c) -> p a b c", a=2, b=2),
                                                          func=AF.Copy), reads=[pY], writes=[Ys])
                        if j % 8 == 7:
                            m = j // 8
                            compress(Ys, 64, kcTs, 64 * m, vcss, 64 * m - 1, m == 0)
                            P.dve(lambda e: e.tensor_copy(out=Ys[:, :, :, 0:8], in_=Ys[:, :, :, 512:520]), reads=[Ys], writes=[Ys])
                    for kvh in range(2):
                        r = slice(kvh * 64, kvh * 64 + 64)
                        for jc in range(4):
                            mms = [(kcTs[:, 1 + jc * 128:1 + (jc + 1) * 128], qTsz[kvh][:, s * 32:(s + 1) * 32], [kcTs, qTsz[kvh]], (0, 32)),
                                   (ckct[0:4, jc * 128:(jc + 1) * 128], augq[kvh][0:4, qc[0]:qc[0] + 32], [ckct, augq[kvh]], (0, 32))]
                            if jc == 3:
                                mms.append((identb[:], maskcst[:, :], [identb, maskcst], (0, 32)))
                            unit(mms, 32, vcss[:, jc, kvh, :], [vcss], obank[kvh], qc, jc == 0, jc == 3, keep=PTk[kvh * 4 + jc])
                        evac_seq(kvh, s)
                        pim = pb[kvh]
                        for jc in range(4):
                            P.pe(lambda e, kvh=kvh, jc=jc, pim=pim, qc=qc: e.matmul(pim[:, qc[0]:qc[0] + 32], lhsT=mmst[:, jc, :],
                                                                             rhs=PTk[kvh * 4 + jc][:, :], start=(jc == 0), stop=(jc == 3)),
                                 reads=[mmst, PTk[kvh * 4 + jc]], writes=[pim])
                        P.act(lambda e, kvh=kvh, pim=pim, s=s: e.activation(
                            out=z[:, kvh * 512:(kvh + 1) * 512].rearrange("p (g s q) -> p s g q", g=4, s=16)[:, s, :, :],
                            in_=pim[:, s * 32:(s + 1) * 32].rearrange("p (g q) -> p g q", g=4),
                            func=AF.Copy), reads=[pim], writes=[z])
                branch_epilogue(0, True, True)
                ld("sp", z, z[:, 2048:2560], vmam_s.rearrange("p a b c -> p (a b c)"))
                select_blocks(128, lambda kvh: z[:, 2048 + kvh * 128:2048 + (kvh + 1) * 128],
                              lambda kvh: z[:, 2304 + kvh * 128:2304 + (kvh + 1) * 128], [z], 6, True)
                for kvh in range(2):
                    build_augq(augq[kvh], kvh, 0, True)
                    build_augq(augq2[kvh], kvh, 64, True)
                for s in range(SB_):
                    qc = (s * 32, 32)
                    ld("sp", ptt, ptt[:], ptab[s:s + 1, :].partition_broadcast(128))
                    P.dve(lambda e: e.tensor_scalar(out=idxt[:], in0=ptt[:], scalar1=256, scalar2=iot[:, 0:1], op0=ALU.mult,
                                                    op1=ALU.add), reads=[ptt, iot], writes=[idxt])
                    P.dve(lambda e: e.tensor_scalar(out=idxt2[:], in0=idxt[:], scalar1=1, scalar2=None, op0=ALU.add),
                          reads=[idxt], writes=[idxt2])
                    for j in range(NPG + 1):
                        if j < NPG:
                            pbf = gather(s, j, 256)
                            kt = kTp[j % 2]; vp = vpg[j % 2]
                            P.pe(lambda e, pbf=pbf: e.transpose(out=pT[:, 768:896], in_=pbf[:, 0:128], identity=identb[:]),
                                 reads=[pbf, identb], writes=[pT])
                            P.act(lambda e, kt=kt: e.activation(out=kt[:], in_=pT[:, 768:896], func=AF.Copy), reads=[pT], writes=[kt])
                            P.dve(lambda e, vp=vp, pbf=pbf: e.tensor_copy(out=vp[:, :, 0:64],
                                                                           in_=pbf[:, 128:256].rearrange("p (k d) -> p k d", k=2)),
                                   reads=[pbf], writes=[vp])
                        for kvh in range(2):
                            r = slice(kvh * 64, kvh * 64 + 64)
                            if j < NPG:
                                aq = augq[kvh] if j < 32 else augq2[kvh]
                                mms = [(kt[:, :], qTsz[kvh][:, s * 32:(s + 1) * 32], [kt, qTsz[kvh]], (0, 32)),
                                       (ckt[0:96, j * 128:(j + 1) * 128], aq[0:96, qc[0]:qc[0] + 32], [ckt, aq], (0, 32))]
                                unit(mms, 32, vp[:, kvh, :], [vp], obank[kvh], qc, j == 0, False)
                            else:
                                mms = [(kTn[:, 0, :], qTsz[kvh][:, s * 32:(s + 1) * 32], [kTn, qTsz[kvh]], (0, 32)),
                                       (ckt[0:4, PAST:PAST + 128], augq[kvh][0:4, qc[0]:qc[0] + 32], [ckt, augq[kvh]], (0, 32)),
                                       (identb[:], newmt[:, s, :], [identb, newmt], (0, 32))]
                                unit(mms, 32, vnw[:, 0, kvh, :], [vnw], obank[kvh], qc, False, True)
                                evac_seq(kvh, s)
                branch_epilogue(1, True, False)
                for s in range(SB_):
                    qc = (s * 32, 32)
                    for w in range(5):
                        if w < 4:
                            pf = pgf[npg[0] % 2]; pbf = pgb[npg[0] % 2]
                            npg[0] += 1
                            ld("sp", pf, pf[:], swin[s, w * 128:(w + 1) * 128, :])
                            P.dve(lambda e, pf=pf, pbf=pbf: e.tensor_copy(out=pbf[:], in_=pf[:]), reads=[pf], writes=[pbf])
                            kt = kTp[w % 2]; vp = vpg[w % 2]
                            P.pe(lambda e, pbf=pbf: e.transpose(out=pT[:, 768:896], in_=pbf[:, 0:128], identity=identb[:]),
                                 reads=[pbf, identb], writes=[pT])
                            P.act(lambda e, kt=kt: e.activation(out=kt[:], in_=pT[:, 768:896], func=AF.Copy), reads=[pT], writes=[kt])
                            P.dve(lambda e, vp=vp, pbf=pbf: e.tensor_copy(out=vp[:, :, 0:64],
                                                                           in_=pbf[:, 128:256].rearrange("p (k d) -> p k d", k=2)),
                                   reads=[pbf], writes=[vp])
                        for kvh in range(2):
                            r = slice(kvh * 64, kvh * 64 + 64)
                            if w < 4:
                                c0 = PAST - WBUF + w * 128
                                mms = [(kt[:, :], qTsz[kvh][:, s * 32:(s + 1) * 32], [kt, qTsz[kvh]], (0, 32)),
                                       (ckt[0:4, c0:c0 + 128], augq[kvh][0:4, qc[0]:qc[0] + 32], [ckt, augq[kvh]], (0, 32))]
                                if w == 0:
                                    mms.append((identb[:], wmst[:, :], [identb, wmst], (0, 32)))
                                unit(mms, 32, vp[:, kvh, :], [vp], obank[kvh], qc, w == 0, False)
                            else:
                                mms = [(kTn[:, 1, :], qTsz[kvh][:, s * 32:(s + 1) * 32], [kTn, qTsz[kvh]], (0, 32)),
                                       (ckt[0:4, PAST:PAST + 128], augq[kvh][0:4, qc[0]:qc[0] + 32], [ckt, augq[kvh]], (0, 32)),
                                       (identb[:], newmt[:, s, :], [identb, newmt], (0, 32))]
                                unit(mms, 32, vnw[:, 1, kvh, :], [vnw], obank[kvh], qc, False, True)
                                evac_seq(kvh, s)
                branch_epilogue(2, True, False)
                back(ti, x)
                P.flush()

        with ExitStack() as S2:
          if CFG["ffn"]:
            sb2 = mk(S2)
            wup_bf = sb2("wup_bf", [128, 8, D_FF], BF16)
            wdn_bf = sb2("wdn_bf", [128, 32, D], BF16)
            MOD2 = sb2("MOD2", [128, 3 * D])
            x1t = [sb2("x1t%d" % i, [128, D]) for i in range(2)]
            junk2 = sb2("junk2", [128, D], BF16)
            st2 = sb2("st2", [128, 8])
            tmp2 = sb2("tmp2", [128, D])
            h2bf = sb2("h2bf", [128, D], BF16); h2T = sb2("h2T", [128, D], BF16)
            hr = [sb2("hr%d" % i, [128, 512], BF16) for i in range(2)]
            hsq = sb2("hsq", [128, 32, 128], BF16)
            yt = sb2("yt", [128, D])
            with ExitStack() as S2a0:
                ada_rows(mk(S2a0), 3 * D, 2, 3)
                P.flush()
            with ExitStack() as S2a:
                sba = mk(S2a)
                stg2 = [sba("stg2_%d" % i, [128, 2048]) for i in range(2)]
                for kc in range(16):
                    s_ = stg2[kc % 2]
                    k8, hh = kc // 2, kc % 2
                    ld("sp" if kc % 2 == 0 else "pool", s_, s_[:], w_up[k8 * 128:(k8 + 1) * 128, hh * 2048:(hh + 1) * 2048])
                    if kc % 2 == 0:
                        P.dve(lambda e, s_=s_, k8=k8, hh=hh: e.tensor_copy(out=wup_bf[:, k8, hh * 2048:(hh + 1) * 2048], in_=s_[:]),
                              reads=[s_], writes=[wup_bf])
                    else:
                        P.act(lambda e, s_=s_, k8=k8, hh=hh: e.activation(out=wup_bf[:, k8, hh * 2048:(hh + 1) * 2048], in_=s_[:],
                                                                          func=AF.Copy), reads=[s_], writes=[wup_bf])
                for q4 in range(16):
                    s_ = stg2[q4 % 2]
                    ld("sp" if q4 % 2 == 0 else "pool", s_, s_[:].rearrange("p (a n) -> p a n", a=2),
                       w_down[q4 * 256:(q4 + 1) * 256, :].rearrange("(a p) n -> p a n", p=128))
                    if q4 % 2 == 0:
                        P.dve(lambda e, s_=s_, q4=q4: e.tensor_copy(out=wdn_bf[:, q4 * 2:(q4 + 1) * 2, :].rearrange("p a n -> p (a n)"),
                                                                    in_=s_[:]), reads=[s_], writes=[wdn_bf])
                    else:
                        P.act(lambda e, s_=s_, q4=q4: e.activation(out=wdn_bf[:, q4 * 2:(q4 + 1) * 2, :].rearrange("p a n -> p (a n)"),
                                                                   in_=s_[:], func=AF.Copy), reads=[s_], writes=[wdn_bf])
                P.flush()
            for ti in range(NTILE):
                kind = min(ti // NT_P, 2)
                if ti % NT_P == 0:
                    load_mod(MOD2, 3 * D, kind)
                xx = x1t[ti % 2]
                ld("sp", xx, xx[:], y_all[ti * 128:(ti + 1) * 128, :])
                rms_rstd(xx[:], [xx], junk2[:], junk2, st2, 0, D)
                P.dve(lambda e, xx=xx: e.scalar_tensor_tensor(out=tmp2[:], in0=xx[:], scalar=st2[:, 1:2], in1=MOD2[:, D:2 * D],
                                                              op0=ALU.mult, op1=ALU.mult), reads=[xx, st2, MOD2], writes=[tmp2])
                P.pool(lambda e: e.tensor_tensor(out=h2bf[:], in0=tmp2[:], in1=MOD2[:, 0:D], op=ALU.add),
                       reads=[tmp2, MOD2], writes=[h2bf])
                transpose8(h2bf, h2T)
                for f4 in range(8):
                    pbk = pb[2 + (f4 % 3)]
                    for ff in range(4):
                        fch = f4 * 4 + ff
                        for kc in range(8):
                            P.pe(lambda e, kc=kc, pbk=pbk, ff=ff, fch=fch: e.matmul(
                                pbk[:, ff * 128:(ff + 1) * 128], lhsT=wup_bf[:, kc, fch * 128:(fch + 1) * 128],
                                rhs=h2T[:, kc * 128:(kc + 1) * 128], start=(kc == 0), stop=(kc == 7)),
                                reads=[wup_bf, h2T], writes=[pbk])
                    hrr = hr[f4 % 2]
                    P.act(lambda e, pbk=pbk, hrr=hrr: e.activation(out=hrr[:], in_=pbk[:, :], func=AF.Relu), reads=[pbk], writes=[hrr])
                    if f4 % 2 == 0:
                        P.dve(lambda e, hrr=hrr, f4=f4: e.tensor_tensor(out=hsq[:, f4 * 4:(f4 + 1) * 4, :].rearrange("p a t -> p (a t)"),
                                                                        in0=hrr[:], in1=hrr[:], op=ALU.mult), reads=[hrr], writes=[hsq])
                    else:
                        P.pool(lambda e, hrr=hrr, f4=f4: e.tensor_tensor(out=hsq[:, f4 * 4:(f4 + 1) * 4, :].rearrange("p a t -> p (a t)"),
                                                                         in0=hrr[:], in1=hrr[:], op=ALU.mult), reads=[hrr], writes=[hsq])
                for nh in range(2):
                    pbk = pb[nh]
                    for fch in range(32):
                        P.pe(lambda e, fch=fch, pbk=pbk, nh=nh: e.matmul(pbk[:, :], lhsT=hsq[:, fch, :],
                                                                         rhs=wdn_bf[:, fch, nh * 512:(nh + 1) * 512],
                                                                         start=(fch == 0), stop=(fch == 31)),
                             reads=[hsq, wdn_bf], writes=[pbk])
                    P.act(lambda e, pbk=pbk, nh=nh: e.activation(out=junk2[:, 0:512], in_=pbk[:, :], func=AF.Square,
                                                                 accum_out=st2[:, 2 + nh:3 + nh]), reads=[pbk], writes=[st2, junk2])
                P.dve(lambda e: e.tensor_tensor(out=st2[:, 4:5], in0=st2[:, 2:3], in1=st2[:, 3:4], op=ALU.add), reads=[st2], writes=[st2])
                P.act(lambda e: e.activation(out=st2[:, 5:6], in_=st2[:, 4:5], func=AF.Sqrt, scale=1.0 / D, bias=epsb[:]),
                      reads=[st2, epsb], writes=[st2])
                P.dve(lambda e: e.reciprocal(out=st2[:, 5:6], in_=st2[:, 5:6]), reads=[st2], writes=[st2])
                for nh in range(2):
                    pbk = pb[nh]
                    P.dve(lambda e, pbk=pbk, nh=nh: e.scalar_tensor_tensor(
                        out=tmp2[:, nh * 512:(nh + 1) * 512], in0=pbk[:, :], scalar=st2[:, 5:6],
                        in1=MOD2[:, 2 * D + nh * 512:2 * D + (nh + 1) * 512], op0=ALU.mult, op1=ALU.mult),
                        reads=[pbk, st2, MOD2], writes=[tmp2])
                P.pool(lambda e, xx=xx: e.tensor_tensor(out=yt[:], in0=tmp2[:], in1=xx[:], op=ALU.add), reads=[tmp2, xx], writes=[yt])
                P.dma("sp", lambda e, ti=ti: e.dma_start(out=y_all[ti * 128:(ti + 1) * 128, :], in_=yt[:]),
                      reads=[yt], writes=[dy], sembuf=dy)
            P.flush()
    return nc


_NC = None


def _consts():
    f32 = np.float32
    c = {}
    c["ident_f"] = np.eye(128, dtype=f32)
    sel = np.zeros((3, 18, 128), f32)
    sel[0, 0, :] = 1.0
    sel[1, 1, :] = 1.0
    for m in range(128):
        sel[2, 2 + m // TS, m] = 1.0
    tp = np.arange(128)
    same = [np.ones((128, 128), bool), (tp[:, None] // TS == tp[None, :] // TS)]
    tri = np.zeros((2, 2, 128, 128), f32)
    causm = np.zeros((2, 128, 512), f32)
    for k in range(2):
        tri[k, 0] = ((tp[:, None] > tp[None, :]) & same[k]) * (-1.0 / 16)
        tri[k, 1] = ((tp[:, None] <= tp[None, :]) & same[k]) * (-1.0 / 16)
        causm[k] = np.tile(((tp[:, None] <= tp[None, :]) & same[k]).astype(f32), (1, 4))
    c["tri"] = tri
    c["causm"] = causm
    seqind = np.zeros((2, 128, 16), f32)
    seqind[0, :, 0] = -1.0 / 16
    seqmask = np.zeros((128, 16), f32)
    for s in range(16):
        seqind[1, s * TS:(s + 1) * TS, s] = -1.0 / 16
        seqmask[s * TS:(s + 1) * TS, s] = 1.0
    c["seqind"] = seqind
    c["seqmask"] = seqmask
    ck = np.zeros((96, NKCOL), f32)
    key = np.arange(PAST)
    ck[32 + (key // 64) % 64, key] = 1.0
    ck[0, :] = 1.0
    ck[1, :] = 1.0
    ck[2, :PAST] = key % 128
    ck[3, :PAST] = key - key % 128
    ck[4, :PAST] = 1.0
    m = np.arange(128)
    ck[2, PAST:] = m % TS
    ck[3, PAST:] = PAST
    c["ck"] = ck
    ckc = np.zeros((4, 512), f32)
    cend = 16 * np.arange(512) + 31
    ckc[0] = 1.0
    ckc[1] = 1.0
    ckc[2] = cend % 128
    ckc[3] = cend - cend % 128
    c["ckc"] = ckc
    slope = np.array([2.0 ** -(h + 1) for h in range(8)], f32).reshape(2, 4)
    qab = np.zeros((2, 2, 5, 512), f32)
    for kvh in range(2):
        for g in range(4):
            sl = slope[kvh, g]
            cols = slice(g * 128, (g + 1) * 128)
            qab[0, kvh, 0, cols] = -128.0 * sl
            qab[0, kvh, 1, cols] = -sl * np.arange(128)
            qab[0, kvh, 2, cols] = sl
            qab[0, kvh, 3, cols] = sl
            qab[0, kvh, 4, cols] = -NEG
            for s in range(16):
                cs = slice(s * 32 + g * 8, s * 32 + g * 8 + 8)
                qab[1, kvh, 0, cs] = -128.0 * 64 * sl
                qab[1, kvh, 1, cs] = -sl * np.arange(8)
                qab[1, kvh, 2, cs] = sl
                qab[1, kvh, 3, cs] = sl
                qab[1, kvh, 4, cs] = -NEG
    c["qab"] = qab
    maskc = np.zeros((17, 128, 128), f32)
    for r in range(17):
        maskc[r] = np.where(16 * tp[:, None] + 31 > 128 * r + tp[None, :], -NEG, 0.0)
    c["maskc"] = maskc
    maskcs = np.zeros((128, 32), f32)
    maskcs[127, :] = -NEG
    c["maskcs"] = maskcs

    def mmap(ncb, nsb):
        start = np.arange(ncb) * 16
        bs = np.arange(nsb) * 64
        ov = np.minimum(start[:, None] + 32, bs[None, :] + 64) - np.maximum(start[:, None], bs[None, :])
        return (np.clip(ov, 0, None) / 32).astype(f32)
    mp = np.zeros((256, 64), f32)
    mp[:255] = mmap(255, 64)
    c["mmap_p"] = np.ascontiguousarray(mp.reshape(2, 128, 64).transpose(1, 0, 2))
    ms = np.zeros((512, 128), f32)
    ms[:511] = mmap(511, 129)[:, :128]
    c["mmap_s"] = np.ascontiguousarray(ms.reshape(4, 128, 128).transpose(1, 0, 2))
    vmam_p = np.zeros((NT_P, 128, 2, 2, 64), f32)
    blk = np.arange(64)
    for it in range(NT_P):
        cur = (128 * it + tp) // 64
        valid = blk[None, :] <= cur[:, None]
        forced = valid & ((blk[None, :] == 0) | (blk[None, :] == cur[:, None]) | (blk[None, :] == cur[:, None] - 1))
        vm = (valid & ~forced).astype(f32)
        am = np.where(forced, 1.0e4, np.where(valid, 0.0, -1.0e4)).astype(f32)
        vmam_p[it, :, 0, :, :] = vm[:, None, :]
        vmam_p[it, :, 1, :, :] = am[:, None, :]
    c["vmam_p"] = vmam_p
    vmam_s = np.zeros((128, 2, 2, 128), f32)
    vmam_s[:, 0, :, :] = 1.0
    vmam_s[:, 0, :, 0] = 0.0
    vmam_s[:, 0, :, 127] = 0.0
    vmam_s[:, 1, :, 0] = 1.0e4
    vmam_s[:, 1, :, 127] = 1.0e4
    c["vmam_s"] = vmam_s
    caus4 = np.zeros((2, 128, 512), f32)
    caus4[0] = np.tile(np.where(tp[:, None] > tp[None, :], -NEG, 0.0), (1, 4))
    caus4[1] = np.tile(np.where(tp[:, None] < tp[None, :], -NEG, 0.0), (1, 4))
    c["caus4"] = caus4
    newmask = np.full((128, 16, 32), -NEG, f32)
    for s in range(16):
        for qk in range(TS):
            for g in range(4):
                for q in range(TS):
                    if qk <= q:
                        newmask[s * TS + qk, s, g * 8 + q] = 0.0
    c["newmask"] = newmask
    wm = np.zeros((128, 32), f32)
    for g in range(4):
        for q in range(TS):
            wm[:q, g * 8 + q] = -NEG
    c["wmasks"] = wm
    selp = np.zeros((128, 2, 64), f32)
    for p in range(64):
        for r in range(2):
            selp[2 * p + r, r, p] = 1.0
    c["selp"] = selp
    import ml_dtypes
    for k in ("causm", "ck", "ckc", "qab", "maskc", "maskcs", "mmap_p", "mmap_s", "caus4", "newmask", "wmasks", "selp"):
        c[k] = c[k].astype(ml_dtypes.bfloat16)
    return c


def kernel(x_prompt, x_sample, cache_kv, state_win, state_gla, page_table, c_prompt, c_sample,
           norm_mix_pre, norm_mix_post, norm_ffn_pre, norm_ffn_post, w_ada, b_ada, w_in,
           gla_w_gate, gla_b_gate, gla_norm, cmp_pos, cmp_w1, cmp_b1, cmp_w2, cmp_b2,
           w_out, w_up, w_down):
    global _NC
    f32 = np.float32
    if _NC is None:
        _NC = build_nc()
    nc = _NC
    A = lambda a: np.asarray(a, f32)
    x_prompt = A(x_prompt); x_sample = A(x_sample)
    consts = _consts()
    w1 = A(cmp_w1)[0].reshape(2, 2, 8, 2, 64, 256)
    w1p = np.ascontiguousarray(w1.transpose(3, 4, 0, 1, 2, 5).reshape(128, 2, 2, 8, 256))
    pos = A(cmp_pos)[0].reshape(2, 2, 8, 2, 64)
    posp = pos.transpose(3, 4, 0, 1, 2).reshape(128, 2, 2, 8)
    posrep = np.ascontiguousarray(np.repeat(posp[..., None], 16, axis=-1))
    w2 = A(cmp_w2)[0].reshape(2, 2, 128, 64)
    w2l = np.ascontiguousarray(w2.transpose(2, 0, 1, 3))
    b2 = A(cmp_b2)[0]
    shared = {
        "w_ada": A(w_ada)[0], "b_ada": A(b_ada)[0].reshape(1, -1),
        "gains": np.ascontiguousarray(np.stack([A(norm_mix_pre)[0], A(norm_mix_post)[0], A(norm_ffn_pre)[0], A(norm_ffn_post)[0]])),
        "w_in": A(w_in)[0], "w_out": A(w_out)[0], "w_up": A(w_up)[0], "w_down": A(w_down)[0],
        "wg": A(gla_w_gate)[0], "bg": A(gla_b_gate)[0].reshape(1, -1),
        "gnorm4": np.ascontiguousarray(np.tile(A(gla_norm)[0], 4).reshape(1, 512)),
        "cache": A(cache_kv)[0].reshape(-1, 256),
        "w1p": w1p, "posrep": posrep, "b1r": A(cmp_b1)[0], "w2l": w2l,
        "b2k": np.ascontiguousarray(np.concatenate([b2[0], b2[0]]).reshape(128, 1)), "b2v": b2[1].reshape(1, 64),
    }
    shared.update(consts)
    in_maps = []
    pt = np.asarray(page_table).astype(np.int32)
    for c in range(NCORES):
        xp = x_prompt[PB * c:PB * (c + 1)].reshape(PB * SEQ, D)
        xsm = x_sample[SB_ * c:SB_ * (c + 1)].reshape(SB_ * TS, D)
        cc = np.concatenate([A(c_prompt)[PB * c:PB * (c + 1)], A(c_sample)[SB_ * c:SB_ * (c + 1)]], axis=0)
        cTa = np.ascontiguousarray(cc.T.reshape(8, 128, 18).transpose(1, 0, 2))
        m = dict(shared)
        m.update({
            "xs": np.ascontiguousarray(np.concatenate([xp, xsm], axis=0)),
            "cT": cTa,
            "sgla": np.ascontiguousarray(A(state_gla)[0, SB_ * c:SB_ * (c + 1)]),
            "swin": np.ascontiguousarray(A(state_win)[0, SB_ * c:SB_ * (c + 1)].reshape(SB_, WBUF, 256)),
            "ptab": np.ascontiguousarray(pt[SB_ * c:SB_ * (c + 1)]),
        })
        in_maps.append(m)
    res = run_bass_kernel_spmd(nc, in_maps, core_ids=list(range(NCORES)))
    R = res.results
    NP = PB * SEQ
    y_p = np.concatenate([r["y_all"][:NP].reshape(PB, SEQ, D) for r in R], axis=0)
    y_s = np.concatenate([r["y_all"][NP:].reshape(SB_, TS, D) for r in R], axis=0)
    kv_p = np.concatenate([r["kvr"][:NP].reshape(PB, SEQ, 4, 2, 64) for r in R], axis=0)[None]
    kv_s = np.concatenate([r["kvr"][NP:].reshape(SB_, TS, 4, 2, 64) for r in R], axis=0)[None]
    wp = np.concatenate([r["win_p"].reshape(PB, WBUF, 2, 2, 64) for r in R], axis=0)[None]
    ws = np.concatenate([r["win_s"].reshape(SB_, WBUF, 2, 2, 64) for r in R], axis=0)[None]
    gp = np.concatenate([r["gla_p"] for r in R], axis=0)[None]
    gs = np.concatenate([r["gla_s"] for r in R], axis=0)[None]
    return (y_p.astype(f32), y_s.astype(f32), kv_p.astype(f32), kv_s.astype(f32),
            wp.astype(f32), ws.astype(f32), gp.astype(f32), gs.astype(f32))
```
